# Optimizing a Trainium2 kernel written in Bass

```python
import jax, jax.numpy as jnp
from jax import lax
import numpy as np

D_MODEL = 2048
BATCH = 4
SEQ = 4096
DEPTH = 1

N_META = 16
D_ATT = D_MODEL // 2
D_MLSTM = D_MODEL - D_ATT
ATT_HEADS = 8
QK_NOPE = 128
QK_ROPE = 64
V_HEAD = D_ATT // ATT_HEADS
Q_LORA = D_MODEL // 4
KV_LORA = D_MODEL // 8
MLSTM_HEADS = 4
MLSTM_HEAD = D_MLSTM // MLSTM_HEADS
CONV_K = 5
CHUNK = 64
META_PAD = (-N_META) % CHUNK
D_FF = 256 * ((8 * D_MODEL // 3 + 255) // 256)
D_IN = Q_LORA + KV_LORA + QK_ROPE + 2 * D_MLSTM
Q_BLOCK = 128
ROPE_BASE = 10000.0
LN_EPS = 1e-5
RMS_EPS = 1e-6
NEG = -1e30
ALPHA = (2 * DEPTH) ** 0.25
BETA = (8 * DEPTH) ** -0.25

kernel_name = 'hybrid_mla_mlstm_macaron_deepnorm_layer'


def _layer_norm(x, g, b):
    xf = x.astype(jnp.float32)
    mu = xf.mean(-1, keepdims=True)
    var = jnp.mean(jnp.square(xf - mu), -1, keepdims=True)
    return ((xf - mu) * lax.rsqrt(var + LN_EPS) * g + b).astype(x.dtype)


def _rms_norm(x, g):
    xf = x.astype(jnp.float32)
    return (xf * lax.rsqrt(jnp.mean(jnp.square(xf), -1, keepdims=True) + RMS_EPS) * g).astype(x.dtype)


def _swiglu(x, w_gate, w_up, w_down):
    return (jax.nn.silu(x @ w_gate) * (x @ w_up)) @ w_down


def _rope_tables(L):
    pos = jnp.arange(L, dtype=jnp.float32)
    inv = ROPE_BASE ** (-jnp.arange(0, QK_ROPE, 2, dtype=jnp.float32) / QK_ROPE)
    ang = pos[:, None] * inv[None, :]
    return jnp.cos(ang), jnp.sin(ang)


def _apply_rope(x, cos, sin):
    xf = x.astype(jnp.float32)
    x1, x2 = jnp.split(xf, 2, axis=-1)
    c, s = cos[:, None, :], sin[:, None, :]
    return jnp.concatenate([x1 * c - x2 * s, x2 * c + x1 * s], -1).astype(x.dtype)


def _mla(u_q, u_kv, u_kr, q_norm_g, w_uq, kv_norm_g, w_ukv, out_g, cos, sin):
    B, L, _ = u_q.shape
    H = ATT_HEADS
    q = (_rms_norm(u_q, q_norm_g) @ w_uq).reshape(B, L, H, QK_NOPE + QK_ROPE)
    q_nope, q_rope = q[..., :QK_NOPE], _apply_rope(q[..., QK_NOPE:], cos, sin)
    kv = (_rms_norm(u_kv, kv_norm_g) @ w_ukv).reshape(B, L, H, QK_NOPE + V_HEAD)
    k_nope, v = kv[..., :QK_NOPE], kv[..., QK_NOPE:]
    k_rope = _apply_rope(u_kr[:, :, None, :], cos, sin)[:, :, 0]
    scale = (QK_NOPE + QK_ROPE) ** -0.5
    n_blk = -(-L // Q_BLOCK)
    pad = n_blk * Q_BLOCK - L

    def to_blocks(a):
        a = jnp.pad(a, ((0, 0), (0, pad), (0, 0), (0, 0)))
        return a.reshape(B, n_blk, Q_BLOCK, H, a.shape[-1]).transpose(1, 0, 2, 3, 4)

    def block(args):
        qn_b, qr_b = args
        s = (jnp.einsum('bqhd,bkhd->bhqk', qn_b, k_nope)
             + jnp.einsum('bqhd,bkd->bhqk', qr_b, k_rope))
        p = jax.nn.softmax(s.astype(jnp.float32) * scale, axis=-1).astype(v.dtype)
        return jnp.einsum('bhqk,bkhd->bqhd', p, v)

    o = lax.map(block, (to_blocks(q_nope), to_blocks(q_rope)))
    o = o.transpose(1, 0, 2, 3, 4).reshape(B, n_blk * Q_BLOCK, H * V_HEAD)[:, :L]
    return _rms_norm(o, out_g)


def _mlstm_chunkwise(q, k, v, li, lf):
    B, H, T, dk = q.shape
    dv = v.shape[-1]
    nc = T // CHUNK
    mask = jnp.tril(jnp.ones((CHUNK, CHUNK), bool))

    def to_chunks(a):
        return jnp.moveaxis(a.reshape(B, H, nc, CHUNK, *a.shape[3:]), 2, 0)

    def step(carry, xs):
        C, n, m = carry
        qc, kc, vc, lic, lfc = xs
        b = jnp.cumsum(lfc, -1)
        D = jnp.where(mask, b[..., :, None] - b[..., None, :] + lic[..., None, :], NEG)
        m_inter = b + m[..., None]
        m_t = jnp.maximum(m_inter, D.max(-1))
        A = jnp.exp(D - m_t[..., None]) * jnp.einsum('bhtd,bhsd->bhts', qc, kc)
        w_inter = jnp.exp(m_inter - m_t)
        num = (jnp.einsum('bhts,bhsv->bhtv', A, vc)
               + w_inter[..., None] * jnp.einsum('bhtd,bhdv->bhtv', qc, C))
        den = A.sum(-1) + w_inter * jnp.einsum('bhtd,bhd->bht', qc, n)
        h = num / jnp.maximum(jnp.abs(den), jnp.exp(-m_t))[..., None]
        bL = b[..., -1]
        g = bL[..., None] - b + lic
        m_new = jnp.maximum(bL + m, g.max(-1))
        decay = jnp.exp(bL + m - m_new)
        wc = jnp.exp(g - m_new[..., None])
        C = decay[..., None, None] * C + jnp.einsum('bhs,bhsd,bhsv->bhdv', wc, kc, vc)
        n = decay[..., None] * n + jnp.einsum('bhs,bhsd->bhd', wc, kc)
        return (C, n, m_new), h

    init = (jnp.zeros((B, H, dk, dv), jnp.float32), jnp.zeros((B, H, dk), jnp.float32),
            jnp.zeros((B, H), jnp.float32))
    _, h = lax.scan(step, init, tuple(to_chunks(a) for a in (q, k, v, li, lf)))
    return jnp.moveaxis(h, 0, 2).reshape(B, H, T, dv)


def _centred_dwconv(x, w, b):
    K = w.shape[0]
    y = lax.conv_general_dilated(x, w[:, None, :].astype(x.dtype), window_strides=(1,),
                                 padding=[(K // 2, K // 2)], dimension_numbers=('NWC', 'WIO', 'NWC'),
                                 feature_group_count=x.shape[-1])
    return y + b


def _mlstm_mixer(x_m, z, conv_w, conv_b, w_q, w_k, w_v, w_gates, b_gates, gn_g, skip):
    B, L, _ = x_m.shape
    H, dh = MLSTM_HEADS, MLSTM_HEAD
    f32 = jnp.float32
    x_c = jax.nn.silu(_centred_dwconv(x_m, conv_w, conv_b))
    xc_h = x_c.reshape(B, L, H, dh)
    xm_h = x_m.reshape(B, L, H, dh)
    q = jnp.einsum('blhd,hde->bhle', xc_h, w_q).astype(f32)
    k = (jnp.einsum('blhd,hde->bhle', xc_h, w_k) * dh ** -0.5).astype(f32)
    v = jnp.einsum('blhd,hde->bhle', xm_h, w_v).astype(f32)
    g = (jnp.concatenate([x_c, x_m], -1) @ w_gates + b_gates).astype(f32)
    g = g.reshape(B, L, 4, H).transpose(2, 0, 3, 1)
    tpad = ((0, 0), (0, 0), (META_PAD, 0))
    pad4 = tpad + ((0, 0),)
    q, k, v = [jnp.pad(a, pad4) for a in (q, k, v)]
    li_f = jnp.pad(g[0], tpad, constant_values=NEG)
    lf_f = jnp.pad(jax.nn.log_sigmoid(g[1]), tpad)
    li_b = jnp.pad(g[2], tpad, constant_values=NEG)
    lf_b = jnp.pad(jax.nn.log_sigmoid(g[3]), tpad)
    h_f = _mlstm_chunkwise(q, k, v, li_f, lf_f)
    flip = lambda a: jnp.flip(a, axis=2)
    h_b = flip(_mlstm_chunkwise(flip(q), flip(k), flip(v), flip(li_b), flip(lf_b)))
    h = (h_f + h_b)[:, :, META_PAD:].transpose(0, 2, 1, 3)
    h = jax.nn.sigmoid(z.astype(f32)).reshape(B, L, H, dh) * h
    mu = h.mean(-1, keepdims=True)
    var = jnp.mean(jnp.square(h - mu), -1, keepdims=True)
    h = ((h - mu) * lax.rsqrt(var + LN_EPS)).reshape(B, L, D_MLSTM)
    return (h * gn_g + skip * x_c).astype(x_m.dtype)


def _normal(key, shape, scale):
    return scale * jax.random.normal(key, shape, jnp.float32)


def setup_inputs(seed: int = 0) -> dict:
    key = jax.random.key(seed)
    ks = iter(jax.random.split(key, 40))

    def w(shape, fan_in, mult=1.0):
        return _normal(next(ks), (DEPTH,) + shape, mult * fan_in ** -0.5)

    def gain(n):
        return 1.0 + _normal(next(ks), (DEPTH, n), 0.02)

    def bias(n):
        return _normal(next(ks), (DEPTH, n), 0.02)

    H = MLSTM_HEADS
    x = _normal(next(ks), (BATCH, SEQ, D_MODEL), 1.0)
    meta_tokens = _normal(next(ks), (N_META, D_MODEL), 1.0)
    ffn1_w_gate = w((D_MODEL, D_FF), D_MODEL)
    ffn1_w_up = w((D_MODEL, D_FF), D_MODEL)
    ffn1_w_down = w((D_FF, D_MODEL), D_FF, BETA)
    ln1_g, ln1_b = gain(D_MODEL), bias(D_MODEL)
    w_in = w((D_MODEL, D_IN), D_MODEL)
    mla_q_norm_g = gain(Q_LORA)
    mla_w_uq = w((Q_LORA, ATT_HEADS * (QK_NOPE + QK_ROPE)), Q_LORA)
    mla_kv_norm_g = gain(KV_LORA)
    mla_w_ukv = w((KV_LORA, ATT_HEADS * (QK_NOPE + V_HEAD)), KV_LORA)
    attn_out_g = gain(D_ATT)
    mlstm_conv_w = w((CONV_K, D_MLSTM), CONV_K)
    mlstm_conv_b = bias(D_MLSTM)
    mlstm_w_q = w((H, MLSTM_HEAD, MLSTM_HEAD), MLSTM_HEAD)
    mlstm_w_k = w((H, MLSTM_HEAD, MLSTM_HEAD), MLSTM_HEAD)
    mlstm_w_v = w((H, MLSTM_HEAD, MLSTM_HEAD), MLSTM_HEAD)
    mlstm_w_gates = w((2 * D_MLSTM, 4 * H), 2 * D_MLSTM)
    i_bias = _normal(next(ks), (DEPTH, 2, H), 0.1)
    f_bias = jnp.linspace(3.0, 6.0, H, dtype=jnp.float32)[None, None, :] + _normal(next(ks), (DEPTH, 2, H), 0.1)
    mlstm_b_gates = jnp.stack([i_bias, f_bias], axis=2).reshape(DEPTH, 4 * H)
    mlstm_gn_g = gain(D_MLSTM)
    mlstm_skip = gain(D_MLSTM)
    w_out = w((D_MODEL, D_MODEL), D_MODEL, BETA)
    ln2_g, ln2_b = gain(D_MODEL), bias(D_MODEL)
    ffn2_w_gate = w((D_MODEL, D_FF), D_MODEL)
    ffn2_w_up = w((D_MODEL, D_FF), D_MODEL)
    ffn2_w_down = w((D_FF, D_MODEL), D_FF, BETA)
    ln3_g, ln3_b = gain(D_MODEL), bias(D_MODEL)
    return {'x': x, 'meta_tokens': meta_tokens,
            'ffn1_w_gate': ffn1_w_gate, 'ffn1_w_up': ffn1_w_up, 'ffn1_w_down': ffn1_w_down,
            'ln1_g': ln1_g, 'ln1_b': ln1_b, 'w_in': w_in,
            'mla_q_norm_g': mla_q_norm_g, 'mla_w_uq': mla_w_uq, 'mla_kv_norm_g': mla_kv_norm_g,
            'mla_w_ukv': mla_w_ukv, 'attn_out_g': attn_out_g,
            'mlstm_conv_w': mlstm_conv_w, 'mlstm_conv_b': mlstm_conv_b, 'mlstm_w_q': mlstm_w_q,
            'mlstm_w_k': mlstm_w_k, 'mlstm_w_v': mlstm_w_v, 'mlstm_w_gates': mlstm_w_gates,
            'mlstm_b_gates': mlstm_b_gates, 'mlstm_gn_g': mlstm_gn_g, 'mlstm_skip': mlstm_skip,
            'w_out': w_out, 'ln2_g': ln2_g, 'ln2_b': ln2_b,
            'ffn2_w_gate': ffn2_w_gate, 'ffn2_w_up': ffn2_w_up, 'ffn2_w_down': ffn2_w_down,
            'ln3_g': ln3_g, 'ln3_b': ln3_b}


def reference(x, meta_tokens, ffn1_w_gate, ffn1_w_up, ffn1_w_down, ln1_g, ln1_b, w_in,
              mla_q_norm_g, mla_w_uq, mla_kv_norm_g, mla_w_ukv, attn_out_g,
              mlstm_conv_w, mlstm_conv_b, mlstm_w_q, mlstm_w_k, mlstm_w_v, mlstm_w_gates,
              mlstm_b_gates, mlstm_gn_g, mlstm_skip, w_out, ln2_g, ln2_b,
              ffn2_w_gate, ffn2_w_up, ffn2_w_down, ln3_g, ln3_b):
    B = x.shape[0]
    meta = jnp.broadcast_to(meta_tokens[None].astype(x.dtype), (B, N_META, x.shape[-1]))
    h = jnp.concatenate([meta, x], axis=1)
    L = h.shape[1]
    cos, sin = _rope_tables(L)
    o1 = Q_LORA
    o2 = o1 + KV_LORA
    o3 = o2 + QK_ROPE
    o4 = o3 + D_MLSTM
    for d in range(DEPTH):
        h = _layer_norm(ALPHA * h + 0.5 * _swiglu(h, ffn1_w_gate[d], ffn1_w_up[d], ffn1_w_down[d]), ln1_g[d], ln1_b[d])
        u = h @ w_in[d]
        y_att = _mla(u[..., :o1], u[..., o1:o2], u[..., o2:o3], mla_q_norm_g[d], mla_w_uq[d],
                     mla_kv_norm_g[d], mla_w_ukv[d], attn_out_g[d], cos, sin)
        y_mlstm = _mlstm_mixer(u[..., o3:o4], u[..., o4:], mlstm_conv_w[d], mlstm_conv_b[d],
                               mlstm_w_q[d], mlstm_w_k[d], mlstm_w_v[d], mlstm_w_gates[d],
                               mlstm_b_gates[d], mlstm_gn_g[d], mlstm_skip[d])
        y = jnp.concatenate([y_att, y_mlstm], axis=-1) @ w_out[d]
        h = _layer_norm(ALPHA * h + y, ln2_g[d], ln2_b[d])
        h = _layer_norm(ALPHA * h + 0.5 * _swiglu(h, ffn2_w_gate[d], ffn2_w_up[d], ffn2_w_down[d]), ln3_g[d], ln3_b[d])
    return h[:, N_META:]
```

```python
import numpy as np
import concourse.bass as bass
import concourse.mybir as mybir
from concourse.bass_utils import run_bass_kernel_spmd

F32 = mybir.dt.float32
BF16 = mybir.dt.bfloat16
AF = mybir.ActivationFunctionType
ALU = mybir.AluOpType

D = 2048
DFF = 5632
T = 2056
import os
NCORES = int(os.environ.get('KCORES', '8'))
ALPHA = 2.0 ** 0.25
LN_EPS = 1e-5
TILES = [(256 * i, 256) for i in range(7)] + [(1792, 264)]
HALVES = [TILES[0:4], TILES[4:8]]
CHUNKS = [(128 * i, 128) for i in range(16)] + [(2048, 8)]

ENG = ("pe", "act", "dve", "pool", "sp")


class Op:
    __slots__ = ("eng", "fn", "deps", "chan", "needed", "seq", "idx", "inc")

    def __init__(self, eng, fn, chan):
        self.eng, self.fn, self.chan = eng, fn, chan
        self.deps = set()
        self.inc = 16
        self.needed = False
        self.seq = None


class Prog:
    def __init__(self, nc):
        self.nc = nc
        self.ops = []
        self.last_w = {}
        self.readers = {}
        self.barrier_deps = set()
        self.last_eng = {}
        self.last_chan = {}
        self.chan_names = []
        self.chan_map = {}

    def op(self, eng, fn, reads=(), writes=(), chan=None, inc=16):
        if chan is not None:
            key = writes[0] if len(writes) else (reads[0] if len(reads) else ("anon", chan))
            if key not in self.chan_map:
                self.chan_map[key] = f"c{len(self.chan_map)}"
            chan = self.chan_map[key]
        o = Op(eng, fn, chan)
        o.inc = inc
        o.idx = len(self.ops)
        deps = set(self.barrier_deps)
        for k in reads:
            w = self.last_w.get(k)
            if w is not None:
                deps.add(w)
        for k in writes:
            w = self.last_w.get(k)
            if w is not None:
                deps.add(w)
            deps.update(self.readers.get(k, ()))
        for k in reads:
            self.readers.setdefault(k, []).append(o)
        for k in writes:
            self.last_w[k] = o
            self.readers[k] = []
        deps.discard(o)
        o.deps = deps
        self.ops.append(o)
        if chan is None:
            self.last_eng[eng] = o
        else:
            if chan not in self.last_chan:
                self.chan_names.append(chan)
            self.last_chan[chan] = o
            self.last_eng[eng] = o
        return o

    def barrier(self):
        self.barrier_deps = set(self.last_eng.values()) | set(self.last_chan.values())
        self.chan_map = {}
        self.last_w = {}
        self.readers = {}

    def emit(self, block_ctx_sems):
        nc = self.nc
        ops = self.ops
        for o in ops:
            for d in o.deps:
                if d.chan is None and d.eng == "pe" and o.eng == "pe" and o.chan is None:
                    continue
                d.needed = True
        eng_cnt = {e: 0 for e in ENG}
        chan_cnt = {}
        for o in ops:
            if o.chan is not None:
                chan_cnt[o.chan] = chan_cnt.get(o.chan, 0) + o.inc
                o.seq = chan_cnt[o.chan]
            elif o.needed:
                eng_cnt[o.eng] += 1
                o.seq = eng_cnt[o.eng]
        sems = block_ctx_sems
        per_eng = {e: [o for o in ops if o.eng == e] for e in ENG}

        def run(e, engine):
            waited = {}
            for o in per_eng[e]:
                need = {}
                for d in o.deps:
                    if d.chan is not None:
                        key = ("c", d.chan)
                    else:
                        if d.eng == "pe" and e == "pe" and o.chan is None:
                            continue
                        key = ("e", d.eng)
                    if d.seq > need.get(key, 0):
                        need[key] = d.seq
                for key, v in need.items():
                    if waited.get(key, 0) >= v:
                        continue
                    waited[key] = v
                    engine.wait_ge(sems[key], v)
                ins = o.fn(engine)
                if o.chan is not None:
                    ins.then_inc(sems[("c", o.chan)], o.inc)
                elif o.needed:
                    ins.then_inc(sems[("e", e)], 1)
        return run


def build_program(debug_out=None):
    nc = bass.Bass("TRN2", target_bir_lowering=False)
    P = Prog(nc)

    def din(name, shape, dt=F32):
        return nc.dram_tensor(name, list(shape), dt, kind="ExternalInput").ap()

    x_in = din("x", [T, D])
    ident = din("ident", [128, 128])
    w1g = din("w1g", [D, DFF]); w1u = din("w1u", [D, DFF]); w1d = din("w1d", [DFF, D])
    ln1g = din("ln1g", [D]); ln1b = din("ln1b", [D])
    out_ap = nc.dram_tensor("out", [T, D], F32, kind="ExternalOutput").ap()
    w_in = din("w_in", [D, 2944])
    gq_in = din("gq", [512]); gkv_in = din("gkv", [256]); gao_in = din("gao", [1024])
    w_uq = din("w_uq", [512, 2048])
    w_ukvk = din("w_ukvk", [256, 1024]); w_ukvv = din("w_ukvv", [256, 1024])
    cos2 = din("cos2", [64, T]); sin2 = din("sin2", [64, T])
    w_out = din("w_out", [D, D])
    ln2g = din("ln2g", [D]); ln2b = din("ln2b", [D]); ln3g = din("ln3g", [D]); ln3b = din("ln3b", [D])
    w2g = din("w2g", [D, DFF]); w2u = din("w2u", [D, DFF]); w2d = din("w2d", [DFF, D])
    conv_w = din("conv_w", [5, 1024]); conv_b = din("conv_b", [1024])
    w_q = din("w_q", [4, 256, 256]); w_k = din("w_k", [4, 256, 256]); w_v = din("w_v", [4, 256, 256])
    w_gates = din("w_gates", [D, 16]); b_gates = din("b_gates", [16])
    gn_g = din("gn_g", [1024]); skip_in = din("skip", [1024])
    om_in = din("om", [128, 2])
    mask_le = din("mask_le", [128, 3, 264]); mask_ge = din("mask_ge", [128, 3, 264])
    dumps = []

    def dram(name, shape, dt=F32):
        return nc.dram_tensor(name, list(shape), dt).ap()

    XT = nc.dram_tensor("XT_s", [16, 128, T], F32).ap()
    ZT = nc.dram_tensor("ZT_s", [16, 128, T], F32).ap()
    H1T = nc.dram_tensor("H1T_s", [16, 128, T], F32).ap()
    UQ = dram("UQ_s", [4, 128, T]); UKV = dram("UKV_s", [2, 128, T]); URR = dram("URR_s", [2, 64, T])
    ZM = dram("ZM_s", [8, 128, T]); NQT = dram("NQT_s", [4, 128, T])
    NGX = 11
    GXrows = [128] * 10 + [64]
    GX_t = [nc.dram_tensor(f"GX{i}_s", [GXrows[i], T], F32) for i in range(NGX)]
    GXA_t = [nc.dram_tensor(f"GXA{i}_s", [2 * GXrows[i], T], F32) for i in range(NGX)]

    def gx(i):
        return GX_t[i].ap()

    def gxa(b, i):
        return GXA_t[i].ap()[b * GXrows[i]:(b + 1) * GXrows[i], :]
    QALL = dram("QALL_s", [8, 256, T]); KNA = dram("KNA_s", [8, 128, 2 * T]); VTA = dram("VTA_s", [2 * T, 1024], BF16)
    OAT = dram("OAT_s", [8, 128, T]); YT = dram("YT_s", [16, 128, T])
    XCM = dram("XCM_s", [2, 16, 128, T]); GRAW = dram("GRAW_s", [2, 16, T]); RW = dram("RW_s", [2, 3, 4, 2 * T])
    QM = dram("QM_s", [4, 2, 128, 2 * T]); KM = dram("KM_s", [4, 2, 128, 2 * T]); VM = dram("VM_s", [4, 2 * T, 256], BF16)
    HM = dram("HM_s", [2, 8, 128, 2 * T]); HG = dram("HG_s", [8, 128, T]); XCO = dram("XCO_s", [8, 128, T])
    Z2T = dram("Z2T_s", [16, 128, T]); H2T = dram("H2T_s", [16, 128, T]); Z3T = dram("Z3T_s", [16, 128, T]); H3T = dram("H3T_s", [16, 128, T])

    arena_off = [16640]

    def sb(name, shape, dt, off=None):
        nbytes = int(np.prod(shape[1:])) * (2 if dt == BF16 else 4)
        if off is None:
            off = arena_off[0]
            off = (off + 63) // 64 * 64
            arena_off[0] = off + nbytes
        assert off + nbytes <= 228 * 1024, (name, off, nbytes)
        return nc.alloc_sbuf_tensor_at(name, list(shape), dt, offset=off)

    ident_sb = sb("ident_sb", [128, 128], F32)
    ones_bf = sb("ones_bf", [128, 128], BF16)
    lng = sb("lng", [128, 3, 16], F32)
    lnb = sb("lnb", [128, 3, 16], F32)
    gq_sb = sb("gq_sb", [128, 4], F32); gkv_sb = sb("gkv_sb", [128, 2], F32); gao_sb = sb("gao_sb", [128, 8], F32)
    gng_sb = sb("gng_sb", [128, 8], F32); skip_sb = sb("skip_sb", [128, 8], F32)
    cw_sb = sb("cw_sb", [128, 5, 8], F32); cb_sb = sb("cb_sb", [128, 8], F32)
    bg_sb = sb("bg_sb", [16, 1], F32); om_sb = sb("om_sb", [128, 2], F32)
    CONST_END = arena_off[0]

    psum = [nc.alloc_psum_tensor(f"ps{i}", [128, 512], F32) for i in range(8)]

    rr = {"i": 0}

    def dma_(out, in_, slow=False):
        if slow:
            return lambda e: e.dma_start(out=out, in_=in_, allow_slow_non_contiguous=True)
        return lambda e: e.dma_start(out=out, in_=in_)

    def evac_eng():
        rr["i"] += 1
        return "act" if rr["i"] % 2 else "dve"

    P.op("sp", lambda e: e.dma_start(out=ident_sb[:], in_=ident[:, :]), writes=["ident"], chan="const")
    P.op("dve", lambda e: e.memset(ones_bf[:], 1.0), writes=["ones"])
    P.op("sp", lambda e: e.dma_start(out=lng[:, 0, :], in_=ln1g.rearrange("(k p) -> p k", p=128),
                                     allow_slow_non_contiguous=True), writes=["lng0"], chan="const")
    P.op("sp", lambda e: e.dma_start(out=lnb[:, 0, :], in_=ln1b.rearrange("(k p) -> p k", p=128),
                                     allow_slow_non_contiguous=True), writes=["lnb0"], chan="const")

    for li_, (g_, b_) in enumerate([(ln2g, ln2b), (ln3g, ln3b)]):
        P.op("sp", dma_(lng[:, li_ + 1, :], g_.rearrange("(k p) -> p k", p=128), True), writes=[f"lng{li_ + 1}"], chan="const")
        P.op("sp", dma_(lnb[:, li_ + 1, :], b_.rearrange("(k p) -> p k", p=128), True), writes=[f"lnb{li_ + 1}"], chan="const")
    for dst_, src_ in [(gq_sb, gq_in), (gkv_sb, gkv_in), (gao_sb, gao_in), (gng_sb, gn_g), (skip_sb, skip_in), (cb_sb, conv_b)]:
        P.op("sp", dma_(dst_[:], src_.rearrange("(k p) -> p k", p=128), True), writes=["consts2"], chan="const")
    for j_ in range(5):
        P.op("sp", dma_(cw_sb[:, j_, :], conv_w[j_, :].rearrange("(k p) -> p k", p=128), True), writes=["consts2"], chan="const")
    P.op("sp", dma_(bg_sb[:], b_gates.rearrange("(p o) -> p o", o=1), True), writes=["consts2"], chan="const")
    P.op("sp", dma_(om_sb[:], om_in[:, :]), writes=["consts2"], chan="const")

    def phase_transpose_in():
        arena_off[0] = CONST_END
        xtok = [sb(f"p0_xtok{i}", [128, D], F32) for i in range(2)]
        stg = [sb(f"p0_stg{i}", [128, 16, 128], F32) for i in range(2)]
        for ci, (t0, nt) in enumerate(CHUNKS):
            b = ci % 2
            P.op("sp", lambda e, b=b, t0=t0, nt=nt: e.dma_start(out=xtok[b][0:nt, :], in_=x_in[t0:t0 + nt, :]),
                 writes=[("xtok", b)], chan=f"ld{b}")
            for g in range(4):
                pb = (ci * 4 + g) % 8
                for q in range(4):
                    f = g * 4 + q
                    P.op("pe", lambda e, b=b, nt=nt, f=f, pb=pb, q=q: e.transpose(
                        psum[pb][:, q * 128:q * 128 + nt], xtok[b][0:nt, f * 128:(f + 1) * 128], ident_sb[0:nt, 0:nt]),
                         reads=[("xtok", b), "ident"], writes=[("ps", pb)])
                en = evac_eng()
                if en == "act":
                    fn = lambda e, b=b, nt=nt, g=g, pb=pb: e.copy(
                        out=stg[b][:, g * 4:(g + 1) * 4, 0:nt],
                        in_=psum[pb][:, :].rearrange("p (q t) -> p q t", q=4)[:, :, 0:nt])
                else:
                    fn = lambda e, b=b, nt=nt, g=g, pb=pb: e.tensor_copy(
                        out=stg[b][:, g * 4:(g + 1) * 4, 0:nt],
                        in_=psum[pb][:, :].rearrange("p (q t) -> p q t", q=4)[:, :, 0:nt])
                P.op(en, fn, reads=[("ps", pb)], writes=[("stg", b, g)])
            P.op("sp", lambda e, b=b, t0=t0, nt=nt: e.dma_start(
                out=XT[:, :, t0:t0 + nt].rearrange("k p t -> p k t"), in_=stg[b][:, :, 0:nt]),
                 reads=[("stg", b, g) for g in range(4)], chan=f"st{b}")
        P.barrier()

    def phase_ffn(tag, XTin, ZTout, wg, wu, wd):
        arena_off[0] = CONST_END
        TH = 1032
        xT = sb(f"{tag}_xT", [128, 16, TH], BF16)
        hT = sb(f"{tag}_hT", [128, 44, TH], BF16)
        wgb = [sb(f"{tag}_wg{i}", [128, 16, 128], BF16) for i in range(2)]
        wub = [sb(f"{tag}_wu{i}", [128, 16, 128], BF16) for i in range(2)]
        wdb = [sb(f"{tag}_wd{i}", [128, 44, 256], BF16) for i in range(2)]
        sil = [sb(f"{tag}_sil{i}", [128, 264], F32) for i in range(2)]
        xres = [sb(f"{tag}_xres{i}", [128, 264], F32) for i in range(2)]
        zst = [sb(f"{tag}_zst{i}", [128, 264], F32) for i in range(2)]
        wg_v = wg.rearrange("(k p) m -> p k m", p=128)
        wu_v = wu.rearrange("(k p) m -> p k m", p=128)
        wd_v = wd.rearrange("(j p) m -> p j m", p=128)
        cnt = {"gu": 0, "dn": 0, "ev": 0}
        for hi, tiles in enumerate(HALVES):
            h0 = tiles[0][0]
            hn = sum(n for _, n in tiles)
            for kq in range(4):
                P.op("pool", lambda e, kq=kq, h0=h0, hn=hn: e.dma_start(
                    out=xT[:, kq * 4:(kq + 1) * 4, 0:hn],
                    in_=XTin[kq * 4:(kq + 1) * 4, :, h0:h0 + hn].rearrange("k p t -> p k t")),
                     writes=[("xT", kq)], chan="ldx")
            for j in range(44):
                b = cnt["gu"] % 2
                cnt["gu"] += 1
                P.op("pool", lambda e, b=b, j=j: e.dma_start(out=wgb[b][:], in_=wg_v[:, :, j * 128:(j + 1) * 128]),
                     writes=[("wg", b)], chan=f"ldw{b}")
                P.op("pool", lambda e, b=b, j=j: e.dma_start(out=wub[b][:], in_=wu_v[:, :, j * 128:(j + 1) * 128]),
                     writes=[("wu", b)], chan=f"ldu{b}")
                for ti, (t0, nt) in enumerate(tiles):
                    lt = t0 - h0
                    slot = cnt["ev"] % 4
                    cnt["ev"] += 1
                    pg, pu = psum[2 * slot], psum[2 * slot + 1]
                    for k in range(16):
                        P.op("pe", lambda e, b=b, k=k, lt=lt, nt=nt, pg=pg: e.matmul(
                            pg[:, 0:nt], wgb[b][:, k, :], xT[:, k, lt:lt + nt], start=(k == 0), stop=(k == 15)),
                             reads=[("wg", b), ("xT", k // 4)], writes=[("ps", 2 * slot)])
                    for k in range(16):
                        P.op("pe", lambda e, b=b, k=k, lt=lt, nt=nt, pu=pu: e.matmul(
                            pu[:, 0:nt], wub[b][:, k, :], xT[:, k, lt:lt + nt], start=(k == 0), stop=(k == 15)),
                             reads=[("wu", b), ("xT", k // 4)], writes=[("ps", 2 * slot + 1)])
                    sb_i = slot % 2
                    P.op("act", lambda e, sb_i=sb_i, nt=nt, pg=pg: e.activation(
                        out=sil[sb_i][:, 0:nt], in_=pg[:, 0:nt], func=AF.Silu),
                         reads=[("ps", 2 * slot)], writes=[("sil", sb_i)])
                    P.op("dve", lambda e, sb_i=sb_i, nt=nt, pu=pu, j=j, lt=lt: e.tensor_tensor(
                        out=hT[:, j, lt:lt + nt], in0=sil[sb_i][:, 0:nt], in1=pu[:, 0:nt], op=ALU.mult),
                         reads=[("sil", sb_i), ("ps", 2 * slot + 1)], writes=[("hT", j)])
            for ip in range(8):
                b = cnt["dn"] % 2
                cnt["dn"] += 1
                P.op("pool", lambda e, b=b, ip=ip: e.dma_start(out=wdb[b][:, 0:22, :], in_=wd_v[:, 0:22, ip * 256:(ip + 1) * 256]),
                     writes=[("wd", b, 0)], chan=f"ldw{b}")
                P.op("pool", lambda e, b=b, ip=ip: e.dma_start(out=wdb[b][:, 22:44, :], in_=wd_v[:, 22:44, ip * 256:(ip + 1) * 256]),
                     writes=[("wd", b, 1)], chan=f"ldw{b}")
                for ii in range(2):
                    i = ip * 2 + ii
                    for ti, (t0, nt) in enumerate(tiles):
                        lt = t0 - h0
                        slot = cnt["ev"] % 8
                        cnt["ev"] += 1
                        rb = slot % 2
                        pz = psum[slot]
                        P.op("sp", lambda e, rb=rb, i=i, t0=t0, nt=nt: e.dma_start(
                            out=xres[rb][:, 0:nt], in_=XTin[i, :, t0:t0 + nt]),
                             writes=[("xres", rb)], chan=f"ldr{rb}")
                        for j in range(44):
                            P.op("pe", lambda e, b=b, j=j, ii=ii, lt=lt, nt=nt, pz=pz: e.matmul(
                                pz[:, 0:nt], wdb[b][:, j, ii * 128:(ii + 1) * 128], hT[:, j, lt:lt + nt],
                                start=(j == 0), stop=(j == 43)),
                                 reads=[("wd", b, j // 22), ("hT", j)], writes=[("ps", slot)])
                        P.op("dve", lambda e, rb=rb, nt=nt, pz=pz: e.scalar_tensor_tensor(
                            out=zst[rb][:, 0:nt], in0=pz[:, 0:nt], scalar=0.5 / ALPHA, in1=xres[rb][:, 0:nt],
                            op0=ALU.mult, op1=ALU.add),
                             reads=[("ps", slot), ("xres", rb)], writes=[("zst", rb)])
                        P.op("sp", lambda e, rb=rb, i=i, t0=t0, nt=nt: e.dma_start(
                            out=ZTout[i, :, t0:t0 + nt], in_=zst[rb][:, 0:nt]),
                             reads=[("zst", rb)], chan=f"st{rb}")
        P.barrier()

    def phase_ln(tag, ZTin, HTout, li, final_out=None):
        arena_off[0] = CONST_END
        zt = [sb(f"{tag}_zt{i}", [128, 16, 264], F32) for i in range(2)]
        zb = [sb(f"{tag}_zb{i}", [128, 16, 264], BF16) for i in range(2)]
        zq = [sb(f"{tag}_zq{i}", [128, 16, 264], BF16) for i in range(2)]
        mean = [sb(f"{tag}_mean{i}", [128, 264], F32) for i in range(2)]
        msq = [sb(f"{tag}_msq{i}", [128, 264], F32) for i in range(2)]
        rstd = [sb(f"{tag}_rstd{i}", [128, 264], F32) for i in range(2)]
        ho = [sb(f"{tag}_ho{i}", [128, 16, 264], F32) for i in range(2)]
        eps = LN_EPS / (ALPHA * ALPHA)
        for ti, (t0, nt) in enumerate(TILES):
            b = ti % 2
            ps_s, ps_q = psum[2 * b], psum[2 * b + 1]
            P.op("sp", lambda e, b=b, t0=t0, nt=nt: e.dma_start(
                out=zt[b][:, :, 0:nt], in_=ZTin[:, :, t0:t0 + nt].rearrange("k p t -> p k t")),
                 writes=[("zt", b)], chan=f"ld{b}")
            P.op("act", lambda e, b=b, nt=nt: e.copy(out=zb[b][:, :, 0:nt], in_=zt[b][:, :, 0:nt]),
                 reads=[("zt", b)], writes=[("zb", b)])
            P.op("act", lambda e, b=b, nt=nt: e.activation(out=zq[b][:, :, 0:nt], in_=zt[b][:, :, 0:nt], func=AF.Square),
                 reads=[("zt", b)], writes=[("zq", b)])
            for k in range(16):
                P.op("pe", lambda e, b=b, k=k, nt=nt, ps_s=ps_s: e.matmul(
                    ps_s[:, 0:nt], ones_bf[:, :], zb[b][:, k, 0:nt], start=(k == 0), stop=(k == 15)),
                     reads=[("zb", b), "ones"], writes=[("ps", 2 * b)])
            for k in range(16):
                P.op("pe", lambda e, b=b, k=k, nt=nt, ps_q=ps_q: e.matmul(
                    ps_q[:, 0:nt], ones_bf[:, :], zq[b][:, k, 0:nt], start=(k == 0), stop=(k == 15)),
                     reads=[("zq", b), "ones"], writes=[("ps", 2 * b + 1)])
            P.op("dve", lambda e, b=b, nt=nt, ps_s=ps_s: e.tensor_scalar(
                out=mean[b][:, 0:nt], in0=ps_s[:, 0:nt], scalar1=1.0 / D, scalar2=None, op0=ALU.mult),
                 reads=[("ps", 2 * b)], writes=[("mean", b)])
            P.op("dve", lambda e, b=b, nt=nt: e.tensor_tensor(
                out=msq[b][:, 0:nt], in0=mean[b][:, 0:nt], in1=mean[b][:, 0:nt], op=ALU.mult),
                 reads=[("mean", b)], writes=[("msq", b)])
            P.op("dve", lambda e, b=b, nt=nt, ps_q=ps_q: e.scalar_tensor_tensor(
                out=rstd[b][:, 0:nt], in0=ps_q[:, 0:nt], scalar=1.0 / D, in1=msq[b][:, 0:nt],
                op0=ALU.mult, op1=ALU.subtract),
                 reads=[("ps", 2 * b + 1), ("msq", b)], writes=[("rstd", b)])
            P.op("act", lambda e, b=b, nt=nt: e.activation(
                out=msq[b][:, 0:nt], in_=rstd[b][:, 0:nt], func=AF.Sqrt, bias=eps, scale=1.0),
                 reads=[("rstd", b)], writes=[("msq", b)])
            P.op("dve", lambda e, b=b, nt=nt: e.reciprocal(out=rstd[b][:, 0:nt], in_=msq[b][:, 0:nt]),
                 reads=[("msq", b)], writes=[("rstd", b)])
            P.op("dve", lambda e, b=b, nt=nt: e.tensor_tensor(
                out=zt[b][:, :, 0:nt], in0=zt[b][:, :, 0:nt],
                in1=mean[b][:, 0:nt].unsqueeze(1).to_broadcast([128, 16, nt]),
                op=ALU.subtract),
                 reads=[("zt", b), ("mean", b)], writes=[("zt", b)])
            P.op("dve", lambda e, b=b, nt=nt: e.tensor_tensor(
                out=zt[b][:, :, 0:nt], in0=zt[b][:, :, 0:nt],
                in1=rstd[b][:, 0:nt].unsqueeze(1).to_broadcast([128, 16, nt]),
                op=ALU.mult),
                 reads=[("zt", b), ("rstd", b)], writes=[("zt", b)])
            for k in range(16):
                P.op("act", lambda e, b=b, k=k, nt=nt: e.activation(
                    out=ho[b][:, k, 0:nt], in_=zt[b][:, k, 0:nt], func=AF.Identity,
                    scale=lng[:, li, k:k + 1], bias=lnb[:, li, k:k + 1]),
                     reads=[("zt", b), f"lng{li}", f"lnb{li}"], writes=[("ho", b)])
            P.op("sp", lambda e, b=b, t0=t0, nt=nt: e.dma_start(
                out=HTout[:, :, t0:t0 + nt].rearrange("k p t -> p k t"), in_=ho[b][:, :, 0:nt]),
                 reads=[("ho", b)], chan=f"st{b}")
        P.barrier()

    def phase_transpose_out(HTin):
        arena_off[0] = CONST_END
        hin = [sb(f"po_in{i}", [128, 16, 128], F32) for i in range(2)]
        otok = [sb(f"po_tok{i}", [128, D], F32) for i in range(2)]
        for ci, (t0, nt) in enumerate(CHUNKS):
            b = ci % 2
            P.op("sp", lambda e, b=b, t0=t0, nt=nt: e.dma_start(
                out=hin[b][:, :, 0:nt], in_=HTin[:, :, t0:t0 + nt].rearrange("k p t -> p k t")),
                 writes=[("hin", b)], chan=f"ld{b}")
            for g in range(4):
                pb = (ci * 4 + g) % 8
                for q in range(4):
                    f = g * 4 + q
                    P.op("pe", lambda e, b=b, nt=nt, f=f, pb=pb, q=q: e.transpose(
                        psum[pb][0:nt, q * 128:(q + 1) * 128], hin[b][:, f, 0:nt], ident_sb[:, :]),
                         reads=[("hin", b), "ident"], writes=[("ps", pb)])
                en = evac_eng()
                if en == "act":
                    fn = lambda e, b=b, nt=nt, g=g, pb=pb: e.copy(out=otok[b][0:nt, g * 512:(g + 1) * 512], in_=psum[pb][0:nt, :])
                else:
                    fn = lambda e, b=b, nt=nt, g=g, pb=pb: e.tensor_copy(out=otok[b][0:nt, g * 512:(g + 1) * 512], in_=psum[pb][0:nt, :])
                P.op(en, fn, reads=[("ps", pb)], writes=[("otok", b, g)])
            P.op("sp", lambda e, b=b, t0=t0, nt=nt: e.dma_start(out=out_ap[t0:t0 + nt, :], in_=otok[b][0:nt, :]),
                 reads=[("otok", b, g) for g in range(4)], chan=f"st{b}")
        P.barrier()


    def dma(out, in_, slow=False):
        if slow:
            return lambda e: e.dma_start(out=out, in_=in_, allow_slow_non_contiguous=True)
        return lambda e: e.dma_start(out=out, in_=in_)

    def mm(out, lhsT, rhs, st, sp_):
        return lambda e: e.matmul(out, lhsT, rhs, start=st, stop=sp_)

    def fm(ap2d):
        return ap2d.rearrange("(k p) t -> k p t", p=128)

    def phase_linear(INap, nk, Wap, mchunks, OUTfn, toks=TILES, scale=1.0, bias_col=None, resid=None):
        arena_off[0] = CONST_END
        Tin = max(t0 + nt for t0, nt in toks)
        inT = sb("lin_in", [128, nk, Tin], BF16)
        wb = [sb(f"lin_w{i}", [128, nk, 128], BF16) for i in range(2)]
        st = [sb(f"lin_st{i}", [128, 264], F32) for i in range(4)]
        rs = [sb(f"lin_rs{i}", [128, 264], F32) for i in range(4)]
        if isinstance(INap, list):
            for k in range(nk):
                P.op("pool", dma(inT[:, k, 0:Tin], INap[k][:, 0:Tin]), writes=[("lin_in", k // 4)], chan="ldx")
        else:
            for kq in range(0, nk, 4):
                kn = min(4, nk - kq)
                P.op("pool", dma(inT[:, kq:kq + kn, 0:Tin], INap[kq:kq + kn, :, 0:Tin].rearrange("k p t -> p k t")),
                     writes=[("lin_in", kq // 4)], chan="ldx")
        Wv = Wap.rearrange("(k p) m -> p k m", p=128)
        cnt = 0
        for mi, (c0, mc) in enumerate(mchunks):
            b = mi % 2
            P.op("pool", dma(wb[b][:, :, 0:mc], Wv[:, :, c0:c0 + mc]), writes=[("lin_w", b)], chan=f"ldw{b}")
            for (t0, nt) in toks:
                slot = cnt % 8
                sti = cnt % 4
                cnt += 1
                if resid is not None:
                    P.op("sp", dma(rs[sti][0:mc, 0:nt], resid(mi, t0, nt)), writes=[("lin_rs", sti)], chan=f"ldr{sti}")
                for k in range(nk):
                    P.op("pe", mm(psum[slot][0:mc, 0:nt], wb[b][:, k, 0:mc], inT[:, k, t0:t0 + nt], k == 0, k == nk - 1),
                         reads=[("lin_w", b), ("lin_in", k // 4)], writes=[("ps", slot)])
                if resid is not None:
                    P.op("dve", lambda e, slot=slot, sti=sti, mc=mc, nt=nt: e.scalar_tensor_tensor(
                        out=st[sti][0:mc, 0:nt], in0=psum[slot][0:mc, 0:nt], scalar=scale, in1=rs[sti][0:mc, 0:nt],
                        op0=ALU.mult, op1=ALU.add), reads=[("ps", slot), ("lin_rs", sti)], writes=[("lin_st", sti)])
                elif bias_col is not None:
                    P.op("act", lambda e, slot=slot, sti=sti, mc=mc, nt=nt: e.activation(
                        out=st[sti][0:mc, 0:nt], in_=psum[slot][0:mc, 0:nt], func=AF.Identity, bias=bias_col[0:mc, 0:1], scale=scale),
                         reads=[("ps", slot), "consts2"], writes=[("lin_st", sti)])
                else:
                    en = evac_eng()
                    if en == "act":
                        P.op("act", lambda e, slot=slot, sti=sti, mc=mc, nt=nt: e.mul(
                            out=st[sti][0:mc, 0:nt], in_=psum[slot][0:mc, 0:nt], mul=scale),
                             reads=[("ps", slot)], writes=[("lin_st", sti)])
                    else:
                        P.op("dve", lambda e, slot=slot, sti=sti, mc=mc, nt=nt: e.tensor_scalar(
                            out=st[sti][0:mc, 0:nt], in0=psum[slot][0:mc, 0:nt], scalar1=scale, scalar2=None, op0=ALU.mult),
                             reads=[("ps", slot)], writes=[("lin_st", sti)])
                P.op("sp", dma(OUTfn(mi, t0, nt), st[sti][0:mc, 0:nt]), reads=[("lin_st", sti)], chan=f"st{sti}")
        P.barrier()

    def phase_tokproj(INap, nk, Wap, M, OUTap, chunks=CHUNKS):
        arena_off[0] = CONST_END
        Tin = max(t0 + nt for t0, nt in chunks)
        inT = sb("tp_in", [128, nk, Tin], BF16)
        wsb = sb("tp_w", [128, nk, M], BF16)
        vst = [sb(f"tp_st{i}", [128, M], BF16) for i in range(2)]
        if isinstance(INap, list):
            for k in range(nk):
                P.op("pool", dma(inT[:, k, 0:Tin], INap[k][:, 0:Tin]), writes=["tp_in"], chan="ldx")
        else:
            P.op("pool", dma(inT[:, :, 0:Tin], INap[:, :, 0:Tin].rearrange("k p t -> p k t")), writes=["tp_in"], chan="ldx")
        P.op("pool", dma(wsb[:], Wap.rearrange("(k p) m -> p k m", p=128)), writes=["tp_w"], chan="ldw0")
        cnt = 0
        for ci, (t0, nt) in enumerate(chunks):
            b = ci % 2
            for n0 in range(0, M, 512):
                w = min(512, M - n0)
                slot = cnt % 8
                cnt += 1
                for k in range(nk):
                    P.op("pe", mm(psum[slot][0:nt, 0:w], inT[:, k, t0:t0 + nt], wsb[:, k, n0:n0 + w], k == 0, k == nk - 1),
                         reads=["tp_in", "tp_w"], writes=[("ps", slot)])
                en = evac_eng()
                if en == "act":
                    P.op("act", lambda e, slot=slot, b=b, nt=nt, n0=n0, w=w: e.copy(out=vst[b][0:nt, n0:n0 + w], in_=psum[slot][0:nt, 0:w]),
                         reads=[("ps", slot)], writes=[("tp_st", b, n0)])
                else:
                    P.op("dve", lambda e, slot=slot, b=b, nt=nt, n0=n0, w=w: e.tensor_copy(out=vst[b][0:nt, n0:n0 + w], in_=psum[slot][0:nt, 0:w]),
                         reads=[("ps", slot)], writes=[("tp_st", b, n0)])
            P.op("sp", dma(OUTap[t0:t0 + nt, :], vst[b][0:nt, :]), reads=[("tp_st", b, n0) for n0 in range(0, M, 512)], chan=f"st{b}")
        P.barrier()

    def phase_norm(INfn, nch, gc, gcol, eps, OUTfn, center=False, addin=None):
        arena_off[0] = CONST_END
        xt = [sb(f"nm_x{i}", [128, nch, 264], F32) for i in range(2)]
        xb = [sb(f"nm_b{i}", [128, nch, 264], BF16) for i in range(2)]
        xq = [sb(f"nm_q{i}", [128, nch, 264], BF16) for i in range(2)]
        ng = nch // gc
        mean = [sb(f"nm_m{i}", [128, ng, 264], F32) for i in range(2)]
        msq = [sb(f"nm_s{i}", [128, ng, 264], F32) for i in range(2)]
        rstd = [sb(f"nm_r{i}", [128, ng, 264], F32) for i in range(2)]
        ho = [sb(f"nm_o{i}", [128, nch, 264], F32) for i in range(2)]
        addt = [sb(f"nm_a{i}", [128, nch, 264], F32) for i in range(2)] if addin is not None else None
        for ti, (t0, nt) in enumerate(TILES):
            b = ti % 2
            P.op("sp", dma(xt[b][:, :, 0:nt], INfn(t0, nt).rearrange("k p t -> p k t")), writes=[("nm_x", b)], chan=f"ld{b}")
            if addin is not None:
                P.op("sp", dma(addt[b][:, :, 0:nt], addin[0](t0, nt).rearrange("k p t -> p k t")), writes=[("nm_a", b)], chan=f"lda{b}")
            P.op("act", lambda e, b=b, nt=nt: e.activation(out=xq[b][:, :, 0:nt], in_=xt[b][:, :, 0:nt], func=AF.Square),
                 reads=[("nm_x", b)], writes=[("nm_q", b)])
            if center:
                P.op("act", lambda e, b=b, nt=nt: e.copy(out=xb[b][:, :, 0:nt], in_=xt[b][:, :, 0:nt]),
                     reads=[("nm_x", b)], writes=[("nm_b", b)])
            for g in range(ng):
                pq = psum[(2 * g) % 8]
                pm_ = psum[(2 * g + 1) % 8]
                for k in range(gc):
                    P.op("pe", mm(pq[:, 0:nt], ones_bf[:, :], xq[b][:, g * gc + k, 0:nt], k == 0, k == gc - 1),
                         reads=[("nm_q", b), "ones"], writes=[("ps", (2 * g) % 8)])
                if center:
                    for k in range(gc):
                        P.op("pe", mm(pm_[:, 0:nt], ones_bf[:, :], xb[b][:, g * gc + k, 0:nt], k == 0, k == gc - 1),
                             reads=[("nm_b", b), "ones"], writes=[("ps", (2 * g + 1) % 8)])
                    P.op("dve", lambda e, b=b, g=g, nt=nt, pm_=pm_: e.tensor_scalar(
                        out=mean[b][:, g, 0:nt], in0=pm_[:, 0:nt], scalar1=1.0 / (gc * 128), scalar2=None, op0=ALU.mult),
                         reads=[("ps", (2 * g + 1) % 8)], writes=[("nm_m", b, g)])
                    P.op("dve", lambda e, b=b, g=g, nt=nt: e.tensor_tensor(
                        out=msq[b][:, g, 0:nt], in0=mean[b][:, g, 0:nt], in1=mean[b][:, g, 0:nt], op=ALU.mult),
                         reads=[("nm_m", b, g)], writes=[("nm_s", b, g)])
                    P.op("dve", lambda e, b=b, g=g, nt=nt, pq=pq: e.scalar_tensor_tensor(
                        out=rstd[b][:, g, 0:nt], in0=pq[:, 0:nt], scalar=1.0 / (gc * 128), in1=msq[b][:, g, 0:nt],
                        op0=ALU.mult, op1=ALU.subtract),
                         reads=[("ps", (2 * g) % 8), ("nm_s", b, g)], writes=[("nm_r", b, g)])
                    P.op("act", lambda e, b=b, g=g, nt=nt: e.activation(
                        out=msq[b][:, g, 0:nt], in_=rstd[b][:, g, 0:nt], func=AF.Sqrt, bias=eps, scale=1.0),
                         reads=[("nm_r", b, g)], writes=[("nm_s", b, g)])
                else:
                    P.op("act", lambda e, b=b, g=g, nt=nt, pq=pq: e.activation(
                        out=msq[b][:, g, 0:nt], in_=pq[:, 0:nt], func=AF.Sqrt, bias=eps, scale=1.0 / (gc * 128)),
                         reads=[("ps", (2 * g) % 8)], writes=[("nm_s", b, g)])
                P.op("dve", lambda e, b=b, g=g, nt=nt: e.reciprocal(out=rstd[b][:, g, 0:nt], in_=msq[b][:, g, 0:nt]),
                     reads=[("nm_s", b, g)], writes=[("nm_r", b, g)])
                sl = slice(g * gc, (g + 1) * gc)
                if center:
                    P.op("dve", lambda e, b=b, g=g, nt=nt, sl=sl: e.tensor_tensor(
                        out=xt[b][:, sl, 0:nt], in0=xt[b][:, sl, 0:nt],
                        in1=mean[b][:, g, 0:nt].unsqueeze(1).to_broadcast([128, gc, nt]), op=ALU.subtract),
                         reads=[("nm_x", b), ("nm_m", b, g)], writes=[("nm_x", b)])
                P.op("dve", lambda e, b=b, g=g, nt=nt, sl=sl: e.tensor_tensor(
                    out=xt[b][:, sl, 0:nt], in0=xt[b][:, sl, 0:nt],
                    in1=rstd[b][:, g, 0:nt].unsqueeze(1).to_broadcast([128, gc, nt]), op=ALU.mult),
                     reads=[("nm_x", b), ("nm_r", b, g)], writes=[("nm_x", b)])
            for k in range(nch):
                P.op("act", lambda e, b=b, k=k, nt=nt: e.activation(
                    out=ho[b][:, k, 0:nt], in_=xt[b][:, k, 0:nt], func=AF.Copy, scale=gcol[:, k:k + 1]),
                     reads=[("nm_x", b), "consts2"], writes=[("nm_o", b)])
                if addin is not None:
                    P.op("dve", lambda e, b=b, k=k, nt=nt: e.scalar_tensor_tensor(
                        out=ho[b][:, k, 0:nt], in0=addt[b][:, k, 0:nt], scalar=addin[1][:, k:k + 1], in1=ho[b][:, k, 0:nt],
                        op0=ALU.mult, op1=ALU.add),
                         reads=[("nm_a", b), ("nm_o", b), "consts2"], writes=[("nm_o", b)])
            o_ = OUTfn(t0, nt)
            if isinstance(o_, list):
                for k in range(nch):
                    P.op("sp", dma(o_[k], ho[b][:, k, 0:nt]), reads=[("nm_o", b)], chan=f"st{b}")
            else:
                P.op("sp", dma(o_.rearrange("k p t -> p k t"), ho[b][:, :, 0:nt]), reads=[("nm_o", b)], chan=f"st{b}")
        P.barrier()


    def dump(name, ap):
        shp = list(ap.shape)
        o = nc.dram_tensor("dbg_" + name, shp, ap.dtype, kind="ExternalOutput").ap()
        dumps.append((o, ap))

    RMS_EPS = 1e-6
    WIN_CH = [(128 * i, 128) for i in range(22)] + [(2816, 64), (2880, 64)]

    def win_out(mi, t0, nt):
        if mi < 4:
            return UQ[mi, :, t0:t0 + nt]
        if mi < 6:
            return UKV[mi - 4, :, t0:t0 + nt]
        if mi < 14:
            return gx(2 + mi - 6)[:, t0:t0 + nt]
        if mi < 22:
            return ZM[mi - 14, :, t0:t0 + nt]
        return URR[mi - 22, :, t0:t0 + nt]

    def phase_krope():
        arena_off[0] = CONST_END
        a = sb("kr_a", [64, T], F32); b_ = sb("kr_b", [64, T], F32); c = sb("kr_c", [64, T], F32); d_ = sb("kr_d", [64, T], F32)
        P.op("sp", dma(a[:], URR[0, :, :]), writes=["kr_a"], chan="ld0")
        P.op("sp", dma(b_[:], URR[1, :, :]), writes=["kr_b"], chan="ld0")
        P.op("sp", dma(c[:], cos2[:, :]), writes=["kr_c"], chan="ld1")
        P.op("sp", dma(d_[:], sin2[:, :]), writes=["kr_d"], chan="ld1")
        P.op("dve", lambda e: e.tensor_tensor(out=a[:], in0=a[:], in1=c[:], op=ALU.mult), reads=["kr_a", "kr_c"], writes=["kr_a"])
        P.op("dve", lambda e: e.tensor_tensor(out=b_[:], in0=b_[:], in1=d_[:], op=ALU.mult), reads=["kr_b", "kr_d"], writes=["kr_b"])
        P.op("dve", lambda e: e.tensor_tensor(out=a[:], in0=a[:], in1=b_[:], op=ALU.add), reads=["kr_a", "kr_b"], writes=["kr_a"])
        P.op("sp", dma(gx(10)[:, :], a[:]), reads=["kr_a"], chan="st0")
        P.barrier()

    def phase_gather():
        groups = [[2 * i, 2 * i + 1] for i in range(NCORES // 2)]
        for i in range(NGX):
            P.op("pool", lambda e, i=i: e.collective_compute("AllGather", ALU.bypass, replica_groups=groups,
                                                             ins=[GX_t[i].ap().opt()], outs=[GXA_t[i].ap().opt()]),
                 chan="cc", writes=[("gather", i)], inc=1)
        P.barrier()

    ATT_SCALE = 192.0 ** -0.5
    KCH = [(b, c, b * T + t0, nt) for b in range(2) for c, (t0, nt) in enumerate(CHUNKS)]

    def phase_mla_attn():
        for hg in range(2):
            arena_off[0] = CONST_END
            QN = sb("at_qn", [128, 4, T], BF16)
            QR = sb("at_qr", [64, 4, T], BF16)
            KN = sb("at_kn", [128, 4, 2 * T], BF16)
            KR = sb("at_kr", [64, 2 * T], BF16)
            V = sb("at_v", [128, 34, 512], BF16)
            cs = sb("at_cos", [64, T], F32); sn = sb("at_sin", [64, T], F32)
            r1 = sb("at_r1", [64, T], F32); r2 = sb("at_r2", [64, T], F32)
            PT = [sb(f"at_p{i}", [128, 264], BF16) for i in range(3)]
            rsb = [sb(f"at_rs{i}", [128, 264], F32) for i in range(2)]
            ost = [sb(f"at_os{i}", [128, 264], F32) for i in range(2)]
            P.op("pool", dma(QN[:], QALL[hg * 4:(hg + 1) * 4, 0:128, :].rearrange("h p t -> p h t")), writes=["at_qn"], chan="ldx")
            P.op("pool", dma(KN[:], KNA[hg * 4:(hg + 1) * 4, :, :].rearrange("h p t -> p h t")), writes=["at_kn"], chan="ldx")
            for b in range(2):
                P.op("pool", dma(KR[:, b * T:(b + 1) * T], gxa(b, 10)), writes=["at_kr"], chan="ldx")
                P.op("sp", dma(V[:, b * 17:b * 17 + 16, :],
                               VTA[b * T:b * T + 2048, hg * 512:(hg + 1) * 512].rearrange("(c p) m -> p c m", p=128)),
                     writes=["at_v"], chan="ld0")
                P.op("sp", dma(V[0:8, b * 17 + 16, :], VTA[b * T + 2048:b * T + 2056, hg * 512:(hg + 1) * 512]),
                     writes=["at_v"], chan="ld0")
            P.op("sp", dma(cs[:], cos2[:, :]), writes=["at_cs"], chan="ld1")
            P.op("sp", dma(sn[:], sin2[:, :]), writes=["at_sn"], chan="ld1")
            for hl in range(4):
                h = hg * 4 + hl
                P.op("sp", dma(r1[:], QALL[h, 128:192, :]), writes=["at_r1"], chan="ld2")
                P.op("sp", dma(r2[:], QALL[h, 192:256, :]), writes=["at_r2"], chan="ld2")
                P.op("dve", lambda e: e.tensor_tensor(out=r1[:], in0=r1[:], in1=cs[:], op=ALU.mult), reads=["at_r1", "at_cs"], writes=["at_r1"])
                P.op("dve", lambda e: e.tensor_tensor(out=r2[:], in0=r2[:], in1=sn[:], op=ALU.mult), reads=["at_r2", "at_sn"], writes=["at_r2"])
                P.op("dve", lambda e, hl=hl: e.tensor_tensor(out=QR[:, hl, :], in0=r1[:], in1=r2[:], op=ALU.add),
                     reads=["at_r1", "at_r2"], writes=["at_qr"])
            cnt = 0
            for hl in range(4):
                h = hg * 4 + hl
                for qi, (t0, nt) in enumerate(TILES):
                    ob = 4 + 2 * (qi % 2)
                    po, pssum = psum[ob], psum[ob + 1]
                    for ci, (b, c, k0, nk) in enumerate(KCH):
                        sbk = cnt % 4
                        pi = cnt % 3
                        cnt += 1
                        P.op("pe", mm(psum[sbk][0:nk, 0:nt], KN[:, hl, k0:k0 + nk], QN[:, hl, t0:t0 + nt], True, False),
                             reads=["at_kn", "at_qn"], writes=[("ps", sbk)])
                        P.op("pe", mm(psum[sbk][0:nk, 0:nt], KR[:, k0:k0 + nk], QR[:, hl, t0:t0 + nt], False, True),
                             reads=["at_kr", "at_qr"], writes=[("ps", sbk)])
                        P.op("act", lambda e, sbk=sbk, pi=pi, nk=nk, nt=nt: e.activation(
                            out=PT[pi][0:nk, 0:nt], in_=psum[sbk][0:nk, 0:nt], func=AF.Exp, scale=ATT_SCALE),
                             reads=[("ps", sbk)], writes=[("at_p", pi)])
                        P.op("pe", mm(po[:, 0:nt], V[0:nk, ci, hl * 128:(hl + 1) * 128], PT[pi][0:nk, 0:nt], ci == 0, ci == 33),
                             reads=["at_v", ("at_p", pi)], writes=[("ps", ob)])
                        P.op("pe", mm(pssum[:, 0:nt], ones_bf[0:nk, :], PT[pi][0:nk, 0:nt], ci == 0, ci == 33),
                             reads=["ones", ("at_p", pi)], writes=[("ps", ob + 1)])
                    rb = qi % 2
                    P.op("dve", lambda e, rb=rb, nt=nt, pssum=pssum: e.reciprocal(out=rsb[rb][:, 0:nt], in_=pssum[:, 0:nt]),
                         reads=[("ps", ob + 1)], writes=[("at_rs", rb)])
                    P.op("dve", lambda e, rb=rb, nt=nt, po=po: e.tensor_tensor(
                        out=ost[rb][:, 0:nt], in0=po[:, 0:nt], in1=rsb[rb][:, 0:nt], op=ALU.mult),
                         reads=[("ps", ob), ("at_rs", rb)], writes=[("at_os", rb)])
                    P.op("sp", dma(OAT[h, :, t0:t0 + nt], ost[rb][:, 0:nt]), reads=[("at_os", rb)], chan=f"st{rb}")
            P.barrier()


    def phase_conv():
        arena_off[0] = CONST_END
        XMt = [sb(f"cv_x{i}", [128, T + 4], F32) for i in range(2)]
        acc = [sb(f"cv_a{i}", [128, T], F32) for i in range(2)]
        xco = [sb(f"cv_o{i}", [128, T], F32) for i in range(2)]
        n = 0
        for b in range(2):
            for k in range(8):
                i = n % 2
                n += 1
                src = gxa(b, 2 + k)
                oth = gxa(1 - b, 2 + k)
                P.op("dve", lambda e, i=i: e.memset(XMt[i][:, 0:2], 0.0), writes=[("cv_x", i)])
                P.op("sp", dma(XMt[i][:, 2:T + 2], src[:, :]), writes=[("cv_x", i)], chan="ld")
                P.op("sp", dma(XMt[i][:, T + 2:T + 3], oth[:, T - 1:T], True), writes=[("cv_x", i)], chan="ld")
                P.op("sp", dma(XMt[i][:, T + 3:T + 4], oth[:, T - 2:T - 1], True), writes=[("cv_x", i)], chan="ld")
                for j in range(5):
                    off = j if b == 0 else 4 - j
                    if j == 0:
                        P.op("dve", lambda e, i=i, k=k, off=off: e.tensor_scalar(
                            out=acc[i][:, :], in0=XMt[i][:, off:off + T], scalar1=cw_sb[:, 0, k:k + 1], scalar2=None, op0=ALU.mult),
                             reads=[("cv_x", i), "consts2"], writes=[("cv_a", i)])
                    else:
                        P.op("dve", lambda e, i=i, k=k, j=j, off=off: e.scalar_tensor_tensor(
                            out=acc[i][:, :], in0=XMt[i][:, off:off + T], scalar=cw_sb[:, j, k:k + 1], in1=acc[i][:, :],
                            op0=ALU.mult, op1=ALU.add),
                             reads=[("cv_x", i), ("cv_a", i), "consts2"], writes=[("cv_a", i)])
                P.op("act", lambda e, i=i, k=k: e.activation(out=xco[i][:, :], in_=acc[i][:, :], func=AF.Silu, bias=cb_sb[:, k:k + 1]),
                     reads=[("cv_a", i), "consts2"], writes=[("cv_o", i)])
                P.op("sp", dma(XCM[b, k, :, :], xco[i][:, :]), reads=[("cv_o", i)], chan="st")
                P.op("sp", dma(XCM[b, 8 + k, :, :], XMt[i][:, 2:T + 2]), reads=[("cv_x", i)], chan="st2")
        P.barrier()

    def phase_scans():
        arena_off[0] = CONST_END
        T2 = 2 * T
        LI = sb("sc_li", [36, T2], F32); LF = sb("sc_lf", [36, T2], F32); ONE = sb("sc_one", [36, T2], F32)
        Bt = sb("sc_b", [36, T2], F32); At = sb("sc_a", [36, T2], F32); Mt = sb("sc_m", [36, T2], F32); mt = sb("sc_mm", [36, T2], F32)
        P.op("dve", lambda e: e.memset(ONE[:, :], 1.0), writes=["sc_one"])
        for d in range(2):
            p0 = 32 * d
            ps_ = slice(p0, p0 + 4)
            for b in range(2):
                P.op("sp", dma(LI[ps_, b * T:(b + 1) * T], GRAW[b, 8 * d:8 * d + 4, :]), writes=[("sc_li", d)], chan="ld")
                P.op("sp", dma(LF[ps_, b * T:(b + 1) * T], GRAW[b, 8 * d + 4:8 * d + 8, :]), writes=[("sc_lf", d)], chan="ld")
            P.op("act", lambda e, ps_=ps_: e.activation(out=LF[ps_, :], in_=LF[ps_, :], func=AF.Exp, scale=-1.0),
                 reads=[("sc_lf", d)], writes=[("sc_lf", d)])
            P.op("act", lambda e, ps_=ps_: e.activation(out=LF[ps_, :], in_=LF[ps_, :], func=AF.Ln, bias=1.0, scale=1.0),
                 reads=[("sc_lf", d)], writes=[("sc_lf", d)])
            if d == 0:
                seg1 = lambda t, ps_=ps_: t[ps_, 0:T]
                seg2 = lambda t, ps_=ps_: t[ps_, T:T2][:, ::-1]
                last1 = lambda t, ps_=ps_: t[ps_, T - 1:T]
            else:
                seg1 = lambda t, ps_=ps_: t[ps_, T:T2]
                seg2 = lambda t, ps_=ps_: t[ps_, 0:T][:, ::-1]
                last1 = lambda t, ps_=ps_: t[ps_, T2 - 1:T2]
            P.op("dve", lambda e, seg1=seg1: e.tensor_tensor_scan(out=seg1(Bt), data0=seg1(ONE), data1=seg1(LF), initial=0.0,
                                                                  op0=ALU.mult, op1=ALU.subtract),
                 reads=[("sc_lf", d), "sc_one"], writes=[("sc_b", d)])
            P.op("dve", lambda e, seg2=seg2, last1=last1: e.tensor_tensor_scan(out=seg2(Bt), data0=seg2(ONE), data1=seg2(LF), initial=last1(Bt),
                                                                               op0=ALU.mult, op1=ALU.subtract),
                 reads=[("sc_lf", d), "sc_one", ("sc_b", d)], writes=[("sc_b", d)])
            P.op("dve", lambda e, ps_=ps_: e.tensor_tensor(out=At[ps_, :], in0=LI[ps_, :], in1=Bt[ps_, :], op=ALU.subtract),
                 reads=[("sc_li", d), ("sc_b", d)], writes=[("sc_a", d)])
            P.op("dve", lambda e, seg1=seg1: e.tensor_tensor_scan(out=seg1(Mt), data0=seg1(At), data1=seg1(At), initial=0.0,
                                                                  op0=ALU.max, op1=ALU.max),
                 reads=[("sc_a", d)], writes=[("sc_m", d)])
            P.op("dve", lambda e, seg2=seg2, last1=last1: e.tensor_tensor_scan(out=seg2(Mt), data0=seg2(At), data1=seg2(At), initial=last1(Mt),
                                                                               op0=ALU.max, op1=ALU.max),
                 reads=[("sc_a", d), ("sc_m", d)], writes=[("sc_m", d)])
            P.op("dve", lambda e, ps_=ps_: e.tensor_tensor(out=mt[ps_, :], in0=Bt[ps_, :], in1=Mt[ps_, :], op=ALU.add),
                 reads=[("sc_b", d), ("sc_m", d)], writes=[("sc_mm", d)])
            P.op("dve", lambda e, ps_=ps_: e.tensor_scalar(out=Mt[ps_, :], in0=Mt[ps_, :], scalar1=-1.0, scalar2=None, op0=ALU.mult),
                 reads=[("sc_m", d), ("sc_mm", d)], writes=[("sc_m", d)])
            P.op("dve", lambda e, ps_=ps_: e.tensor_scalar(out=mt[ps_, :], in0=mt[ps_, :], scalar1=-1.0, scalar2=None, op0=ALU.mult),
                 reads=[("sc_mm", d)], writes=[("sc_mm", d)])
            P.op("sp", dma(RW[d, 0, :, :], Mt[ps_, :]), reads=[("sc_m", d)], chan="st0")
            P.op("sp", dma(RW[d, 1, :, :], mt[ps_, :]), reads=[("sc_mm", d)], chan="st1")
            P.op("sp", dma(RW[d, 2, :, :], At[ps_, :]), reads=[("sc_a", d)], chan="st2")
        P.barrier()

    def phase_mix(h, d):
        arena_off[0] = CONST_END
        T2 = 2 * T
        Q = sb("mx_q", [128, 2, T2], BF16); Kt = sb("mx_k", [128, 2, T2], BF16); V = sb("mx_v", [128, 34, 256], BF16)
        nM = sb("mx_nm", [128, T2], F32); en = sb("mx_en", [128, T2], F32); acol = sb("mx_ac", [128, 34], F32)
        mle = sb("mx_le", [128, 3, 264], F32); mge = sb("mx_ge", [128, 3, 264], F32)
        es = [sb(f"mx_es{i}", [128, 264], F32) for i in range(3)]
        at = [sb(f"mx_at{i}", [128, 264], BF16) for i in range(3)]
        xs = [sb(f"mx_xs{i}", [128, 264], F32) for i in range(2)]
        dn = [sb(f"mx_dn{i}", [128, 264], F32) for i in range(2)]
        ost = [sb(f"mx_os{i}", [128, 2, 264], F32) for i in range(2)]
        P.op("pool", dma(Q[:], QM[h].rearrange("c p t -> p c t")), writes=["mx_q"], chan="ld")
        P.op("pool", dma(Kt[:], KM[h].rearrange("c p t -> p c t")), writes=["mx_k"], chan="ld")
        for b in range(2):
            P.op("sp", dma(V[:, b * 17:b * 17 + 16, :], VM[h, b * T:b * T + 2048, :].rearrange("(c p) m -> p c m", p=128)),
                 writes=[("mx_v", b)], chan="ld")
            P.op("sp", dma(V[0:8, b * 17 + 16, :], VM[h, b * T + 2048:b * T + 2056, :]), writes=[("mx_v", b, 1)], chan="ld")
            P.op("sp", dma(acol[:, b * 17:b * 17 + 16], RW[d, 2, h, b * T:b * T + 2048].rearrange("(c p) -> p c", p=128), True),
                 writes=[("mx_ac", b)], chan="ld")
            P.op("sp", dma(acol[0:8, b * 17 + 16:b * 17 + 17], RW[d, 2, h, b * T + 2048:b * T + 2056].rearrange("(p o) -> p o", o=1), True),
                 writes=[("mx_ac", b, 1)], chan="ld")
        P.op("sp", dma(nM[:], RW[d, 0, h:h + 1, :].partition_broadcast(128)), writes=["mx_nm"], chan="ld")
        P.op("sp", dma(en[:], RW[d, 1, h:h + 1, :].partition_broadcast(128)), writes=["mx_en"], chan="ld")
        P.op("sp", dma(mle[:], mask_le[:, :, :]), writes=["mx_le"], chan="ld")
        P.op("sp", dma(mge[:], mask_ge[:, :, :]), writes=["mx_ge"], chan="ld")
        P.op("act", lambda e: e.activation(out=en[:], in_=en[:], func=AF.Exp), reads=["mx_en"], writes=["mx_en"])
        vkeys = [("mx_v", 0), ("mx_v", 0, 1), ("mx_v", 1), ("mx_v", 1, 1)]
        akeys = [("mx_ac", 0), ("mx_ac", 0, 1), ("mx_ac", 1), ("mx_ac", 1, 1)]
        cnt = 0
        qn = 0
        for qb in range(2):
            rel = "le" if d == qb else "ge"
            for (t0, nt) in TILES:
                tq = qb * T + t0
                lst = []
                if rel == "ge":
                    ob = 1 - qb
                    for c, (s0, nk) in enumerate(CHUNKS):
                        lst.append((ob * 17 + c, ob * T + s0, nk, None))
                for c, (s0, nk) in enumerate(CHUNKS):
                    if rel == "le":
                        if s0 >= t0 + nt:
                            continue
                        mk = None if s0 < t0 else (mle, (s0 - t0) // 128)
                    else:
                        if s0 < t0:
                            continue
                        mk = None if s0 >= t0 + nt else (mge, (s0 - t0) // 128)
                    lst.append((qb * 17 + c, qb * T + s0, nk, mk))
                oi = qn % 2
                qn += 1
                pO0, pO1, pD = psum[4], psum[5], psum[6]
                for li_, (ci, k0, nk, mk) in enumerate(lst):
                    sbk = cnt % 4
                    pi = cnt % 3
                    xi = cnt % 2
                    cnt += 1
                    first, last = li_ == 0, li_ == len(lst) - 1
                    for c2 in range(2):
                        P.op("pe", mm(psum[sbk][0:nk, 0:nt], Kt[:, c2, k0:k0 + nk], Q[:, c2, tq:tq + nt], c2 == 0, c2 == 1),
                             reads=["mx_k", "mx_q"], writes=[("ps", sbk)])
                    if mk is None:
                        P.op("act", lambda e, pi=pi, nk=nk, nt=nt, tq=tq, ci=ci: e.activation(
                            out=es[pi][0:nk, 0:nt], in_=nM[0:nk, tq:tq + nt], func=AF.Exp, bias=acol[0:nk, ci:ci + 1]),
                             reads=["mx_nm"] + akeys, writes=[("mx_es", pi)])
                    else:
                        mt_, mj = mk
                        P.op("dve", lambda e, xi=xi, nk=nk, nt=nt, tq=tq, ci=ci, mt_=mt_, mj=mj: e.scalar_tensor_tensor(
                            out=xs[xi][0:nk, 0:nt], in0=nM[0:nk, tq:tq + nt], scalar=acol[0:nk, ci:ci + 1], in1=mt_[0:nk, mj, 0:nt],
                            op0=ALU.add, op1=ALU.min),
                             reads=["mx_nm", "mx_le", "mx_ge"] + akeys, writes=[("mx_xs", xi)])
                        P.op("act", lambda e, pi=pi, xi=xi, nk=nk, nt=nt: e.activation(
                            out=es[pi][0:nk, 0:nt], in_=xs[xi][0:nk, 0:nt], func=AF.Exp),
                             reads=[("mx_xs", xi)], writes=[("mx_es", pi)])
                    P.op("dve", lambda e, sbk=sbk, pi=pi, nk=nk, nt=nt: e.scalar_tensor_tensor(
                        out=at[pi][0:nk, 0:nt], in0=psum[sbk][0:nk, 0:nt], scalar=1.0 / 16.0, in1=es[pi][0:nk, 0:nt],
                        op0=ALU.mult, op1=ALU.mult),
                         reads=[("ps", sbk), ("mx_es", pi)], writes=[("mx_at", pi)])
                    P.op("pe", mm(pO0[:, 0:nt], V[0:nk, ci, 0:128], at[pi][0:nk, 0:nt], first, last),
                         reads=vkeys + [("mx_at", pi)], writes=[("ps", 4)])
                    P.op("pe", mm(pO1[:, 0:nt], V[0:nk, ci, 128:256], at[pi][0:nk, 0:nt], first, last),
                         reads=vkeys + [("mx_at", pi)], writes=[("ps", 5)])
                    P.op("pe", mm(pD[:, 0:nt], ones_bf[0:nk, :], at[pi][0:nk, 0:nt], first, last),
                         reads=["ones", ("mx_at", pi)], writes=[("ps", 6)])
                P.op("act", lambda e, oi=oi, nt=nt, pD=pD: e.activation(out=dn[oi][:, 0:nt], in_=pD[:, 0:nt], func=AF.Abs),
                     reads=[("ps", 6)], writes=[("mx_dn", oi)])
                P.op("dve", lambda e, oi=oi, nt=nt, tq=tq: e.tensor_tensor(
                    out=dn[oi][:, 0:nt], in0=dn[oi][:, 0:nt], in1=en[:, tq:tq + nt], op=ALU.max),
                     reads=[("mx_dn", oi), "mx_en"], writes=[("mx_dn", oi)])
                P.op("dve", lambda e, oi=oi, nt=nt: e.reciprocal(out=dn[oi][:, 0:nt], in_=dn[oi][:, 0:nt]),
                     reads=[("mx_dn", oi)], writes=[("mx_dn", oi)])
                P.op("dve", lambda e, oi=oi, nt=nt, pO0=pO0: e.tensor_tensor(out=ost[oi][:, 0, 0:nt], in0=pO0[:, 0:nt], in1=dn[oi][:, 0:nt], op=ALU.mult),
                     reads=[("ps", 4), ("mx_dn", oi)], writes=[("mx_os", oi)])
                P.op("dve", lambda e, oi=oi, nt=nt, pO1=pO1: e.tensor_tensor(out=ost[oi][:, 1, 0:nt], in0=pO1[:, 0:nt], in1=dn[oi][:, 0:nt], op=ALU.mult),
                     reads=[("ps", 5), ("mx_dn", oi)], writes=[("mx_os", oi)])
                P.op("sp", dma(HM[d, 2 * h:2 * h + 2, :, tq:tq + nt].rearrange("c p t -> p c t"), ost[oi][:, :, 0:nt]),
                     reads=[("mx_os", oi)], chan="st")
        P.barrier()

    def phase_combine():
        arena_off[0] = CONST_END
        ha = [sb(f"cb_a{i}", [128, 8, 264], F32) for i in range(2)]
        hb = [sb(f"cb_b{i}", [128, 8, 264], F32) for i in range(2)]
        zz = [sb(f"cb_z{i}", [128, 8, 264], F32) for i in range(2)]
        for ti, (t0, nt) in enumerate(TILES):
            i = ti % 2
            for blk in range(2):
                for d in range(2):
                    tgt = ha if (blk, d) == (0, 0) else hb
                    P.op("sp", dma(tgt[i][:, :, 0:nt], HM[d, :, :, blk * T + t0:blk * T + t0 + nt].rearrange("k p t -> p k t")),
                         writes=[("cb_a" if tgt is ha else "cb_b", i)], chan="ld")
                    if (blk, d) == (0, 0):
                        P.op("dve", lambda e, i=i, nt=nt: e.tensor_scalar(out=ha[i][:, :, 0:nt], in0=ha[i][:, :, 0:nt], scalar1=om_sb[:, 0:1],
                                                                          scalar2=None, op0=ALU.mult),
                             reads=[("cb_a", i), "consts2"], writes=[("cb_a", i)])
                    else:
                        P.op("dve", lambda e, i=i, nt=nt, blk=blk: e.scalar_tensor_tensor(
                            out=ha[i][:, :, 0:nt], in0=hb[i][:, :, 0:nt], scalar=om_sb[:, blk:blk + 1], in1=ha[i][:, :, 0:nt],
                            op0=ALU.mult, op1=ALU.add),
                             reads=[("cb_a", i), ("cb_b", i), "consts2"], writes=[("cb_a", i)])
            P.op("sp", dma(zz[i][:, :, 0:nt], ZM[:, :, t0:t0 + nt].rearrange("k p t -> p k t")), writes=[("cb_z", i)], chan="ld")
            P.op("act", lambda e, i=i, nt=nt: e.activation(out=zz[i][:, :, 0:nt], in_=zz[i][:, :, 0:nt], func=AF.Sigmoid),
                 reads=[("cb_z", i)], writes=[("cb_z", i)])
            P.op("dve", lambda e, i=i, nt=nt: e.tensor_tensor(out=ha[i][:, :, 0:nt], in0=ha[i][:, :, 0:nt], in1=zz[i][:, :, 0:nt], op=ALU.mult),
                 reads=[("cb_a", i), ("cb_z", i)], writes=[("cb_a", i)])
            P.op("sp", dma(HG[:, :, t0:t0 + nt].rearrange("k p t -> p k t"), ha[i][:, :, 0:nt]), reads=[("cb_a", i)], chan="st")
            for blk in range(2):
                P.op("sp", dma(zz[i][:, :, 0:nt] if blk == 0 else hb[i][:, :, 0:nt],
                               XCM[blk, 0:8, :, t0:t0 + nt].rearrange("k p t -> p k t")),
                     writes=[("cb_z", i) if blk == 0 else ("cb_b", i)], chan="ld")
            P.op("dve", lambda e, i=i, nt=nt: e.tensor_scalar(out=zz[i][:, :, 0:nt], in0=zz[i][:, :, 0:nt], scalar1=om_sb[:, 0:1],
                                                              scalar2=None, op0=ALU.mult),
                 reads=[("cb_z", i), "consts2"], writes=[("cb_z", i)])
            P.op("dve", lambda e, i=i, nt=nt: e.scalar_tensor_tensor(
                out=zz[i][:, :, 0:nt], in0=hb[i][:, :, 0:nt], scalar=om_sb[:, 1:2], in1=zz[i][:, :, 0:nt], op0=ALU.mult, op1=ALU.add),
                 reads=[("cb_z", i), ("cb_b", i), "consts2"], writes=[("cb_z", i)])
            P.op("sp", dma(XCO[:, :, t0:t0 + nt].rearrange("k p t -> p k t"), zz[i][:, :, 0:nt]), reads=[("cb_z", i)], chan="st2")
        P.barrier()

    def ph_mproj(h, b):
        xin = [XCM[b, 2 * h], XCM[b, 2 * h + 1]]
        phase_linear(xin, 2, w_q[h], [(0, 128), (128, 128)], lambda mi, t0, nt: QM[h, mi, :, b * T + t0:b * T + t0 + nt])
        phase_linear(xin, 2, w_k[h], [(0, 128), (128, 128)], lambda mi, t0, nt: KM[h, mi, :, b * T + t0:b * T + t0 + nt])
        phase_tokproj([XCM[b, 8 + 2 * h], XCM[b, 8 + 2 * h + 1]], 2, w_v[h], 256, VM[h, b * T:(b + 1) * T, :])

    UQ_CH = []
    for h in range(8):
        UQ_CH += [(h * 256, 128), (h * 256 + 128, 64), (h * 256 + 192, 64)]

    def uq_out(mi, t0, nt):
        h, j = mi // 3, mi % 3
        r0 = [0, 128, 192][j]
        rn = [128, 64, 64][j]
        return QALL[h, r0:r0 + rn, t0:t0 + nt]

    def ph_kv(b):
        phase_linear([gxa(b, 0), gxa(b, 1)], 2, w_ukvk, [(128 * h, 128) for h in range(8)],
                     lambda mi, t0, nt, b=b: KNA[mi, :, b * T + t0:b * T + t0 + nt])
        phase_tokproj([gxa(b, 0), gxa(b, 1)], 2, w_ukvv, 1024, VTA[b * T:(b + 1) * T, :])

    PH = [
        ("tin", phase_transpose_in),
        ("ffn1", lambda: phase_ffn("f1", XT, ZT, w1g, w1u, w1d)),
        ("ln1", lambda: phase_ln("l1", ZT, H1T, 0)),
        ("win", lambda: phase_linear(H1T, 16, w_in, WIN_CH, win_out)),
        ("nq", lambda: phase_norm(lambda t0, nt: UQ[:, :, t0:t0 + nt], 4, 4, gq_sb, RMS_EPS, lambda t0, nt: NQT[:, :, t0:t0 + nt])),
        ("nkv", lambda: phase_norm(lambda t0, nt: UKV[:, :, t0:t0 + nt], 2, 2, gkv_sb, RMS_EPS,
                                   lambda t0, nt: [gx(0)[:, t0:t0 + nt], gx(1)[:, t0:t0 + nt]])),
        ("krope", phase_krope),
        ("gather", phase_gather),
        ("uq", lambda: phase_linear(NQT, 4, w_uq, UQ_CH, uq_out)),
        ("kv0", lambda: ph_kv(0)),
        ("kv1", lambda: ph_kv(1)),
        ("attn", phase_mla_attn),
        ("nao", lambda: phase_norm(lambda t0, nt: OAT[:, :, t0:t0 + nt], 8, 8, gao_sb, RMS_EPS, lambda t0, nt: YT[0:8, :, t0:t0 + nt])),
        ("conv", phase_conv),
        ("gates0", lambda: phase_linear(XCM[0], 16, w_gates, [(0, 16)], lambda mi, t0, nt: GRAW[0, :, t0:t0 + nt], bias_col=bg_sb)),
        ("gates1", lambda: phase_linear(XCM[1], 16, w_gates, [(0, 16)], lambda mi, t0, nt: GRAW[1, :, t0:t0 + nt], bias_col=bg_sb)),
        ("scans", phase_scans),
    ]
    for h_ in range(4):
        for b_ in range(2):
            PH.append((f"mproj{h_}{b_}", lambda h_=h_, b_=b_: ph_mproj(h_, b_)))
    for h_ in range(4):
        for d_ in range(2):
            PH.append((f"mix{h_}{d_}", lambda h_=h_, d_=d_: phase_mix(h_, d_)))
    PH += [
        ("combine", phase_combine),
        ("gnorm", lambda: phase_norm(lambda t0, nt: HG[:, :, t0:t0 + nt], 8, 2, gng_sb, LN_EPS, lambda t0, nt: YT[8:16, :, t0:t0 + nt],
                                     center=True, addin=(lambda t0, nt: XCO[:, :, t0:t0 + nt], skip_sb))),
        ("wout", lambda: phase_linear(YT, 16, w_out, [(128 * i, 128) for i in range(16)], lambda mi, t0, nt: Z2T[mi, :, t0:t0 + nt],
                                      scale=1.0 / ALPHA, resid=lambda mi, t0, nt: H1T[mi, :, t0:t0 + nt])),
        ("ln2", lambda: phase_ln("l2", Z2T, H2T, 1)),
        ("ffn2", lambda: phase_ffn("f2", H2T, Z3T, w2g, w2u, w2d)),
        ("ln3", lambda: phase_ln("l3", Z3T, H3T, 2)),
    ]
    kstop = int(os.environ.get("KSTOP", "999"))
    kskip = os.environ.get("KSKIP", "").split(",")
    for i_, (nm_, f_) in enumerate(PH):
        if i_ < kstop and nm_ not in kskip:
            f_()
    if debug_out:
        for nm in debug_out:
            dump(nm, {"H1T": H1T, "UQ": UQ, "NQT": NQT, "GX0": gx(0), "GX10": gx(10), "GX2": gx(2), "GXA10": GXA_t[10].ap(), "QALL": QALL, "KNA": KNA, "VTA": VTA, "OAT": OAT, "YT": YT, "XCM": XCM, "GRAW": GRAW, "RW": RW, "HM": HM, "HG": HG, "H2T": H2T, "H3T": H3T, "Z2T": Z2T}[nm])
    for i_, (o_, a_) in enumerate(dumps):
        P.op("sp", dma(o_, a_), chan=f"st{i_ % 4}")
    phase_transpose_out(H3T)
    P.op("sp", lambda e: e.nop(), reads=[], writes=[])

    sem_names = [("e", e) for e in ENG] + [("c", c) for c in P.chan_names]
    assert len(sem_names) <= 95, len(sem_names)
    sems = {}
    for i, k in enumerate(sem_names):
        sems[k] = nc.alloc_semaphore(f"s{i}_{k[1]}")
    run = P.emit(sems)
    with nc.Block() as block:
        @block.tensor
        def _(e):
            run("pe", e)

        @block.scalar
        def _(e):
            run("act", e)

        @block.vector
        def _(e):
            run("dve", e)

        @block.gpsimd
        def _(e):
            run("pool", e)

        @block.sync
        def _(e):
            run("sp", e)
    return nc


_NC_CACHE = {}


def make_core_inputs(inputs):
    g = lambda k: np.ascontiguousarray(np.asarray(inputs[k], dtype=np.float32)[0])
    x = np.asarray(inputs["x"], dtype=np.float32)
    meta = np.asarray(inputs["meta_tokens"], dtype=np.float32)
    w_in = g("w_in")
    kr = w_in[:, 768:832]
    w_in_ext = np.ascontiguousarray(np.concatenate(
        [w_in[:, 0:768], w_in[:, 832:1856], w_in[:, 1856:2880], kr, kr[:, 32:], kr[:, :32]], axis=1))
    wuq = g("mla_w_uq")
    cols = []
    for h in range(8):
        b0 = h * 192
        cols += list(range(b0, b0 + 192)) + list(range(b0 + 160, b0 + 192)) + list(range(b0 + 128, b0 + 160))
    w_uq_ext = np.ascontiguousarray(wuq[:, cols])
    wukv = g("mla_w_ukv")
    kc = [h * 256 + j for h in range(8) for j in range(128)]
    vc = [h * 256 + 128 + j for h in range(8) for j in range(128)]
    w_ukvk = np.ascontiguousarray(wukv[:, kc])
    w_ukvv = np.ascontiguousarray(wukv[:, vc])
    inv = 10000.0 ** (-np.arange(0, 64, 2, dtype=np.float32) / 64.0)
    NEGM = -30000.0
    s_idx = np.arange(128)[:, None, None]
    j_idx = np.arange(3)[None, :, None]
    t_idx = np.arange(264)[None, None, :]
    mask_le = np.where(128 * j_idx + s_idx <= t_idx, 0.0, NEGM).astype(np.float32)
    mask_ge = np.where(128 * j_idx + s_idx >= t_idx, 0.0, NEGM).astype(np.float32)
    shared = {
        "ident": np.eye(128, dtype=np.float32),
        "w1g": g("ffn1_w_gate"), "w1u": g("ffn1_w_up"), "w1d": g("ffn1_w_down"),
        "ln1g": g("ln1_g"), "ln1b": g("ln1_b"),
        "w_in": w_in_ext, "gq": g("mla_q_norm_g"), "gkv": g("mla_kv_norm_g"), "gao": g("attn_out_g"),
        "w_uq": w_uq_ext, "w_ukvk": w_ukvk, "w_ukvv": w_ukvv,
        "w_out": g("w_out"), "ln2g": g("ln2_g"), "ln2b": g("ln2_b"), "ln3g": g("ln3_g"), "ln3b": g("ln3_b"),
        "w2g": g("ffn2_w_gate"), "w2u": g("ffn2_w_up"), "w2d": g("ffn2_w_down"),
        "conv_w": g("mlstm_conv_w"), "conv_b": g("mlstm_conv_b"),
        "w_q": g("mlstm_w_q"), "w_k": g("mlstm_w_k"), "w_v": g("mlstm_w_v"),
        "w_gates": g("mlstm_w_gates"), "b_gates": g("mlstm_b_gates"),
        "gn_g": g("mlstm_gn_g"), "skip": g("mlstm_skip"),
        "mask_le": mask_le, "mask_ge": mask_ge,
    }
    maps = []
    for c in range(NCORES):
        b, r = c // 2, c % 2
        if r == 0:
            xl = np.concatenate([meta, x[b, :T - 16]], axis=0)
            pos = np.arange(T, dtype=np.float32)
        else:
            xl = x[b, T - 16:][::-1]
            pos = (2 * T - 1 - np.arange(T)).astype(np.float32)
        ang = (pos[None, :] * inv[:, None]).astype(np.float32)
        cs, sn = np.cos(ang), np.sin(ang)
        om = np.zeros((128, 2), np.float32)
        om[:, r] = 1.0
        m = dict(shared)
        m["x"] = np.ascontiguousarray(xl)
        m["cos2"] = np.ascontiguousarray(np.concatenate([cs, cs], 0).astype(np.float32))
        m["sin2"] = np.ascontiguousarray(np.concatenate([-sn, sn], 0).astype(np.float32))
        m["om"] = om
        maps.append(m)
    return maps


def assemble(outs):
    B = NCORES // 2
    res = np.empty((B, 4096, D), np.float32)
    for c in range(NCORES):
        b, r = c // 2, c % 2
        o = outs[c]
        if r == 0:
            res[b, :T - 16] = o[16:]
        else:
            res[b, T - 16:] = o[::-1]
    return res


def kernel(**inputs):
    if "nc" not in _NC_CACHE:
        _NC_CACHE["nc"] = build_program()
    nc = _NC_CACHE["nc"]
    maps = make_core_inputs(inputs)
    res = run_bass_kernel_spmd(nc, maps, core_ids=list(range(NCORES)))
    outs = [np.asarray(r["out"]) for r in res.results]
    return assemble(outs)
```

```python
import numpy as np
import concourse.bass as bass
import concourse.mybir as mybir
from concourse.bass_utils import run_bass_kernel_spmd

F32 = mybir.dt.float32
BF16 = mybir.dt.bfloat16
AF = mybir.ActivationFunctionType
ALU = mybir.AluOpType

D = 2048
DFF = 5632
T = 2056
import os
NCORES = int(os.environ.get('KCORES', '8'))
ALPHA = 2.0 ** 0.25
LN_EPS = 1e-5
TILES = [(256 * i, 256) for i in range(7)] + [(1792, 264)]
HALVES = [TILES[0:4], TILES[4:8]]
CHUNKS = [(128 * i, 128) for i in range(16)] + [(2048, 8)]
ATILES = [(0, 512), (512, 512), (1024, 512), (1536, 256), (1792, 264)]

ENG = ("pe", "act", "dve", "pool", "sp")


class Op:
    __slots__ = ("eng", "fn", "deps", "chan", "needed", "seq", "idx", "inc")

    def __init__(self, eng, fn, chan):
        self.eng, self.fn, self.chan = eng, fn, chan
        self.deps = set()
        self.inc = 16
        self.needed = False
        self.seq = None


class Prog:
    def __init__(self, nc):
        self.nc = nc
        self.ops = []
        self.last_w = {}
        self.readers = {}
        self.barrier_deps = set()
        self.last_eng = {}
        self.last_chan = {}
        self.chan_names = []
        self.chan_map = {}

    def op(self, eng, fn, reads=(), writes=(), chan=None, inc=16):
        if chan is not None:
            key = writes[0] if len(writes) else (reads[0] if len(reads) else ("anon", chan))
            if key not in self.chan_map:
                self.chan_map[key] = f"c{len(self.chan_map)}"
            chan = self.chan_map[key]
        o = Op(eng, fn, chan)
        o.inc = inc
        o.idx = len(self.ops)
        deps = set(self.barrier_deps)
        for k in reads:
            w = self.last_w.get(k)
            if w is not None:
                deps.add(w)
        for k in writes:
            w = self.last_w.get(k)
            if w is not None:
                deps.add(w)
            deps.update(self.readers.get(k, ()))
        for k in reads:
            self.readers.setdefault(k, []).append(o)
        for k in writes:
            self.last_w[k] = o
            self.readers[k] = []
        deps.discard(o)
        o.deps = deps
        self.ops.append(o)
        if chan is None:
            self.last_eng[eng] = o
        else:
            if chan not in self.last_chan:
                self.chan_names.append(chan)
            self.last_chan[chan] = o
            self.last_eng[eng] = o
        return o

    def barrier(self):
        self.barrier_deps = set(self.last_eng.values()) | set(self.last_chan.values())
        self.chan_map = {}
        self.last_w = {}
        self.readers = {}

    def emit(self, block_ctx_sems):
        nc = self.nc
        ops = self.ops
        for o in ops:
            for d in o.deps:
                if d.chan is None and d.eng == "pe" and o.eng == "pe" and o.chan is None:
                    continue
                d.needed = True
        eng_cnt = {e: 0 for e in ENG}
        chan_cnt = {}
        for o in ops:
            if o.chan is not None:
                chan_cnt[o.chan] = chan_cnt.get(o.chan, 0) + o.inc
                o.seq = chan_cnt[o.chan]
            elif o.needed:
                eng_cnt[o.eng] += 1
                o.seq = eng_cnt[o.eng]
        sems = block_ctx_sems
        per_eng = {e: [o for o in ops if o.eng == e] for e in ENG}

        def run(e, engine):
            waited = {}
            for o in per_eng[e]:
                need = {}
                for d in o.deps:
                    if d.chan is not None:
                        key = ("c", d.chan)
                    else:
                        if d.eng == "pe" and e == "pe" and o.chan is None:
                            continue
                        key = ("e", d.eng)
                    if d.seq > need.get(key, 0):
                        need[key] = d.seq
                for key, v in need.items():
                    if waited.get(key, 0) >= v:
                        continue
                    waited[key] = v
                    engine.wait_ge(sems[key], v)
                ins = o.fn(engine)
                if o.chan is not None:
                    ins.then_inc(sems[("c", o.chan)], o.inc)
                elif o.needed:
                    ins.then_inc(sems[("e", e)], 1)
        return run


def build_program(debug_out=None):
    nc = bass.Bass("TRN2", target_bir_lowering=False)
    P = Prog(nc)

    def din(name, shape, dt=F32):
        return nc.dram_tensor(name, list(shape), dt, kind="ExternalInput").ap()

    x_in = din("x", [T, D])
    ident = din("ident", [128, 128])
    w1g = din("w1g", [D, DFF]); w1u = din("w1u", [D, DFF]); w1d = din("w1d", [DFF, D])
    ln1g = din("ln1g", [D]); ln1b = din("ln1b", [D])
    out_ap = nc.dram_tensor("out", [T, D], F32, kind="ExternalOutput").ap()
    w_in = din("w_in", [D, 2944])
    gq_in = din("gq", [512]); gkv_in = din("gkv", [256]); gao_in = din("gao", [1024])
    w_uq = din("w_uq", [512, 2048])
    w_ukvk = din("w_ukvk", [256, 1024]); w_ukvv = din("w_ukvv", [256, 1024])
    cos2 = din("cos2", [64, T]); sin2 = din("sin2", [64, T])
    w_out = din("w_out", [D, D])
    ln2g = din("ln2g", [D]); ln2b = din("ln2b", [D]); ln3g = din("ln3g", [D]); ln3b = din("ln3b", [D])
    w2g = din("w2g", [D, DFF]); w2u = din("w2u", [D, DFF]); w2d = din("w2d", [DFF, D])
    conv_w = din("conv_w", [5, 1024]); conv_b = din("conv_b", [1024])
    w_q = din("w_q", [4, 256, 256]); w_k = din("w_k", [4, 256, 256]); w_v = din("w_v", [4, 256, 256])
    w_gates = din("w_gates", [D, 16]); b_gates = din("b_gates", [16])
    gn_g = din("gn_g", [1024]); skip_in = din("skip", [1024])
    om_in = din("om", [128, 2])
    mask_le = din("mask_le", [128, 4, 512]); mask_ge = din("mask_ge", [128, 4, 512])
    dumps = []

    def dram(name, shape, dt=F32):
        return nc.dram_tensor(name, list(shape), dt).ap()

    XT = nc.dram_tensor("XT_s", [16, 128, T], F32).ap()
    ZT = nc.dram_tensor("ZT_s", [16, 128, T], F32).ap()
    H1T = nc.dram_tensor("H1T_s", [16, 128, T], F32).ap()
    UQ = dram("UQ_s", [4, 128, T]); UKV = dram("UKV_s", [2, 128, T]); URR = dram("URR_s", [2, 64, T])
    ZM = dram("ZM_s", [8, 128, T]); NQT = dram("NQT_s", [4, 128, T])
    NGX = 11
    GXrows = [128] * 10 + [64]
    GX_t = [nc.dram_tensor(f"GX{i}_s", [GXrows[i], T], F32) for i in range(NGX)]
    GXA_t = [nc.dram_tensor(f"GXA{i}_s", [2 * GXrows[i], T], F32) for i in range(NGX)]

    def gx(i):
        return GX_t[i].ap()

    def gxa(b, i):
        return GXA_t[i].ap()[b * GXrows[i]:(b + 1) * GXrows[i], :]
    QALL = dram("QALL_s", [8, 256, T]); KNA = dram("KNA_s", [8, 128, 2 * T]); VTA = dram("VTA_s", [2 * T, 1024], BF16)
    OAT = dram("OAT_s", [8, 128, T]); YT = dram("YT_s", [16, 128, T])
    XCM = dram("XCM_s", [2, 16, 128, T]); GRAW = dram("GRAW_s", [2, 16, T]); RW = dram("RW_s", [2, 3, 4, 2 * T])
    QM = dram("QM_s", [4, 2, 128, 2 * T]); KM = dram("KM_s", [4, 2, 128, 2 * T]); VM = dram("VM_s", [4, 2 * T, 256], BF16)
    HM = dram("HM_s", [2, 8, 128, 2 * T]); HG = dram("HG_s", [8, 128, T]); XCO = dram("XCO_s", [8, 128, T])
    Z2T = dram("Z2T_s", [16, 128, T]); H2T = dram("H2T_s", [16, 128, T]); Z3T = dram("Z3T_s", [16, 128, T]); H3T = dram("H3T_s", [16, 128, T])

    arena_off = [16640]

    def sb(name, shape, dt, off=None):
        nbytes = int(np.prod(shape[1:])) * (2 if dt == BF16 else 4)
        if off is None:
            off = arena_off[0]
            off = (off + 63) // 64 * 64
            arena_off[0] = off + nbytes
        assert off + nbytes <= 228 * 1024, (name, off, nbytes)
        return nc.alloc_sbuf_tensor_at(name, list(shape), dt, offset=off)

    ident_sb = sb("ident_sb", [128, 128], F32)
    ones_bf = sb("ones_bf", [128, 128], BF16)
    lng = sb("lng", [128, 3, 16], F32)
    lnb = sb("lnb", [128, 3, 16], F32)
    gq_sb = sb("gq_sb", [128, 4], F32); gkv_sb = sb("gkv_sb", [128, 2], F32); gao_sb = sb("gao_sb", [128, 8], F32)
    gng_sb = sb("gng_sb", [128, 8], F32); skip_sb = sb("skip_sb", [128, 8], F32)
    cw_sb = sb("cw_sb", [128, 5, 8], F32); cb_sb = sb("cb_sb", [128, 8], F32)
    bg_sb = sb("bg_sb", [16, 1], F32); om_sb = sb("om_sb", [128, 2], F32)
    CONST_END = arena_off[0]

    psum = [nc.alloc_psum_tensor(f"ps{i}", [128, 512], F32) for i in range(8)]

    rr = {"i": 0}

    def dma_(out, in_, slow=False):
        if slow:
            return lambda e: e.dma_start(out=out, in_=in_, allow_slow_non_contiguous=True)
        return lambda e: e.dma_start(out=out, in_=in_)

    def evac_eng():
        rr["i"] += 1
        return "act" if rr["i"] % 2 else "dve"

    P.op("sp", lambda e: e.dma_start(out=ident_sb[:], in_=ident[:, :]), writes=["ident"], chan="const")
    P.op("dve", lambda e: e.memset(ones_bf[:], 1.0), writes=["ones"])
    P.op("sp", lambda e: e.dma_start(out=lng[:, 0, :], in_=ln1g.rearrange("(k p) -> p k", p=128),
                                     allow_slow_non_contiguous=True), writes=["lng0"], chan="const")
    P.op("sp", lambda e: e.dma_start(out=lnb[:, 0, :], in_=ln1b.rearrange("(k p) -> p k", p=128),
                                     allow_slow_non_contiguous=True), writes=["lnb0"], chan="const")

    for li_, (g_, b_) in enumerate([(ln2g, ln2b), (ln3g, ln3b)]):
        P.op("sp", dma_(lng[:, li_ + 1, :], g_.rearrange("(k p) -> p k", p=128), True), writes=[f"lng{li_ + 1}"], chan="const")
        P.op("sp", dma_(lnb[:, li_ + 1, :], b_.rearrange("(k p) -> p k", p=128), True), writes=[f"lnb{li_ + 1}"], chan="const")
    for dst_, src_ in [(gq_sb, gq_in), (gkv_sb, gkv_in), (gao_sb, gao_in), (gng_sb, gn_g), (skip_sb, skip_in), (cb_sb, conv_b)]:
        P.op("sp", dma_(dst_[:], src_.rearrange("(k p) -> p k", p=128), True), writes=["consts2"], chan="const")
    for j_ in range(5):
        P.op("sp", dma_(cw_sb[:, j_, :], conv_w[j_, :].rearrange("(k p) -> p k", p=128), True), writes=["consts2"], chan="const")
    P.op("sp", dma_(bg_sb[:], b_gates.rearrange("(p o) -> p o", o=1), True), writes=["consts2"], chan="const")
    P.op("sp", dma_(om_sb[:], om_in[:, :]), writes=["consts2"], chan="const")

    def phase_transpose_in():
        arena_off[0] = CONST_END
        xtok = [sb(f"p0_xtok{i}", [128, D], F32) for i in range(2)]
        stg = [sb(f"p0_stg{i}", [128, 16, 128], F32) for i in range(2)]
        for ci, (t0, nt) in enumerate(CHUNKS):
            b = ci % 2
            P.op("sp", lambda e, b=b, t0=t0, nt=nt: e.dma_start(out=xtok[b][0:nt, :], in_=x_in[t0:t0 + nt, :]),
                 writes=[("xtok", b)], chan=f"ld{b}")
            for g in range(4):
                pb = (ci * 4 + g) % 8
                for q in range(4):
                    f = g * 4 + q
                    P.op("pe", lambda e, b=b, nt=nt, f=f, pb=pb, q=q: e.transpose(
                        psum[pb][:, q * 128:q * 128 + nt], xtok[b][0:nt, f * 128:(f + 1) * 128], ident_sb[0:nt, 0:nt]),
                         reads=[("xtok", b), "ident"], writes=[("ps", pb)])
                en = evac_eng()
                if en == "act":
                    fn = lambda e, b=b, nt=nt, g=g, pb=pb: e.copy(
                        out=stg[b][:, g * 4:(g + 1) * 4, 0:nt],
                        in_=psum[pb][:, :].rearrange("p (q t) -> p q t", q=4)[:, :, 0:nt])
                else:
                    fn = lambda e, b=b, nt=nt, g=g, pb=pb: e.tensor_copy(
                        out=stg[b][:, g * 4:(g + 1) * 4, 0:nt],
                        in_=psum[pb][:, :].rearrange("p (q t) -> p q t", q=4)[:, :, 0:nt])
                P.op(en, fn, reads=[("ps", pb)], writes=[("stg", b, g)])
            P.op("sp", lambda e, b=b, t0=t0, nt=nt: e.dma_start(
                out=XT[:, :, t0:t0 + nt].rearrange("k p t -> p k t"), in_=stg[b][:, :, 0:nt]),
                 reads=[("stg", b, g) for g in range(4)], chan=f"st{b}")
        P.barrier()

    def phase_ffn(tag, XTin, ZTout, wg, wu, wd):
        arena_off[0] = CONST_END
        TH = 1032
        xT = sb(f"{tag}_xT", [128, 16, TH], BF16)
        hT = sb(f"{tag}_hT", [128, 44, TH], BF16)
        wgb = [sb(f"{tag}_wg{i}", [128, 16, 128], BF16) for i in range(2)]
        wub = [sb(f"{tag}_wu{i}", [128, 16, 128], BF16) for i in range(2)]
        wdb = [sb(f"{tag}_wd{i}", [128, 44, 256], BF16) for i in range(2)]
        sil = [sb(f"{tag}_sil{i}", [128, 264], F32) for i in range(2)]
        xres = [sb(f"{tag}_xres{i}", [128, 264], F32) for i in range(2)]
        zst = [sb(f"{tag}_zst{i}", [128, 264], F32) for i in range(2)]
        wg_v = wg.rearrange("(k p) m -> p k m", p=128)
        wu_v = wu.rearrange("(k p) m -> p k m", p=128)
        wd_v = wd.rearrange("(j p) m -> p j m", p=128)
        cnt = {"gu": 0, "dn": 0, "ev": 0}
        for hi, tiles in enumerate(HALVES):
            h0 = tiles[0][0]
            hn = sum(n for _, n in tiles)
            for kq in range(4):
                P.op("pool", lambda e, kq=kq, h0=h0, hn=hn: e.dma_start(
                    out=xT[:, kq * 4:(kq + 1) * 4, 0:hn],
                    in_=XTin[kq * 4:(kq + 1) * 4, :, h0:h0 + hn].rearrange("k p t -> p k t")),
                     writes=[("xT", kq)], chan="ldx")
            for j in range(44):
                b = cnt["gu"] % 2
                cnt["gu"] += 1
                P.op("pool", lambda e, b=b, j=j: e.dma_start(out=wgb[b][:], in_=wg_v[:, :, j * 128:(j + 1) * 128]),
                     writes=[("wg", b)], chan=f"ldw{b}")
                P.op("pool", lambda e, b=b, j=j: e.dma_start(out=wub[b][:], in_=wu_v[:, :, j * 128:(j + 1) * 128]),
                     writes=[("wu", b)], chan=f"ldu{b}")
                for ti, (t0, nt) in enumerate(tiles):
                    lt = t0 - h0
                    slot = cnt["ev"] % 4
                    cnt["ev"] += 1
                    pg, pu = psum[2 * slot], psum[2 * slot + 1]
                    for k in range(16):
                        P.op("pe", lambda e, b=b, k=k, lt=lt, nt=nt, pg=pg: e.matmul(
                            pg[:, 0:nt], wgb[b][:, k, :], xT[:, k, lt:lt + nt], start=(k == 0), stop=(k == 15)),
                             reads=[("wg", b), ("xT", k // 4)], writes=[("ps", 2 * slot)])
                    for k in range(16):
                        P.op("pe", lambda e, b=b, k=k, lt=lt, nt=nt, pu=pu: e.matmul(
                            pu[:, 0:nt], wub[b][:, k, :], xT[:, k, lt:lt + nt], start=(k == 0), stop=(k == 15)),
                             reads=[("wu", b), ("xT", k // 4)], writes=[("ps", 2 * slot + 1)])
                    sb_i = slot % 2
                    P.op("act", lambda e, sb_i=sb_i, nt=nt, pg=pg: e.activation(
                        out=sil[sb_i][:, 0:nt], in_=pg[:, 0:nt], func=AF.Silu),
                         reads=[("ps", 2 * slot)], writes=[("sil", sb_i)])
                    P.op("dve", lambda e, sb_i=sb_i, nt=nt, pu=pu, j=j, lt=lt: e.tensor_tensor(
                        out=hT[:, j, lt:lt + nt], in0=sil[sb_i][:, 0:nt], in1=pu[:, 0:nt], op=ALU.mult),
                         reads=[("sil", sb_i), ("ps", 2 * slot + 1)], writes=[("hT", j)])
            for ip in range(8):
                b = cnt["dn"] % 2
                cnt["dn"] += 1
                P.op("pool", lambda e, b=b, ip=ip: e.dma_start(out=wdb[b][:, 0:22, :], in_=wd_v[:, 0:22, ip * 256:(ip + 1) * 256]),
                     writes=[("wd", b, 0)], chan=f"ldw{b}")
                P.op("pool", lambda e, b=b, ip=ip: e.dma_start(out=wdb[b][:, 22:44, :], in_=wd_v[:, 22:44, ip * 256:(ip + 1) * 256]),
                     writes=[("wd", b, 1)], chan=f"ldw{b}")
                for ii in range(2):
                    i = ip * 2 + ii
                    for ti, (t0, nt) in enumerate(tiles):
                        lt = t0 - h0
                        slot = cnt["ev"] % 8
                        cnt["ev"] += 1
                        rb = slot % 2
                        pz = psum[slot]
                        P.op("sp", lambda e, rb=rb, i=i, t0=t0, nt=nt: e.dma_start(
                            out=xres[rb][:, 0:nt], in_=XTin[i, :, t0:t0 + nt]),
                             writes=[("xres", rb)], chan=f"ldr{rb}")
                        for j in range(44):
                            P.op("pe", lambda e, b=b, j=j, ii=ii, lt=lt, nt=nt, pz=pz: e.matmul(
                                pz[:, 0:nt], wdb[b][:, j, ii * 128:(ii + 1) * 128], hT[:, j, lt:lt + nt],
                                start=(j == 0), stop=(j == 43)),
                                 reads=[("wd", b, j // 22), ("hT", j)], writes=[("ps", slot)])
                        P.op("dve", lambda e, rb=rb, nt=nt, pz=pz: e.scalar_tensor_tensor(
                            out=zst[rb][:, 0:nt], in0=pz[:, 0:nt], scalar=0.5 / ALPHA, in1=xres[rb][:, 0:nt],
                            op0=ALU.mult, op1=ALU.add),
                             reads=[("ps", slot), ("xres", rb)], writes=[("zst", rb)])
                        P.op("sp", lambda e, rb=rb, i=i, t0=t0, nt=nt: e.dma_start(
                            out=ZTout[i, :, t0:t0 + nt], in_=zst[rb][:, 0:nt]),
                             reads=[("zst", rb)], chan=f"st{rb}")
        P.barrier()

    def phase_ln(tag, ZTin, HTout, li, final_out=None):
        arena_off[0] = CONST_END
        zt = [sb(f"{tag}_zt{i}", [128, 16, 264], F32) for i in range(2)]
        zb = [sb(f"{tag}_zb{i}", [128, 16, 264], BF16) for i in range(2)]
        zq = [sb(f"{tag}_zq{i}", [128, 16, 264], BF16) for i in range(2)]
        mean = [sb(f"{tag}_mean{i}", [128, 264], F32) for i in range(2)]
        msq = [sb(f"{tag}_msq{i}", [128, 264], F32) for i in range(2)]
        rstd = [sb(f"{tag}_rstd{i}", [128, 264], F32) for i in range(2)]
        ho = [sb(f"{tag}_ho{i}", [128, 16, 264], F32) for i in range(2)]
        eps = LN_EPS / (ALPHA * ALPHA)
        for ti, (t0, nt) in enumerate(TILES):
            b = ti % 2
            ps_s, ps_q = psum[2 * b], psum[2 * b + 1]
            P.op("sp", lambda e, b=b, t0=t0, nt=nt: e.dma_start(
                out=zt[b][:, :, 0:nt], in_=ZTin[:, :, t0:t0 + nt].rearrange("k p t -> p k t")),
                 writes=[("zt", b)], chan=f"ld{b}")
            P.op("act", lambda e, b=b, nt=nt: e.copy(out=zb[b][:, :, 0:nt], in_=zt[b][:, :, 0:nt]),
                 reads=[("zt", b)], writes=[("zb", b)])
            P.op("act", lambda e, b=b, nt=nt: e.activation(out=zq[b][:, :, 0:nt], in_=zt[b][:, :, 0:nt], func=AF.Square),
                 reads=[("zt", b)], writes=[("zq", b)])
            for k in range(16):
                P.op("pe", lambda e, b=b, k=k, nt=nt, ps_s=ps_s: e.matmul(
                    ps_s[:, 0:nt], ones_bf[:, :], zb[b][:, k, 0:nt], start=(k == 0), stop=(k == 15)),
                     reads=[("zb", b), "ones"], writes=[("ps", 2 * b)])
            for k in range(16):
                P.op("pe", lambda e, b=b, k=k, nt=nt, ps_q=ps_q: e.matmul(
                    ps_q[:, 0:nt], ones_bf[:, :], zq[b][:, k, 0:nt], start=(k == 0), stop=(k == 15)),
                     reads=[("zq", b), "ones"], writes=[("ps", 2 * b + 1)])
            P.op("dve", lambda e, b=b, nt=nt, ps_s=ps_s: e.tensor_scalar(
                out=mean[b][:, 0:nt], in0=ps_s[:, 0:nt], scalar1=1.0 / D, scalar2=None, op0=ALU.mult),
                 reads=[("ps", 2 * b)], writes=[("mean", b)])
            P.op("dve", lambda e, b=b, nt=nt: e.tensor_tensor(
                out=msq[b][:, 0:nt], in0=mean[b][:, 0:nt], in1=mean[b][:, 0:nt], op=ALU.mult),
                 reads=[("mean", b)], writes=[("msq", b)])
            P.op("dve", lambda e, b=b, nt=nt, ps_q=ps_q: e.scalar_tensor_tensor(
                out=rstd[b][:, 0:nt], in0=ps_q[:, 0:nt], scalar=1.0 / D, in1=msq[b][:, 0:nt],
                op0=ALU.mult, op1=ALU.subtract),
                 reads=[("ps", 2 * b + 1), ("msq", b)], writes=[("rstd", b)])
            P.op("act", lambda e, b=b, nt=nt: e.activation(
                out=msq[b][:, 0:nt], in_=rstd[b][:, 0:nt], func=AF.Sqrt, bias=eps, scale=1.0),
                 reads=[("rstd", b)], writes=[("msq", b)])
            P.op("dve", lambda e, b=b, nt=nt: e.reciprocal(out=rstd[b][:, 0:nt], in_=msq[b][:, 0:nt]),
                 reads=[("msq", b)], writes=[("rstd", b)])
            P.op("dve", lambda e, b=b, nt=nt: e.tensor_tensor(
                out=zt[b][:, :, 0:nt], in0=zt[b][:, :, 0:nt],
                in1=mean[b][:, 0:nt].unsqueeze(1).to_broadcast([128, 16, nt]),
                op=ALU.subtract),
                 reads=[("zt", b), ("mean", b)], writes=[("zt", b)])
            P.op("dve", lambda e, b=b, nt=nt: e.tensor_tensor(
                out=zt[b][:, :, 0:nt], in0=zt[b][:, :, 0:nt],
                in1=rstd[b][:, 0:nt].unsqueeze(1).to_broadcast([128, 16, nt]),
                op=ALU.mult),
                 reads=[("zt", b), ("rstd", b)], writes=[("zt", b)])
            for k in range(16):
                P.op("act", lambda e, b=b, k=k, nt=nt: e.activation(
                    out=ho[b][:, k, 0:nt], in_=zt[b][:, k, 0:nt], func=AF.Identity,
                    scale=lng[:, li, k:k + 1], bias=lnb[:, li, k:k + 1]),
                     reads=[("zt", b), f"lng{li}", f"lnb{li}"], writes=[("ho", b)])
            P.op("sp", lambda e, b=b, t0=t0, nt=nt: e.dma_start(
                out=HTout[:, :, t0:t0 + nt].rearrange("k p t -> p k t"), in_=ho[b][:, :, 0:nt]),
                 reads=[("ho", b)], chan=f"st{b}")
        P.barrier()

    def phase_transpose_out(HTin):
        arena_off[0] = CONST_END
        hin = [sb(f"po_in{i}", [128, 16, 128], F32) for i in range(2)]
        otok = [sb(f"po_tok{i}", [128, D], F32) for i in range(2)]
        for ci, (t0, nt) in enumerate(CHUNKS):
            b = ci % 2
            P.op("sp", lambda e, b=b, t0=t0, nt=nt: e.dma_start(
                out=hin[b][:, :, 0:nt], in_=HTin[:, :, t0:t0 + nt].rearrange("k p t -> p k t")),
                 writes=[("hin", b)], chan=f"ld{b}")
            for g in range(4):
                pb = (ci * 4 + g) % 8
                for q in range(4):
                    f = g * 4 + q
                    P.op("pe", lambda e, b=b, nt=nt, f=f, pb=pb, q=q: e.transpose(
                        psum[pb][0:nt, q * 128:(q + 1) * 128], hin[b][:, f, 0:nt], ident_sb[:, :]),
                         reads=[("hin", b), "ident"], writes=[("ps", pb)])
                en = evac_eng()
                if en == "act":
                    fn = lambda e, b=b, nt=nt, g=g, pb=pb: e.copy(out=otok[b][0:nt, g * 512:(g + 1) * 512], in_=psum[pb][0:nt, :])
                else:
                    fn = lambda e, b=b, nt=nt, g=g, pb=pb: e.tensor_copy(out=otok[b][0:nt, g * 512:(g + 1) * 512], in_=psum[pb][0:nt, :])
                P.op(en, fn, reads=[("ps", pb)], writes=[("otok", b, g)])
            P.op("sp", lambda e, b=b, t0=t0, nt=nt: e.dma_start(out=out_ap[t0:t0 + nt, :], in_=otok[b][0:nt, :]),
                 reads=[("otok", b, g) for g in range(4)], chan=f"st{b}")
        P.barrier()


    def dma(out, in_, slow=False):
        if slow:
            return lambda e: e.dma_start(out=out, in_=in_, allow_slow_non_contiguous=True)
        return lambda e: e.dma_start(out=out, in_=in_)

    def mm(out, lhsT, rhs, st, sp_):
        return lambda e: e.matmul(out, lhsT, rhs, start=st, stop=sp_)

    def fm(ap2d):
        return ap2d.rearrange("(k p) t -> k p t", p=128)

    def phase_linear(INap, nk, Wap, mchunks, OUTfn, toks=TILES, scale=1.0, bias_col=None, resid=None):
        arena_off[0] = CONST_END
        Tin = max(t0 + nt for t0, nt in toks)
        inT = sb("lin_in", [128, nk, Tin], BF16)
        wb = [sb(f"lin_w{i}", [128, nk, 128], BF16) for i in range(2)]
        st = [sb(f"lin_st{i}", [128, 264], F32) for i in range(4)]
        rs = [sb(f"lin_rs{i}", [128, 264], F32) for i in range(4)]
        if isinstance(INap, list):
            for k in range(nk):
                P.op("pool", dma(inT[:, k, 0:Tin], INap[k][:, 0:Tin]), writes=[("lin_in", k // 4)], chan="ldx")
        else:
            for kq in range(0, nk, 4):
                kn = min(4, nk - kq)
                P.op("pool", dma(inT[:, kq:kq + kn, 0:Tin], INap[kq:kq + kn, :, 0:Tin].rearrange("k p t -> p k t")),
                     writes=[("lin_in", kq // 4)], chan="ldx")
        Wv = Wap.rearrange("(k p) m -> p k m", p=128)
        cnt = 0
        for mi, (c0, mc) in enumerate(mchunks):
            b = mi % 2
            P.op("pool", dma(wb[b][:, :, 0:mc], Wv[:, :, c0:c0 + mc]), writes=[("lin_w", b)], chan=f"ldw{b}")
            for (t0, nt) in toks:
                slot = cnt % 8
                sti = cnt % 4
                cnt += 1
                if resid is not None:
                    P.op("sp", dma(rs[sti][0:mc, 0:nt], resid(mi, t0, nt)), writes=[("lin_rs", sti)], chan=f"ldr{sti}")
                for k in range(nk):
                    P.op("pe", mm(psum[slot][0:mc, 0:nt], wb[b][:, k, 0:mc], inT[:, k, t0:t0 + nt], k == 0, k == nk - 1),
                         reads=[("lin_w", b), ("lin_in", k // 4)], writes=[("ps", slot)])
                if resid is not None:
                    P.op("dve", lambda e, slot=slot, sti=sti, mc=mc, nt=nt: e.scalar_tensor_tensor(
                        out=st[sti][0:mc, 0:nt], in0=psum[slot][0:mc, 0:nt], scalar=scale, in1=rs[sti][0:mc, 0:nt],
                        op0=ALU.mult, op1=ALU.add), reads=[("ps", slot), ("lin_rs", sti)], writes=[("lin_st", sti)])
                elif bias_col is not None:
                    P.op("act", lambda e, slot=slot, sti=sti, mc=mc, nt=nt: e.activation(
                        out=st[sti][0:mc, 0:nt], in_=psum[slot][0:mc, 0:nt], func=AF.Identity, bias=bias_col[0:mc, 0:1], scale=scale),
                         reads=[("ps", slot), "consts2"], writes=[("lin_st", sti)])
                else:
                    en = evac_eng()
                    if en == "act":
                        P.op("act", lambda e, slot=slot, sti=sti, mc=mc, nt=nt: e.mul(
                            out=st[sti][0:mc, 0:nt], in_=psum[slot][0:mc, 0:nt], mul=scale),
                             reads=[("ps", slot)], writes=[("lin_st", sti)])
                    else:
                        P.op("dve", lambda e, slot=slot, sti=sti, mc=mc, nt=nt: e.tensor_scalar(
                            out=st[sti][0:mc, 0:nt], in0=psum[slot][0:mc, 0:nt], scalar1=scale, scalar2=None, op0=ALU.mult),
                             reads=[("ps", slot)], writes=[("lin_st", sti)])
                P.op("sp", dma(OUTfn(mi, t0, nt), st[sti][0:mc, 0:nt]), reads=[("lin_st", sti)], chan=f"st{sti}")
        P.barrier()

    def phase_tokproj(INap, nk, Wap, M, OUTap, chunks=CHUNKS):
        arena_off[0] = CONST_END
        Tin = max(t0 + nt for t0, nt in chunks)
        inT = sb("tp_in", [128, nk, Tin], BF16)
        wsb = sb("tp_w", [128, nk, M], BF16)
        vst = [sb(f"tp_st{i}", [128, M], BF16) for i in range(2)]
        if isinstance(INap, list):
            for k in range(nk):
                P.op("pool", dma(inT[:, k, 0:Tin], INap[k][:, 0:Tin]), writes=["tp_in"], chan="ldx")
        else:
            P.op("pool", dma(inT[:, :, 0:Tin], INap[:, :, 0:Tin].rearrange("k p t -> p k t")), writes=["tp_in"], chan="ldx")
        P.op("pool", dma(wsb[:], Wap.rearrange("(k p) m -> p k m", p=128)), writes=["tp_w"], chan="ldw0")
        cnt = 0
        for ci, (t0, nt) in enumerate(chunks):
            b = ci % 2
            for n0 in range(0, M, 512):
                w = min(512, M - n0)
                slot = cnt % 8
                cnt += 1
                for k in range(nk):
                    P.op("pe", mm(psum[slot][0:nt, 0:w], inT[:, k, t0:t0 + nt], wsb[:, k, n0:n0 + w], k == 0, k == nk - 1),
                         reads=["tp_in", "tp_w"], writes=[("ps", slot)])
                en = evac_eng()
                if en == "act":
                    P.op("act", lambda e, slot=slot, b=b, nt=nt, n0=n0, w=w: e.copy(out=vst[b][0:nt, n0:n0 + w], in_=psum[slot][0:nt, 0:w]),
                         reads=[("ps", slot)], writes=[("tp_st", b, n0)])
                else:
                    P.op("dve", lambda e, slot=slot, b=b, nt=nt, n0=n0, w=w: e.tensor_copy(out=vst[b][0:nt, n0:n0 + w], in_=psum[slot][0:nt, 0:w]),
                         reads=[("ps", slot)], writes=[("tp_st", b, n0)])
            P.op("sp", dma(OUTap[t0:t0 + nt, :], vst[b][0:nt, :]), reads=[("tp_st", b, n0) for n0 in range(0, M, 512)], chan=f"st{b}")
        P.barrier()

    def phase_norm(INfn, nch, gc, gcol, eps, OUTfn, center=False, addin=None):
        arena_off[0] = CONST_END
        xt = [sb(f"nm_x{i}", [128, nch, 264], F32) for i in range(2)]
        xb = [sb(f"nm_b{i}", [128, nch, 264], BF16) for i in range(2)]
        xq = [sb(f"nm_q{i}", [128, nch, 264], BF16) for i in range(2)]
        ng = nch // gc
        mean = [sb(f"nm_m{i}", [128, ng, 264], F32) for i in range(2)]
        msq = [sb(f"nm_s{i}", [128, ng, 264], F32) for i in range(2)]
        rstd = [sb(f"nm_r{i}", [128, ng, 264], F32) for i in range(2)]
        ho = [sb(f"nm_o{i}", [128, nch, 264], F32) for i in range(2)]
        addt = [sb(f"nm_a{i}", [128, nch, 264], F32) for i in range(2)] if addin is not None else None
        for ti, (t0, nt) in enumerate(TILES):
            b = ti % 2
            P.op("sp", dma(xt[b][:, :, 0:nt], INfn(t0, nt).rearrange("k p t -> p k t")), writes=[("nm_x", b)], chan=f"ld{b}")
            if addin is not None:
                P.op("sp", dma(addt[b][:, :, 0:nt], addin[0](t0, nt).rearrange("k p t -> p k t")), writes=[("nm_a", b)], chan=f"lda{b}")
            P.op("act", lambda e, b=b, nt=nt: e.activation(out=xq[b][:, :, 0:nt], in_=xt[b][:, :, 0:nt], func=AF.Square),
                 reads=[("nm_x", b)], writes=[("nm_q", b)])
            if center:
                P.op("act", lambda e, b=b, nt=nt: e.copy(out=xb[b][:, :, 0:nt], in_=xt[b][:, :, 0:nt]),
                     reads=[("nm_x", b)], writes=[("nm_b", b)])
            for g in range(ng):
                pq = psum[(2 * g) % 8]
                pm_ = psum[(2 * g + 1) % 8]
                for k in range(gc):
                    P.op("pe", mm(pq[:, 0:nt], ones_bf[:, :], xq[b][:, g * gc + k, 0:nt], k == 0, k == gc - 1),
                         reads=[("nm_q", b), "ones"], writes=[("ps", (2 * g) % 8)])
                if center:
                    for k in range(gc):
                        P.op("pe", mm(pm_[:, 0:nt], ones_bf[:, :], xb[b][:, g * gc + k, 0:nt], k == 0, k == gc - 1),
                             reads=[("nm_b", b), "ones"], writes=[("ps", (2 * g + 1) % 8)])
                    P.op("dve", lambda e, b=b, g=g, nt=nt, pm_=pm_: e.tensor_scalar(
                        out=mean[b][:, g, 0:nt], in0=pm_[:, 0:nt], scalar1=1.0 / (gc * 128), scalar2=None, op0=ALU.mult),
                         reads=[("ps", (2 * g + 1) % 8)], writes=[("nm_m", b, g)])
                    P.op("dve", lambda e, b=b, g=g, nt=nt: e.tensor_tensor(
                        out=msq[b][:, g, 0:nt], in0=mean[b][:, g, 0:nt], in1=mean[b][:, g, 0:nt], op=ALU.mult),
                         reads=[("nm_m", b, g)], writes=[("nm_s", b, g)])
                    P.op("dve", lambda e, b=b, g=g, nt=nt, pq=pq: e.scalar_tensor_tensor(
                        out=rstd[b][:, g, 0:nt], in0=pq[:, 0:nt], scalar=1.0 / (gc * 128), in1=msq[b][:, g, 0:nt],
                        op0=ALU.mult, op1=ALU.subtract),
                         reads=[("ps", (2 * g) % 8), ("nm_s", b, g)], writes=[("nm_r", b, g)])
                    P.op("act", lambda e, b=b, g=g, nt=nt: e.activation(
                        out=msq[b][:, g, 0:nt], in_=rstd[b][:, g, 0:nt], func=AF.Sqrt, bias=eps, scale=1.0),
                         reads=[("nm_r", b, g)], writes=[("nm_s", b, g)])
                else:
                    P.op("act", lambda e, b=b, g=g, nt=nt, pq=pq: e.activation(
                        out=msq[b][:, g, 0:nt], in_=pq[:, 0:nt], func=AF.Sqrt, bias=eps, scale=1.0 / (gc * 128)),
                         reads=[("ps", (2 * g) % 8)], writes=[("nm_s", b, g)])
                P.op("dve", lambda e, b=b, g=g, nt=nt: e.reciprocal(out=rstd[b][:, g, 0:nt], in_=msq[b][:, g, 0:nt]),
                     reads=[("nm_s", b, g)], writes=[("nm_r", b, g)])
                sl = slice(g * gc, (g + 1) * gc)
                if center:
                    P.op("dve", lambda e, b=b, g=g, nt=nt, sl=sl: e.tensor_tensor(
                        out=xt[b][:, sl, 0:nt], in0=xt[b][:, sl, 0:nt],
                        in1=mean[b][:, g, 0:nt].unsqueeze(1).to_broadcast([128, gc, nt]), op=ALU.subtract),
                         reads=[("nm_x", b), ("nm_m", b, g)], writes=[("nm_x", b)])
                P.op("dve", lambda e, b=b, g=g, nt=nt, sl=sl: e.tensor_tensor(
                    out=xt[b][:, sl, 0:nt], in0=xt[b][:, sl, 0:nt],
                    in1=rstd[b][:, g, 0:nt].unsqueeze(1).to_broadcast([128, gc, nt]), op=ALU.mult),
                     reads=[("nm_x", b), ("nm_r", b, g)], writes=[("nm_x", b)])
            for k in range(nch):
                P.op("act", lambda e, b=b, k=k, nt=nt: e.activation(
                    out=ho[b][:, k, 0:nt], in_=xt[b][:, k, 0:nt], func=AF.Copy, scale=gcol[:, k:k + 1]),
                     reads=[("nm_x", b), "consts2"], writes=[("nm_o", b)])
                if addin is not None:
                    P.op("dve", lambda e, b=b, k=k, nt=nt: e.scalar_tensor_tensor(
                        out=ho[b][:, k, 0:nt], in0=addt[b][:, k, 0:nt], scalar=addin[1][:, k:k + 1], in1=ho[b][:, k, 0:nt],
                        op0=ALU.mult, op1=ALU.add),
                         reads=[("nm_a", b), ("nm_o", b), "consts2"], writes=[("nm_o", b)])
            o_ = OUTfn(t0, nt)
            if isinstance(o_, list):
                for k in range(nch):
                    P.op("sp", dma(o_[k], ho[b][:, k, 0:nt]), reads=[("nm_o", b)], chan=f"st{b}")
            else:
                P.op("sp", dma(o_.rearrange("k p t -> p k t"), ho[b][:, :, 0:nt]), reads=[("nm_o", b)], chan=f"st{b}")
        P.barrier()


    def dump(name, ap):
        shp = list(ap.shape)
        o = nc.dram_tensor("dbg_" + name, shp, ap.dtype, kind="ExternalOutput").ap()
        dumps.append((o, ap))

    RMS_EPS = 1e-6
    WIN_CH = [(128 * i, 128) for i in range(22)] + [(2816, 64), (2880, 64)]

    def win_out(mi, t0, nt):
        if mi < 4:
            return UQ[mi, :, t0:t0 + nt]
        if mi < 6:
            return UKV[mi - 4, :, t0:t0 + nt]
        if mi < 14:
            return gx(2 + mi - 6)[:, t0:t0 + nt]
        if mi < 22:
            return ZM[mi - 14, :, t0:t0 + nt]
        return URR[mi - 22, :, t0:t0 + nt]

    def phase_krope():
        arena_off[0] = CONST_END
        a = sb("kr_a", [64, T], F32); b_ = sb("kr_b", [64, T], F32); c = sb("kr_c", [64, T], F32); d_ = sb("kr_d", [64, T], F32)
        P.op("sp", dma(a[:], URR[0, :, :]), writes=["kr_a"], chan="ld0")
        P.op("sp", dma(b_[:], URR[1, :, :]), writes=["kr_b"], chan="ld0")
        P.op("sp", dma(c[:], cos2[:, :]), writes=["kr_c"], chan="ld1")
        P.op("sp", dma(d_[:], sin2[:, :]), writes=["kr_d"], chan="ld1")
        P.op("dve", lambda e: e.tensor_tensor(out=a[:], in0=a[:], in1=c[:], op=ALU.mult), reads=["kr_a", "kr_c"], writes=["kr_a"])
        P.op("dve", lambda e: e.tensor_tensor(out=b_[:], in0=b_[:], in1=d_[:], op=ALU.mult), reads=["kr_b", "kr_d"], writes=["kr_b"])
        P.op("dve", lambda e: e.tensor_tensor(out=a[:], in0=a[:], in1=b_[:], op=ALU.add), reads=["kr_a", "kr_b"], writes=["kr_a"])
        P.op("sp", dma(gx(10)[:, :], a[:]), reads=["kr_a"], chan="st0")
        P.barrier()

    def phase_gather():
        groups = [[2 * i, 2 * i + 1] for i in range(NCORES // 2)]
        for i in range(NGX):
            P.op("pool", lambda e, i=i: e.collective_compute("AllGather", ALU.bypass, replica_groups=groups,
                                                             ins=[GX_t[i].ap().opt()], outs=[GXA_t[i].ap().opt()]),
                 chan="cc", writes=[("gather", i)], inc=1)
        P.barrier()

    ATT_SCALE = 192.0 ** -0.5
    KCH = [(b, c, b * T + t0, nt) for b in range(2) for c, (t0, nt) in enumerate(CHUNKS)]

    def phase_mla_attn():
        for hg in range(2):
            arena_off[0] = CONST_END
            QN = sb("at_qn", [128, 4, T], BF16)
            QR = sb("at_qr", [64, 4, T], BF16)
            KN = sb("at_kn", [128, 4, 2 * T], BF16)
            KR = sb("at_kr", [64, 2 * T], BF16)
            V = sb("at_v", [128, 34, 512], BF16)
            cs = sb("at_cos", [64, T], F32); sn = sb("at_sin", [64, T], F32)
            r1 = sb("at_r1", [64, T], F32); r2 = sb("at_r2", [64, T], F32)
            PT = [sb(f"at_p{i}", [128, 512], BF16) for i in range(4)]
            rsb = [sb(f"at_rs{i}", [128, 512], F32) for i in range(2)]
            ost = [sb(f"at_os{i}", [128, 512], F32) for i in range(2)]
            P.op("pool", dma(QN[:], QALL[hg * 4:(hg + 1) * 4, 0:128, :].rearrange("h p t -> p h t")), writes=["at_qn"], chan="ldx")
            P.op("pool", dma(KN[:], KNA[hg * 4:(hg + 1) * 4, :, :].rearrange("h p t -> p h t")), writes=["at_kn"], chan="ldx")
            for b in range(2):
                P.op("pool", dma(KR[:, b * T:(b + 1) * T], gxa(b, 10)), writes=["at_kr"], chan="ldx")
                P.op("sp", dma(V[:, b * 17:b * 17 + 16, :],
                               VTA[b * T:b * T + 2048, hg * 512:(hg + 1) * 512].rearrange("(c p) m -> p c m", p=128)),
                     writes=["at_v"], chan="ld0")
                P.op("sp", dma(V[0:8, b * 17 + 16, :], VTA[b * T + 2048:b * T + 2056, hg * 512:(hg + 1) * 512]),
                     writes=["at_v"], chan="ld0")
            P.op("sp", dma(cs[:], cos2[:, :]), writes=["at_cs"], chan="ld1")
            P.op("sp", dma(sn[:], sin2[:, :]), writes=["at_sn"], chan="ld1")
            for hl in range(4):
                h = hg * 4 + hl
                P.op("sp", dma(r1[:], QALL[h, 128:192, :]), writes=["at_r1"], chan="ld2")
                P.op("sp", dma(r2[:], QALL[h, 192:256, :]), writes=["at_r2"], chan="ld2")
                P.op("dve", lambda e: e.tensor_tensor(out=r1[:], in0=r1[:], in1=cs[:], op=ALU.mult), reads=["at_r1", "at_cs"], writes=["at_r1"])
                P.op("dve", lambda e: e.tensor_tensor(out=r2[:], in0=r2[:], in1=sn[:], op=ALU.mult), reads=["at_r2", "at_sn"], writes=["at_r2"])
                P.op("dve", lambda e, hl=hl: e.tensor_tensor(out=QR[:, hl, :], in0=r1[:], in1=r2[:], op=ALU.add),
                     reads=["at_r1", "at_r2"], writes=["at_qr"])
            blocks = []
            for hl in range(4):
                for qi, (t0, nt) in enumerate(ATILES):
                    for ci, (b, c, k0, nk) in enumerate(KCH):
                        blocks.append((hl, hl * 5 + qi, t0, nt, ci, k0, nk))
            DEP = 3

            def front(i, blk):
                hl, qi, t0, nt, ci, k0, nk = blk
                sbk = i % 4
                P.op("pe", mm(psum[sbk][0:nk, 0:nt], KN[:, hl, k0:k0 + nk], QN[:, hl, t0:t0 + nt], True, False),
                     reads=["at_kn", "at_qn"], writes=[("ps", sbk)])
                P.op("pe", mm(psum[sbk][0:nk, 0:nt], KR[:, k0:k0 + nk], QR[:, hl, t0:t0 + nt], False, True),
                     reads=["at_kr", "at_qr"], writes=[("ps", sbk)])
                P.op("act", lambda e, sbk=sbk, nk=nk, nt=nt: e.activation(
                    out=PT[sbk][0:nk, 0:nt], in_=psum[sbk][0:nk, 0:nt], func=AF.Exp, scale=ATT_SCALE),
                     reads=[("ps", sbk)], writes=[("at_p", sbk)])

            def back(i, blk, hg=hg):
                hl, qi, t0, nt, ci, k0, nk = blk
                h = hg * 4 + hl
                pi = i % 4
                ob = 4 + 2 * (qi % 2)
                po, pssum = psum[ob], psum[ob + 1]
                P.op("pe", mm(po[:, 0:nt], V[0:nk, ci, hl * 128:(hl + 1) * 128], PT[pi][0:nk, 0:nt], ci == 0, ci == 33),
                     reads=["at_v", ("at_p", pi)], writes=[("ps", ob)])
                P.op("pe", mm(pssum[:, 0:nt], ones_bf[0:nk, :], PT[pi][0:nk, 0:nt], ci == 0, ci == 33),
                     reads=["ones", ("at_p", pi)], writes=[("ps", ob + 1)])
                if ci == 33:
                    rb = qi % 2
                    P.op("dve", lambda e, rb=rb, nt=nt, pssum=pssum: e.reciprocal(out=rsb[rb][:, 0:nt], in_=pssum[:, 0:nt]),
                         reads=[("ps", ob + 1)], writes=[("at_rs", rb)])
                    P.op("dve", lambda e, rb=rb, nt=nt, po=po: e.tensor_tensor(
                        out=ost[rb][:, 0:nt], in0=po[:, 0:nt], in1=rsb[rb][:, 0:nt], op=ALU.mult),
                         reads=[("ps", ob), ("at_rs", rb)], writes=[("at_os", rb)])
                    P.op("sp", dma(OAT[h, :, t0:t0 + nt], ost[rb][:, 0:nt]), reads=[("at_os", rb)], chan=f"st{rb}")

            for i in range(len(blocks) + DEP):
                if i < len(blocks):
                    front(i, blocks[i])
                if i >= DEP:
                    back(i - DEP, blocks[i - DEP])
            P.barrier()


    def phase_conv():
        arena_off[0] = CONST_END
        XMt = [sb(f"cv_x{i}", [128, T + 4], F32) for i in range(2)]
        acc = [sb(f"cv_a{i}", [128, T], F32) for i in range(2)]
        xco = [sb(f"cv_o{i}", [128, T], F32) for i in range(2)]
        n = 0
        for b in range(2):
            for k in range(8):
                i = n % 2
                n += 1
                src = gxa(b, 2 + k)
                oth = gxa(1 - b, 2 + k)
                P.op("dve", lambda e, i=i: e.memset(XMt[i][:, 0:2], 0.0), writes=[("cv_x", i)])
                P.op("sp", dma(XMt[i][:, 2:T + 2], src[:, :]), writes=[("cv_x", i)], chan="ld")
                P.op("sp", dma(XMt[i][:, T + 2:T + 3], oth[:, T - 1:T], True), writes=[("cv_x", i)], chan="ld")
                P.op("sp", dma(XMt[i][:, T + 3:T + 4], oth[:, T - 2:T - 1], True), writes=[("cv_x", i)], chan="ld")
                for j in range(5):
                    off = j if b == 0 else 4 - j
                    if j == 0:
                        P.op("dve", lambda e, i=i, k=k, off=off: e.tensor_scalar(
                            out=acc[i][:, :], in0=XMt[i][:, off:off + T], scalar1=cw_sb[:, 0, k:k + 1], scalar2=None, op0=ALU.mult),
                             reads=[("cv_x", i), "consts2"], writes=[("cv_a", i)])
                    else:
                        P.op("dve", lambda e, i=i, k=k, j=j, off=off: e.scalar_tensor_tensor(
                            out=acc[i][:, :], in0=XMt[i][:, off:off + T], scalar=cw_sb[:, j, k:k + 1], in1=acc[i][:, :],
                            op0=ALU.mult, op1=ALU.add),
                             reads=[("cv_x", i), ("cv_a", i), "consts2"], writes=[("cv_a", i)])
                P.op("act", lambda e, i=i, k=k: e.activation(out=xco[i][:, :], in_=acc[i][:, :], func=AF.Silu, bias=cb_sb[:, k:k + 1]),
                     reads=[("cv_a", i), "consts2"], writes=[("cv_o", i)])
                P.op("sp", dma(XCM[b, k, :, :], xco[i][:, :]), reads=[("cv_o", i)], chan="st")
                P.op("sp", dma(XCM[b, 8 + k, :, :], XMt[i][:, 2:T + 2]), reads=[("cv_x", i)], chan="st2")
        P.barrier()

    def phase_scans():
        arena_off[0] = CONST_END
        T2 = 2 * T
        LI = sb("sc_li", [36, T2], F32); LF = sb("sc_lf", [36, T2], F32); ONE = sb("sc_one", [36, T2], F32)
        Bt = sb("sc_b", [36, T2], F32); At = sb("sc_a", [36, T2], F32); Mt = sb("sc_m", [36, T2], F32); mt = sb("sc_mm", [36, T2], F32)
        P.op("dve", lambda e: e.memset(ONE[:, :], 1.0), writes=["sc_one"])
        for d in range(2):
            p0 = 32 * d
            ps_ = slice(p0, p0 + 4)
            for b in range(2):
                P.op("sp", dma(LI[ps_, b * T:(b + 1) * T], GRAW[b, 8 * d:8 * d + 4, :]), writes=[("sc_li", d)], chan="ld")
                P.op("sp", dma(LF[ps_, b * T:(b + 1) * T], GRAW[b, 8 * d + 4:8 * d + 8, :]), writes=[("sc_lf", d)], chan="ld")
            P.op("act", lambda e, ps_=ps_: e.activation(out=LF[ps_, :], in_=LF[ps_, :], func=AF.Exp, scale=-1.0),
                 reads=[("sc_lf", d)], writes=[("sc_lf", d)])
            P.op("act", lambda e, ps_=ps_: e.activation(out=LF[ps_, :], in_=LF[ps_, :], func=AF.Ln, bias=1.0, scale=1.0),
                 reads=[("sc_lf", d)], writes=[("sc_lf", d)])
            if d == 0:
                seg1 = lambda t, ps_=ps_: t[ps_, 0:T]
                seg2 = lambda t, ps_=ps_: t[ps_, T:T2][:, ::-1]
                last1 = lambda t, ps_=ps_: t[ps_, T - 1:T]
            else:
                seg1 = lambda t, ps_=ps_: t[ps_, T:T2]
                seg2 = lambda t, ps_=ps_: t[ps_, 0:T][:, ::-1]
                last1 = lambda t, ps_=ps_: t[ps_, T2 - 1:T2]
            P.op("dve", lambda e, seg1=seg1: e.tensor_tensor_scan(out=seg1(Bt), data0=seg1(ONE), data1=seg1(LF), initial=0.0,
                                                                  op0=ALU.mult, op1=ALU.subtract),
                 reads=[("sc_lf", d), "sc_one"], writes=[("sc_b", d)])
            P.op("dve", lambda e, seg2=seg2, last1=last1: e.tensor_tensor_scan(out=seg2(Bt), data0=seg2(ONE), data1=seg2(LF), initial=last1(Bt),
                                                                               op0=ALU.mult, op1=ALU.subtract),
                 reads=[("sc_lf", d), "sc_one", ("sc_b", d)], writes=[("sc_b", d)])
            P.op("dve", lambda e, ps_=ps_: e.tensor_tensor(out=At[ps_, :], in0=LI[ps_, :], in1=Bt[ps_, :], op=ALU.subtract),
                 reads=[("sc_li", d), ("sc_b", d)], writes=[("sc_a", d)])
            P.op("dve", lambda e, seg1=seg1: e.tensor_tensor_scan(out=seg1(Mt), data0=seg1(At), data1=seg1(At), initial=0.0,
                                                                  op0=ALU.max, op1=ALU.max),
                 reads=[("sc_a", d)], writes=[("sc_m", d)])
            P.op("dve", lambda e, seg2=seg2, last1=last1: e.tensor_tensor_scan(out=seg2(Mt), data0=seg2(At), data1=seg2(At), initial=last1(Mt),
                                                                               op0=ALU.max, op1=ALU.max),
                 reads=[("sc_a", d), ("sc_m", d)], writes=[("sc_m", d)])
            P.op("dve", lambda e, ps_=ps_: e.tensor_tensor(out=mt[ps_, :], in0=Bt[ps_, :], in1=Mt[ps_, :], op=ALU.add),
                 reads=[("sc_b", d), ("sc_m", d)], writes=[("sc_mm", d)])
            P.op("dve", lambda e, ps_=ps_: e.tensor_scalar(out=Mt[ps_, :], in0=Mt[ps_, :], scalar1=-1.0, scalar2=None, op0=ALU.mult),
                 reads=[("sc_m", d), ("sc_mm", d)], writes=[("sc_m", d)])
            P.op("dve", lambda e, ps_=ps_: e.tensor_scalar(out=mt[ps_, :], in0=mt[ps_, :], scalar1=-1.0, scalar2=None, op0=ALU.mult),
                 reads=[("sc_mm", d)], writes=[("sc_mm", d)])
            P.op("sp", dma(RW[d, 0, :, :], Mt[ps_, :]), reads=[("sc_m", d)], chan="st0")
            P.op("sp", dma(RW[d, 1, :, :], mt[ps_, :]), reads=[("sc_mm", d)], chan="st1")
            P.op("sp", dma(RW[d, 2, :, :], At[ps_, :]), reads=[("sc_a", d)], chan="st2")
        P.barrier()

    def phase_mix(h, d):
        arena_off[0] = CONST_END
        T2 = 2 * T
        Q = sb("mx_q", [128, 2, T2], BF16); Kt = sb("mx_k", [128, 2, T2], BF16); V = sb("mx_v", [128, 34, 256], BF16)
        nM = sb("mx_nm", [128, T2], F32); en = sb("mx_en", [128, T2], F32); acol = sb("mx_ac", [128, 34], F32)
        mle = sb("mx_le", [128, 4, 512], F32); mge = sb("mx_ge", [128, 4, 512], F32)
        es = [sb(f"mx_es{i}", [128, 512], F32) for i in range(4)]
        at = [sb(f"mx_at{i}", [128, 512], BF16) for i in range(4)]
        xs = [sb(f"mx_xs{i}", [128, 512], F32) for i in range(4)]
        dn = [sb(f"mx_dn{i}", [128, 512], F32) for i in range(2)]
        ost = [sb(f"mx_os{i}", [128, 2, 512], F32) for i in range(2)]
        P.op("pool", dma(Q[:], QM[h].rearrange("c p t -> p c t")), writes=["mx_q"], chan="ld")
        P.op("pool", dma(Kt[:], KM[h].rearrange("c p t -> p c t")), writes=["mx_k"], chan="ld")
        for b in range(2):
            P.op("sp", dma(V[:, b * 17:b * 17 + 16, :], VM[h, b * T:b * T + 2048, :].rearrange("(c p) m -> p c m", p=128)),
                 writes=[("mx_v", b)], chan="ld")
            P.op("sp", dma(V[0:8, b * 17 + 16, :], VM[h, b * T + 2048:b * T + 2056, :]), writes=[("mx_v", b, 1)], chan="ld")
            P.op("sp", dma(acol[:, b * 17:b * 17 + 16], RW[d, 2, h, b * T:b * T + 2048].rearrange("(c p) -> p c", p=128), True),
                 writes=[("mx_ac", b)], chan="ld")
            P.op("sp", dma(acol[0:8, b * 17 + 16:b * 17 + 17], RW[d, 2, h, b * T + 2048:b * T + 2056].rearrange("(p o) -> p o", o=1), True),
                 writes=[("mx_ac", b, 1)], chan="ld")
        P.op("sp", dma(nM[:], RW[d, 0, h:h + 1, :].partition_broadcast(128)), writes=["mx_nm"], chan="ld")
        P.op("sp", dma(en[:], RW[d, 1, h:h + 1, :].partition_broadcast(128)), writes=["mx_en"], chan="ld")
        P.op("sp", dma(mle[:], mask_le[:, :, :]), writes=["mx_le"], chan="ld")
        P.op("sp", dma(mge[:], mask_ge[:, :, :]), writes=["mx_ge"], chan="ld")
        P.op("act", lambda e: e.activation(out=en[:], in_=en[:], func=AF.Exp), reads=["mx_en"], writes=["mx_en"])
        vkeys = [("mx_v", 0), ("mx_v", 0, 1), ("mx_v", 1), ("mx_v", 1, 1)]
        akeys = [("mx_ac", 0), ("mx_ac", 0, 1), ("mx_ac", 1), ("mx_ac", 1, 1)]
        blocks = []
        gi = 0
        for qb in range(2):
            rel = "le" if d == qb else "ge"
            for (t0, nt) in ATILES:
                tq = qb * T + t0
                lst = []
                if rel == "ge":
                    ob = 1 - qb
                    for c, (s0, nk) in enumerate(CHUNKS):
                        lst.append((ob * 17 + c, ob * T + s0, nk, None))
                for c, (s0, nk) in enumerate(CHUNKS):
                    if rel == "le":
                        if s0 >= t0 + nt:
                            continue
                        mk = None if s0 < t0 else (mle, (s0 - t0) // 128)
                    else:
                        if s0 < t0:
                            continue
                        mk = None if s0 >= t0 + nt else (mge, (s0 - t0) // 128)
                    lst.append((qb * 17 + c, qb * T + s0, nk, mk))
                for li_, (ci, k0, nk, mk) in enumerate(lst):
                    blocks.append((gi, tq, nt, ci, k0, nk, mk, li_ == 0, li_ == len(lst) - 1))
                gi += 1
        DEP = 3
        pO0, pO1, pD = psum[4], psum[5], psum[6]

        def front(i, blk):
            g_, tq, nt, ci, k0, nk, mk, first, last = blk
            sbk = i % 4
            pi = i % 4
            for c2 in range(2):
                P.op("pe", mm(psum[sbk][0:nk, 0:nt], Kt[:, c2, k0:k0 + nk], Q[:, c2, tq:tq + nt], c2 == 0, c2 == 1),
                     reads=["mx_k", "mx_q"], writes=[("ps", sbk)])
            if mk is None:
                P.op("act", lambda e, pi=pi, nk=nk, nt=nt, tq=tq, ci=ci: e.activation(
                    out=es[pi][0:nk, 0:nt], in_=nM[0:nk, tq:tq + nt], func=AF.Exp, bias=acol[0:nk, ci:ci + 1]),
                     reads=["mx_nm"] + akeys, writes=[("mx_es", pi)])
            else:
                mt_, mj = mk
                P.op("dve", lambda e, pi=pi, nk=nk, nt=nt, tq=tq, ci=ci, mt_=mt_, mj=mj: e.scalar_tensor_tensor(
                    out=xs[pi][0:nk, 0:nt], in0=nM[0:nk, tq:tq + nt], scalar=acol[0:nk, ci:ci + 1], in1=mt_[0:nk, mj, 0:nt],
                    op0=ALU.add, op1=ALU.min),
                     reads=["mx_nm", "mx_le", "mx_ge"] + akeys, writes=[("mx_xs", pi)])
                P.op("act", lambda e, pi=pi, nk=nk, nt=nt: e.activation(
                    out=es[pi][0:nk, 0:nt], in_=xs[pi][0:nk, 0:nt], func=AF.Exp),
                     reads=[("mx_xs", pi)], writes=[("mx_es", pi)])
            P.op("dve", lambda e, sbk=sbk, pi=pi, nk=nk, nt=nt: e.scalar_tensor_tensor(
                out=at[pi][0:nk, 0:nt], in0=psum[sbk][0:nk, 0:nt], scalar=1.0 / 16.0, in1=es[pi][0:nk, 0:nt],
                op0=ALU.mult, op1=ALU.mult),
                 reads=[("ps", sbk), ("mx_es", pi)], writes=[("mx_at", pi)])

        def back(i, blk):
            g_, tq, nt, ci, k0, nk, mk, first, last = blk
            pi = i % 4
            P.op("pe", mm(pO0[:, 0:nt], V[0:nk, ci, 0:128], at[pi][0:nk, 0:nt], first, last),
                 reads=vkeys + [("mx_at", pi)], writes=[("ps", 4)])
            P.op("pe", mm(pO1[:, 0:nt], V[0:nk, ci, 128:256], at[pi][0:nk, 0:nt], first, last),
                 reads=vkeys + [("mx_at", pi)], writes=[("ps", 5)])
            P.op("pe", mm(pD[:, 0:nt], ones_bf[0:nk, :], at[pi][0:nk, 0:nt], first, last),
                 reads=["ones", ("mx_at", pi)], writes=[("ps", 6)])
            if last:
                oi = g_ % 2
                P.op("act", lambda e, oi=oi, nt=nt: e.activation(out=dn[oi][:, 0:nt], in_=pD[:, 0:nt], func=AF.Abs),
                     reads=[("ps", 6)], writes=[("mx_dn", oi)])
                P.op("dve", lambda e, oi=oi, nt=nt, tq=tq: e.tensor_tensor(
                    out=dn[oi][:, 0:nt], in0=dn[oi][:, 0:nt], in1=en[:, tq:tq + nt], op=ALU.max),
                     reads=[("mx_dn", oi), "mx_en"], writes=[("mx_dn", oi)])
                P.op("dve", lambda e, oi=oi, nt=nt: e.reciprocal(out=dn[oi][:, 0:nt], in_=dn[oi][:, 0:nt]),
                     reads=[("mx_dn", oi)], writes=[("mx_dn", oi)])
                P.op("dve", lambda e, oi=oi, nt=nt: e.tensor_tensor(out=ost[oi][:, 0, 0:nt], in0=pO0[:, 0:nt], in1=dn[oi][:, 0:nt], op=ALU.mult),
                     reads=[("ps", 4), ("mx_dn", oi)], writes=[("mx_os", oi)])
                P.op("dve", lambda e, oi=oi, nt=nt: e.tensor_tensor(out=ost[oi][:, 1, 0:nt], in0=pO1[:, 0:nt], in1=dn[oi][:, 0:nt], op=ALU.mult),
                     reads=[("ps", 5), ("mx_dn", oi)], writes=[("mx_os", oi)])
                P.op("sp", dma(HM[d, 2 * h:2 * h + 2, :, tq:tq + nt].rearrange("c p t -> p c t"), ost[oi][:, :, 0:nt]),
                     reads=[("mx_os", oi)], chan="st")

        for i in range(len(blocks) + DEP):
            if i < len(blocks):
                front(i, blocks[i])
            if i >= DEP:
                back(i - DEP, blocks[i - DEP])
        P.barrier()

    def phase_combine():
        arena_off[0] = CONST_END
        ha = [sb(f"cb_a{i}", [128, 8, 264], F32) for i in range(2)]
        hb = [sb(f"cb_b{i}", [128, 8, 264], F32) for i in range(2)]
        zz = [sb(f"cb_z{i}", [128, 8, 264], F32) for i in range(2)]
        for ti, (t0, nt) in enumerate(TILES):
            i = ti % 2
            for blk in range(2):
                for d in range(2):
                    tgt = ha if (blk, d) == (0, 0) else hb
                    P.op("sp", dma(tgt[i][:, :, 0:nt], HM[d, :, :, blk * T + t0:blk * T + t0 + nt].rearrange("k p t -> p k t")),
                         writes=[("cb_a" if tgt is ha else "cb_b", i)], chan="ld")
                    if (blk, d) == (0, 0):
                        P.op("dve", lambda e, i=i, nt=nt: e.tensor_scalar(out=ha[i][:, :, 0:nt], in0=ha[i][:, :, 0:nt], scalar1=om_sb[:, 0:1],
                                                                          scalar2=None, op0=ALU.mult),
                             reads=[("cb_a", i), "consts2"], writes=[("cb_a", i)])
                    else:
                        P.op("dve", lambda e, i=i, nt=nt, blk=blk: e.scalar_tensor_tensor(
                            out=ha[i][:, :, 0:nt], in0=hb[i][:, :, 0:nt], scalar=om_sb[:, blk:blk + 1], in1=ha[i][:, :, 0:nt],
                            op0=ALU.mult, op1=ALU.add),
                             reads=[("cb_a", i), ("cb_b", i), "consts2"], writes=[("cb_a", i)])
            P.op("sp", dma(zz[i][:, :, 0:nt], ZM[:, :, t0:t0 + nt].rearrange("k p t -> p k t")), writes=[("cb_z", i)], chan="ld")
            P.op("act", lambda e, i=i, nt=nt: e.activation(out=zz[i][:, :, 0:nt], in_=zz[i][:, :, 0:nt], func=AF.Sigmoid),
                 reads=[("cb_z", i)], writes=[("cb_z", i)])
            P.op("dve", lambda e, i=i, nt=nt: e.tensor_tensor(out=ha[i][:, :, 0:nt], in0=ha[i][:, :, 0:nt], in1=zz[i][:, :, 0:nt], op=ALU.mult),
                 reads=[("cb_a", i), ("cb_z", i)], writes=[("cb_a", i)])
            P.op("sp", dma(HG[:, :, t0:t0 + nt].rearrange("k p t -> p k t"), ha[i][:, :, 0:nt]), reads=[("cb_a", i)], chan="st")
            for blk in range(2):
                P.op("sp", dma(zz[i][:, :, 0:nt] if blk == 0 else hb[i][:, :, 0:nt],
                               XCM[blk, 0:8, :, t0:t0 + nt].rearrange("k p t -> p k t")),
                     writes=[("cb_z", i) if blk == 0 else ("cb_b", i)], chan="ld")
            P.op("dve", lambda e, i=i, nt=nt: e.tensor_scalar(out=zz[i][:, :, 0:nt], in0=zz[i][:, :, 0:nt], scalar1=om_sb[:, 0:1],
                                                              scalar2=None, op0=ALU.mult),
                 reads=[("cb_z", i), "consts2"], writes=[("cb_z", i)])
            P.op("dve", lambda e, i=i, nt=nt: e.scalar_tensor_tensor(
                out=zz[i][:, :, 0:nt], in0=hb[i][:, :, 0:nt], scalar=om_sb[:, 1:2], in1=zz[i][:, :, 0:nt], op0=ALU.mult, op1=ALU.add),
                 reads=[("cb_z", i), ("cb_b", i), "consts2"], writes=[("cb_z", i)])
            P.op("sp", dma(XCO[:, :, t0:t0 + nt].rearrange("k p t -> p k t"), zz[i][:, :, 0:nt]), reads=[("cb_z", i)], chan="st2")
        P.barrier()

    def ph_mproj(h, b):
        xin = [XCM[b, 2 * h], XCM[b, 2 * h + 1]]
        phase_linear(xin, 2, w_q[h], [(0, 128), (128, 128)], lambda mi, t0, nt: QM[h, mi, :, b * T + t0:b * T + t0 + nt])
        phase_linear(xin, 2, w_k[h], [(0, 128), (128, 128)], lambda mi, t0, nt: KM[h, mi, :, b * T + t0:b * T + t0 + nt])
        phase_tokproj([XCM[b, 8 + 2 * h], XCM[b, 8 + 2 * h + 1]], 2, w_v[h], 256, VM[h, b * T:(b + 1) * T, :])

    UQ_CH = []
    for h in range(8):
        UQ_CH += [(h * 256, 128), (h * 256 + 128, 64), (h * 256 + 192, 64)]

    def uq_out(mi, t0, nt):
        h, j = mi // 3, mi % 3
        r0 = [0, 128, 192][j]
        rn = [128, 64, 64][j]
        return QALL[h, r0:r0 + rn, t0:t0 + nt]

    def ph_kv(b):
        phase_linear([gxa(b, 0), gxa(b, 1)], 2, w_ukvk, [(128 * h, 128) for h in range(8)],
                     lambda mi, t0, nt, b=b: KNA[mi, :, b * T + t0:b * T + t0 + nt])
        phase_tokproj([gxa(b, 0), gxa(b, 1)], 2, w_ukvv, 1024, VTA[b * T:(b + 1) * T, :])

    PH = [
        ("tin", phase_transpose_in),
        ("ffn1", lambda: phase_ffn("f1", XT, ZT, w1g, w1u, w1d)),
        ("ln1", lambda: phase_ln("l1", ZT, H1T, 0)),
        ("win", lambda: phase_linear(H1T, 16, w_in, WIN_CH, win_out)),
        ("nq", lambda: phase_norm(lambda t0, nt: UQ[:, :, t0:t0 + nt], 4, 4, gq_sb, RMS_EPS, lambda t0, nt: NQT[:, :, t0:t0 + nt])),
        ("nkv", lambda: phase_norm(lambda t0, nt: UKV[:, :, t0:t0 + nt], 2, 2, gkv_sb, RMS_EPS,
                                   lambda t0, nt: [gx(0)[:, t0:t0 + nt], gx(1)[:, t0:t0 + nt]])),
        ("krope", phase_krope),
        ("gather", phase_gather),
        ("uq", lambda: phase_linear(NQT, 4, w_uq, UQ_CH, uq_out)),
        ("kv0", lambda: ph_kv(0)),
        ("kv1", lambda: ph_kv(1)),
        ("attn", phase_mla_attn),
        ("nao", lambda: phase_norm(lambda t0, nt: OAT[:, :, t0:t0 + nt], 8, 8, gao_sb, RMS_EPS, lambda t0, nt: YT[0:8, :, t0:t0 + nt])),
        ("conv", phase_conv),
        ("gates0", lambda: phase_linear(XCM[0], 16, w_gates, [(0, 16)], lambda mi, t0, nt: GRAW[0, :, t0:t0 + nt], bias_col=bg_sb)),
        ("gates1", lambda: phase_linear(XCM[1], 16, w_gates, [(0, 16)], lambda mi, t0, nt: GRAW[1, :, t0:t0 + nt], bias_col=bg_sb)),
        ("scans", phase_scans),
    ]
    for h_ in range(4):
        for b_ in range(2):
            PH.append((f"mproj{h_}{b_}", lambda h_=h_, b_=b_: ph_mproj(h_, b_)))
    for h_ in range(4):
        for d_ in range(2):
            PH.append((f"mix{h_}{d_}", lambda h_=h_, d_=d_: phase_mix(h_, d_)))
    PH += [
        ("combine", phase_combine),
        ("gnorm", lambda: phase_norm(lambda t0, nt: HG[:, :, t0:t0 + nt], 8, 2, gng_sb, LN_EPS, lambda t0, nt: YT[8:16, :, t0:t0 + nt],
                                     center=True, addin=(lambda t0, nt: XCO[:, :, t0:t0 + nt], skip_sb))),
        ("wout", lambda: phase_linear(YT, 16, w_out, [(128 * i, 128) for i in range(16)], lambda mi, t0, nt: Z2T[mi, :, t0:t0 + nt],
                                      scale=1.0 / ALPHA, resid=lambda mi, t0, nt: H1T[mi, :, t0:t0 + nt])),
        ("ln2", lambda: phase_ln("l2", Z2T, H2T, 1)),
        ("ffn2", lambda: phase_ffn("f2", H2T, Z3T, w2g, w2u, w2d)),
        ("ln3", lambda: phase_ln("l3", Z3T, H3T, 2)),
    ]
    kstop = int(os.environ.get("KSTOP", "999"))
    kskip = os.environ.get("KSKIP", "").split(",")
    for i_, (nm_, f_) in enumerate(PH):
        if i_ < kstop and nm_ not in kskip:
            f_()
    if debug_out:
        for nm in debug_out:
            dump(nm, {"H1T": H1T, "UQ": UQ, "NQT": NQT, "GX0": gx(0), "GX10": gx(10), "GX2": gx(2), "GXA10": GXA_t[10].ap(), "QALL": QALL, "KNA": KNA, "VTA": VTA, "OAT": OAT, "YT": YT, "XCM": XCM, "GRAW": GRAW, "RW": RW, "HM": HM, "HG": HG, "H2T": H2T, "H3T": H3T, "Z2T": Z2T}[nm])
    for i_, (o_, a_) in enumerate(dumps):
        P.op("sp", dma(o_, a_), chan=f"st{i_ % 4}")
    phase_transpose_out(H3T)
    P.op("sp", lambda e: e.nop(), reads=[], writes=[])

    sem_names = [("e", e) for e in ENG] + [("c", c) for c in P.chan_names]
    assert len(sem_names) <= 95, len(sem_names)
    sems = {}
    for i, k in enumerate(sem_names):
        sems[k] = nc.alloc_semaphore(f"s{i}_{k[1]}")
    run = P.emit(sems)
    with nc.Block() as block:
        @block.tensor
        def _(e):
            run("pe", e)

        @block.scalar
        def _(e):
            run("act", e)

        @block.vector
        def _(e):
            run("dve", e)

        @block.gpsimd
        def _(e):
            run("pool", e)

        @block.sync
        def _(e):
            run("sp", e)
    return nc


_NC_CACHE = {}


def make_core_inputs(inputs):
    g = lambda k: np.ascontiguousarray(np.asarray(inputs[k], dtype=np.float32)[0])
    x = np.asarray(inputs["x"], dtype=np.float32)
    meta = np.asarray(inputs["meta_tokens"], dtype=np.float32)
    w_in = g("w_in")
    kr = w_in[:, 768:832]
    w_in_ext = np.ascontiguousarray(np.concatenate(
        [w_in[:, 0:768], w_in[:, 832:1856], w_in[:, 1856:2880], kr, kr[:, 32:], kr[:, :32]], axis=1))
    wuq = g("mla_w_uq")
    cols = []
    for h in range(8):
        b0 = h * 192
        cols += list(range(b0, b0 + 192)) + list(range(b0 + 160, b0 + 192)) + list(range(b0 + 128, b0 + 160))
    w_uq_ext = np.ascontiguousarray(wuq[:, cols])
    wukv = g("mla_w_ukv")
    kc = [h * 256 + j for h in range(8) for j in range(128)]
    vc = [h * 256 + 128 + j for h in range(8) for j in range(128)]
    w_ukvk = np.ascontiguousarray(wukv[:, kc])
    w_ukvv = np.ascontiguousarray(wukv[:, vc])
    inv = 10000.0 ** (-np.arange(0, 64, 2, dtype=np.float32) / 64.0)
    NEGM = -30000.0
    s_idx = np.arange(128)[:, None, None]
    j_idx = np.arange(4)[None, :, None]
    t_idx = np.arange(512)[None, None, :]
    mask_le = np.where(128 * j_idx + s_idx <= t_idx, 0.0, NEGM).astype(np.float32)
    mask_ge = np.where(128 * j_idx + s_idx >= t_idx, 0.0, NEGM).astype(np.float32)
    shared = {
        "ident": np.eye(128, dtype=np.float32),
        "w1g": g("ffn1_w_gate"), "w1u": g("ffn1_w_up"), "w1d": g("ffn1_w_down"),
        "ln1g": g("ln1_g"), "ln1b": g("ln1_b"),
        "w_in": w_in_ext, "gq": g("mla_q_norm_g"), "gkv": g("mla_kv_norm_g"), "gao": g("attn_out_g"),
        "w_uq": w_uq_ext, "w_ukvk": w_ukvk, "w_ukvv": w_ukvv,
        "w_out": g("w_out"), "ln2g": g("ln2_g"), "ln2b": g("ln2_b"), "ln3g": g("ln3_g"), "ln3b": g("ln3_b"),
        "w2g": g("ffn2_w_gate"), "w2u": g("ffn2_w_up"), "w2d": g("ffn2_w_down"),
        "conv_w": g("mlstm_conv_w"), "conv_b": g("mlstm_conv_b"),
        "w_q": g("mlstm_w_q"), "w_k": g("mlstm_w_k"), "w_v": g("mlstm_w_v"),
        "w_gates": g("mlstm_w_gates"), "b_gates": g("mlstm_b_gates"),
        "gn_g": g("mlstm_gn_g"), "skip": g("mlstm_skip"),
        "mask_le": mask_le, "mask_ge": mask_ge,
    }
    maps = []
    for c in range(NCORES):
        b, r = c // 2, c % 2
        if r == 0:
            xl = np.concatenate([meta, x[b, :T - 16]], axis=0)
            pos = np.arange(T, dtype=np.float32)
        else:
            xl = x[b, T - 16:][::-1]
            pos = (2 * T - 1 - np.arange(T)).astype(np.float32)
        ang = (pos[None, :] * inv[:, None]).astype(np.float32)
        cs, sn = np.cos(ang), np.sin(ang)
        om = np.zeros((128, 2), np.float32)
        om[:, r] = 1.0
        m = dict(shared)
        m["x"] = np.ascontiguousarray(xl)
        m["cos2"] = np.ascontiguousarray(np.concatenate([cs, cs], 0).astype(np.float32))
        m["sin2"] = np.ascontiguousarray(np.concatenate([-sn, sn], 0).astype(np.float32))
        m["om"] = om
        maps.append(m)
    return maps


def assemble(outs):
    B = NCORES // 2
    res = np.empty((B, 4096, D), np.float32)
    for c in range(NCORES):
        b, r = c // 2, c % 2
        o = outs[c]
        if r == 0:
            res[b, :T - 16] = o[16:]
        else:
            res[b, T - 16:] = o[::-1]
    return res


def kernel(**inputs):
    if "nc" not in _NC_CACHE:
        _NC_CACHE["nc"] = build_program()
    nc = _NC_CACHE["nc"]
    maps = make_core_inputs(inputs)
    res = run_bass_kernel_spmd(nc, maps, core_ids=list(range(NCORES)))
    outs = [np.asarray(r["out"]) for r in res.results]
    return assemble(outs)
```

```python
import numpy as np
import concourse.bass as bass
import concourse.mybir as mybir
from concourse.bass_utils import run_bass_kernel_spmd

F32 = mybir.dt.float32
BF16 = mybir.dt.bfloat16
AF = mybir.ActivationFunctionType
ALU = mybir.AluOpType

D = 2048
DFF = 5632
T = 2056
import os
NCORES = int(os.environ.get('KCORES', '8'))
ALPHA = 2.0 ** 0.25
LN_EPS = 1e-5
TILES = [(256 * i, 256) for i in range(7)] + [(1792, 264)]
HALVES = [TILES[0:4], TILES[4:8]]
CHUNKS = [(128 * i, 128) for i in range(16)] + [(2048, 8)]
ATILES = [(0, 512), (512, 512), (1024, 512), (1536, 256), (1792, 264)]
FHALVES = [ATILES[0:2], ATILES[2:5]]

ENG = ("pe", "act", "dve", "pool", "sp")


class Op:
    __slots__ = ("eng", "fn", "deps", "chan", "needed", "seq", "idx", "inc")

    def __init__(self, eng, fn, chan):
        self.eng, self.fn, self.chan = eng, fn, chan
        self.deps = set()
        self.inc = 16
        self.needed = False
        self.seq = None


class Prog:
    def __init__(self, nc):
        self.nc = nc
        self.ops = []
        self.last_w = {}
        self.readers = {}
        self.barrier_deps = set()
        self.last_eng = {}
        self.last_chan = {}
        self.chan_names = []
        self.chan_map = {}

    def op(self, eng, fn, reads=(), writes=(), chan=None, inc=16):
        if chan is not None:
            key = writes[0] if len(writes) else (reads[0] if len(reads) else ("anon", chan))
            if key not in self.chan_map:
                self.chan_map[key] = f"c{len(self.chan_map)}"
            chan = self.chan_map[key]
        o = Op(eng, fn, chan)
        o.inc = inc
        o.idx = len(self.ops)
        deps = set(self.barrier_deps)
        for k in reads:
            w = self.last_w.get(k)
            if w is not None:
                deps.add(w)
        for k in writes:
            w = self.last_w.get(k)
            if w is not None:
                deps.add(w)
            deps.update(self.readers.get(k, ()))
        for k in reads:
            self.readers.setdefault(k, []).append(o)
        for k in writes:
            self.last_w[k] = o
            self.readers[k] = []
        deps.discard(o)
        o.deps = deps
        self.ops.append(o)
        if chan is None:
            self.last_eng[eng] = o
        else:
            if chan not in self.last_chan:
                self.chan_names.append(chan)
            self.last_chan[chan] = o
            self.last_eng[eng] = o
        return o

    def barrier(self):
        self.barrier_deps = set(self.last_eng.values()) | set(self.last_chan.values())
        self.chan_map = {}
        self.last_w = {}
        self.readers = {}

    def emit(self, block_ctx_sems):
        nc = self.nc
        ops = self.ops
        for o in ops:
            for d in o.deps:
                if d.chan is None and d.eng == "pe" and o.eng == "pe" and o.chan is None:
                    continue
                d.needed = True
        eng_cnt = {e: 0 for e in ENG}
        chan_cnt = {}
        for o in ops:
            if o.chan is not None:
                chan_cnt[o.chan] = chan_cnt.get(o.chan, 0) + o.inc
                o.seq = chan_cnt[o.chan]
            elif o.needed:
                eng_cnt[o.eng] += 1
                o.seq = eng_cnt[o.eng]
        sems = block_ctx_sems
        per_eng = {e: [o for o in ops if o.eng == e] for e in ENG}

        def run(e, engine):
            waited = {}
            for o in per_eng[e]:
                need = {}
                for d in o.deps:
                    if d.chan is not None:
                        key = ("c", d.chan)
                    else:
                        if d.eng == "pe" and e == "pe" and o.chan is None:
                            continue
                        key = ("e", d.eng)
                    if d.seq > need.get(key, 0):
                        need[key] = d.seq
                for key, v in need.items():
                    if waited.get(key, 0) >= v:
                        continue
                    waited[key] = v
                    engine.wait_ge(sems[key], v)
                ins = o.fn(engine)
                if o.chan is not None:
                    ins.then_inc(sems[("c", o.chan)], o.inc)
                elif o.needed:
                    ins.then_inc(sems[("e", e)], 1)
        return run


def build_program(debug_out=None):
    nc = bass.Bass("TRN2", target_bir_lowering=False)
    P = Prog(nc)

    def din(name, shape, dt=F32):
        return nc.dram_tensor(name, list(shape), dt, kind="ExternalInput").ap()

    x_in = din("x", [T, D])
    ident = din("ident", [128, 128])
    w1g = din("w1g", [D, DFF]); w1u = din("w1u", [D, DFF]); w1d = din("w1d", [DFF, D])
    ln1g = din("ln1g", [D]); ln1b = din("ln1b", [D])
    out_ap = nc.dram_tensor("out", [T, D], F32, kind="ExternalOutput").ap()
    w_in = din("w_in", [D, 2944])
    gq_in = din("gq", [512]); gkv_in = din("gkv", [256]); gao_in = din("gao", [1024])
    w_uq = din("w_uq", [512, 2048])
    w_ukvk = din("w_ukvk", [256, 1024]); w_ukvv = din("w_ukvv", [256, 1024])
    cos2 = din("cos2", [64, T]); sin2 = din("sin2", [64, T])
    w_out = din("w_out", [D, D])
    ln2g = din("ln2g", [D]); ln2b = din("ln2b", [D]); ln3g = din("ln3g", [D]); ln3b = din("ln3b", [D])
    w2g = din("w2g", [D, DFF]); w2u = din("w2u", [D, DFF]); w2d = din("w2d", [DFF, D])
    conv_w = din("conv_w", [5, 1024]); conv_b = din("conv_b", [1024])
    w_q = din("w_q", [4, 256, 256]); w_k = din("w_k", [4, 256, 256]); w_v = din("w_v", [4, 256, 256])
    w_gates = din("w_gates", [D, 16]); b_gates = din("b_gates", [16])
    gn_g = din("gn_g", [1024]); skip_in = din("skip", [1024])
    om_in = din("om", [128, 2])
    mask_le = din("mask_le", [128, 4, 512]); mask_ge = din("mask_ge", [128, 4, 512])
    dumps = []

    def dram(name, shape, dt=F32):
        return nc.dram_tensor(name, list(shape), dt).ap()

    XT = nc.dram_tensor("XT_s", [16, 128, T], F32).ap()
    ZT = nc.dram_tensor("ZT_s", [16, 128, T], F32).ap()
    H1T = nc.dram_tensor("H1T_s", [16, 128, T], F32).ap()
    UQ = dram("UQ_s", [4, 128, T]); UKV = dram("UKV_s", [2, 128, T]); URR = dram("URR_s", [2, 64, T])
    ZM = dram("ZM_s", [8, 128, T]); NQT = dram("NQT_s", [4, 128, T])
    NGX = 11
    GXrows = [128] * 10 + [64]
    GX_t = [nc.dram_tensor(f"GX{i}_s", [GXrows[i], T], F32) for i in range(NGX)]
    GXA_t = [nc.dram_tensor(f"GXA{i}_s", [2 * GXrows[i], T], F32) for i in range(NGX)]

    def gx(i):
        return GX_t[i].ap()

    def gxa(b, i):
        return GXA_t[i].ap()[b * GXrows[i]:(b + 1) * GXrows[i], :]
    QALL = dram("QALL_s", [8, 256, T]); KNA = dram("KNA_s", [8, 128, 2 * T]); VTA = dram("VTA_s", [2 * T, 1024], BF16)
    OAT = dram("OAT_s", [8, 128, T]); YT = dram("YT_s", [16, 128, T])
    XO = dram("XO_s", [2, 8, 128, T])
    XCM = dram("XCM_s", [2, 16, 128, T]); GRAW = dram("GRAW_s", [2, 16, T]); RW = dram("RW_s", [2, 3, 4, 2 * T])
    QM = dram("QM_s", [4, 2, 128, 2 * T]); KM = dram("KM_s", [4, 2, 128, 2 * T]); VM = dram("VM_s", [4, 2 * T, 256], BF16)
    HM = dram("HM_s", [2, 8, 128, 2 * T]); HG = dram("HG_s", [8, 128, T]); XCO = dram("XCO_s", [8, 128, T])
    Z2T = dram("Z2T_s", [16, 128, T]); H2T = dram("H2T_s", [16, 128, T]); Z3T = dram("Z3T_s", [16, 128, T]); H3T = dram("H3T_s", [16, 128, T])

    arena_off = [16640]

    def sb(name, shape, dt, off=None):
        nbytes = int(np.prod(shape[1:])) * (2 if dt == BF16 else 4)
        if off is None:
            off = arena_off[0]
            off = (off + 63) // 64 * 64
            arena_off[0] = off + nbytes
        assert off + nbytes <= 228 * 1024, (name, off, nbytes)
        return nc.alloc_sbuf_tensor_at(name, list(shape), dt, offset=off)

    ident_sb = sb("ident_sb", [128, 128], F32)
    ones_bf = sb("ones_bf", [128, 128], BF16)
    lng = sb("lng", [128, 3, 16], F32)
    lnb = sb("lnb", [128, 3, 16], F32)
    gq_sb = sb("gq_sb", [128, 4], F32); gkv_sb = sb("gkv_sb", [128, 2], F32); gao_sb = sb("gao_sb", [128, 8], F32)
    gng_sb = sb("gng_sb", [128, 8], F32); skip_sb = sb("skip_sb", [128, 8], F32)
    cw_sb = sb("cw_sb", [128, 5, 8], F32); cb_sb = sb("cb_sb", [128, 8], F32)
    bg_sb = sb("bg_sb", [16, 1], F32); om_sb = sb("om_sb", [128, 2], F32)
    CONST_END = arena_off[0]

    psum = [nc.alloc_psum_tensor(f"ps{i}", [128, 512], F32) for i in range(8)]

    rr = {"i": 0}

    def dma_(out, in_, slow=False):
        if slow:
            return lambda e: e.dma_start(out=out, in_=in_, allow_slow_non_contiguous=True)
        return lambda e: e.dma_start(out=out, in_=in_)

    def evac_eng():
        rr["i"] += 1
        return "act" if rr["i"] % 2 else "dve"

    P.op("sp", lambda e: e.dma_start(out=ident_sb[:], in_=ident[:, :]), writes=["ident"], chan="const")
    P.op("dve", lambda e: e.memset(ones_bf[:], 1.0), writes=["ones"])
    P.op("sp", lambda e: e.dma_start(out=lng[:, 0, :], in_=ln1g.rearrange("(k p) -> p k", p=128),
                                     allow_slow_non_contiguous=True), writes=["lng0"], chan="const")
    P.op("sp", lambda e: e.dma_start(out=lnb[:, 0, :], in_=ln1b.rearrange("(k p) -> p k", p=128),
                                     allow_slow_non_contiguous=True), writes=["lnb0"], chan="const")

    for li_, (g_, b_) in enumerate([(ln2g, ln2b), (ln3g, ln3b)]):
        P.op("sp", dma_(lng[:, li_ + 1, :], g_.rearrange("(k p) -> p k", p=128), True), writes=[f"lng{li_ + 1}"], chan="const")
        P.op("sp", dma_(lnb[:, li_ + 1, :], b_.rearrange("(k p) -> p k", p=128), True), writes=[f"lnb{li_ + 1}"], chan="const")
    for dst_, src_ in [(gq_sb, gq_in), (gkv_sb, gkv_in), (gao_sb, gao_in), (gng_sb, gn_g), (skip_sb, skip_in), (cb_sb, conv_b)]:
        P.op("sp", dma_(dst_[:], src_.rearrange("(k p) -> p k", p=128), True), writes=["consts2"], chan="const")
    for j_ in range(5):
        P.op("sp", dma_(cw_sb[:, j_, :], conv_w[j_, :].rearrange("(k p) -> p k", p=128), True), writes=["consts2"], chan="const")
    P.op("sp", dma_(bg_sb[:], b_gates.rearrange("(p o) -> p o", o=1), True), writes=["consts2"], chan="const")
    P.op("sp", dma_(om_sb[:], om_in[:, :]), writes=["consts2"], chan="const")

    def phase_transpose_in():
        arena_off[0] = CONST_END
        xtok = [sb(f"p0_xtok{i}", [128, D], F32) for i in range(2)]
        stg = [sb(f"p0_stg{i}", [128, 16, 128], F32) for i in range(2)]
        for ci, (t0, nt) in enumerate(CHUNKS):
            b = ci % 2
            P.op("sp", lambda e, b=b, t0=t0, nt=nt: e.dma_start(out=xtok[b][0:nt, :], in_=x_in[t0:t0 + nt, :]),
                 writes=[("xtok", b)], chan=f"ld{b}")
            for g in range(4):
                pb = (ci * 4 + g) % 8
                for q in range(4):
                    f = g * 4 + q
                    P.op("pe", lambda e, b=b, nt=nt, f=f, pb=pb, q=q: e.transpose(
                        psum[pb][:, q * 128:q * 128 + nt], xtok[b][0:nt, f * 128:(f + 1) * 128], ident_sb[0:nt, 0:nt]),
                         reads=[("xtok", b), "ident"], writes=[("ps", pb)])
                en = evac_eng()
                if en == "act":
                    fn = lambda e, b=b, nt=nt, g=g, pb=pb: e.copy(
                        out=stg[b][:, g * 4:(g + 1) * 4, 0:nt],
                        in_=psum[pb][:, :].rearrange("p (q t) -> p q t", q=4)[:, :, 0:nt])
                else:
                    fn = lambda e, b=b, nt=nt, g=g, pb=pb: e.tensor_copy(
                        out=stg[b][:, g * 4:(g + 1) * 4, 0:nt],
                        in_=psum[pb][:, :].rearrange("p (q t) -> p q t", q=4)[:, :, 0:nt])
                P.op(en, fn, reads=[("ps", pb)], writes=[("stg", b, g)])
            P.op("sp", lambda e, b=b, t0=t0, nt=nt: e.dma_start(
                out=XT[:, :, t0:t0 + nt].rearrange("k p t -> p k t"), in_=stg[b][:, :, 0:nt]),
                 reads=[("stg", b, g) for g in range(4)], chan=f"st{b}")
        P.barrier()

    def phase_ffn(tag, XTin, ZTout, wg, wu, wd):
        arena_off[0] = CONST_END
        TH = 1032
        xT = sb(f"{tag}_xT", [128, 16, TH], BF16)
        hT = sb(f"{tag}_hT", [128, 44, TH], BF16)
        wgb = [sb(f"{tag}_wg{i}", [128, 16, 128], BF16) for i in range(2)]
        wub = [sb(f"{tag}_wu{i}", [128, 16, 128], BF16) for i in range(2)]
        wdb = [sb(f"{tag}_wd{i}", [128, 44, 256], BF16) for i in range(2)]
        sil = [sb(f"{tag}_sil{i}", [128, 512], F32) for i in range(2)]
        xres = [sb(f"{tag}_xres{i}", [128, 512], F32) for i in range(2)]
        zst = [sb(f"{tag}_zst{i}", [128, 512], F32) for i in range(2)]
        wg_v = wg.rearrange("(k p) m -> p k m", p=128)
        wu_v = wu.rearrange("(k p) m -> p k m", p=128)
        wd_v = wd.rearrange("(j p) m -> p j m", p=128)
        cnt = {"gu": 0, "dn": 0, "ev": 0}
        for hi, tiles in enumerate(FHALVES):
            h0 = tiles[0][0]
            hn = sum(n for _, n in tiles)
            for kq in range(4):
                P.op("pool", lambda e, kq=kq, h0=h0, hn=hn: e.dma_start(
                    out=xT[:, kq * 4:(kq + 1) * 4, 0:hn],
                    in_=XTin[kq * 4:(kq + 1) * 4, :, h0:h0 + hn].rearrange("k p t -> p k t")),
                     writes=[("xT", kq)], chan="ldx")
            for j in range(44):
                b = cnt["gu"] % 2
                cnt["gu"] += 1
                P.op("pool", lambda e, b=b, j=j: e.dma_start(out=wgb[b][:], in_=wg_v[:, :, j * 128:(j + 1) * 128]),
                     writes=[("wg", b)], chan=f"ldw{b}")
                P.op("pool", lambda e, b=b, j=j: e.dma_start(out=wub[b][:], in_=wu_v[:, :, j * 128:(j + 1) * 128]),
                     writes=[("wu", b)], chan=f"ldu{b}")
                for ti, (t0, nt) in enumerate(tiles):
                    lt = t0 - h0
                    slot = cnt["ev"] % 4
                    cnt["ev"] += 1
                    pg, pu = psum[2 * slot], psum[2 * slot + 1]
                    for k in range(16):
                        P.op("pe", lambda e, b=b, k=k, lt=lt, nt=nt, pg=pg: e.matmul(
                            pg[:, 0:nt], wgb[b][:, k, :], xT[:, k, lt:lt + nt], start=(k == 0), stop=(k == 15)),
                             reads=[("wg", b), ("xT", k // 4)], writes=[("ps", 2 * slot)])
                    for k in range(16):
                        P.op("pe", lambda e, b=b, k=k, lt=lt, nt=nt, pu=pu: e.matmul(
                            pu[:, 0:nt], wub[b][:, k, :], xT[:, k, lt:lt + nt], start=(k == 0), stop=(k == 15)),
                             reads=[("wu", b), ("xT", k // 4)], writes=[("ps", 2 * slot + 1)])
                    sb_i = slot % 2
                    P.op("act", lambda e, sb_i=sb_i, nt=nt, pg=pg: e.activation(
                        out=sil[sb_i][:, 0:nt], in_=pg[:, 0:nt], func=AF.Silu),
                         reads=[("ps", 2 * slot)], writes=[("sil", sb_i)])
                    P.op("dve", lambda e, sb_i=sb_i, nt=nt, pu=pu, j=j, lt=lt: e.tensor_tensor(
                        out=hT[:, j, lt:lt + nt], in0=sil[sb_i][:, 0:nt], in1=pu[:, 0:nt], op=ALU.mult),
                         reads=[("sil", sb_i), ("ps", 2 * slot + 1)], writes=[("hT", j)])
            for ip in range(8):
                b = cnt["dn"] % 2
                cnt["dn"] += 1
                P.op("pool", lambda e, b=b, ip=ip: e.dma_start(out=wdb[b][:, 0:22, :], in_=wd_v[:, 0:22, ip * 256:(ip + 1) * 256]),
                     writes=[("wd", b, 0)], chan=f"ldw{b}")
                P.op("pool", lambda e, b=b, ip=ip: e.dma_start(out=wdb[b][:, 22:44, :], in_=wd_v[:, 22:44, ip * 256:(ip + 1) * 256]),
                     writes=[("wd", b, 1)], chan=f"ldw{b}")
                for ii in range(2):
                    i = ip * 2 + ii
                    for ti, (t0, nt) in enumerate(tiles):
                        lt = t0 - h0
                        slot = cnt["ev"] % 8
                        cnt["ev"] += 1
                        rb = slot % 2
                        pz = psum[slot]
                        P.op("sp", lambda e, rb=rb, i=i, t0=t0, nt=nt: e.dma_start(
                            out=xres[rb][:, 0:nt], in_=XTin[i, :, t0:t0 + nt]),
                             writes=[("xres", rb)], chan=f"ldr{rb}")
                        for j in range(44):
                            P.op("pe", lambda e, b=b, j=j, ii=ii, lt=lt, nt=nt, pz=pz: e.matmul(
                                pz[:, 0:nt], wdb[b][:, j, ii * 128:(ii + 1) * 128], hT[:, j, lt:lt + nt],
                                start=(j == 0), stop=(j == 43)),
                                 reads=[("wd", b, j // 22), ("hT", j)], writes=[("ps", slot)])
                        P.op("dve", lambda e, rb=rb, nt=nt, pz=pz: e.scalar_tensor_tensor(
                            out=zst[rb][:, 0:nt], in0=pz[:, 0:nt], scalar=0.5 / ALPHA, in1=xres[rb][:, 0:nt],
                            op0=ALU.mult, op1=ALU.add),
                             reads=[("ps", slot), ("xres", rb)], writes=[("zst", rb)])
                        P.op("sp", lambda e, rb=rb, i=i, t0=t0, nt=nt: e.dma_start(
                            out=ZTout[i, :, t0:t0 + nt], in_=zst[rb][:, 0:nt]),
                             reads=[("zst", rb)], chan=f"st{rb}")
        P.barrier()

    def phase_ln(tag, ZTin, HTout, li, final_out=None):
        arena_off[0] = CONST_END
        zt = [sb(f"{tag}_zt{i}", [128, 16, 264], F32) for i in range(2)]
        zb = [sb(f"{tag}_zb{i}", [128, 16, 264], BF16) for i in range(2)]
        zq = [sb(f"{tag}_zq{i}", [128, 16, 264], BF16) for i in range(2)]
        mean = [sb(f"{tag}_mean{i}", [128, 264], F32) for i in range(2)]
        msq = [sb(f"{tag}_msq{i}", [128, 264], F32) for i in range(2)]
        rstd = [sb(f"{tag}_rstd{i}", [128, 264], F32) for i in range(2)]
        ho = [sb(f"{tag}_ho{i}", [128, 16, 264], F32) for i in range(2)]
        eps = LN_EPS / (ALPHA * ALPHA)
        for ti, (t0, nt) in enumerate(TILES):
            b = ti % 2
            ps_s, ps_q = psum[2 * b], psum[2 * b + 1]
            P.op("sp", lambda e, b=b, t0=t0, nt=nt: e.dma_start(
                out=zt[b][:, :, 0:nt], in_=ZTin[:, :, t0:t0 + nt].rearrange("k p t -> p k t")),
                 writes=[("zt", b)], chan=f"ld{b}")
            P.op("act", lambda e, b=b, nt=nt: e.copy(out=zb[b][:, :, 0:nt], in_=zt[b][:, :, 0:nt]),
                 reads=[("zt", b)], writes=[("zb", b)])
            P.op("act", lambda e, b=b, nt=nt: e.activation(out=zq[b][:, :, 0:nt], in_=zt[b][:, :, 0:nt], func=AF.Square),
                 reads=[("zt", b)], writes=[("zq", b)])
            for k in range(16):
                P.op("pe", lambda e, b=b, k=k, nt=nt, ps_s=ps_s: e.matmul(
                    ps_s[:, 0:nt], ones_bf[:, :], zb[b][:, k, 0:nt], start=(k == 0), stop=(k == 15)),
                     reads=[("zb", b), "ones"], writes=[("ps", 2 * b)])
            for k in range(16):
                P.op("pe", lambda e, b=b, k=k, nt=nt, ps_q=ps_q: e.matmul(
                    ps_q[:, 0:nt], ones_bf[:, :], zq[b][:, k, 0:nt], start=(k == 0), stop=(k == 15)),
                     reads=[("zq", b), "ones"], writes=[("ps", 2 * b + 1)])
            P.op("dve", lambda e, b=b, nt=nt, ps_s=ps_s: e.tensor_scalar(
                out=mean[b][:, 0:nt], in0=ps_s[:, 0:nt], scalar1=1.0 / D, scalar2=None, op0=ALU.mult),
                 reads=[("ps", 2 * b)], writes=[("mean", b)])
            P.op("dve", lambda e, b=b, nt=nt: e.tensor_tensor(
                out=msq[b][:, 0:nt], in0=mean[b][:, 0:nt], in1=mean[b][:, 0:nt], op=ALU.mult),
                 reads=[("mean", b)], writes=[("msq", b)])
            P.op("dve", lambda e, b=b, nt=nt, ps_q=ps_q: e.scalar_tensor_tensor(
                out=rstd[b][:, 0:nt], in0=ps_q[:, 0:nt], scalar=1.0 / D, in1=msq[b][:, 0:nt],
                op0=ALU.mult, op1=ALU.subtract),
                 reads=[("ps", 2 * b + 1), ("msq", b)], writes=[("rstd", b)])
            P.op("act", lambda e, b=b, nt=nt: e.activation(
                out=msq[b][:, 0:nt], in_=rstd[b][:, 0:nt], func=AF.Sqrt, bias=eps, scale=1.0),
                 reads=[("rstd", b)], writes=[("msq", b)])
            P.op("dve", lambda e, b=b, nt=nt: e.reciprocal(out=rstd[b][:, 0:nt], in_=msq[b][:, 0:nt]),
                 reads=[("msq", b)], writes=[("rstd", b)])
            P.op("dve", lambda e, b=b, nt=nt: e.tensor_tensor(
                out=zt[b][:, :, 0:nt], in0=zt[b][:, :, 0:nt],
                in1=mean[b][:, 0:nt].unsqueeze(1).to_broadcast([128, 16, nt]),
                op=ALU.subtract),
                 reads=[("zt", b), ("mean", b)], writes=[("zt", b)])
            P.op("dve", lambda e, b=b, nt=nt: e.tensor_tensor(
                out=zt[b][:, :, 0:nt], in0=zt[b][:, :, 0:nt],
                in1=rstd[b][:, 0:nt].unsqueeze(1).to_broadcast([128, 16, nt]),
                op=ALU.mult),
                 reads=[("zt", b), ("rstd", b)], writes=[("zt", b)])
            for k in range(16):
                P.op("act", lambda e, b=b, k=k, nt=nt: e.activation(
                    out=ho[b][:, k, 0:nt], in_=zt[b][:, k, 0:nt], func=AF.Identity,
                    scale=lng[:, li, k:k + 1], bias=lnb[:, li, k:k + 1]),
                     reads=[("zt", b), f"lng{li}", f"lnb{li}"], writes=[("ho", b)])
            P.op("sp", lambda e, b=b, t0=t0, nt=nt: e.dma_start(
                out=HTout[:, :, t0:t0 + nt].rearrange("k p t -> p k t"), in_=ho[b][:, :, 0:nt]),
                 reads=[("ho", b)], chan=f"st{b}")
        P.barrier()

    def phase_transpose_out(HTin):
        arena_off[0] = CONST_END
        hin = [sb(f"po_in{i}", [128, 16, 128], F32) for i in range(2)]
        otok = [sb(f"po_tok{i}", [128, D], F32) for i in range(2)]
        for ci, (t0, nt) in enumerate(CHUNKS):
            b = ci % 2
            P.op("sp", lambda e, b=b, t0=t0, nt=nt: e.dma_start(
                out=hin[b][:, :, 0:nt], in_=HTin[:, :, t0:t0 + nt].rearrange("k p t -> p k t")),
                 writes=[("hin", b)], chan=f"ld{b}")
            for g in range(4):
                pb = (ci * 4 + g) % 8
                for q in range(4):
                    f = g * 4 + q
                    P.op("pe", lambda e, b=b, nt=nt, f=f, pb=pb, q=q: e.transpose(
                        psum[pb][0:nt, q * 128:(q + 1) * 128], hin[b][:, f, 0:nt], ident_sb[:, :]),
                         reads=[("hin", b), "ident"], writes=[("ps", pb)])
                en = evac_eng()
                if en == "act":
                    fn = lambda e, b=b, nt=nt, g=g, pb=pb: e.copy(out=otok[b][0:nt, g * 512:(g + 1) * 512], in_=psum[pb][0:nt, :])
                else:
                    fn = lambda e, b=b, nt=nt, g=g, pb=pb: e.tensor_copy(out=otok[b][0:nt, g * 512:(g + 1) * 512], in_=psum[pb][0:nt, :])
                P.op(en, fn, reads=[("ps", pb)], writes=[("otok", b, g)])
            P.op("sp", lambda e, b=b, t0=t0, nt=nt: e.dma_start(out=out_ap[t0:t0 + nt, :], in_=otok[b][0:nt, :]),
                 reads=[("otok", b, g) for g in range(4)], chan=f"st{b}")
        P.barrier()


    def dma(out, in_, slow=False):
        if slow:
            return lambda e: e.dma_start(out=out, in_=in_, allow_slow_non_contiguous=True)
        return lambda e: e.dma_start(out=out, in_=in_)

    def mm(out, lhsT, rhs, st, sp_):
        return lambda e: e.matmul(out, lhsT, rhs, start=st, stop=sp_)

    def fm(ap2d):
        return ap2d.rearrange("(k p) t -> k p t", p=128)

    def phase_linear(INap, nk, Wap, mchunks, OUTfn, toks=TILES, scale=1.0, bias_col=None, resid=None):
        arena_off[0] = CONST_END
        Tin = max(t0 + nt for t0, nt in toks)
        inT = sb("lin_in", [128, nk, Tin], BF16)
        wb = [sb(f"lin_w{i}", [128, nk, 128], BF16) for i in range(2)]
        st = [sb(f"lin_st{i}", [128, 512], F32) for i in range(4)]
        rs = [sb(f"lin_rs{i}", [128, 512], F32) for i in range(4)]
        if isinstance(INap, list):
            for k in range(nk):
                P.op("pool", dma(inT[:, k, 0:Tin], INap[k][:, 0:Tin]), writes=[("lin_in", k // 4)], chan="ldx")
        else:
            for kq in range(0, nk, 4):
                kn = min(4, nk - kq)
                P.op("pool", dma(inT[:, kq:kq + kn, 0:Tin], INap[kq:kq + kn, :, 0:Tin].rearrange("k p t -> p k t")),
                     writes=[("lin_in", kq // 4)], chan="ldx")
        Wv = Wap.rearrange("(k p) m -> p k m", p=128)
        cnt = 0
        for mi, (c0, mc) in enumerate(mchunks):
            b = mi % 2
            P.op("pool", dma(wb[b][:, :, 0:mc], Wv[:, :, c0:c0 + mc]), writes=[("lin_w", b)], chan=f"ldw{b}")
            for (t0, nt) in toks:
                slot = cnt % 8
                sti = cnt % 4
                cnt += 1
                if resid is not None:
                    P.op("sp", dma(rs[sti][0:mc, 0:nt], resid(mi, t0, nt)), writes=[("lin_rs", sti)], chan=f"ldr{sti}")
                for k in range(nk):
                    P.op("pe", mm(psum[slot][0:mc, 0:nt], wb[b][:, k, 0:mc], inT[:, k, t0:t0 + nt], k == 0, k == nk - 1),
                         reads=[("lin_w", b), ("lin_in", k // 4)], writes=[("ps", slot)])
                if resid is not None:
                    P.op("dve", lambda e, slot=slot, sti=sti, mc=mc, nt=nt: e.scalar_tensor_tensor(
                        out=st[sti][0:mc, 0:nt], in0=psum[slot][0:mc, 0:nt], scalar=scale, in1=rs[sti][0:mc, 0:nt],
                        op0=ALU.mult, op1=ALU.add), reads=[("ps", slot), ("lin_rs", sti)], writes=[("lin_st", sti)])
                elif bias_col is not None:
                    P.op("act", lambda e, slot=slot, sti=sti, mc=mc, nt=nt: e.activation(
                        out=st[sti][0:mc, 0:nt], in_=psum[slot][0:mc, 0:nt], func=AF.Identity, bias=bias_col[0:mc, 0:1], scale=scale),
                         reads=[("ps", slot), "consts2"], writes=[("lin_st", sti)])
                else:
                    en = evac_eng()
                    if en == "act":
                        P.op("act", lambda e, slot=slot, sti=sti, mc=mc, nt=nt: e.mul(
                            out=st[sti][0:mc, 0:nt], in_=psum[slot][0:mc, 0:nt], mul=scale),
                             reads=[("ps", slot)], writes=[("lin_st", sti)])
                    else:
                        P.op("dve", lambda e, slot=slot, sti=sti, mc=mc, nt=nt: e.tensor_scalar(
                            out=st[sti][0:mc, 0:nt], in0=psum[slot][0:mc, 0:nt], scalar1=scale, scalar2=None, op0=ALU.mult),
                             reads=[("ps", slot)], writes=[("lin_st", sti)])
                P.op("sp", dma(OUTfn(mi, t0, nt), st[sti][0:mc, 0:nt]), reads=[("lin_st", sti)], chan=f"st{sti}")
        P.barrier()

    def phase_tokproj(INap, nk, Wap, M, OUTap, chunks=CHUNKS):
        arena_off[0] = CONST_END
        Tin = max(t0 + nt for t0, nt in chunks)
        inT = sb("tp_in", [128, nk, Tin], BF16)
        wsb = sb("tp_w", [128, nk, M], BF16)
        vst = [sb(f"tp_st{i}", [128, M], BF16) for i in range(2)]
        if isinstance(INap, list):
            for k in range(nk):
                P.op("pool", dma(inT[:, k, 0:Tin], INap[k][:, 0:Tin]), writes=["tp_in"], chan="ldx")
        else:
            P.op("pool", dma(inT[:, :, 0:Tin], INap[:, :, 0:Tin].rearrange("k p t -> p k t")), writes=["tp_in"], chan="ldx")
        P.op("pool", dma(wsb[:], Wap.rearrange("(k p) m -> p k m", p=128)), writes=["tp_w"], chan="ldw0")
        cnt = 0
        for ci, (t0, nt) in enumerate(chunks):
            b = ci % 2
            for n0 in range(0, M, 512):
                w = min(512, M - n0)
                slot = cnt % 8
                cnt += 1
                for k in range(nk):
                    P.op("pe", mm(psum[slot][0:nt, 0:w], inT[:, k, t0:t0 + nt], wsb[:, k, n0:n0 + w], k == 0, k == nk - 1),
                         reads=["tp_in", "tp_w"], writes=[("ps", slot)])
                en = evac_eng()
                if en == "act":
                    P.op("act", lambda e, slot=slot, b=b, nt=nt, n0=n0, w=w: e.copy(out=vst[b][0:nt, n0:n0 + w], in_=psum[slot][0:nt, 0:w]),
                         reads=[("ps", slot)], writes=[("tp_st", b, n0)])
                else:
                    P.op("dve", lambda e, slot=slot, b=b, nt=nt, n0=n0, w=w: e.tensor_copy(out=vst[b][0:nt, n0:n0 + w], in_=psum[slot][0:nt, 0:w]),
                         reads=[("ps", slot)], writes=[("tp_st", b, n0)])
            P.op("sp", dma(OUTap[t0:t0 + nt, :], vst[b][0:nt, :]), reads=[("tp_st", b, n0) for n0 in range(0, M, 512)], chan=f"st{b}")
        P.barrier()

    def phase_norm(INfn, nch, gc, gcol, eps, OUTfn, center=False, addin=None):
        arena_off[0] = CONST_END
        xt = [sb(f"nm_x{i}", [128, nch, 264], F32) for i in range(2)]
        xb = [sb(f"nm_b{i}", [128, nch, 264], BF16) for i in range(2)]
        xq = [sb(f"nm_q{i}", [128, nch, 264], BF16) for i in range(2)]
        ng = nch // gc
        mean = [sb(f"nm_m{i}", [128, ng, 264], F32) for i in range(2)]
        msq = [sb(f"nm_s{i}", [128, ng, 264], F32) for i in range(2)]
        rstd = [sb(f"nm_r{i}", [128, ng, 264], F32) for i in range(2)]
        ho = [sb(f"nm_o{i}", [128, nch, 264], F32) for i in range(2)]
        addt = [sb(f"nm_a{i}", [128, nch, 264], F32) for i in range(2)] if addin is not None else None
        for ti, (t0, nt) in enumerate(TILES):
            b = ti % 2
            P.op("sp", dma(xt[b][:, :, 0:nt], INfn(t0, nt).rearrange("k p t -> p k t")), writes=[("nm_x", b)], chan=f"ld{b}")
            if addin is not None:
                P.op("sp", dma(addt[b][:, :, 0:nt], addin[0](t0, nt).rearrange("k p t -> p k t")), writes=[("nm_a", b)], chan=f"lda{b}")
            P.op("act", lambda e, b=b, nt=nt: e.activation(out=xq[b][:, :, 0:nt], in_=xt[b][:, :, 0:nt], func=AF.Square),
                 reads=[("nm_x", b)], writes=[("nm_q", b)])
            if center:
                P.op("act", lambda e, b=b, nt=nt: e.copy(out=xb[b][:, :, 0:nt], in_=xt[b][:, :, 0:nt]),
                     reads=[("nm_x", b)], writes=[("nm_b", b)])
            for g in range(ng):
                pq = psum[(2 * g) % 8]
                pm_ = psum[(2 * g + 1) % 8]
                for k in range(gc):
                    P.op("pe", mm(pq[:, 0:nt], ones_bf[:, :], xq[b][:, g * gc + k, 0:nt], k == 0, k == gc - 1),
                         reads=[("nm_q", b), "ones"], writes=[("ps", (2 * g) % 8)])
                if center:
                    for k in range(gc):
                        P.op("pe", mm(pm_[:, 0:nt], ones_bf[:, :], xb[b][:, g * gc + k, 0:nt], k == 0, k == gc - 1),
                             reads=[("nm_b", b), "ones"], writes=[("ps", (2 * g + 1) % 8)])
                    P.op("dve", lambda e, b=b, g=g, nt=nt, pm_=pm_: e.tensor_scalar(
                        out=mean[b][:, g, 0:nt], in0=pm_[:, 0:nt], scalar1=1.0 / (gc * 128), scalar2=None, op0=ALU.mult),
                         reads=[("ps", (2 * g + 1) % 8)], writes=[("nm_m", b, g)])
                    P.op("dve", lambda e, b=b, g=g, nt=nt: e.tensor_tensor(
                        out=msq[b][:, g, 0:nt], in0=mean[b][:, g, 0:nt], in1=mean[b][:, g, 0:nt], op=ALU.mult),
                         reads=[("nm_m", b, g)], writes=[("nm_s", b, g)])
                    P.op("dve", lambda e, b=b, g=g, nt=nt, pq=pq: e.scalar_tensor_tensor(
                        out=rstd[b][:, g, 0:nt], in0=pq[:, 0:nt], scalar=1.0 / (gc * 128), in1=msq[b][:, g, 0:nt],
                        op0=ALU.mult, op1=ALU.subtract),
                         reads=[("ps", (2 * g) % 8), ("nm_s", b, g)], writes=[("nm_r", b, g)])
                    P.op("act", lambda e, b=b, g=g, nt=nt: e.activation(
                        out=msq[b][:, g, 0:nt], in_=rstd[b][:, g, 0:nt], func=AF.Sqrt, bias=eps, scale=1.0),
                         reads=[("nm_r", b, g)], writes=[("nm_s", b, g)])
                else:
                    P.op("act", lambda e, b=b, g=g, nt=nt, pq=pq: e.activation(
                        out=msq[b][:, g, 0:nt], in_=pq[:, 0:nt], func=AF.Sqrt, bias=eps, scale=1.0 / (gc * 128)),
                         reads=[("ps", (2 * g) % 8)], writes=[("nm_s", b, g)])
                P.op("dve", lambda e, b=b, g=g, nt=nt: e.reciprocal(out=rstd[b][:, g, 0:nt], in_=msq[b][:, g, 0:nt]),
                     reads=[("nm_s", b, g)], writes=[("nm_r", b, g)])
                sl = slice(g * gc, (g + 1) * gc)
                if center:
                    P.op("dve", lambda e, b=b, g=g, nt=nt, sl=sl: e.tensor_tensor(
                        out=xt[b][:, sl, 0:nt], in0=xt[b][:, sl, 0:nt],
                        in1=mean[b][:, g, 0:nt].unsqueeze(1).to_broadcast([128, gc, nt]), op=ALU.subtract),
                         reads=[("nm_x", b), ("nm_m", b, g)], writes=[("nm_x", b)])
                P.op("dve", lambda e, b=b, g=g, nt=nt, sl=sl: e.tensor_tensor(
                    out=xt[b][:, sl, 0:nt], in0=xt[b][:, sl, 0:nt],
                    in1=rstd[b][:, g, 0:nt].unsqueeze(1).to_broadcast([128, gc, nt]), op=ALU.mult),
                     reads=[("nm_x", b), ("nm_r", b, g)], writes=[("nm_x", b)])
            for k in range(nch):
                P.op("act", lambda e, b=b, k=k, nt=nt: e.activation(
                    out=ho[b][:, k, 0:nt], in_=xt[b][:, k, 0:nt], func=AF.Copy, scale=gcol[:, k:k + 1]),
                     reads=[("nm_x", b), "consts2"], writes=[("nm_o", b)])
                if addin is not None:
                    P.op("dve", lambda e, b=b, k=k, nt=nt: e.scalar_tensor_tensor(
                        out=ho[b][:, k, 0:nt], in0=addt[b][:, k, 0:nt], scalar=addin[1][:, k:k + 1], in1=ho[b][:, k, 0:nt],
                        op0=ALU.mult, op1=ALU.add),
                         reads=[("nm_a", b), ("nm_o", b), "consts2"], writes=[("nm_o", b)])
            o_ = OUTfn(t0, nt)
            if isinstance(o_, list):
                for k in range(nch):
                    P.op("sp", dma(o_[k], ho[b][:, k, 0:nt]), reads=[("nm_o", b)], chan=f"st{b}")
            else:
                P.op("sp", dma(o_.rearrange("k p t -> p k t"), ho[b][:, :, 0:nt]), reads=[("nm_o", b)], chan=f"st{b}")
        P.barrier()


    def dump(name, ap):
        shp = list(ap.shape)
        o = nc.dram_tensor("dbg_" + name, shp, ap.dtype, kind="ExternalOutput").ap()
        dumps.append((o, ap))

    RMS_EPS = 1e-6
    WIN_CH = [(128 * i, 128) for i in range(22)] + [(2816, 64), (2880, 64)]

    def win_out(mi, t0, nt):
        if mi < 4:
            return UQ[mi, :, t0:t0 + nt]
        if mi < 6:
            return UKV[mi - 4, :, t0:t0 + nt]
        if mi < 14:
            return gx(2 + mi - 6)[:, t0:t0 + nt]
        if mi < 22:
            return ZM[mi - 14, :, t0:t0 + nt]
        return URR[mi - 22, :, t0:t0 + nt]

    def phase_krope():
        arena_off[0] = CONST_END
        a = sb("kr_a", [64, T], F32); b_ = sb("kr_b", [64, T], F32); c = sb("kr_c", [64, T], F32); d_ = sb("kr_d", [64, T], F32)
        P.op("sp", dma(a[:], URR[0, :, :]), writes=["kr_a"], chan="ld0")
        P.op("sp", dma(b_[:], URR[1, :, :]), writes=["kr_b"], chan="ld0")
        P.op("sp", dma(c[:], cos2[:, :]), writes=["kr_c"], chan="ld1")
        P.op("sp", dma(d_[:], sin2[:, :]), writes=["kr_d"], chan="ld1")
        P.op("dve", lambda e: e.tensor_tensor(out=a[:], in0=a[:], in1=c[:], op=ALU.mult), reads=["kr_a", "kr_c"], writes=["kr_a"])
        P.op("dve", lambda e: e.tensor_tensor(out=b_[:], in0=b_[:], in1=d_[:], op=ALU.mult), reads=["kr_b", "kr_d"], writes=["kr_b"])
        P.op("dve", lambda e: e.tensor_tensor(out=a[:], in0=a[:], in1=b_[:], op=ALU.add), reads=["kr_a", "kr_b"], writes=["kr_a"])
        P.op("sp", dma(gx(10)[:, :], a[:]), reads=["kr_a"], chan="st0")
        P.barrier()

    def phase_gather():
        groups = [[2 * i, 2 * i + 1] for i in range(NCORES // 2)]
        for i in range(NGX):
            P.op("pool", lambda e, i=i: e.collective_compute("AllGather", ALU.bypass, replica_groups=groups,
                                                             ins=[GX_t[i].ap().opt()], outs=[GXA_t[i].ap().opt()]),
                 chan="cc", writes=[("gather", i)], inc=1)
        P.barrier()

    ATT_SCALE = 192.0 ** -0.5
    KCH = [(b, c, b * T + t0, nt) for b in range(2) for c, (t0, nt) in enumerate(CHUNKS)]

    def phase_mla_attn():
        for hg in range(2):
            arena_off[0] = CONST_END
            QN = sb("at_qn", [128, 4, T], BF16)
            QR = sb("at_qr", [64, 4, T], BF16)
            KN = sb("at_kn", [128, 4, 2 * T], BF16)
            KR = sb("at_kr", [64, 2 * T], BF16)
            V = sb("at_v", [128, 34, 512], BF16)
            cs = sb("at_cos", [64, T], F32); sn = sb("at_sin", [64, T], F32)
            r1 = sb("at_r1", [64, T], F32); r2 = sb("at_r2", [64, T], F32)
            PT = [sb(f"at_p{i}", [128, 512], BF16) for i in range(4)]
            rsb = [sb(f"at_rs{i}", [128, 512], F32) for i in range(2)]
            ost = [sb(f"at_os{i}", [128, 512], F32) for i in range(2)]
            P.op("pool", dma(QN[:], QALL[hg * 4:(hg + 1) * 4, 0:128, :].rearrange("h p t -> p h t")), writes=["at_qn"], chan="ldx")
            P.op("pool", dma(KN[:], KNA[hg * 4:(hg + 1) * 4, :, :].rearrange("h p t -> p h t")), writes=["at_kn"], chan="ldx")
            for b in range(2):
                P.op("pool", dma(KR[:, b * T:(b + 1) * T], gxa(b, 10)), writes=["at_kr"], chan="ldx")
                P.op("sp", dma(V[:, b * 17:b * 17 + 16, :],
                               VTA[b * T:b * T + 2048, hg * 512:(hg + 1) * 512].rearrange("(c p) m -> p c m", p=128)),
                     writes=["at_v"], chan="ld0")
                P.op("sp", dma(V[0:8, b * 17 + 16, :], VTA[b * T + 2048:b * T + 2056, hg * 512:(hg + 1) * 512]),
                     writes=["at_v"], chan="ld0")
            P.op("sp", dma(cs[:], cos2[:, :]), writes=["at_cs"], chan="ld1")
            P.op("sp", dma(sn[:], sin2[:, :]), writes=["at_sn"], chan="ld1")
            for hl in range(4):
                h = hg * 4 + hl
                P.op("sp", dma(r1[:], QALL[h, 128:192, :]), writes=["at_r1"], chan="ld2")
                P.op("sp", dma(r2[:], QALL[h, 192:256, :]), writes=["at_r2"], chan="ld2")
                P.op("dve", lambda e: e.tensor_tensor(out=r1[:], in0=r1[:], in1=cs[:], op=ALU.mult), reads=["at_r1", "at_cs"], writes=["at_r1"])
                P.op("dve", lambda e: e.tensor_tensor(out=r2[:], in0=r2[:], in1=sn[:], op=ALU.mult), reads=["at_r2", "at_sn"], writes=["at_r2"])
                P.op("dve", lambda e, hl=hl: e.tensor_tensor(out=QR[:, hl, :], in0=r1[:], in1=r2[:], op=ALU.add),
                     reads=["at_r1", "at_r2"], writes=["at_qr"])
            blocks = []
            for hl in range(4):
                for qi, (t0, nt) in enumerate(ATILES):
                    for ci, (b, c, k0, nk) in enumerate(KCH):
                        blocks.append((hl, hl * 5 + qi, t0, nt, ci, k0, nk))
            DEP = 3

            def front(i, blk):
                hl, qi, t0, nt, ci, k0, nk = blk
                sbk = i % 4
                P.op("pe", mm(psum[sbk][0:nk, 0:nt], KN[:, hl, k0:k0 + nk], QN[:, hl, t0:t0 + nt], True, False),
                     reads=["at_kn", "at_qn"], writes=[("ps", sbk)])
                P.op("pe", mm(psum[sbk][0:nk, 0:nt], KR[:, k0:k0 + nk], QR[:, hl, t0:t0 + nt], False, True),
                     reads=["at_kr", "at_qr"], writes=[("ps", sbk)])
                P.op("act", lambda e, sbk=sbk, nk=nk, nt=nt: e.activation(
                    out=PT[sbk][0:nk, 0:nt], in_=psum[sbk][0:nk, 0:nt], func=AF.Exp, scale=ATT_SCALE),
                     reads=[("ps", sbk)], writes=[("at_p", sbk)])

            def back(i, blk, hg=hg):
                hl, qi, t0, nt, ci, k0, nk = blk
                h = hg * 4 + hl
                pi = i % 4
                ob = 4 + 2 * (qi % 2)
                po, pssum = psum[ob], psum[ob + 1]
                P.op("pe", mm(po[:, 0:nt], V[0:nk, ci, hl * 128:(hl + 1) * 128], PT[pi][0:nk, 0:nt], ci == 0, ci == 33),
                     reads=["at_v", ("at_p", pi)], writes=[("ps", ob)])
                P.op("pe", mm(pssum[:, 0:nt], ones_bf[0:nk, :], PT[pi][0:nk, 0:nt], ci == 0, ci == 33),
                     reads=["ones", ("at_p", pi)], writes=[("ps", ob + 1)])
                if ci == 33:
                    rb = qi % 2
                    P.op("dve", lambda e, rb=rb, nt=nt, pssum=pssum: e.reciprocal(out=rsb[rb][:, 0:nt], in_=pssum[:, 0:nt]),
                         reads=[("ps", ob + 1)], writes=[("at_rs", rb)])
                    P.op("dve", lambda e, rb=rb, nt=nt, po=po: e.tensor_tensor(
                        out=ost[rb][:, 0:nt], in0=po[:, 0:nt], in1=rsb[rb][:, 0:nt], op=ALU.mult),
                         reads=[("ps", ob), ("at_rs", rb)], writes=[("at_os", rb)])
                    P.op("sp", dma(OAT[h, :, t0:t0 + nt], ost[rb][:, 0:nt]), reads=[("at_os", rb)], chan=f"st{rb}")

            for i in range(len(blocks) + DEP):
                if i < len(blocks):
                    front(i, blocks[i])
                if i >= DEP:
                    back(i - DEP, blocks[i - DEP])
            P.barrier()


    def phase_select():
        arena_off[0] = CONST_END
        ga = [sb(f"sl_a{i}", [128, T], F32) for i in range(2)]
        gb = [sb(f"sl_b{i}", [128, T], F32) for i in range(2)]
        oo = [sb(f"sl_o{i}", [128, T], F32) for i in range(2)]
        ot = [sb(f"sl_t{i}", [128, T], F32) for i in range(2)]
        for k in range(8):
            i = k % 2
            P.op("sp", dma(ga[i][:, :], gxa(0, 2 + k)), writes=[("sl_a", i)], chan="ld")
            P.op("sp", dma(gb[i][:, :], gxa(1, 2 + k)), writes=[("sl_b", i)], chan="ld")
            P.op("dve", lambda e, i=i: e.tensor_scalar(out=oo[i][:, :], in0=ga[i][:, :], scalar1=om_sb[:, 0:1], scalar2=None, op0=ALU.mult),
                 reads=[("sl_a", i), "consts2"], writes=[("sl_o", i)])
            P.op("dve", lambda e, i=i: e.scalar_tensor_tensor(out=oo[i][:, :], in0=gb[i][:, :], scalar=om_sb[:, 1:2], in1=oo[i][:, :],
                                                             op0=ALU.mult, op1=ALU.add),
                 reads=[("sl_b", i), ("sl_o", i), "consts2"], writes=[("sl_o", i)])
            P.op("dve", lambda e, i=i: e.tensor_scalar(out=ot[i][:, :], in0=ga[i][:, :], scalar1=om_sb[:, 1:2], scalar2=None, op0=ALU.mult),
                 reads=[("sl_a", i), "consts2"], writes=[("sl_t", i)])
            P.op("dve", lambda e, i=i: e.scalar_tensor_tensor(out=ot[i][:, :], in0=gb[i][:, :], scalar=om_sb[:, 0:1], in1=ot[i][:, :],
                                                             op0=ALU.mult, op1=ALU.add),
                 reads=[("sl_b", i), ("sl_t", i), "consts2"], writes=[("sl_t", i)])
            P.op("sp", dma(XO[0, k, :, :], oo[i][:, :]), reads=[("sl_o", i)], chan="st")
            P.op("sp", dma(XO[1, k, :, :], ot[i][:, :]), reads=[("sl_t", i)], chan="st2")
        P.barrier()

    def phase_conv():
        arena_off[0] = CONST_END
        XMt = [sb(f"cv_x{i}", [128, T + 4], F32) for i in range(2)]
        acc = [sb(f"cv_a{i}", [128, T], F32) for i in range(2)]
        xco = [sb(f"cv_o{i}", [128, T], F32) for i in range(2)]
        n = 0
        for b in range(2):
            for k in range(8):
                i = n % 2
                n += 1
                src = XO[b, k]
                oth = XO[1 - b, k]
                P.op("dve", lambda e, i=i: e.memset(XMt[i][:, 0:2], 0.0), writes=[("cv_x", i)])
                P.op("sp", dma(XMt[i][:, 2:T + 2], src[:, :]), writes=[("cv_x", i)], chan="ld")
                P.op("sp", dma(XMt[i][:, T + 2:T + 3], oth[:, T - 1:T], True), writes=[("cv_x", i)], chan="ld")
                P.op("sp", dma(XMt[i][:, T + 3:T + 4], oth[:, T - 2:T - 1], True), writes=[("cv_x", i)], chan="ld")
                for j in range(5):
                    off = j if b == 0 else 4 - j
                    if j == 0:
                        P.op("dve", lambda e, i=i, k=k, off=off: e.tensor_scalar(
                            out=acc[i][:, :], in0=XMt[i][:, off:off + T], scalar1=cw_sb[:, 0, k:k + 1], scalar2=None, op0=ALU.mult),
                             reads=[("cv_x", i), "consts2"], writes=[("cv_a", i)])
                    else:
                        P.op("dve", lambda e, i=i, k=k, j=j, off=off: e.scalar_tensor_tensor(
                            out=acc[i][:, :], in0=XMt[i][:, off:off + T], scalar=cw_sb[:, j, k:k + 1], in1=acc[i][:, :],
                            op0=ALU.mult, op1=ALU.add),
                             reads=[("cv_x", i), ("cv_a", i), "consts2"], writes=[("cv_a", i)])
                P.op("act", lambda e, i=i, k=k: e.activation(out=xco[i][:, :], in_=acc[i][:, :], func=AF.Silu, bias=cb_sb[:, k:k + 1]),
                     reads=[("cv_a", i), "consts2"], writes=[("cv_o", i)])
                P.op("sp", dma(XCM[b, k, :, :], xco[i][:, :]), reads=[("cv_o", i)], chan="st")
                P.op("sp", dma(XCM[b, 8 + k, :, :], XMt[i][:, 2:T + 2]), reads=[("cv_x", i)], chan="st2")
        P.barrier()

    def phase_scans():
        arena_off[0] = CONST_END
        T2 = 2 * T
        LI = sb("sc_li", [36, T2], F32); LF = sb("sc_lf", [36, T2], F32); ONE = sb("sc_one", [36, T2], F32)
        Bt = sb("sc_b", [36, T2], F32); At = sb("sc_a", [36, T2], F32); Mt = sb("sc_m", [36, T2], F32); mt = sb("sc_mm", [36, T2], F32)
        P.op("dve", lambda e: e.memset(ONE[:, :], 1.0), writes=["sc_one"])
        for d in range(2):
            p0 = 32 * d
            ps_ = slice(p0, p0 + 4)
            for b in range(2):
                P.op("sp", dma(LI[ps_, b * T:(b + 1) * T], GRAW[b, 8 * d:8 * d + 4, :]), writes=[("sc_li", d)], chan="ld")
                P.op("sp", dma(LF[ps_, b * T:(b + 1) * T], GRAW[b, 8 * d + 4:8 * d + 8, :]), writes=[("sc_lf", d)], chan="ld")
            P.op("act", lambda e, ps_=ps_: e.activation(out=LF[ps_, :], in_=LF[ps_, :], func=AF.Exp, scale=-1.0),
                 reads=[("sc_lf", d)], writes=[("sc_lf", d)])
            P.op("act", lambda e, ps_=ps_: e.activation(out=LF[ps_, :], in_=LF[ps_, :], func=AF.Ln, bias=1.0, scale=1.0),
                 reads=[("sc_lf", d)], writes=[("sc_lf", d)])
            if d == 0:
                seg1 = lambda t, ps_=ps_: t[ps_, 0:T]
                seg2 = lambda t, ps_=ps_: t[ps_, T:T2][:, ::-1]
                last1 = lambda t, ps_=ps_: t[ps_, T - 1:T]
            else:
                seg1 = lambda t, ps_=ps_: t[ps_, T:T2]
                seg2 = lambda t, ps_=ps_: t[ps_, 0:T][:, ::-1]
                last1 = lambda t, ps_=ps_: t[ps_, T2 - 1:T2]
            P.op("dve", lambda e, seg1=seg1: e.tensor_tensor_scan(out=seg1(Bt), data0=seg1(ONE), data1=seg1(LF), initial=0.0,
                                                                  op0=ALU.mult, op1=ALU.subtract),
                 reads=[("sc_lf", d), "sc_one"], writes=[("sc_b", d)])
            P.op("dve", lambda e, seg2=seg2, last1=last1: e.tensor_tensor_scan(out=seg2(Bt), data0=seg2(ONE), data1=seg2(LF), initial=last1(Bt),
                                                                               op0=ALU.mult, op1=ALU.subtract),
                 reads=[("sc_lf", d), "sc_one", ("sc_b", d)], writes=[("sc_b", d)])
            P.op("dve", lambda e, ps_=ps_: e.tensor_tensor(out=At[ps_, :], in0=LI[ps_, :], in1=Bt[ps_, :], op=ALU.subtract),
                 reads=[("sc_li", d), ("sc_b", d)], writes=[("sc_a", d)])
            P.op("dve", lambda e, seg1=seg1: e.tensor_tensor_scan(out=seg1(Mt), data0=seg1(At), data1=seg1(At), initial=0.0,
                                                                  op0=ALU.max, op1=ALU.max),
                 reads=[("sc_a", d)], writes=[("sc_m", d)])
            P.op("dve", lambda e, seg2=seg2, last1=last1: e.tensor_tensor_scan(out=seg2(Mt), data0=seg2(At), data1=seg2(At), initial=last1(Mt),
                                                                               op0=ALU.max, op1=ALU.max),
                 reads=[("sc_a", d), ("sc_m", d)], writes=[("sc_m", d)])
            P.op("dve", lambda e, ps_=ps_: e.tensor_tensor(out=mt[ps_, :], in0=Bt[ps_, :], in1=Mt[ps_, :], op=ALU.add),
                 reads=[("sc_b", d), ("sc_m", d)], writes=[("sc_mm", d)])
            P.op("dve", lambda e, ps_=ps_: e.tensor_scalar(out=Mt[ps_, :], in0=Mt[ps_, :], scalar1=-1.0, scalar2=None, op0=ALU.mult),
                 reads=[("sc_m", d), ("sc_mm", d)], writes=[("sc_m", d)])
            P.op("dve", lambda e, ps_=ps_: e.tensor_scalar(out=mt[ps_, :], in0=mt[ps_, :], scalar1=-1.0, scalar2=None, op0=ALU.mult),
                 reads=[("sc_mm", d)], writes=[("sc_mm", d)])
            P.op("sp", dma(RW[d, 0, :, :], Mt[ps_, :]), reads=[("sc_m", d)], chan="st0")
            P.op("sp", dma(RW[d, 1, :, :], mt[ps_, :]), reads=[("sc_mm", d)], chan="st1")
            P.op("sp", dma(RW[d, 2, :, :], At[ps_, :]), reads=[("sc_a", d)], chan="st2")
        P.barrier()

    def phase_mix(h, d):
        arena_off[0] = CONST_END
        T2 = 2 * T
        Q = sb("mx_q", [128, 2, T2], BF16); Kt = sb("mx_k", [128, 2, T2], BF16); V = sb("mx_v", [128, 34, 256], BF16)
        nM = sb("mx_nm", [128, T2], F32); en = sb("mx_en", [128, T2], F32); acol = sb("mx_ac", [128, 34], F32)
        mle = sb("mx_le", [128, 4, 512], F32); mge = sb("mx_ge", [128, 4, 512], F32)
        es = [sb(f"mx_es{i}", [128, 512], F32) for i in range(4)]
        at = [sb(f"mx_at{i}", [128, 512], BF16) for i in range(4)]
        xs = [sb(f"mx_xs{i}", [128, 512], F32) for i in range(4)]
        dn = [sb(f"mx_dn{i}", [128, 512], F32) for i in range(2)]
        ost = [sb(f"mx_os{i}", [128, 2, 512], F32) for i in range(2)]
        P.op("pool", dma(Q[:], QM[h].rearrange("c p t -> p c t")), writes=["mx_q"], chan="ld")
        P.op("pool", dma(Kt[:], KM[h].rearrange("c p t -> p c t")), writes=["mx_k"], chan="ld")
        for b in range(2):
            P.op("sp", dma(V[:, b * 17:b * 17 + 16, :], VM[h, b * T:b * T + 2048, :].rearrange("(c p) m -> p c m", p=128)),
                 writes=[("mx_v", b)], chan="ld")
            P.op("sp", dma(V[0:8, b * 17 + 16, :], VM[h, b * T + 2048:b * T + 2056, :]), writes=[("mx_v", b, 1)], chan="ld")
            P.op("sp", dma(acol[:, b * 17:b * 17 + 16], RW[d, 2, h, b * T:b * T + 2048].rearrange("(c p) -> p c", p=128), True),
                 writes=[("mx_ac", b)], chan="ld")
            P.op("sp", dma(acol[0:8, b * 17 + 16:b * 17 + 17], RW[d, 2, h, b * T + 2048:b * T + 2056].rearrange("(p o) -> p o", o=1), True),
                 writes=[("mx_ac", b, 1)], chan="ld")
        P.op("sp", dma(nM[:], RW[d, 0, h:h + 1, :].partition_broadcast(128)), writes=["mx_nm"], chan="ld")
        P.op("sp", dma(en[:], RW[d, 1, h:h + 1, :].partition_broadcast(128)), writes=["mx_en"], chan="ld")
        P.op("sp", dma(mle[:], mask_le[:, :, :]), writes=["mx_le"], chan="ld")
        P.op("sp", dma(mge[:], mask_ge[:, :, :]), writes=["mx_ge"], chan="ld")
        P.op("act", lambda e: e.activation(out=en[:], in_=en[:], func=AF.Exp), reads=["mx_en"], writes=["mx_en"])
        vkeys = [("mx_v", 0), ("mx_v", 0, 1), ("mx_v", 1), ("mx_v", 1, 1)]
        akeys = [("mx_ac", 0), ("mx_ac", 0, 1), ("mx_ac", 1), ("mx_ac", 1, 1)]
        blocks = []
        gi = 0
        for qb in range(1):
            rel = "le" if d == qb else "ge"
            for (t0, nt) in ATILES:
                tq = qb * T + t0
                lst = []
                if rel == "ge":
                    ob = 1 - qb
                    for c, (s0, nk) in enumerate(CHUNKS):
                        lst.append((ob * 17 + c, ob * T + s0, nk, None))
                for c, (s0, nk) in enumerate(CHUNKS):
                    if rel == "le":
                        if s0 >= t0 + nt:
                            continue
                        mk = None if s0 < t0 else (mle, (s0 - t0) // 128)
                    else:
                        if s0 < t0:
                            continue
                        mk = None if s0 >= t0 + nt else (mge, (s0 - t0) // 128)
                    lst.append((qb * 17 + c, qb * T + s0, nk, mk))
                for li_, (ci, k0, nk, mk) in enumerate(lst):
                    blocks.append((gi, tq, nt, ci, k0, nk, mk, li_ == 0, li_ == len(lst) - 1))
                gi += 1
        DEP = 3
        pO0, pO1, pD = psum[4], psum[5], psum[6]

        def front(i, blk):
            g_, tq, nt, ci, k0, nk, mk, first, last = blk
            sbk = i % 4
            pi = i % 4
            for c2 in range(2):
                P.op("pe", mm(psum[sbk][0:nk, 0:nt], Kt[:, c2, k0:k0 + nk], Q[:, c2, tq:tq + nt], c2 == 0, c2 == 1),
                     reads=["mx_k", "mx_q"], writes=[("ps", sbk)])
            if mk is None:
                P.op("act", lambda e, pi=pi, nk=nk, nt=nt, tq=tq, ci=ci: e.activation(
                    out=es[pi][0:nk, 0:nt], in_=nM[0:nk, tq:tq + nt], func=AF.Exp, bias=acol[0:nk, ci:ci + 1]),
                     reads=["mx_nm"] + akeys, writes=[("mx_es", pi)])
            else:
                mt_, mj = mk
                P.op("dve", lambda e, pi=pi, nk=nk, nt=nt, tq=tq, ci=ci, mt_=mt_, mj=mj: e.scalar_tensor_tensor(
                    out=xs[pi][0:nk, 0:nt], in0=nM[0:nk, tq:tq + nt], scalar=acol[0:nk, ci:ci + 1], in1=mt_[0:nk, mj, 0:nt],
                    op0=ALU.add, op1=ALU.min),
                     reads=["mx_nm", "mx_le", "mx_ge"] + akeys, writes=[("mx_xs", pi)])
                P.op("act", lambda e, pi=pi, nk=nk, nt=nt: e.activation(
                    out=es[pi][0:nk, 0:nt], in_=xs[pi][0:nk, 0:nt], func=AF.Exp),
                     reads=[("mx_xs", pi)], writes=[("mx_es", pi)])
            P.op("dve", lambda e, sbk=sbk, pi=pi, nk=nk, nt=nt: e.scalar_tensor_tensor(
                out=at[pi][0:nk, 0:nt], in0=psum[sbk][0:nk, 0:nt], scalar=1.0 / 16.0, in1=es[pi][0:nk, 0:nt],
                op0=ALU.mult, op1=ALU.mult),
                 reads=[("ps", sbk), ("mx_es", pi)], writes=[("mx_at", pi)])

        def back(i, blk):
            g_, tq, nt, ci, k0, nk, mk, first, last = blk
            pi = i % 4
            P.op("pe", mm(pO0[:, 0:nt], V[0:nk, ci, 0:128], at[pi][0:nk, 0:nt], first, last),
                 reads=vkeys + [("mx_at", pi)], writes=[("ps", 4)])
            P.op("pe", mm(pO1[:, 0:nt], V[0:nk, ci, 128:256], at[pi][0:nk, 0:nt], first, last),
                 reads=vkeys + [("mx_at", pi)], writes=[("ps", 5)])
            P.op("pe", mm(pD[:, 0:nt], ones_bf[0:nk, :], at[pi][0:nk, 0:nt], first, last),
                 reads=["ones", ("mx_at", pi)], writes=[("ps", 6)])
            if last:
                oi = g_ % 2
                P.op("act", lambda e, oi=oi, nt=nt: e.activation(out=dn[oi][:, 0:nt], in_=pD[:, 0:nt], func=AF.Abs),
                     reads=[("ps", 6)], writes=[("mx_dn", oi)])
                P.op("dve", lambda e, oi=oi, nt=nt, tq=tq: e.tensor_tensor(
                    out=dn[oi][:, 0:nt], in0=dn[oi][:, 0:nt], in1=en[:, tq:tq + nt], op=ALU.max),
                     reads=[("mx_dn", oi), "mx_en"], writes=[("mx_dn", oi)])
                P.op("dve", lambda e, oi=oi, nt=nt: e.reciprocal(out=dn[oi][:, 0:nt], in_=dn[oi][:, 0:nt]),
                     reads=[("mx_dn", oi)], writes=[("mx_dn", oi)])
                P.op("dve", lambda e, oi=oi, nt=nt: e.tensor_tensor(out=ost[oi][:, 0, 0:nt], in0=pO0[:, 0:nt], in1=dn[oi][:, 0:nt], op=ALU.mult),
                     reads=[("ps", 4), ("mx_dn", oi)], writes=[("mx_os", oi)])
                P.op("dve", lambda e, oi=oi, nt=nt: e.tensor_tensor(out=ost[oi][:, 1, 0:nt], in0=pO1[:, 0:nt], in1=dn[oi][:, 0:nt], op=ALU.mult),
                     reads=[("ps", 5), ("mx_dn", oi)], writes=[("mx_os", oi)])
                P.op("sp", dma(HM[d, 2 * h:2 * h + 2, :, tq:tq + nt].rearrange("c p t -> p c t"), ost[oi][:, :, 0:nt]),
                     reads=[("mx_os", oi)], chan="st")

        for i in range(len(blocks) + DEP):
            if i < len(blocks):
                front(i, blocks[i])
            if i >= DEP:
                back(i - DEP, blocks[i - DEP])
        P.barrier()

    def phase_combine():
        arena_off[0] = CONST_END
        ha = [sb(f"cb_a{i}", [128, 8, 264], F32) for i in range(2)]
        hb = [sb(f"cb_b{i}", [128, 8, 264], F32) for i in range(2)]
        zz = [sb(f"cb_z{i}", [128, 8, 264], F32) for i in range(2)]
        for ti, (t0, nt) in enumerate(TILES):
            i = ti % 2
            P.op("sp", dma(ha[i][:, :, 0:nt], HM[0, :, :, t0:t0 + nt].rearrange("k p t -> p k t")), writes=[("cb_a", i)], chan="ld")
            P.op("sp", dma(hb[i][:, :, 0:nt], HM[1, :, :, t0:t0 + nt].rearrange("k p t -> p k t")), writes=[("cb_b", i)], chan="ld")
            P.op("sp", dma(zz[i][:, :, 0:nt], ZM[:, :, t0:t0 + nt].rearrange("k p t -> p k t")), writes=[("cb_z", i)], chan="ld")
            P.op("dve", lambda e, i=i, nt=nt: e.tensor_tensor(out=ha[i][:, :, 0:nt], in0=ha[i][:, :, 0:nt], in1=hb[i][:, :, 0:nt], op=ALU.add),
                 reads=[("cb_a", i), ("cb_b", i)], writes=[("cb_a", i)])
            P.op("act", lambda e, i=i, nt=nt: e.activation(out=zz[i][:, :, 0:nt], in_=zz[i][:, :, 0:nt], func=AF.Sigmoid),
                 reads=[("cb_z", i)], writes=[("cb_z", i)])
            P.op("dve", lambda e, i=i, nt=nt: e.tensor_tensor(out=ha[i][:, :, 0:nt], in0=ha[i][:, :, 0:nt], in1=zz[i][:, :, 0:nt], op=ALU.mult),
                 reads=[("cb_a", i), ("cb_z", i)], writes=[("cb_a", i)])
            P.op("sp", dma(HG[:, :, t0:t0 + nt].rearrange("k p t -> p k t"), ha[i][:, :, 0:nt]), reads=[("cb_a", i)], chan="st")
        P.barrier()

    def ph_mproj(h, b):
        xin = [XCM[b, 2 * h], XCM[b, 2 * h + 1]]
        if b == 0:
            phase_linear(xin, 2, w_q[h], [(0, 128), (128, 128)], lambda mi, t0, nt: QM[h, mi, :, b * T + t0:b * T + t0 + nt], toks=ATILES)
        phase_linear(xin, 2, w_k[h], [(0, 128), (128, 128)], lambda mi, t0, nt: KM[h, mi, :, b * T + t0:b * T + t0 + nt], toks=ATILES)
        phase_tokproj([XCM[b, 8 + 2 * h], XCM[b, 8 + 2 * h + 1]], 2, w_v[h], 256, VM[h, b * T:(b + 1) * T, :])

    UQ_CH = []
    for h in range(8):
        UQ_CH += [(h * 256, 128), (h * 256 + 128, 64), (h * 256 + 192, 64)]

    def uq_out(mi, t0, nt):
        h, j = mi // 3, mi % 3
        r0 = [0, 128, 192][j]
        rn = [128, 64, 64][j]
        return QALL[h, r0:r0 + rn, t0:t0 + nt]

    def ph_kv(b):
        phase_linear([gxa(b, 0), gxa(b, 1)], 2, w_ukvk, [(128 * h, 128) for h in range(8)],
                     lambda mi, t0, nt, b=b: KNA[mi, :, b * T + t0:b * T + t0 + nt])
        phase_tokproj([gxa(b, 0), gxa(b, 1)], 2, w_ukvv, 1024, VTA[b * T:(b + 1) * T, :])

    PH = [
        ("tin", phase_transpose_in),
        ("ffn1", lambda: phase_ffn("f1", XT, ZT, w1g, w1u, w1d)),
        ("ln1", lambda: phase_ln("l1", ZT, H1T, 0)),
        ("win", lambda: phase_linear(H1T, 16, w_in, WIN_CH, win_out, toks=ATILES)),
        ("nq", lambda: phase_norm(lambda t0, nt: UQ[:, :, t0:t0 + nt], 4, 4, gq_sb, RMS_EPS, lambda t0, nt: NQT[:, :, t0:t0 + nt])),
        ("nkv", lambda: phase_norm(lambda t0, nt: UKV[:, :, t0:t0 + nt], 2, 2, gkv_sb, RMS_EPS,
                                   lambda t0, nt: [gx(0)[:, t0:t0 + nt], gx(1)[:, t0:t0 + nt]])),
        ("krope", phase_krope),
        ("gather", phase_gather),
        ("uq", lambda: phase_linear(NQT, 4, w_uq, UQ_CH, uq_out)),
        ("kv0", lambda: ph_kv(0)),
        ("kv1", lambda: ph_kv(1)),
        ("attn", phase_mla_attn),
        ("nao", lambda: phase_norm(lambda t0, nt: OAT[:, :, t0:t0 + nt], 8, 8, gao_sb, RMS_EPS, lambda t0, nt: YT[0:8, :, t0:t0 + nt])),
        ("select", phase_select),
        ("conv", phase_conv),
        ("gates0", lambda: phase_linear(XCM[0], 16, w_gates, [(0, 16)], lambda mi, t0, nt: GRAW[0, :, t0:t0 + nt], bias_col=bg_sb)),
        ("gates1", lambda: phase_linear(XCM[1], 16, w_gates, [(0, 16)], lambda mi, t0, nt: GRAW[1, :, t0:t0 + nt], bias_col=bg_sb)),
        ("scans", phase_scans),
    ]
    for h_ in range(4):
        for b_ in range(2):
            PH.append((f"mproj{h_}{b_}", lambda h_=h_, b_=b_: ph_mproj(h_, b_)))
    for h_ in range(4):
        for d_ in range(2):
            PH.append((f"mix{h_}{d_}", lambda h_=h_, d_=d_: phase_mix(h_, d_)))
    PH += [
        ("combine", phase_combine),
        ("gnorm", lambda: phase_norm(lambda t0, nt: HG[:, :, t0:t0 + nt], 8, 2, gng_sb, LN_EPS, lambda t0, nt: YT[8:16, :, t0:t0 + nt],
                                     center=True, addin=(lambda t0, nt: XCM[0, 0:8, :, t0:t0 + nt], skip_sb))),
        ("wout", lambda: phase_linear(YT, 16, w_out, [(128 * i, 128) for i in range(16)], lambda mi, t0, nt: Z2T[mi, :, t0:t0 + nt], toks=ATILES,
                                      scale=1.0 / ALPHA, resid=lambda mi, t0, nt: H1T[mi, :, t0:t0 + nt])),
        ("ln2", lambda: phase_ln("l2", Z2T, H2T, 1)),
        ("ffn2", lambda: phase_ffn("f2", H2T, Z3T, w2g, w2u, w2d)),
        ("ln3", lambda: phase_ln("l3", Z3T, H3T, 2)),
    ]
    kstop = int(os.environ.get("KSTOP", "999"))
    kskip = os.environ.get("KSKIP", "").split(",")
    for i_, (nm_, f_) in enumerate(PH):
        if i_ < kstop and nm_ not in kskip:
            f_()
    if debug_out:
        for nm in debug_out:
            dump(nm, {"H1T": H1T, "UQ": UQ, "NQT": NQT, "GX0": gx(0), "GX10": gx(10), "GX2": gx(2), "GXA10": GXA_t[10].ap(), "QALL": QALL, "KNA": KNA, "VTA": VTA, "OAT": OAT, "YT": YT, "XCM": XCM, "GRAW": GRAW, "RW": RW, "HM": HM, "HG": HG, "H2T": H2T, "H3T": H3T, "Z2T": Z2T}[nm])
    for i_, (o_, a_) in enumerate(dumps):
        P.op("sp", dma(o_, a_), chan=f"st{i_ % 4}")
    phase_transpose_out(H3T)
    P.op("sp", lambda e: e.nop(), reads=[], writes=[])

    sem_names = [("e", e) for e in ENG] + [("c", c) for c in P.chan_names]
    assert len(sem_names) <= 95, len(sem_names)
    sems = {}
    for i, k in enumerate(sem_names):
        sems[k] = nc.alloc_semaphore(f"s{i}_{k[1]}")
    run = P.emit(sems)
    with nc.Block() as block:
        @block.tensor
        def _(e):
            run("pe", e)

        @block.scalar
        def _(e):
            run("act", e)

        @block.vector
        def _(e):
            run("dve", e)

        @block.gpsimd
        def _(e):
            run("pool", e)

        @block.sync
        def _(e):
            run("sp", e)
    return nc


_NC_CACHE = {}


def make_core_inputs(inputs):
    g = lambda k: np.ascontiguousarray(np.asarray(inputs[k], dtype=np.float32)[0])
    x = np.asarray(inputs["x"], dtype=np.float32)
    meta = np.asarray(inputs["meta_tokens"], dtype=np.float32)
    w_in = g("w_in")
    kr = w_in[:, 768:832]
    w_in_ext = np.ascontiguousarray(np.concatenate(
        [w_in[:, 0:768], w_in[:, 832:1856], w_in[:, 1856:2880], kr, kr[:, 32:], kr[:, :32]], axis=1))
    wuq = g("mla_w_uq")
    cols = []
    for h in range(8):
        b0 = h * 192
        cols += list(range(b0, b0 + 192)) + list(range(b0 + 160, b0 + 192)) + list(range(b0 + 128, b0 + 160))
    w_uq_ext = np.ascontiguousarray(wuq[:, cols])
    wukv = g("mla_w_ukv")
    kc = [h * 256 + j for h in range(8) for j in range(128)]
    vc = [h * 256 + 128 + j for h in range(8) for j in range(128)]
    w_ukvk = np.ascontiguousarray(wukv[:, kc])
    w_ukvv = np.ascontiguousarray(wukv[:, vc])
    inv = 10000.0 ** (-np.arange(0, 64, 2, dtype=np.float32) / 64.0)
    NEGM = -30000.0
    s_idx = np.arange(128)[:, None, None]
    j_idx = np.arange(4)[None, :, None]
    t_idx = np.arange(512)[None, None, :]
    mask_le = np.where(128 * j_idx + s_idx <= t_idx, 0.0, NEGM).astype(np.float32)
    mask_ge = np.where(128 * j_idx + s_idx >= t_idx, 0.0, NEGM).astype(np.float32)
    shared = {
        "ident": np.eye(128, dtype=np.float32),
        "w1g": g("ffn1_w_gate"), "w1u": g("ffn1_w_up"), "w1d": g("ffn1_w_down"),
        "ln1g": g("ln1_g"), "ln1b": g("ln1_b"),
        "w_in": w_in_ext, "gq": g("mla_q_norm_g"), "gkv": g("mla_kv_norm_g"), "gao": g("attn_out_g"),
        "w_uq": w_uq_ext, "w_ukvk": w_ukvk, "w_ukvv": w_ukvv,
        "w_out": g("w_out"), "ln2g": g("ln2_g"), "ln2b": g("ln2_b"), "ln3g": g("ln3_g"), "ln3b": g("ln3_b"),
        "w2g": g("ffn2_w_gate"), "w2u": g("ffn2_w_up"), "w2d": g("ffn2_w_down"),
        "conv_w": g("mlstm_conv_w"), "conv_b": g("mlstm_conv_b"),
        "w_q": g("mlstm_w_q"), "w_k": g("mlstm_w_k"), "w_v": g("mlstm_w_v"),
        "w_gates": g("mlstm_w_gates"), "b_gates": g("mlstm_b_gates"),
        "gn_g": g("mlstm_gn_g"), "skip": g("mlstm_skip"),
        "mask_le": mask_le, "mask_ge": mask_ge,
    }
    maps = []
    for c in range(NCORES):
        b, r = c // 2, c % 2
        if r == 0:
            xl = np.concatenate([meta, x[b, :T - 16]], axis=0)
            pos = np.arange(T, dtype=np.float32)
        else:
            xl = x[b, T - 16:][::-1]
            pos = (2 * T - 1 - np.arange(T)).astype(np.float32)
        ang = (pos[None, :] * inv[:, None]).astype(np.float32)
        cs, sn = np.cos(ang), np.sin(ang)
        om = np.zeros((128, 2), np.float32)
        om[:, r] = 1.0
        m = dict(shared)
        if r == 1:
            perm = list(range(8, 16)) + list(range(0, 8))
            m["w_gates"] = np.ascontiguousarray(shared["w_gates"][:, perm])
            m["b_gates"] = np.ascontiguousarray(shared["b_gates"][perm])
            m["conv_w"] = np.ascontiguousarray(shared["conv_w"][::-1])
        m["x"] = np.ascontiguousarray(xl)
        m["cos2"] = np.ascontiguousarray(np.concatenate([cs, cs], 0).astype(np.float32))
        m["sin2"] = np.ascontiguousarray(np.concatenate([-sn, sn], 0).astype(np.float32))
        m["om"] = om
        maps.append(m)
    return maps


def assemble(outs):
    B = NCORES // 2
    res = np.empty((B, 4096, D), np.float32)
    for c in range(NCORES):
        b, r = c // 2, c % 2
        o = outs[c]
        if r == 0:
            res[b, :T - 16] = o[16:]
        else:
            res[b, T - 16:] = o[::-1]
    return res


def kernel(**inputs):
    if "nc" not in _NC_CACHE:
        _NC_CACHE["nc"] = build_program()
    nc = _NC_CACHE["nc"]
    maps = make_core_inputs(inputs)
    res = run_bass_kernel_spmd(nc, maps, core_ids=list(range(NCORES)))
    outs = [np.asarray(r["out"]) for r in res.results]
    return assemble(outs)
```

```python
import numpy as np
import concourse.bass as bass
import concourse.mybir as mybir
from concourse.bass_utils import run_bass_kernel_spmd

F32 = mybir.dt.float32
BF16 = mybir.dt.bfloat16
AF = mybir.ActivationFunctionType
ALU = mybir.AluOpType

D = 2048
DFF = 5632
T = 2056
import os
NCORES = int(os.environ.get('KCORES', '8'))
ALPHA = 2.0 ** 0.25
LN_EPS = 1e-5
TILES = [(256 * i, 256) for i in range(7)] + [(1792, 264)]
HALVES = [TILES[0:4], TILES[4:8]]
CHUNKS = [(128 * i, 128) for i in range(16)] + [(2048, 8)]
ATILES = [(0, 512), (512, 512), (1024, 512), (1536, 256), (1792, 264)]
FHALVES = [ATILES[0:2], ATILES[2:5]]

ENG = ("pe", "act", "dve", "pool", "sp")


class Op:
    __slots__ = ("eng", "fn", "deps", "chan", "needed", "seq", "idx", "inc")

    def __init__(self, eng, fn, chan):
        self.eng, self.fn, self.chan = eng, fn, chan
        self.deps = set()
        self.inc = 16
        self.needed = False
        self.seq = None


class Prog:
    def __init__(self, nc):
        self.nc = nc
        self.ops = []
        self.last_w = {}
        self.readers = {}
        self.barrier_deps = set()
        self.last_eng = {}
        self.last_chan = {}
        self.chan_names = []
        self.chan_map = {}

    def op(self, eng, fn, reads=(), writes=(), chan=None, inc=16):
        if chan is not None:
            key = writes[0] if len(writes) else (reads[0] if len(reads) else ("anon", chan))
            if key not in self.chan_map:
                self.chan_map[key] = f"c{len(self.chan_map)}"
            chan = self.chan_map[key]
        o = Op(eng, fn, chan)
        o.inc = inc
        o.idx = len(self.ops)
        deps = set(self.barrier_deps)
        for k in reads:
            w = self.last_w.get(k)
            if w is not None:
                deps.add(w)
        for k in writes:
            w = self.last_w.get(k)
            if w is not None:
                deps.add(w)
            deps.update(self.readers.get(k, ()))
        for k in reads:
            self.readers.setdefault(k, []).append(o)
        for k in writes:
            self.last_w[k] = o
            self.readers[k] = []
        deps.discard(o)
        o.deps = deps
        self.ops.append(o)
        if chan is None:
            self.last_eng[eng] = o
        else:
            if chan not in self.last_chan:
                self.chan_names.append(chan)
            self.last_chan[chan] = o
            self.last_eng[eng] = o
        return o

    def barrier(self):
        self.barrier_deps = set(self.last_eng.values()) | set(self.last_chan.values())
        self.chan_map = {}
        self.last_w = {}
        self.readers = {}

    def emit(self, block_ctx_sems):
        nc = self.nc
        ops = self.ops
        for o in ops:
            for d in o.deps:
                if d.chan is None and d.eng == "pe" and o.eng == "pe" and o.chan is None:
                    continue
                d.needed = True
        eng_cnt = {e: 0 for e in ENG}
        chan_cnt = {}
        for o in ops:
            if o.chan is not None:
                chan_cnt[o.chan] = chan_cnt.get(o.chan, 0) + o.inc
                o.seq = chan_cnt[o.chan]
            elif o.needed:
                eng_cnt[o.eng] += 1
                o.seq = eng_cnt[o.eng]
        sems = block_ctx_sems
        per_eng = {e: [o for o in ops if o.eng == e] for e in ENG}

        def run(e, engine):
            waited = {}
            for o in per_eng[e]:
                need = {}
                for d in o.deps:
                    if d.chan is not None:
                        key = ("c", d.chan)
                    else:
                        if d.eng == "pe" and e == "pe" and o.chan is None:
                            continue
                        key = ("e", d.eng)
                    if d.seq > need.get(key, 0):
                        need[key] = d.seq
                for key, v in need.items():
                    if waited.get(key, 0) >= v:
                        continue
                    waited[key] = v
                    engine.wait_ge(sems[key], v)
                ins = o.fn(engine)
                if o.chan is not None:
                    ins.then_inc(sems[("c", o.chan)], o.inc)
                elif o.needed:
                    ins.then_inc(sems[("e", e)], 1)
        return run


def build_program(debug_out=None):
    nc = bass.Bass("TRN2", target_bir_lowering=False)
    P = Prog(nc)

    def din(name, shape, dt=F32):
        return nc.dram_tensor(name, list(shape), dt, kind="ExternalInput").ap()

    x_in = din("x", [T, D])
    ident = din("ident", [128, 128])
    w1g = din("w1g", [D, DFF]); w1u = din("w1u", [D, DFF]); w1d = din("w1d", [DFF, D])
    ln1g = din("ln1g", [D]); ln1b = din("ln1b", [D])
    out_ap = nc.dram_tensor("out", [T, D], F32, kind="ExternalOutput").ap()
    w_in = din("w_in", [D, 2944])
    gq_in = din("gq", [512]); gkv_in = din("gkv", [256]); gao_in = din("gao", [1024])
    w_uq = din("w_uq", [512, 2048])
    w_ukvk = din("w_ukvk", [256, 1024]); w_ukvv = din("w_ukvv", [256, 1024])
    cos2 = din("cos2", [64, T]); sin2 = din("sin2", [64, T])
    w_out = din("w_out", [D, D])
    ln2g = din("ln2g", [D]); ln2b = din("ln2b", [D]); ln3g = din("ln3g", [D]); ln3b = din("ln3b", [D])
    w2g = din("w2g", [D, DFF]); w2u = din("w2u", [D, DFF]); w2d = din("w2d", [DFF, D])
    conv_w = din("conv_w", [5, 1024]); conv_b = din("conv_b", [1024])
    w_q = din("w_q", [4, 256, 256]); w_k = din("w_k", [4, 256, 256]); w_v = din("w_v", [4, 256, 256])
    w_gates = din("w_gates", [D, 16]); b_gates = din("b_gates", [16])
    gn_g = din("gn_g", [1024]); skip_in = din("skip", [1024])
    om_in = din("om", [128, 2])
    mask_le = din("mask_le", [128, 4, 512]); mask_ge = din("mask_ge", [128, 4, 512])
    dumps = []

    def dram(name, shape, dt=F32):
        return nc.dram_tensor(name, list(shape), dt).ap()

    XT = nc.dram_tensor("XT_s", [16, 128, T], F32).ap()
    ZT = nc.dram_tensor("ZT_s", [16, 128, T], F32).ap()
    H1T = nc.dram_tensor("H1T_s", [16, 128, T], F32).ap()
    UQ = dram("UQ_s", [4, 128, T]); UKV = dram("UKV_s", [2, 128, T]); URR = dram("URR_s", [2, 64, T])
    ZM = dram("ZM_s", [8, 128, T]); NQT = dram("NQT_s", [4, 128, T])
    NGX = 11
    GXrows = [128] * 10 + [64]
    GX_t = [nc.dram_tensor(f"GX{i}_s", [GXrows[i], T], F32) for i in range(NGX)]
    GXA_t = [nc.dram_tensor(f"GXA{i}_s", [2 * GXrows[i], T], F32) for i in range(NGX)]

    def gx(i):
        return GX_t[i].ap()

    def gxa(b, i):
        return GXA_t[i].ap()[b * GXrows[i]:(b + 1) * GXrows[i], :]
    QALL = dram("QALL_s", [8, 256, T]); KNA = dram("KNA_s", [8, 128, 2 * T]); VTA = dram("VTA_s", [2 * T, 1024], BF16)
    OAT = dram("OAT_s", [8, 128, T]); YT = dram("YT_s", [16, 128, T])
    XO = dram("XO_s", [2, 8, 128, T])
    XCM = dram("XCM_s", [2, 16, 128, T]); GRAW = dram("GRAW_s", [2, 16, T]); RW = dram("RW_s", [2, 3, 4, 2 * T])
    QM = dram("QM_s", [4, 2, 128, 2 * T]); KM = dram("KM_s", [4, 2, 128, 2 * T]); VM = dram("VM_s", [4, 2 * T, 256], BF16)
    HM = dram("HM_s", [2, 8, 128, 2 * T]); HG = dram("HG_s", [8, 128, T]); XCO = dram("XCO_s", [8, 128, T])
    Z2T = dram("Z2T_s", [16, 128, T]); H2T = dram("H2T_s", [16, 128, T]); Z3T = dram("Z3T_s", [16, 128, T]); H3T = dram("H3T_s", [16, 128, T])

    arena_off = [16640]

    def sb(name, shape, dt, off=None):
        nbytes = int(np.prod(shape[1:])) * (2 if dt == BF16 else 4)
        if off is None:
            off = arena_off[0]
            off = (off + 63) // 64 * 64
            arena_off[0] = off + nbytes
        assert off + nbytes <= 228 * 1024, (name, off, nbytes)
        return nc.alloc_sbuf_tensor_at(name, list(shape), dt, offset=off)

    ident_sb = sb("ident_sb", [128, 128], F32)
    ones_bf = sb("ones_bf", [128, 128], BF16)
    lng = sb("lng", [128, 3, 16], F32)
    lnb = sb("lnb", [128, 3, 16], F32)
    gq_sb = sb("gq_sb", [128, 4], F32); gkv_sb = sb("gkv_sb", [128, 2], F32); gao_sb = sb("gao_sb", [128, 8], F32)
    gng_sb = sb("gng_sb", [128, 8], F32); skip_sb = sb("skip_sb", [128, 8], F32)
    cw_sb = sb("cw_sb", [128, 5, 8], F32); cb_sb = sb("cb_sb", [128, 8], F32)
    bg_sb = sb("bg_sb", [16, 1], F32); om_sb = sb("om_sb", [128, 2], F32)
    CONST_END = arena_off[0]

    psum = [nc.alloc_psum_tensor(f"ps{i}", [128, 512], F32) for i in range(8)]

    rr = {"i": 0}

    def dma_(out, in_, slow=False):
        if slow:
            return lambda e: e.dma_start(out=out, in_=in_, allow_slow_non_contiguous=True)
        return lambda e: e.dma_start(out=out, in_=in_)

    def evac_eng():
        rr["i"] += 1
        return "act" if rr["i"] % 2 else "dve"

    P.op("sp", lambda e: e.dma_start(out=ident_sb[:], in_=ident[:, :]), writes=["ident"], chan="const")
    P.op("dve", lambda e: e.memset(ones_bf[:], 1.0), writes=["ones"])
    P.op("sp", lambda e: e.dma_start(out=lng[:, 0, :], in_=ln1g.rearrange("(k p) -> p k", p=128),
                                     allow_slow_non_contiguous=True), writes=["lng0"], chan="const")
    P.op("sp", lambda e: e.dma_start(out=lnb[:, 0, :], in_=ln1b.rearrange("(k p) -> p k", p=128),
                                     allow_slow_non_contiguous=True), writes=["lnb0"], chan="const")

    for li_, (g_, b_) in enumerate([(ln2g, ln2b), (ln3g, ln3b)]):
        P.op("sp", dma_(lng[:, li_ + 1, :], g_.rearrange("(k p) -> p k", p=128), True), writes=[f"lng{li_ + 1}"], chan="const")
        P.op("sp", dma_(lnb[:, li_ + 1, :], b_.rearrange("(k p) -> p k", p=128), True), writes=[f"lnb{li_ + 1}"], chan="const")
    for dst_, src_ in [(gq_sb, gq_in), (gkv_sb, gkv_in), (gao_sb, gao_in), (gng_sb, gn_g), (skip_sb, skip_in), (cb_sb, conv_b)]:
        P.op("sp", dma_(dst_[:], src_.rearrange("(k p) -> p k", p=128), True), writes=["consts2"], chan="const")
    for j_ in range(5):
        P.op("sp", dma_(cw_sb[:, j_, :], conv_w[j_, :].rearrange("(k p) -> p k", p=128), True), writes=["consts2"], chan="const")
    P.op("sp", dma_(bg_sb[:], b_gates.rearrange("(p o) -> p o", o=1), True), writes=["consts2"], chan="const")
    P.op("sp", dma_(om_sb[:], om_in[:, :]), writes=["consts2"], chan="const")

    def phase_transpose_in():
        arena_off[0] = CONST_END
        xtok = [sb(f"p0_xtok{i}", [128, D], F32) for i in range(2)]
        stg = [sb(f"p0_stg{i}", [128, 16, 128], F32) for i in range(2)]
        for ci, (t0, nt) in enumerate(CHUNKS):
            b = ci % 2
            P.op("sp", lambda e, b=b, t0=t0, nt=nt: e.dma_start(out=xtok[b][0:nt, :], in_=x_in[t0:t0 + nt, :]),
                 writes=[("xtok", b)], chan=f"ld{b}")
            for g in range(4):
                pb = (ci * 4 + g) % 8
                for q in range(4):
                    f = g * 4 + q
                    P.op("pe", lambda e, b=b, nt=nt, f=f, pb=pb, q=q: e.transpose(
                        psum[pb][:, q * 128:q * 128 + nt], xtok[b][0:nt, f * 128:(f + 1) * 128], ident_sb[0:nt, 0:nt]),
                         reads=[("xtok", b), "ident"], writes=[("ps", pb)])
                en = evac_eng()
                if en == "act":
                    fn = lambda e, b=b, nt=nt, g=g, pb=pb: e.copy(
                        out=stg[b][:, g * 4:(g + 1) * 4, 0:nt],
                        in_=psum[pb][:, :].rearrange("p (q t) -> p q t", q=4)[:, :, 0:nt])
                else:
                    fn = lambda e, b=b, nt=nt, g=g, pb=pb: e.tensor_copy(
                        out=stg[b][:, g * 4:(g + 1) * 4, 0:nt],
                        in_=psum[pb][:, :].rearrange("p (q t) -> p q t", q=4)[:, :, 0:nt])
                P.op(en, fn, reads=[("ps", pb)], writes=[("stg", b, g)])
            P.op("sp", lambda e, b=b, t0=t0, nt=nt: e.dma_start(
                out=XT[:, :, t0:t0 + nt].rearrange("k p t -> p k t"), in_=stg[b][:, :, 0:nt]),
                 reads=[("stg", b, g) for g in range(4)], chan=f"st{b}")
        P.barrier()

    def phase_ffn(tag, XTin, ZTout, wg, wu, wd):
        arena_off[0] = CONST_END
        TH = 1032
        xT = sb(f"{tag}_xT", [128, 16, TH], BF16)
        hT = sb(f"{tag}_hT", [128, 44, TH], BF16)
        wgb = [sb(f"{tag}_wg{i}", [128, 16, 128], BF16) for i in range(2)]
        wub = [sb(f"{tag}_wu{i}", [128, 16, 128], BF16) for i in range(2)]
        wdb = [sb(f"{tag}_wd{i}", [128, 44, 256], BF16) for i in range(2)]
        sil = [sb(f"{tag}_sil{i}", [128, 512], F32) for i in range(2)]
        xres = [sb(f"{tag}_xres{i}", [128, 512], F32) for i in range(2)]
        zst = [sb(f"{tag}_zst{i}", [128, 512], F32) for i in range(2)]
        wg_v = wg.rearrange("(k p) m -> p k m", p=128)
        wu_v = wu.rearrange("(k p) m -> p k m", p=128)
        wd_v = wd.rearrange("(j p) m -> p j m", p=128)
        cnt = {"gu": 0, "dn": 0, "ev": 0}
        for hi, tiles in enumerate(FHALVES):
            h0 = tiles[0][0]
            hn = sum(n for _, n in tiles)
            for kq in range(4):
                P.op("pool", lambda e, kq=kq, h0=h0, hn=hn: e.dma_start(
                    out=xT[:, kq * 4:(kq + 1) * 4, 0:hn],
                    in_=XTin[kq * 4:(kq + 1) * 4, :, h0:h0 + hn].rearrange("k p t -> p k t")),
                     writes=[("xT", kq)], chan="ldx")
            for j in range(44):
                b = cnt["gu"] % 2
                cnt["gu"] += 1
                P.op("pool", lambda e, b=b, j=j: e.dma_start(out=wgb[b][:], in_=wg_v[:, :, j * 128:(j + 1) * 128]),
                     writes=[("wg", b)], chan=f"ldw{b}")
                P.op("pool", lambda e, b=b, j=j: e.dma_start(out=wub[b][:], in_=wu_v[:, :, j * 128:(j + 1) * 128]),
                     writes=[("wu", b)], chan=f"ldu{b}")
                for ti, (t0, nt) in enumerate(tiles):
                    lt = t0 - h0
                    slot = cnt["ev"] % 4
                    cnt["ev"] += 1
                    pg, pu = psum[2 * slot], psum[2 * slot + 1]
                    for k in range(16):
                        P.op("pe", lambda e, b=b, k=k, lt=lt, nt=nt, pg=pg: e.matmul(
                            pg[:, 0:nt], wgb[b][:, k, :], xT[:, k, lt:lt + nt], start=(k == 0), stop=(k == 15)),
                             reads=[("wg", b), ("xT", k // 4)], writes=[("ps", 2 * slot)])
                    for k in range(16):
                        P.op("pe", lambda e, b=b, k=k, lt=lt, nt=nt, pu=pu: e.matmul(
                            pu[:, 0:nt], wub[b][:, k, :], xT[:, k, lt:lt + nt], start=(k == 0), stop=(k == 15)),
                             reads=[("wu", b), ("xT", k // 4)], writes=[("ps", 2 * slot + 1)])
                    sb_i = slot % 2
                    P.op("act", lambda e, sb_i=sb_i, nt=nt, pg=pg: e.activation(
                        out=sil[sb_i][:, 0:nt], in_=pg[:, 0:nt], func=AF.Silu),
                         reads=[("ps", 2 * slot)], writes=[("sil", sb_i)])
                    P.op("dve", lambda e, sb_i=sb_i, nt=nt, pu=pu, j=j, lt=lt: e.tensor_tensor(
                        out=hT[:, j, lt:lt + nt], in0=sil[sb_i][:, 0:nt], in1=pu[:, 0:nt], op=ALU.mult),
                         reads=[("sil", sb_i), ("ps", 2 * slot + 1)], writes=[("hT", j)])
            for ip in range(8):
                b = cnt["dn"] % 2
                cnt["dn"] += 1
                P.op("pool", lambda e, b=b, ip=ip: e.dma_start(out=wdb[b][:, 0:22, :], in_=wd_v[:, 0:22, ip * 256:(ip + 1) * 256]),
                     writes=[("wd", b, 0)], chan=f"ldw{b}")
                P.op("pool", lambda e, b=b, ip=ip: e.dma_start(out=wdb[b][:, 22:44, :], in_=wd_v[:, 22:44, ip * 256:(ip + 1) * 256]),
                     writes=[("wd", b, 1)], chan=f"ldw{b}")
                for ii in range(2):
                    i = ip * 2 + ii
                    for ti, (t0, nt) in enumerate(tiles):
                        lt = t0 - h0
                        slot = cnt["ev"] % 8
                        cnt["ev"] += 1
                        rb = slot % 2
                        pz = psum[slot]
                        P.op("sp", lambda e, rb=rb, i=i, t0=t0, nt=nt: e.dma_start(
                            out=xres[rb][:, 0:nt], in_=XTin[i, :, t0:t0 + nt]),
                             writes=[("xres", rb)], chan=f"ldr{rb}")
                        for j in range(44):
                            P.op("pe", lambda e, b=b, j=j, ii=ii, lt=lt, nt=nt, pz=pz: e.matmul(
                                pz[:, 0:nt], wdb[b][:, j, ii * 128:(ii + 1) * 128], hT[:, j, lt:lt + nt],
                                start=(j == 0), stop=(j == 43)),
                                 reads=[("wd", b, j // 22), ("hT", j)], writes=[("ps", slot)])
                        P.op("dve", lambda e, rb=rb, nt=nt, pz=pz: e.scalar_tensor_tensor(
                            out=zst[rb][:, 0:nt], in0=pz[:, 0:nt], scalar=0.5 / ALPHA, in1=xres[rb][:, 0:nt],
                            op0=ALU.mult, op1=ALU.add),
                             reads=[("ps", slot), ("xres", rb)], writes=[("zst", rb)])
                        P.op("sp", lambda e, rb=rb, i=i, t0=t0, nt=nt: e.dma_start(
                            out=ZTout[i, :, t0:t0 + nt], in_=zst[rb][:, 0:nt]),
                             reads=[("zst", rb)], chan=f"st{rb}")
        P.barrier()

    def phase_ln(tag, ZTin, HTout, li, final_out=None):
        arena_off[0] = CONST_END
        zt = [sb(f"{tag}_zt{i}", [128, 16, 264], F32) for i in range(2)]
        zb = [sb(f"{tag}_zb{i}", [128, 16, 264], BF16) for i in range(2)]
        zq = [sb(f"{tag}_zq{i}", [128, 16, 264], BF16) for i in range(2)]
        mean = [sb(f"{tag}_mean{i}", [128, 264], F32) for i in range(2)]
        msq = [sb(f"{tag}_msq{i}", [128, 264], F32) for i in range(2)]
        rstd = [sb(f"{tag}_rstd{i}", [128, 264], F32) for i in range(2)]
        ho = [sb(f"{tag}_ho{i}", [128, 16, 264], F32) for i in range(2)]
        eps = LN_EPS / (ALPHA * ALPHA)
        for ti, (t0, nt) in enumerate(TILES):
            b = ti % 2
            ps_s, ps_q = psum[2 * b], psum[2 * b + 1]
            P.op("sp", lambda e, b=b, t0=t0, nt=nt: e.dma_start(
                out=zt[b][:, :, 0:nt], in_=ZTin[:, :, t0:t0 + nt].rearrange("k p t -> p k t")),
                 writes=[("zt", b)], chan=f"ld{b}")
            P.op("act", lambda e, b=b, nt=nt: e.copy(out=zb[b][:, :, 0:nt], in_=zt[b][:, :, 0:nt]),
                 reads=[("zt", b)], writes=[("zb", b)])
            P.op("act", lambda e, b=b, nt=nt: e.activation(out=zq[b][:, :, 0:nt], in_=zt[b][:, :, 0:nt], func=AF.Square),
                 reads=[("zt", b)], writes=[("zq", b)])
            for k in range(16):
                P.op("pe", lambda e, b=b, k=k, nt=nt, ps_s=ps_s: e.matmul(
                    ps_s[:, 0:nt], ones_bf[:, :], zb[b][:, k, 0:nt], start=(k == 0), stop=(k == 15)),
                     reads=[("zb", b), "ones"], writes=[("ps", 2 * b)])
            for k in range(16):
                P.op("pe", lambda e, b=b, k=k, nt=nt, ps_q=ps_q: e.matmul(
                    ps_q[:, 0:nt], ones_bf[:, :], zq[b][:, k, 0:nt], start=(k == 0), stop=(k == 15)),
                     reads=[("zq", b), "ones"], writes=[("ps", 2 * b + 1)])
            P.op("dve", lambda e, b=b, nt=nt, ps_s=ps_s: e.tensor_scalar(
                out=mean[b][:, 0:nt], in0=ps_s[:, 0:nt], scalar1=1.0 / D, scalar2=None, op0=ALU.mult),
                 reads=[("ps", 2 * b)], writes=[("mean", b)])
            P.op("dve", lambda e, b=b, nt=nt: e.tensor_tensor(
                out=msq[b][:, 0:nt], in0=mean[b][:, 0:nt], in1=mean[b][:, 0:nt], op=ALU.mult),
                 reads=[("mean", b)], writes=[("msq", b)])
            P.op("dve", lambda e, b=b, nt=nt, ps_q=ps_q: e.scalar_tensor_tensor(
                out=rstd[b][:, 0:nt], in0=ps_q[:, 0:nt], scalar=1.0 / D, in1=msq[b][:, 0:nt],
                op0=ALU.mult, op1=ALU.subtract),
                 reads=[("ps", 2 * b + 1), ("msq", b)], writes=[("rstd", b)])
            P.op("act", lambda e, b=b, nt=nt: e.activation(
                out=msq[b][:, 0:nt], in_=rstd[b][:, 0:nt], func=AF.Sqrt, bias=eps, scale=1.0),
                 reads=[("rstd", b)], writes=[("msq", b)])
            P.op("dve", lambda e, b=b, nt=nt: e.reciprocal(out=rstd[b][:, 0:nt], in_=msq[b][:, 0:nt]),
                 reads=[("msq", b)], writes=[("rstd", b)])
            P.op("dve", lambda e, b=b, nt=nt: e.tensor_tensor(
                out=zt[b][:, :, 0:nt], in0=zt[b][:, :, 0:nt],
                in1=mean[b][:, 0:nt].unsqueeze(1).to_broadcast([128, 16, nt]),
                op=ALU.subtract),
                 reads=[("zt", b), ("mean", b)], writes=[("zt", b)])
            P.op("dve", lambda e, b=b, nt=nt: e.tensor_tensor(
                out=zt[b][:, :, 0:nt], in0=zt[b][:, :, 0:nt],
                in1=rstd[b][:, 0:nt].unsqueeze(1).to_broadcast([128, 16, nt]),
                op=ALU.mult),
                 reads=[("zt", b), ("rstd", b)], writes=[("zt", b)])
            for k in range(16):
                P.op("act", lambda e, b=b, k=k, nt=nt: e.activation(
                    out=ho[b][:, k, 0:nt], in_=zt[b][:, k, 0:nt], func=AF.Identity,
                    scale=lng[:, li, k:k + 1], bias=lnb[:, li, k:k + 1]),
                     reads=[("zt", b), f"lng{li}", f"lnb{li}"], writes=[("ho", b)])
            P.op("sp", lambda e, b=b, t0=t0, nt=nt: e.dma_start(
                out=HTout[:, :, t0:t0 + nt].rearrange("k p t -> p k t"), in_=ho[b][:, :, 0:nt]),
                 reads=[("ho", b)], chan=f"st{b}")
        P.barrier()

    def phase_transpose_out(HTin):
        arena_off[0] = CONST_END
        hin = [sb(f"po_in{i}", [128, 16, 128], F32) for i in range(2)]
        otok = [sb(f"po_tok{i}", [128, D], F32) for i in range(2)]
        for ci, (t0, nt) in enumerate(CHUNKS):
            b = ci % 2
            P.op("sp", lambda e, b=b, t0=t0, nt=nt: e.dma_start(
                out=hin[b][:, :, 0:nt], in_=HTin[:, :, t0:t0 + nt].rearrange("k p t -> p k t")),
                 writes=[("hin", b)], chan=f"ld{b}")
            for g in range(4):
                pb = (ci * 4 + g) % 8
                for q in range(4):
                    f = g * 4 + q
                    P.op("pe", lambda e, b=b, nt=nt, f=f, pb=pb, q=q: e.transpose(
                        psum[pb][0:nt, q * 128:(q + 1) * 128], hin[b][:, f, 0:nt], ident_sb[:, :]),
                         reads=[("hin", b), "ident"], writes=[("ps", pb)])
                en = evac_eng()
                if en == "act":
                    fn = lambda e, b=b, nt=nt, g=g, pb=pb: e.copy(out=otok[b][0:nt, g * 512:(g + 1) * 512], in_=psum[pb][0:nt, :])
                else:
                    fn = lambda e, b=b, nt=nt, g=g, pb=pb: e.tensor_copy(out=otok[b][0:nt, g * 512:(g + 1) * 512], in_=psum[pb][0:nt, :])
                P.op(en, fn, reads=[("ps", pb)], writes=[("otok", b, g)])
            P.op("sp", lambda e, b=b, t0=t0, nt=nt: e.dma_start(out=out_ap[t0:t0 + nt, :], in_=otok[b][0:nt, :]),
                 reads=[("otok", b, g) for g in range(4)], chan=f"st{b}")
        P.barrier()


    def dma(out, in_, slow=False):
        if slow:
            return lambda e: e.dma_start(out=out, in_=in_, allow_slow_non_contiguous=True)
        return lambda e: e.dma_start(out=out, in_=in_)

    def mm(out, lhsT, rhs, st, sp_):
        return lambda e: e.matmul(out, lhsT, rhs, start=st, stop=sp_)

    def fm(ap2d):
        return ap2d.rearrange("(k p) t -> k p t", p=128)

    def phase_linear(INap, nk, Wap, mchunks, OUTfn, toks=TILES, scale=1.0, bias_col=None, resid=None):
        arena_off[0] = CONST_END
        Tin = max(t0 + nt for t0, nt in toks)
        inT = sb("lin_in", [128, nk, Tin], BF16)
        wb = [sb(f"lin_w{i}", [128, nk, 128], BF16) for i in range(2)]
        st = [sb(f"lin_st{i}", [128, 512], F32) for i in range(4)]
        rs = [sb(f"lin_rs{i}", [128, 512], F32) for i in range(4)]
        if isinstance(INap, list):
            for k in range(nk):
                P.op("pool", dma(inT[:, k, 0:Tin], INap[k][:, 0:Tin]), writes=[("lin_in", k // 4)], chan="ldx")
        else:
            for kq in range(0, nk, 4):
                kn = min(4, nk - kq)
                P.op("pool", dma(inT[:, kq:kq + kn, 0:Tin], INap[kq:kq + kn, :, 0:Tin].rearrange("k p t -> p k t")),
                     writes=[("lin_in", kq // 4)], chan="ldx")
        Wv = Wap.rearrange("(k p) m -> p k m", p=128)
        cnt = 0
        for mi, (c0, mc) in enumerate(mchunks):
            b = mi % 2
            P.op("pool", dma(wb[b][:, :, 0:mc], Wv[:, :, c0:c0 + mc]), writes=[("lin_w", b)], chan=f"ldw{b}")
            for (t0, nt) in toks:
                slot = cnt % 8
                sti = cnt % 4
                cnt += 1
                if resid is not None:
                    P.op("sp", dma(rs[sti][0:mc, 0:nt], resid(mi, t0, nt)), writes=[("lin_rs", sti)], chan=f"ldr{sti}")
                for k in range(nk):
                    P.op("pe", mm(psum[slot][0:mc, 0:nt], wb[b][:, k, 0:mc], inT[:, k, t0:t0 + nt], k == 0, k == nk - 1),
                         reads=[("lin_w", b), ("lin_in", k // 4)], writes=[("ps", slot)])
                if resid is not None:
                    P.op("dve", lambda e, slot=slot, sti=sti, mc=mc, nt=nt: e.scalar_tensor_tensor(
                        out=st[sti][0:mc, 0:nt], in0=psum[slot][0:mc, 0:nt], scalar=scale, in1=rs[sti][0:mc, 0:nt],
                        op0=ALU.mult, op1=ALU.add), reads=[("ps", slot), ("lin_rs", sti)], writes=[("lin_st", sti)])
                elif bias_col is not None:
                    P.op("act", lambda e, slot=slot, sti=sti, mc=mc, nt=nt: e.activation(
                        out=st[sti][0:mc, 0:nt], in_=psum[slot][0:mc, 0:nt], func=AF.Identity, bias=bias_col[0:mc, 0:1], scale=scale),
                         reads=[("ps", slot), "consts2"], writes=[("lin_st", sti)])
                else:
                    en = evac_eng()
                    if en == "act":
                        P.op("act", lambda e, slot=slot, sti=sti, mc=mc, nt=nt: e.mul(
                            out=st[sti][0:mc, 0:nt], in_=psum[slot][0:mc, 0:nt], mul=scale),
                             reads=[("ps", slot)], writes=[("lin_st", sti)])
                    else:
                        P.op("dve", lambda e, slot=slot, sti=sti, mc=mc, nt=nt: e.tensor_scalar(
                            out=st[sti][0:mc, 0:nt], in0=psum[slot][0:mc, 0:nt], scalar1=scale, scalar2=None, op0=ALU.mult),
                             reads=[("ps", slot)], writes=[("lin_st", sti)])
                P.op("sp", dma(OUTfn(mi, t0, nt), st[sti][0:mc, 0:nt]), reads=[("lin_st", sti)], chan=f"st{sti}")
        P.barrier()

    def phase_tokproj(INap, nk, Wap, M, OUTap, chunks=CHUNKS):
        arena_off[0] = CONST_END
        Tin = max(t0 + nt for t0, nt in chunks)
        inT = sb("tp_in", [128, nk, Tin], BF16)
        wsb = sb("tp_w", [128, nk, M], BF16)
        vst = [sb(f"tp_st{i}", [128, M], BF16) for i in range(2)]
        if isinstance(INap, list):
            for k in range(nk):
                P.op("pool", dma(inT[:, k, 0:Tin], INap[k][:, 0:Tin]), writes=["tp_in"], chan="ldx")
        else:
            P.op("pool", dma(inT[:, :, 0:Tin], INap[:, :, 0:Tin].rearrange("k p t -> p k t")), writes=["tp_in"], chan="ldx")
        P.op("pool", dma(wsb[:], Wap.rearrange("(k p) m -> p k m", p=128)), writes=["tp_w"], chan="ldw0")
        cnt = 0
        for ci, (t0, nt) in enumerate(chunks):
            b = ci % 2
            for n0 in range(0, M, 512):
                w = min(512, M - n0)
                slot = cnt % 8
                cnt += 1
                for k in range(nk):
                    P.op("pe", mm(psum[slot][0:nt, 0:w], inT[:, k, t0:t0 + nt], wsb[:, k, n0:n0 + w], k == 0, k == nk - 1),
                         reads=["tp_in", "tp_w"], writes=[("ps", slot)])
                en = evac_eng()
                if en == "act":
                    P.op("act", lambda e, slot=slot, b=b, nt=nt, n0=n0, w=w: e.copy(out=vst[b][0:nt, n0:n0 + w], in_=psum[slot][0:nt, 0:w]),
                         reads=[("ps", slot)], writes=[("tp_st", b, n0)])
                else:
                    P.op("dve", lambda e, slot=slot, b=b, nt=nt, n0=n0, w=w: e.tensor_copy(out=vst[b][0:nt, n0:n0 + w], in_=psum[slot][0:nt, 0:w]),
                         reads=[("ps", slot)], writes=[("tp_st", b, n0)])
            P.op("sp", dma(OUTap[t0:t0 + nt, :], vst[b][0:nt, :]), reads=[("tp_st", b, n0) for n0 in range(0, M, 512)], chan=f"st{b}")
        P.barrier()

    def phase_norm(INfn, nch, gc, gcol, eps, OUTfn, center=False, addin=None):
        arena_off[0] = CONST_END
        xt = [sb(f"nm_x{i}", [128, nch, 264], F32) for i in range(2)]
        xb = [sb(f"nm_b{i}", [128, nch, 264], BF16) for i in range(2)]
        xq = [sb(f"nm_q{i}", [128, nch, 264], BF16) for i in range(2)]
        ng = nch // gc
        mean = [sb(f"nm_m{i}", [128, ng, 264], F32) for i in range(2)]
        msq = [sb(f"nm_s{i}", [128, ng, 264], F32) for i in range(2)]
        rstd = [sb(f"nm_r{i}", [128, ng, 264], F32) for i in range(2)]
        ho = [sb(f"nm_o{i}", [128, nch, 264], F32) for i in range(2)]
        addt = [sb(f"nm_a{i}", [128, nch, 264], F32) for i in range(2)] if addin is not None else None
        for ti, (t0, nt) in enumerate(TILES):
            b = ti % 2
            P.op("sp", dma(xt[b][:, :, 0:nt], INfn(t0, nt).rearrange("k p t -> p k t")), writes=[("nm_x", b)], chan=f"ld{b}")
            if addin is not None:
                P.op("sp", dma(addt[b][:, :, 0:nt], addin[0](t0, nt).rearrange("k p t -> p k t")), writes=[("nm_a", b)], chan=f"lda{b}")
            P.op("act", lambda e, b=b, nt=nt: e.activation(out=xq[b][:, :, 0:nt], in_=xt[b][:, :, 0:nt], func=AF.Square),
                 reads=[("nm_x", b)], writes=[("nm_q", b)])
            if center:
                P.op("act", lambda e, b=b, nt=nt: e.copy(out=xb[b][:, :, 0:nt], in_=xt[b][:, :, 0:nt]),
                     reads=[("nm_x", b)], writes=[("nm_b", b)])
            for g in range(ng):
                pq = psum[(2 * g) % 8]
                pm_ = psum[(2 * g + 1) % 8]
                for k in range(gc):
                    P.op("pe", mm(pq[:, 0:nt], ones_bf[:, :], xq[b][:, g * gc + k, 0:nt], k == 0, k == gc - 1),
                         reads=[("nm_q", b), "ones"], writes=[("ps", (2 * g) % 8)])
                if center:
                    for k in range(gc):
                        P.op("pe", mm(pm_[:, 0:nt], ones_bf[:, :], xb[b][:, g * gc + k, 0:nt], k == 0, k == gc - 1),
                             reads=[("nm_b", b), "ones"], writes=[("ps", (2 * g + 1) % 8)])
                    P.op("dve", lambda e, b=b, g=g, nt=nt, pm_=pm_: e.tensor_scalar(
                        out=mean[b][:, g, 0:nt], in0=pm_[:, 0:nt], scalar1=1.0 / (gc * 128), scalar2=None, op0=ALU.mult),
                         reads=[("ps", (2 * g + 1) % 8)], writes=[("nm_m", b, g)])
                    P.op("dve", lambda e, b=b, g=g, nt=nt: e.tensor_tensor(
                        out=msq[b][:, g, 0:nt], in0=mean[b][:, g, 0:nt], in1=mean[b][:, g, 0:nt], op=ALU.mult),
                         reads=[("nm_m", b, g)], writes=[("nm_s", b, g)])
                    P.op("dve", lambda e, b=b, g=g, nt=nt, pq=pq: e.scalar_tensor_tensor(
                        out=rstd[b][:, g, 0:nt], in0=pq[:, 0:nt], scalar=1.0 / (gc * 128), in1=msq[b][:, g, 0:nt],
                        op0=ALU.mult, op1=ALU.subtract),
                         reads=[("ps", (2 * g) % 8), ("nm_s", b, g)], writes=[("nm_r", b, g)])
                    P.op("act", lambda e, b=b, g=g, nt=nt: e.activation(
                        out=msq[b][:, g, 0:nt], in_=rstd[b][:, g, 0:nt], func=AF.Sqrt, bias=eps, scale=1.0),
                         reads=[("nm_r", b, g)], writes=[("nm_s", b, g)])
                else:
                    P.op("act", lambda e, b=b, g=g, nt=nt, pq=pq: e.activation(
                        out=msq[b][:, g, 0:nt], in_=pq[:, 0:nt], func=AF.Sqrt, bias=eps, scale=1.0 / (gc * 128)),
                         reads=[("ps", (2 * g) % 8)], writes=[("nm_s", b, g)])
                P.op("dve", lambda e, b=b, g=g, nt=nt: e.reciprocal(out=rstd[b][:, g, 0:nt], in_=msq[b][:, g, 0:nt]),
                     reads=[("nm_s", b, g)], writes=[("nm_r", b, g)])
                sl = slice(g * gc, (g + 1) * gc)
                if center:
                    P.op("dve", lambda e, b=b, g=g, nt=nt, sl=sl: e.tensor_tensor(
                        out=xt[b][:, sl, 0:nt], in0=xt[b][:, sl, 0:nt],
                        in1=mean[b][:, g, 0:nt].unsqueeze(1).to_broadcast([128, gc, nt]), op=ALU.subtract),
                         reads=[("nm_x", b), ("nm_m", b, g)], writes=[("nm_x", b)])
                P.op("dve", lambda e, b=b, g=g, nt=nt, sl=sl: e.tensor_tensor(
                    out=xt[b][:, sl, 0:nt], in0=xt[b][:, sl, 0:nt],
                    in1=rstd[b][:, g, 0:nt].unsqueeze(1).to_broadcast([128, gc, nt]), op=ALU.mult),
                     reads=[("nm_x", b), ("nm_r", b, g)], writes=[("nm_x", b)])
            for k in range(nch):
                P.op("act", lambda e, b=b, k=k, nt=nt: e.activation(
                    out=ho[b][:, k, 0:nt], in_=xt[b][:, k, 0:nt], func=AF.Copy, scale=gcol[:, k:k + 1]),
                     reads=[("nm_x", b), "consts2"], writes=[("nm_o", b)])
                if addin is not None:
                    P.op("dve", lambda e, b=b, k=k, nt=nt: e.scalar_tensor_tensor(
                        out=ho[b][:, k, 0:nt], in0=addt[b][:, k, 0:nt], scalar=addin[1][:, k:k + 1], in1=ho[b][:, k, 0:nt],
                        op0=ALU.mult, op1=ALU.add),
                         reads=[("nm_a", b), ("nm_o", b), "consts2"], writes=[("nm_o", b)])
            o_ = OUTfn(t0, nt)
            if isinstance(o_, list):
                for k in range(nch):
                    P.op("sp", dma(o_[k], ho[b][:, k, 0:nt]), reads=[("nm_o", b)], chan=f"st{b}")
            else:
                P.op("sp", dma(o_.rearrange("k p t -> p k t"), ho[b][:, :, 0:nt]), reads=[("nm_o", b)], chan=f"st{b}")
        P.barrier()


    def dump(name, ap):
        shp = list(ap.shape)
        o = nc.dram_tensor("dbg_" + name, shp, ap.dtype, kind="ExternalOutput").ap()
        dumps.append((o, ap))

    RMS_EPS = 1e-6
    WIN_CH = [(128 * i, 128) for i in range(22)] + [(2816, 64), (2880, 64)]

    def win_out(mi, t0, nt):
        if mi < 4:
            return UQ[mi, :, t0:t0 + nt]
        if mi < 6:
            return UKV[mi - 4, :, t0:t0 + nt]
        if mi < 14:
            return gx(2 + mi - 6)[:, t0:t0 + nt]
        if mi < 22:
            return ZM[mi - 14, :, t0:t0 + nt]
        return URR[mi - 22, :, t0:t0 + nt]

    def phase_krope():
        arena_off[0] = CONST_END
        a = sb("kr_a", [64, T], F32); b_ = sb("kr_b", [64, T], F32); c = sb("kr_c", [64, T], F32); d_ = sb("kr_d", [64, T], F32)
        P.op("sp", dma(a[:], URR[0, :, :]), writes=["kr_a"], chan="ld0")
        P.op("sp", dma(b_[:], URR[1, :, :]), writes=["kr_b"], chan="ld0")
        P.op("sp", dma(c[:], cos2[:, :]), writes=["kr_c"], chan="ld1")
        P.op("sp", dma(d_[:], sin2[:, :]), writes=["kr_d"], chan="ld1")
        P.op("dve", lambda e: e.tensor_tensor(out=a[:], in0=a[:], in1=c[:], op=ALU.mult), reads=["kr_a", "kr_c"], writes=["kr_a"])
        P.op("dve", lambda e: e.tensor_tensor(out=b_[:], in0=b_[:], in1=d_[:], op=ALU.mult), reads=["kr_b", "kr_d"], writes=["kr_b"])
        P.op("dve", lambda e: e.tensor_tensor(out=a[:], in0=a[:], in1=b_[:], op=ALU.add), reads=["kr_a", "kr_b"], writes=["kr_a"])
        P.op("sp", dma(gx(10)[:, :], a[:]), reads=["kr_a"], chan="st0")
        P.barrier()

    def phase_gather():
        groups = [[2 * i, 2 * i + 1] for i in range(NCORES // 2)]
        for i in range(NGX):
            P.op("pool", lambda e, i=i: e.collective_compute("AllGather", ALU.bypass, replica_groups=groups,
                                                             ins=[GX_t[i].ap().opt()], outs=[GXA_t[i].ap().opt()]),
                 chan="cc", writes=[("gather", i)], inc=1)
        P.barrier()

    ATT_SCALE = 192.0 ** -0.5
    KCH = [(b, c, b * T + t0, nt) for b in range(2) for c, (t0, nt) in enumerate(CHUNKS)]

    def phase_mla_attn():
        for hg in range(2):
            arena_off[0] = CONST_END
            QN = sb("at_qn", [128, 4, T], BF16)
            QR = sb("at_qr", [64, 4, T], BF16)
            KN = sb("at_kn", [128, 4, 2 * T], BF16)
            KR = sb("at_kr", [64, 2 * T], BF16)
            V = sb("at_v", [128, 34, 512], BF16)
            cs = sb("at_cos", [64, T], F32); sn = sb("at_sin", [64, T], F32)
            r1 = sb("at_r1", [64, T], F32); r2 = sb("at_r2", [64, T], F32)
            PT = [sb(f"at_p{i}", [128, 512], BF16) for i in range(4)]
            rsb = [sb(f"at_rs{i}", [128, 512], F32) for i in range(2)]
            ost = [sb(f"at_os{i}", [128, 512], F32) for i in range(2)]
            P.op("pool", dma(QN[:], QALL[hg * 4:(hg + 1) * 4, 0:128, :].rearrange("h p t -> p h t")), writes=["at_qn"], chan="ldx")
            P.op("pool", dma(KN[:], KNA[hg * 4:(hg + 1) * 4, :, :].rearrange("h p t -> p h t")), writes=["at_kn"], chan="ldx")
            for b in range(2):
                P.op("pool", dma(KR[:, b * T:(b + 1) * T], gxa(b, 10)), writes=["at_kr"], chan="ldx")
                P.op("sp", dma(V[:, b * 17:b * 17 + 16, :],
                               VTA[b * T:b * T + 2048, hg * 512:(hg + 1) * 512].rearrange("(c p) m -> p c m", p=128)),
                     writes=["at_v"], chan="ld0")
                P.op("sp", dma(V[0:8, b * 17 + 16, :], VTA[b * T + 2048:b * T + 2056, hg * 512:(hg + 1) * 512]),
                     writes=["at_v"], chan="ld0")
            P.op("sp", dma(cs[:], cos2[:, :]), writes=["at_cs"], chan="ld1")
            P.op("sp", dma(sn[:], sin2[:, :]), writes=["at_sn"], chan="ld1")
            for hl in range(4):
                h = hg * 4 + hl
                P.op("sp", dma(r1[:], QALL[h, 128:192, :]), writes=["at_r1"], chan="ld2")
                P.op("sp", dma(r2[:], QALL[h, 192:256, :]), writes=["at_r2"], chan="ld2")
                P.op("dve", lambda e: e.tensor_tensor(out=r1[:], in0=r1[:], in1=cs[:], op=ALU.mult), reads=["at_r1", "at_cs"], writes=["at_r1"])
                P.op("dve", lambda e: e.tensor_tensor(out=r2[:], in0=r2[:], in1=sn[:], op=ALU.mult), reads=["at_r2", "at_sn"], writes=["at_r2"])
                P.op("dve", lambda e, hl=hl: e.tensor_tensor(out=QR[:, hl, :], in0=r1[:], in1=r2[:], op=ALU.add),
                     reads=["at_r1", "at_r2"], writes=["at_qr"])
            blocks = []
            for hl in range(4):
                for qi, (t0, nt) in enumerate(ATILES):
                    for ci, (b, c, k0, nk) in enumerate(KCH):
                        blocks.append((hl, hl * 5 + qi, t0, nt, ci, k0, nk))
            DEP = 3

            def front(i, blk):
                hl, qi, t0, nt, ci, k0, nk = blk
                sbk = i % 4
                P.op("pe", mm(psum[sbk][0:nk, 0:nt], KN[:, hl, k0:k0 + nk], QN[:, hl, t0:t0 + nt], True, False),
                     reads=["at_kn", "at_qn"], writes=[("ps", sbk)])
                P.op("pe", mm(psum[sbk][0:nk, 0:nt], KR[:, k0:k0 + nk], QR[:, hl, t0:t0 + nt], False, True),
                     reads=["at_kr", "at_qr"], writes=[("ps", sbk)])
                P.op("act", lambda e, sbk=sbk, nk=nk, nt=nt: e.activation(
                    out=PT[sbk][0:nk, 0:nt], in_=psum[sbk][0:nk, 0:nt], func=AF.Exp, scale=ATT_SCALE),
                     reads=[("ps", sbk)], writes=[("at_p", sbk)])

            def back(i, blk, hg=hg):
                hl, qi, t0, nt, ci, k0, nk = blk
                h = hg * 4 + hl
                pi = i % 4
                ob = 4 + 2 * (qi % 2)
                po, pssum = psum[ob], psum[ob + 1]
                P.op("pe", mm(po[:, 0:nt], V[0:nk, ci, hl * 128:(hl + 1) * 128], PT[pi][0:nk, 0:nt], ci == 0, ci == 33),
                     reads=["at_v", ("at_p", pi)], writes=[("ps", ob)])
                P.op("pe", mm(pssum[:, 0:nt], ones_bf[0:nk, :], PT[pi][0:nk, 0:nt], ci == 0, ci == 33),
                     reads=["ones", ("at_p", pi)], writes=[("ps", ob + 1)])
                if ci == 33:
                    rb = qi % 2
                    P.op("dve", lambda e, rb=rb, nt=nt, pssum=pssum: e.reciprocal(out=rsb[rb][:, 0:nt], in_=pssum[:, 0:nt]),
                         reads=[("ps", ob + 1)], writes=[("at_rs", rb)])
                    P.op("dve", lambda e, rb=rb, nt=nt, po=po: e.tensor_tensor(
                        out=ost[rb][:, 0:nt], in0=po[:, 0:nt], in1=rsb[rb][:, 0:nt], op=ALU.mult),
                         reads=[("ps", ob), ("at_rs", rb)], writes=[("at_os", rb)])
                    P.op("sp", dma(OAT[h, :, t0:t0 + nt], ost[rb][:, 0:nt]), reads=[("at_os", rb)], chan=f"st{rb}")

            for i in range(len(blocks) + DEP):
                if i < len(blocks):
                    front(i, blocks[i])
                if i >= DEP:
                    back(i - DEP, blocks[i - DEP])
            P.barrier()


    def phase_select():
        arena_off[0] = CONST_END
        ga = [sb(f"sl_a{i}", [128, T], F32) for i in range(2)]
        gb = [sb(f"sl_b{i}", [128, T], F32) for i in range(2)]
        oo = [sb(f"sl_o{i}", [128, T], F32) for i in range(2)]
        ot = [sb(f"sl_t{i}", [128, T], F32) for i in range(2)]
        for k in range(8):
            i = k % 2
            P.op("sp", dma(ga[i][:, :], gxa(0, 2 + k)), writes=[("sl_a", i)], chan="ld")
            P.op("sp", dma(gb[i][:, :], gxa(1, 2 + k)), writes=[("sl_b", i)], chan="ld")
            P.op("dve", lambda e, i=i: e.tensor_scalar(out=oo[i][:, :], in0=ga[i][:, :], scalar1=om_sb[:, 0:1], scalar2=None, op0=ALU.mult),
                 reads=[("sl_a", i), "consts2"], writes=[("sl_o", i)])
            P.op("dve", lambda e, i=i: e.scalar_tensor_tensor(out=oo[i][:, :], in0=gb[i][:, :], scalar=om_sb[:, 1:2], in1=oo[i][:, :],
                                                             op0=ALU.mult, op1=ALU.add),
                 reads=[("sl_b", i), ("sl_o", i), "consts2"], writes=[("sl_o", i)])
            P.op("dve", lambda e, i=i: e.tensor_scalar(out=ot[i][:, :], in0=ga[i][:, :], scalar1=om_sb[:, 1:2], scalar2=None, op0=ALU.mult),
                 reads=[("sl_a", i), "consts2"], writes=[("sl_t", i)])
            P.op("dve", lambda e, i=i: e.scalar_tensor_tensor(out=ot[i][:, :], in0=gb[i][:, :], scalar=om_sb[:, 0:1], in1=ot[i][:, :],
                                                             op0=ALU.mult, op1=ALU.add),
                 reads=[("sl_b", i), ("sl_t", i), "consts2"], writes=[("sl_t", i)])
            P.op("sp", dma(XO[0, k, :, :], oo[i][:, :]), reads=[("sl_o", i)], chan="st")
            P.op("sp", dma(XO[1, k, :, :], ot[i][:, :]), reads=[("sl_t", i)], chan="st2")
        P.barrier()

    def phase_conv():
        arena_off[0] = CONST_END
        XMt = [sb(f"cv_x{i}", [128, T + 4], F32) for i in range(2)]
        acc = [sb(f"cv_a{i}", [128, T], F32) for i in range(2)]
        xco = [sb(f"cv_o{i}", [128, T], F32) for i in range(2)]
        n = 0
        for b in range(2):
            for k in range(8):
                i = n % 2
                n += 1
                src = XO[b, k]
                oth = XO[1 - b, k]
                P.op("dve", lambda e, i=i: e.memset(XMt[i][:, 0:2], 0.0), writes=[("cv_x", i)])
                P.op("sp", dma(XMt[i][:, 2:T + 2], src[:, :]), writes=[("cv_x", i)], chan="ld")
                P.op("sp", dma(XMt[i][:, T + 2:T + 3], oth[:, T - 1:T], True), writes=[("cv_x", i)], chan="ld")
                P.op("sp", dma(XMt[i][:, T + 3:T + 4], oth[:, T - 2:T - 1], True), writes=[("cv_x", i)], chan="ld")
                for j in range(5):
                    off = j if b == 0 else 4 - j
                    if j == 0:
                        P.op("dve", lambda e, i=i, k=k, off=off: e.tensor_scalar(
                            out=acc[i][:, :], in0=XMt[i][:, off:off + T], scalar1=cw_sb[:, 0, k:k + 1], scalar2=None, op0=ALU.mult),
                             reads=[("cv_x", i), "consts2"], writes=[("cv_a", i)])
                    else:
                        P.op("dve", lambda e, i=i, k=k, j=j, off=off: e.scalar_tensor_tensor(
                            out=acc[i][:, :], in0=XMt[i][:, off:off + T], scalar=cw_sb[:, j, k:k + 1], in1=acc[i][:, :],
                            op0=ALU.mult, op1=ALU.add),
                             reads=[("cv_x", i), ("cv_a", i), "consts2"], writes=[("cv_a", i)])
                P.op("act", lambda e, i=i, k=k: e.activation(out=xco[i][:, :], in_=acc[i][:, :], func=AF.Silu, bias=cb_sb[:, k:k + 1]),
                     reads=[("cv_a", i), "consts2"], writes=[("cv_o", i)])
                P.op("sp", dma(XCM[b, k, :, :], xco[i][:, :]), reads=[("cv_o", i)], chan="st")
                P.op("sp", dma(XCM[b, 8 + k, :, :], XMt[i][:, 2:T + 2]), reads=[("cv_x", i)], chan="st2")
        P.barrier()

    def phase_scans():
        arena_off[0] = CONST_END
        T2 = 2 * T
        LI = sb("sc_li", [36, T2], F32); LF = sb("sc_lf", [36, T2], F32); ONE = sb("sc_one", [36, T2], F32)
        Bt = sb("sc_b", [36, T2], F32); At = sb("sc_a", [36, T2], F32); Mt = sb("sc_m", [36, T2], F32); mt = sb("sc_mm", [36, T2], F32)
        P.op("dve", lambda e: e.memset(ONE[:, :], 1.0), writes=["sc_one"])
        for d in range(2):
            p0 = 32 * d
            ps_ = slice(p0, p0 + 4)
            for b in range(2):
                P.op("sp", dma(LI[ps_, b * T:(b + 1) * T], GRAW[b, 8 * d:8 * d + 4, :]), writes=[("sc_li", d)], chan="ld")
                P.op("sp", dma(LF[ps_, b * T:(b + 1) * T], GRAW[b, 8 * d + 4:8 * d + 8, :]), writes=[("sc_lf", d)], chan="ld")
            P.op("act", lambda e, ps_=ps_: e.activation(out=LF[ps_, :], in_=LF[ps_, :], func=AF.Exp, scale=-1.0),
                 reads=[("sc_lf", d)], writes=[("sc_lf", d)])
            P.op("act", lambda e, ps_=ps_: e.activation(out=LF[ps_, :], in_=LF[ps_, :], func=AF.Ln, bias=1.0, scale=1.0),
                 reads=[("sc_lf", d)], writes=[("sc_lf", d)])
            if d == 0:
                seg1 = lambda t, ps_=ps_: t[ps_, 0:T]
                seg2 = lambda t, ps_=ps_: t[ps_, T:T2][:, ::-1]
                last1 = lambda t, ps_=ps_: t[ps_, T - 1:T]
            else:
                seg1 = lambda t, ps_=ps_: t[ps_, T:T2]
                seg2 = lambda t, ps_=ps_: t[ps_, 0:T][:, ::-1]
                last1 = lambda t, ps_=ps_: t[ps_, T2 - 1:T2]
            P.op("dve", lambda e, seg1=seg1: e.tensor_tensor_scan(out=seg1(Bt), data0=seg1(ONE), data1=seg1(LF), initial=0.0,
                                                                  op0=ALU.mult, op1=ALU.subtract),
                 reads=[("sc_lf", d), "sc_one"], writes=[("sc_b", d)])
            P.op("dve", lambda e, seg2=seg2, last1=last1: e.tensor_tensor_scan(out=seg2(Bt), data0=seg2(ONE), data1=seg2(LF), initial=last1(Bt),
                                                                               op0=ALU.mult, op1=ALU.subtract),
                 reads=[("sc_lf", d), "sc_one", ("sc_b", d)], writes=[("sc_b", d)])
            P.op("dve", lambda e, ps_=ps_: e.tensor_tensor(out=At[ps_, :], in0=LI[ps_, :], in1=Bt[ps_, :], op=ALU.subtract),
                 reads=[("sc_li", d), ("sc_b", d)], writes=[("sc_a", d)])
            P.op("dve", lambda e, seg1=seg1: e.tensor_tensor_scan(out=seg1(Mt), data0=seg1(At), data1=seg1(At), initial=0.0,
                                                                  op0=ALU.max, op1=ALU.max),
                 reads=[("sc_a", d)], writes=[("sc_m", d)])
            P.op("dve", lambda e, seg2=seg2, last1=last1: e.tensor_tensor_scan(out=seg2(Mt), data0=seg2(At), data1=seg2(At), initial=last1(Mt),
                                                                               op0=ALU.max, op1=ALU.max),
                 reads=[("sc_a", d), ("sc_m", d)], writes=[("sc_m", d)])
            P.op("dve", lambda e, ps_=ps_: e.tensor_tensor(out=mt[ps_, :], in0=Bt[ps_, :], in1=Mt[ps_, :], op=ALU.add),
                 reads=[("sc_b", d), ("sc_m", d)], writes=[("sc_mm", d)])
            P.op("dve", lambda e, ps_=ps_: e.tensor_scalar(out=Mt[ps_, :], in0=Mt[ps_, :], scalar1=-1.0, scalar2=None, op0=ALU.mult),
                 reads=[("sc_m", d), ("sc_mm", d)], writes=[("sc_m", d)])
            P.op("dve", lambda e, ps_=ps_: e.tensor_scalar(out=mt[ps_, :], in0=mt[ps_, :], scalar1=-1.0, scalar2=None, op0=ALU.mult),
                 reads=[("sc_mm", d)], writes=[("sc_mm", d)])
            P.op("sp", dma(RW[d, 0, :, :], Mt[ps_, :]), reads=[("sc_m", d)], chan="st0")
            P.op("sp", dma(RW[d, 1, :, :], mt[ps_, :]), reads=[("sc_mm", d)], chan="st1")
            P.op("sp", dma(RW[d, 2, :, :], At[ps_, :]), reads=[("sc_a", d)], chan="st2")
        P.barrier()

    def phase_mix(h, d):
        arena_off[0] = CONST_END
        T2 = 2 * T
        Q = sb("mx_q", [128, 2, T], BF16); Kt = sb("mx_k", [128, 2, T2], BF16); V = sb("mx_v", [128, 34, 256], BF16)
        nM = sb("mx_nm", [128, T2], F32); en = sb("mx_en", [128, T2], F32); acol = sb("mx_ac", [128, 34], F32)
        mle = sb("mx_le", [128, 4, 512], F32); mge = sb("mx_ge", [128, 4, 512], F32)
        es = [sb(f"mx_es{i}", [128, 512], F32) for i in range(4)]
        at = [sb(f"mx_at{i}", [128, 512], BF16) for i in range(4)]
        xs = [sb(f"mx_xs{i}", [128, 512], F32) for i in range(4)]
        dn = [sb(f"mx_dn{i}", [128, 512], F32) for i in range(2)]
        ost = [sb(f"mx_os{i}", [128, 2, 512], F32) for i in range(2)]
        P.op("pool", dma(Q[:], QM[h, :, :, 0:T].rearrange("c p t -> p c t")), writes=["mx_q"], chan="ld")
        P.op("pool", dma(Kt[:], KM[h].rearrange("c p t -> p c t")), writes=["mx_k"], chan="ld")
        for b in range(2):
            P.op("sp", dma(V[:, b * 17:b * 17 + 16, :], VM[h, b * T:b * T + 2048, :].rearrange("(c p) m -> p c m", p=128)),
                 writes=[("mx_v", b)], chan="ld")
            P.op("sp", dma(V[0:8, b * 17 + 16, :], VM[h, b * T + 2048:b * T + 2056, :]), writes=[("mx_v", b, 1)], chan="ld")
            P.op("sp", dma(acol[:, b * 17:b * 17 + 16], RW[d, 2, h, b * T:b * T + 2048].rearrange("(c p) -> p c", p=128), True),
                 writes=[("mx_ac", b)], chan="ld")
            P.op("sp", dma(acol[0:8, b * 17 + 16:b * 17 + 17], RW[d, 2, h, b * T + 2048:b * T + 2056].rearrange("(p o) -> p o", o=1), True),
                 writes=[("mx_ac", b, 1)], chan="ld")
        P.op("sp", dma(nM[:], RW[d, 0, h:h + 1, :].partition_broadcast(128)), writes=["mx_nm"], chan="ld")
        P.op("sp", dma(en[:], RW[d, 1, h:h + 1, :].partition_broadcast(128)), writes=["mx_en"], chan="ld")
        P.op("sp", dma(mle[:], mask_le[:, :, :]), writes=["mx_le"], chan="ld")
        P.op("sp", dma(mge[:], mask_ge[:, :, :]), writes=["mx_ge"], chan="ld")
        P.op("act", lambda e: e.activation(out=en[:], in_=en[:], func=AF.Exp), reads=["mx_en"], writes=["mx_en"])
        vkeys = [("mx_v", 0), ("mx_v", 0, 1), ("mx_v", 1), ("mx_v", 1, 1)]
        akeys = [("mx_ac", 0), ("mx_ac", 0, 1), ("mx_ac", 1), ("mx_ac", 1, 1)]
        blocks = []
        gi = 0
        for qb in range(1):
            rel = "le" if d == qb else "ge"
            for (t0, nt) in ATILES:
                tq = qb * T + t0
                lst = []
                if rel == "ge":
                    ob = 1 - qb
                    for c, (s0, nk) in enumerate(CHUNKS):
                        lst.append((ob * 17 + c, ob * T + s0, nk, None))
                for c, (s0, nk) in enumerate(CHUNKS):
                    if rel == "le":
                        if s0 >= t0 + nt:
                            continue
                        mk = None if s0 < t0 else (mle, (s0 - t0) // 128)
                    else:
                        if s0 < t0:
                            continue
                        mk = None if s0 >= t0 + nt else (mge, (s0 - t0) // 128)
                    lst.append((qb * 17 + c, qb * T + s0, nk, mk))
                for li_, (ci, k0, nk, mk) in enumerate(lst):
                    blocks.append((gi, tq, nt, ci, k0, nk, mk, li_ == 0, li_ == len(lst) - 1))
                gi += 1
        DEP = 3
        pO0, pO1, pD = psum[4], psum[5], psum[6]

        def front(i, blk):
            g_, tq, nt, ci, k0, nk, mk, first, last = blk
            sbk = i % 4
            pi = i % 4
            for c2 in range(2):
                P.op("pe", mm(psum[sbk][0:nk, 0:nt], Kt[:, c2, k0:k0 + nk], Q[:, c2, tq:tq + nt], c2 == 0, c2 == 1),
                     reads=["mx_k", "mx_q"], writes=[("ps", sbk)])
            if mk is None:
                P.op("act", lambda e, pi=pi, nk=nk, nt=nt, tq=tq, ci=ci: e.activation(
                    out=es[pi][0:nk, 0:nt], in_=nM[0:nk, tq:tq + nt], func=AF.Exp, bias=acol[0:nk, ci:ci + 1]),
                     reads=["mx_nm"] + akeys, writes=[("mx_es", pi)])
            else:
                mt_, mj = mk
                P.op("dve", lambda e, pi=pi, nk=nk, nt=nt, tq=tq, ci=ci, mt_=mt_, mj=mj: e.scalar_tensor_tensor(
                    out=xs[pi][0:nk, 0:nt], in0=nM[0:nk, tq:tq + nt], scalar=acol[0:nk, ci:ci + 1], in1=mt_[0:nk, mj, 0:nt],
                    op0=ALU.add, op1=ALU.min),
                     reads=["mx_nm", "mx_le", "mx_ge"] + akeys, writes=[("mx_xs", pi)])
                P.op("act", lambda e, pi=pi, nk=nk, nt=nt: e.activation(
                    out=es[pi][0:nk, 0:nt], in_=xs[pi][0:nk, 0:nt], func=AF.Exp),
                     reads=[("mx_xs", pi)], writes=[("mx_es", pi)])
            P.op("dve", lambda e, sbk=sbk, pi=pi, nk=nk, nt=nt: e.scalar_tensor_tensor(
                out=at[pi][0:nk, 0:nt], in0=psum[sbk][0:nk, 0:nt], scalar=1.0 / 16.0, in1=es[pi][0:nk, 0:nt],
                op0=ALU.mult, op1=ALU.mult),
                 reads=[("ps", sbk), ("mx_es", pi)], writes=[("mx_at", pi)])

        def back(i, blk):
            g_, tq, nt, ci, k0, nk, mk, first, last = blk
            pi = i % 4
            P.op("pe", mm(pO0[:, 0:nt], V[0:nk, ci, 0:128], at[pi][0:nk, 0:nt], first, last),
                 reads=vkeys + [("mx_at", pi)], writes=[("ps", 4)])
            P.op("pe", mm(pO1[:, 0:nt], V[0:nk, ci, 128:256], at[pi][0:nk, 0:nt], first, last),
                 reads=vkeys + [("mx_at", pi)], writes=[("ps", 5)])
            P.op("pe", mm(pD[:, 0:nt], ones_bf[0:nk, :], at[pi][0:nk, 0:nt], first, last),
                 reads=["ones", ("mx_at", pi)], writes=[("ps", 6)])
            if last:
                oi = g_ % 2
                P.op("act", lambda e, oi=oi, nt=nt: e.activation(out=dn[oi][:, 0:nt], in_=pD[:, 0:nt], func=AF.Abs),
                     reads=[("ps", 6)], writes=[("mx_dn", oi)])
                P.op("dve", lambda e, oi=oi, nt=nt, tq=tq: e.tensor_tensor(
                    out=dn[oi][:, 0:nt], in0=dn[oi][:, 0:nt], in1=en[:, tq:tq + nt], op=ALU.max),
                     reads=[("mx_dn", oi), "mx_en"], writes=[("mx_dn", oi)])
                P.op("dve", lambda e, oi=oi, nt=nt: e.reciprocal(out=dn[oi][:, 0:nt], in_=dn[oi][:, 0:nt]),
                     reads=[("mx_dn", oi)], writes=[("mx_dn", oi)])
                P.op("dve", lambda e, oi=oi, nt=nt: e.tensor_tensor(out=ost[oi][:, 0, 0:nt], in0=pO0[:, 0:nt], in1=dn[oi][:, 0:nt], op=ALU.mult),
                     reads=[("ps", 4), ("mx_dn", oi)], writes=[("mx_os", oi)])
                P.op("dve", lambda e, oi=oi, nt=nt: e.tensor_tensor(out=ost[oi][:, 1, 0:nt], in0=pO1[:, 0:nt], in1=dn[oi][:, 0:nt], op=ALU.mult),
                     reads=[("ps", 5), ("mx_dn", oi)], writes=[("mx_os", oi)])
                P.op("sp", dma(HM[d, 2 * h:2 * h + 2, :, tq:tq + nt].rearrange("c p t -> p c t"), ost[oi][:, :, 0:nt]),
                     reads=[("mx_os", oi)], chan="st")

        for i in range(len(blocks) + DEP):
            if i < len(blocks):
                front(i, blocks[i])
            if i >= DEP:
                back(i - DEP, blocks[i - DEP])
        P.barrier()

    def phase_combine():
        arena_off[0] = CONST_END
        ha = [sb(f"cb_a{i}", [128, 8, 264], F32) for i in range(2)]
        hb = [sb(f"cb_b{i}", [128, 8, 264], F32) for i in range(2)]
        zz = [sb(f"cb_z{i}", [128, 8, 264], F32) for i in range(2)]
        for ti, (t0, nt) in enumerate(TILES):
            i = ti % 2
            P.op("sp", dma(ha[i][:, :, 0:nt], HM[0, :, :, t0:t0 + nt].rearrange("k p t -> p k t")), writes=[("cb_a", i)], chan="ld")
            P.op("sp", dma(hb[i][:, :, 0:nt], HM[1, :, :, t0:t0 + nt].rearrange("k p t -> p k t")), writes=[("cb_b", i)], chan="ld")
            P.op("sp", dma(zz[i][:, :, 0:nt], ZM[:, :, t0:t0 + nt].rearrange("k p t -> p k t")), writes=[("cb_z", i)], chan="ld")
            P.op("dve", lambda e, i=i, nt=nt: e.tensor_tensor(out=ha[i][:, :, 0:nt], in0=ha[i][:, :, 0:nt], in1=hb[i][:, :, 0:nt], op=ALU.add),
                 reads=[("cb_a", i), ("cb_b", i)], writes=[("cb_a", i)])
            P.op("act", lambda e, i=i, nt=nt: e.activation(out=zz[i][:, :, 0:nt], in_=zz[i][:, :, 0:nt], func=AF.Sigmoid),
                 reads=[("cb_z", i)], writes=[("cb_z", i)])
            P.op("dve", lambda e, i=i, nt=nt: e.tensor_tensor(out=ha[i][:, :, 0:nt], in0=ha[i][:, :, 0:nt], in1=zz[i][:, :, 0:nt], op=ALU.mult),
                 reads=[("cb_a", i), ("cb_z", i)], writes=[("cb_a", i)])
            P.op("sp", dma(HG[:, :, t0:t0 + nt].rearrange("k p t -> p k t"), ha[i][:, :, 0:nt]), reads=[("cb_a", i)], chan="st")
        P.barrier()

    def ph_mproj(h, b):
        xin = [XCM[b, 2 * h], XCM[b, 2 * h + 1]]
        if b == 0:
            phase_linear(xin, 2, w_q[h], [(0, 128), (128, 128)], lambda mi, t0, nt: QM[h, mi, :, b * T + t0:b * T + t0 + nt], toks=ATILES)
        phase_linear(xin, 2, w_k[h], [(0, 128), (128, 128)], lambda mi, t0, nt: KM[h, mi, :, b * T + t0:b * T + t0 + nt], toks=ATILES)
        phase_tokproj([XCM[b, 8 + 2 * h], XCM[b, 8 + 2 * h + 1]], 2, w_v[h], 256, VM[h, b * T:(b + 1) * T, :])

    UQ_CH = []
    for h in range(8):
        UQ_CH += [(h * 256, 128), (h * 256 + 128, 64), (h * 256 + 192, 64)]

    def uq_out(mi, t0, nt):
        h, j = mi // 3, mi % 3
        r0 = [0, 128, 192][j]
        rn = [128, 64, 64][j]
        return QALL[h, r0:r0 + rn, t0:t0 + nt]

    def ph_kv(b):
        phase_linear([gxa(b, 0), gxa(b, 1)], 2, w_ukvk, [(128 * h, 128) for h in range(8)],
                     lambda mi, t0, nt, b=b: KNA[mi, :, b * T + t0:b * T + t0 + nt])
        phase_tokproj([gxa(b, 0), gxa(b, 1)], 2, w_ukvv, 1024, VTA[b * T:(b + 1) * T, :])

    PH = [
        ("tin", phase_transpose_in),
        ("ffn1", lambda: phase_ffn("f1", XT, ZT, w1g, w1u, w1d)),
        ("ln1", lambda: phase_ln("l1", ZT, H1T, 0)),
        ("win", lambda: phase_linear(H1T, 16, w_in, WIN_CH, win_out, toks=ATILES)),
        ("nq", lambda: phase_norm(lambda t0, nt: UQ[:, :, t0:t0 + nt], 4, 4, gq_sb, RMS_EPS, lambda t0, nt: NQT[:, :, t0:t0 + nt])),
        ("nkv", lambda: phase_norm(lambda t0, nt: UKV[:, :, t0:t0 + nt], 2, 2, gkv_sb, RMS_EPS,
                                   lambda t0, nt: [gx(0)[:, t0:t0 + nt], gx(1)[:, t0:t0 + nt]])),
        ("krope", phase_krope),
        ("gather", phase_gather),
        ("uq", lambda: phase_linear(NQT, 4, w_uq, UQ_CH, uq_out)),
        ("kv0", lambda: ph_kv(0)),
        ("kv1", lambda: ph_kv(1)),
        ("attn", phase_mla_attn),
        ("nao", lambda: phase_norm(lambda t0, nt: OAT[:, :, t0:t0 + nt], 8, 8, gao_sb, RMS_EPS, lambda t0, nt: YT[0:8, :, t0:t0 + nt])),
        ("select", phase_select),
        ("conv", phase_conv),
        ("gates0", lambda: phase_linear(XCM[0], 16, w_gates, [(0, 16)], lambda mi, t0, nt: GRAW[0, :, t0:t0 + nt], bias_col=bg_sb)),
        ("gates1", lambda: phase_linear(XCM[1], 16, w_gates, [(0, 16)], lambda mi, t0, nt: GRAW[1, :, t0:t0 + nt], bias_col=bg_sb)),
        ("scans", phase_scans),
    ]
    for h_ in range(4):
        for b_ in range(2):
            PH.append((f"mproj{h_}{b_}", lambda h_=h_, b_=b_: ph_mproj(h_, b_)))
    for h_ in range(4):
        for d_ in range(2):
            PH.append((f"mix{h_}{d_}", lambda h_=h_, d_=d_: phase_mix(h_, d_)))
    PH += [
        ("combine", phase_combine),
        ("gnorm", lambda: phase_norm(lambda t0, nt: HG[:, :, t0:t0 + nt], 8, 2, gng_sb, LN_EPS, lambda t0, nt: YT[8:16, :, t0:t0 + nt],
                                     center=True, addin=(lambda t0, nt: XCM[0, 0:8, :, t0:t0 + nt], skip_sb))),
        ("wout", lambda: phase_linear(YT, 16, w_out, [(128 * i, 128) for i in range(16)], lambda mi, t0, nt: Z2T[mi, :, t0:t0 + nt], toks=ATILES,
                                      scale=1.0 / ALPHA, resid=lambda mi, t0, nt: H1T[mi, :, t0:t0 + nt])),
        ("ln2", lambda: phase_ln("l2", Z2T, H2T, 1)),
        ("ffn2", lambda: phase_ffn("f2", H2T, Z3T, w2g, w2u, w2d)),
        ("ln3", lambda: phase_ln("l3", Z3T, H3T, 2)),
    ]
    kstop = int(os.environ.get("KSTOP", "999"))
    kskip = os.environ.get("KSKIP", "").split(",")
    for i_, (nm_, f_) in enumerate(PH):
        if i_ < kstop and nm_ not in kskip:
            f_()
    if debug_out:
        for nm in debug_out:
            dump(nm, {"H1T": H1T, "UQ": UQ, "NQT": NQT, "GX0": gx(0), "GX10": gx(10), "GX2": gx(2), "GXA10": GXA_t[10].ap(), "QALL": QALL, "KNA": KNA, "VTA": VTA, "OAT": OAT, "YT": YT, "XCM": XCM, "GRAW": GRAW, "RW": RW, "HM": HM, "HG": HG, "H2T": H2T, "H3T": H3T, "Z2T": Z2T}[nm])
    for i_, (o_, a_) in enumerate(dumps):
        P.op("sp", dma(o_, a_), chan=f"st{i_ % 4}")
    phase_transpose_out(H3T)
    P.op("sp", lambda e: e.nop(), reads=[], writes=[])

    sem_names = [("e", e) for e in ENG] + [("c", c) for c in P.chan_names]
    assert len(sem_names) <= 95, len(sem_names)
    sems = {}
    for i, k in enumerate(sem_names):
        sems[k] = nc.alloc_semaphore(f"s{i}_{k[1]}")
    run = P.emit(sems)
    with nc.Block() as block:
        @block.tensor
        def _(e):
            run("pe", e)

        @block.scalar
        def _(e):
            run("act", e)

        @block.vector
        def _(e):
            run("dve", e)

        @block.gpsimd
        def _(e):
            run("pool", e)

        @block.sync
        def _(e):
            run("sp", e)
    return nc


_NC_CACHE = {}


def make_core_inputs(inputs):
    g = lambda k: np.ascontiguousarray(np.asarray(inputs[k], dtype=np.float32)[0])
    x = np.asarray(inputs["x"], dtype=np.float32)
    meta = np.asarray(inputs["meta_tokens"], dtype=np.float32)
    w_in = g("w_in")
    kr = w_in[:, 768:832]
    w_in_ext = np.ascontiguousarray(np.concatenate(
        [w_in[:, 0:768], w_in[:, 832:1856], w_in[:, 1856:2880], kr, kr[:, 32:], kr[:, :32]], axis=1))
    wuq = g("mla_w_uq")
    cols = []
    for h in range(8):
        b0 = h * 192
        cols += list(range(b0, b0 + 192)) + list(range(b0 + 160, b0 + 192)) + list(range(b0 + 128, b0 + 160))
    w_uq_ext = np.ascontiguousarray(wuq[:, cols])
    wukv = g("mla_w_ukv")
    kc = [h * 256 + j for h in range(8) for j in range(128)]
    vc = [h * 256 + 128 + j for h in range(8) for j in range(128)]
    w_ukvk = np.ascontiguousarray(wukv[:, kc])
    w_ukvv = np.ascontiguousarray(wukv[:, vc])
    inv = 10000.0 ** (-np.arange(0, 64, 2, dtype=np.float32) / 64.0)
    NEGM = -30000.0
    s_idx = np.arange(128)[:, None, None]
    j_idx = np.arange(4)[None, :, None]
    t_idx = np.arange(512)[None, None, :]
    mask_le = np.where(128 * j_idx + s_idx <= t_idx, 0.0, NEGM).astype(np.float32)
    mask_ge = np.where(128 * j_idx + s_idx >= t_idx, 0.0, NEGM).astype(np.float32)
    shared = {
        "ident": np.eye(128, dtype=np.float32),
        "w1g": g("ffn1_w_gate"), "w1u": g("ffn1_w_up"), "w1d": g("ffn1_w_down"),
        "ln1g": g("ln1_g"), "ln1b": g("ln1_b"),
        "w_in": w_in_ext, "gq": g("mla_q_norm_g"), "gkv": g("mla_kv_norm_g"), "gao": g("attn_out_g"),
        "w_uq": w_uq_ext, "w_ukvk": w_ukvk, "w_ukvv": w_ukvv,
        "w_out": g("w_out"), "ln2g": g("ln2_g"), "ln2b": g("ln2_b"), "ln3g": g("ln3_g"), "ln3b": g("ln3_b"),
        "w2g": g("ffn2_w_gate"), "w2u": g("ffn2_w_up"), "w2d": g("ffn2_w_down"),
        "conv_w": g("mlstm_conv_w"), "conv_b": g("mlstm_conv_b"),
        "w_q": g("mlstm_w_q"), "w_k": g("mlstm_w_k"), "w_v": g("mlstm_w_v"),
        "w_gates": g("mlstm_w_gates"), "b_gates": g("mlstm_b_gates"),
        "gn_g": g("mlstm_gn_g"), "skip": g("mlstm_skip"),
        "mask_le": mask_le, "mask_ge": mask_ge,
    }
    maps = []
    for c in range(NCORES):
        b, r = c // 2, c % 2
        if r == 0:
            xl = np.concatenate([meta, x[b, :T - 16]], axis=0)
            pos = np.arange(T, dtype=np.float32)
        else:
            xl = x[b, T - 16:][::-1]
            pos = (2 * T - 1 - np.arange(T)).astype(np.float32)
        ang = (pos[None, :] * inv[:, None]).astype(np.float32)
        cs, sn = np.cos(ang), np.sin(ang)
        om = np.zeros((128, 2), np.float32)
        om[:, r] = 1.0
        m = dict(shared)
        if r == 1:
            perm = list(range(8, 16)) + list(range(0, 8))
            m["w_gates"] = np.ascontiguousarray(shared["w_gates"][:, perm])
            m["b_gates"] = np.ascontiguousarray(shared["b_gates"][perm])
            m["conv_w"] = np.ascontiguousarray(shared["conv_w"][::-1])
        m["x"] = np.ascontiguousarray(xl)
        m["cos2"] = np.ascontiguousarray(np.concatenate([cs, cs], 0).astype(np.float32))
        m["sin2"] = np.ascontiguousarray(np.concatenate([-sn, sn], 0).astype(np.float32))
        m["om"] = om
        maps.append(m)
    return maps


def assemble(outs):
    B = NCORES // 2
    res = np.empty((B, 4096, D), np.float32)
    for c in range(NCORES):
        b, r = c // 2, c % 2
        o = outs[c]
        if r == 0:
            res[b, :T - 16] = o[16:]
        else:
            res[b, T - 16:] = o[::-1]
    return res


def kernel(**inputs):
    if "nc" not in _NC_CACHE:
        _NC_CACHE["nc"] = build_program()
    nc = _NC_CACHE["nc"]
    maps = make_core_inputs(inputs)
    res = run_bass_kernel_spmd(nc, maps, core_ids=list(range(NCORES)))
    outs = [np.asarray(r["out"]) for r in res.results]
    return assemble(outs)
```

```python
import numpy as np
import concourse.bass as bass
import concourse.mybir as mybir
from concourse.bass_utils import run_bass_kernel_spmd

F32 = mybir.dt.float32
BF16 = mybir.dt.bfloat16
AF = mybir.ActivationFunctionType
ALU = mybir.AluOpType

D = 2048
DFF = 5632
T = 2056
import os
NCORES = int(os.environ.get('KCORES', '8'))
ALPHA = 2.0 ** 0.25
LN_EPS = 1e-5
TILES = [(256 * i, 256) for i in range(7)] + [(1792, 264)]
HALVES = [TILES[0:4], TILES[4:8]]
CHUNKS = [(128 * i, 128) for i in range(16)] + [(2048, 8)]
ATILES = [(0, 512), (512, 512), (1024, 512), (1536, 256), (1792, 264)]
FHALVES = [ATILES[0:2], ATILES[2:5]]

ENG = ("pe", "act", "dve", "pool", "sp")


class Op:
    __slots__ = ("eng", "fn", "deps", "chan", "needed", "seq", "idx", "inc")

    def __init__(self, eng, fn, chan):
        self.eng, self.fn, self.chan = eng, fn, chan
        self.deps = set()
        self.inc = 16
        self.needed = False
        self.seq = None


class Prog:
    def __init__(self, nc):
        self.nc = nc
        self.ops = []
        self.last_w = {}
        self.readers = {}
        self.barrier_deps = set()
        self.last_eng = {}
        self.last_chan = {}
        self.chan_names = []
        self.chan_map = {}

    def op(self, eng, fn, reads=(), writes=(), chan=None, inc=16):
        if chan is not None:
            key = writes[0] if len(writes) else (reads[0] if len(reads) else ("anon", chan))
            if key not in self.chan_map:
                self.chan_map[key] = f"c{len(self.chan_map)}"
            chan = self.chan_map[key]
        o = Op(eng, fn, chan)
        o.inc = inc
        o.idx = len(self.ops)
        deps = set(self.barrier_deps)
        for k in reads:
            w = self.last_w.get(k)
            if w is not None:
                deps.add(w)
        for k in writes:
            w = self.last_w.get(k)
            if w is not None:
                deps.add(w)
            deps.update(self.readers.get(k, ()))
        for k in reads:
            self.readers.setdefault(k, []).append(o)
        for k in writes:
            self.last_w[k] = o
            self.readers[k] = []
        deps.discard(o)
        o.deps = deps
        self.ops.append(o)
        if chan is None:
            self.last_eng[eng] = o
        else:
            if chan not in self.last_chan:
                self.chan_names.append(chan)
            self.last_chan[chan] = o
            self.last_eng[eng] = o
        return o

    def barrier(self):
        self.barrier_deps = set(self.last_eng.values()) | set(self.last_chan.values())
        self.chan_map = {}
        self.last_w = {}
        self.readers = {}

    def emit(self, block_ctx_sems):
        nc = self.nc
        ops = self.ops
        for o in ops:
            for d in o.deps:
                if d.chan is None and d.eng == "pe" and o.eng == "pe" and o.chan is None:
                    continue
                d.needed = True
        eng_cnt = {e: 0 for e in ENG}
        chan_cnt = {}
        for o in ops:
            if o.chan is not None:
                chan_cnt[o.chan] = chan_cnt.get(o.chan, 0) + o.inc
                o.seq = chan_cnt[o.chan]
            elif o.needed:
                eng_cnt[o.eng] += 1
                o.seq = eng_cnt[o.eng]
        sems = block_ctx_sems
        per_eng = {e: [o for o in ops if o.eng == e] for e in ENG}

        def run(e, engine):
            waited = {}
            for o in per_eng[e]:
                need = {}
                for d in o.deps:
                    if d.chan is not None:
                        key = ("c", d.chan)
                    else:
                        if d.eng == "pe" and e == "pe" and o.chan is None:
                            continue
                        key = ("e", d.eng)
                    if d.seq > need.get(key, 0):
                        need[key] = d.seq
                for key, v in need.items():
                    if waited.get(key, 0) >= v:
                        continue
                    waited[key] = v
                    engine.wait_ge(sems[key], v)
                ins = o.fn(engine)
                if o.chan is not None:
                    ins.then_inc(sems[("c", o.chan)], o.inc)
                elif o.needed:
                    ins.then_inc(sems[("e", e)], 1)
        return run


def build_program(debug_out=None):
    nc = bass.Bass("TRN2", target_bir_lowering=False)
    P = Prog(nc)

    def din(name, shape, dt=F32):
        return nc.dram_tensor(name, list(shape), dt, kind="ExternalInput").ap()

    x_in = din("x", [T, D])
    ident = din("ident", [128, 128])
    w1g = din("w1g", [D, DFF]); w1u = din("w1u", [D, DFF]); w1d = din("w1d", [DFF, D])
    ln1g = din("ln1g", [D]); ln1b = din("ln1b", [D])
    out_ap = nc.dram_tensor("out", [T, D], F32, kind="ExternalOutput").ap()
    w_in = din("w_in", [D, 2944])
    gq_in = din("gq", [512]); gkv_in = din("gkv", [256]); gao_in = din("gao", [1024])
    w_uq = din("w_uq", [512, 2048])
    w_ukvk = din("w_ukvk", [256, 1024]); w_ukvv = din("w_ukvv", [256, 1024])
    cos2 = din("cos2", [64, T]); sin2 = din("sin2", [64, T])
    w_out = din("w_out", [D, D])
    ln2g = din("ln2g", [D]); ln2b = din("ln2b", [D]); ln3g = din("ln3g", [D]); ln3b = din("ln3b", [D])
    w2g = din("w2g", [D, DFF]); w2u = din("w2u", [D, DFF]); w2d = din("w2d", [DFF, D])
    conv_w = din("conv_w", [5, 1024]); conv_b = din("conv_b", [1024])
    w_q = din("w_q", [4, 256, 256]); w_k = din("w_k", [4, 256, 256]); w_v = din("w_v", [4, 256, 256])
    w_gates = din("w_gates", [D, 16]); b_gates = din("b_gates", [16])
    gn_g = din("gn_g", [1024]); skip_in = din("skip", [1024])
    om_in = din("om", [128, 2])
    mask_le = din("mask_le", [128, 4, 512]); mask_ge = din("mask_ge", [128, 4, 512])
    dumps = []

    def dram(name, shape, dt=F32):
        return nc.dram_tensor(name, list(shape), dt).ap()

    XT = nc.dram_tensor("XT_s", [16, 128, T], F32).ap()
    ZT = nc.dram_tensor("ZT_s", [16, 128, T], F32).ap()
    H1T = nc.dram_tensor("H1T_s", [16, 128, T], F32).ap()
    UQ = dram("UQ_s", [4, 128, T]); UKV = dram("UKV_s", [2, 128, T]); URR = dram("URR_s", [2, 64, T])
    ZM = dram("ZM_s", [8, 128, T]); NQT = dram("NQT_s", [4, 128, T])
    NGX = 11
    GXrows = [128] * 10 + [64]
    GX_t = [nc.dram_tensor(f"GX{i}_s", [GXrows[i], T], F32) for i in range(NGX)]
    GXA_t = [nc.dram_tensor(f"GXA{i}_s", [2 * GXrows[i], T], F32) for i in range(NGX)]

    def gx(i):
        return GX_t[i].ap()

    def gxa(b, i):
        return GXA_t[i].ap()[b * GXrows[i]:(b + 1) * GXrows[i], :]
    QALL = dram("QALL_s", [8, 256, T]); KNA = dram("KNA_s", [8, 128, 2 * T]); VTA = dram("VTA_s", [2 * T, 1024], BF16)
    OAT = dram("OAT_s", [8, 128, T]); YT = dram("YT_s", [16, 128, T])
    XO = dram("XO_s", [2, 8, 128, T])
    XCM = dram("XCM_s", [2, 16, 128, T]); GRAW = dram("GRAW_s", [2, 16, T]); RW = dram("RW_s", [2, 3, 4, 2 * T])
    QM = dram("QM_s", [4, 2, 128, 2 * T]); KM = dram("KM_s", [4, 2, 128, 2 * T]); VM = dram("VM_s", [4, 2 * T, 256], BF16)
    HM = dram("HM_s", [2, 8, 128, 2 * T]); HG = dram("HG_s", [8, 128, T]); XCO = dram("XCO_s", [8, 128, T])
    Z2T = dram("Z2T_s", [16, 128, T]); H2T = dram("H2T_s", [16, 128, T]); Z3T = dram("Z3T_s", [16, 128, T]); H3T = dram("H3T_s", [16, 128, T])

    arena_off = [16640]

    def sb(name, shape, dt, off=None):
        nbytes = int(np.prod(shape[1:])) * (2 if dt == BF16 else 4)
        if off is None:
            off = arena_off[0]
            off = (off + 63) // 64 * 64
            arena_off[0] = off + nbytes
        assert off + nbytes <= 228 * 1024, (name, off, nbytes)
        return nc.alloc_sbuf_tensor_at(name, list(shape), dt, offset=off)

    ident_sb = sb("ident_sb", [128, 128], F32)
    ones_bf = sb("ones_bf", [128, 128], BF16)
    lng = sb("lng", [128, 3, 16], F32)
    lnb = sb("lnb", [128, 3, 16], F32)
    gq_sb = sb("gq_sb", [128, 4], F32); gkv_sb = sb("gkv_sb", [128, 2], F32); gao_sb = sb("gao_sb", [128, 8], F32)
    gng_sb = sb("gng_sb", [128, 8], F32); skip_sb = sb("skip_sb", [128, 8], F32)
    cw_sb = sb("cw_sb", [128, 5, 8], F32); cb_sb = sb("cb_sb", [128, 8], F32)
    bg_sb = sb("bg_sb", [16, 1], F32); om_sb = sb("om_sb", [128, 2], F32)
    CONST_END = arena_off[0]

    psum = [nc.alloc_psum_tensor(f"ps{i}", [128, 512], F32) for i in range(8)]

    rr = {"i": 0}

    def dma_(out, in_, slow=False):
        if slow:
            return lambda e: e.dma_start(out=out, in_=in_, allow_slow_non_contiguous=True)
        return lambda e: e.dma_start(out=out, in_=in_)

    def evac_eng():
        rr["i"] += 1
        return "act" if rr["i"] % 2 else "dve"

    P.op("sp", lambda e: e.dma_start(out=ident_sb[:], in_=ident[:, :]), writes=["ident"], chan="const")
    P.op("dve", lambda e: e.memset(ones_bf[:], 1.0), writes=["ones"])
    P.op("sp", lambda e: e.dma_start(out=lng[:, 0, :], in_=ln1g.rearrange("(k p) -> p k", p=128),
                                     allow_slow_non_contiguous=True), writes=["lng0"], chan="const")
    P.op("sp", lambda e: e.dma_start(out=lnb[:, 0, :], in_=ln1b.rearrange("(k p) -> p k", p=128),
                                     allow_slow_non_contiguous=True), writes=["lnb0"], chan="const")

    for li_, (g_, b_) in enumerate([(ln2g, ln2b), (ln3g, ln3b)]):
        P.op("sp", dma_(lng[:, li_ + 1, :], g_.rearrange("(k p) -> p k", p=128), True), writes=[f"lng{li_ + 1}"], chan="const")
        P.op("sp", dma_(lnb[:, li_ + 1, :], b_.rearrange("(k p) -> p k", p=128), True), writes=[f"lnb{li_ + 1}"], chan="const")
    for dst_, src_ in [(gq_sb, gq_in), (gkv_sb, gkv_in), (gao_sb, gao_in), (gng_sb, gn_g), (skip_sb, skip_in), (cb_sb, conv_b)]:
        P.op("sp", dma_(dst_[:], src_.rearrange("(k p) -> p k", p=128), True), writes=["consts2"], chan="const")
    for j_ in range(5):
        P.op("sp", dma_(cw_sb[:, j_, :], conv_w[j_, :].rearrange("(k p) -> p k", p=128), True), writes=["consts2"], chan="const")
    P.op("sp", dma_(bg_sb[:], b_gates.rearrange("(p o) -> p o", o=1), True), writes=["consts2"], chan="const")
    P.op("sp", dma_(om_sb[:], om_in[:, :]), writes=["consts2"], chan="const")

    def phase_transpose_in():
        arena_off[0] = CONST_END
        xtok = [sb(f"p0_xtok{i}", [128, D], F32) for i in range(2)]
        stg = [sb(f"p0_stg{i}", [128, 16, 128], F32) for i in range(2)]
        for ci, (t0, nt) in enumerate(CHUNKS):
            b = ci % 2
            P.op("sp", lambda e, b=b, t0=t0, nt=nt: e.dma_start(out=xtok[b][0:nt, :], in_=x_in[t0:t0 + nt, :]),
                 writes=[("xtok", b)], chan=f"ld{b}")
            for g in range(4):
                pb = (ci * 4 + g) % 8
                for q in range(4):
                    f = g * 4 + q
                    P.op("pe", lambda e, b=b, nt=nt, f=f, pb=pb, q=q: e.transpose(
                        psum[pb][:, q * 128:q * 128 + nt], xtok[b][0:nt, f * 128:(f + 1) * 128], ident_sb[0:nt, 0:nt]),
                         reads=[("xtok", b), "ident"], writes=[("ps", pb)])
                en = evac_eng()
                if en == "act":
                    fn = lambda e, b=b, nt=nt, g=g, pb=pb: e.copy(
                        out=stg[b][:, g * 4:(g + 1) * 4, 0:nt],
                        in_=psum[pb][:, :].rearrange("p (q t) -> p q t", q=4)[:, :, 0:nt])
                else:
                    fn = lambda e, b=b, nt=nt, g=g, pb=pb: e.tensor_copy(
                        out=stg[b][:, g * 4:(g + 1) * 4, 0:nt],
                        in_=psum[pb][:, :].rearrange("p (q t) -> p q t", q=4)[:, :, 0:nt])
                P.op(en, fn, reads=[("ps", pb)], writes=[("stg", b, g)])
            P.op("sp", lambda e, b=b, t0=t0, nt=nt: e.dma_start(
                out=XT[:, :, t0:t0 + nt].rearrange("k p t -> p k t"), in_=stg[b][:, :, 0:nt]),
                 reads=[("stg", b, g) for g in range(4)], chan=f"st{b}")
        P.barrier()

    def phase_ffn(tag, XTin, ZTout, wg, wu, wd):
        arena_off[0] = CONST_END
        TH = 1032
        xT = sb(f"{tag}_xT", [128, 16, TH], BF16)
        hT = sb(f"{tag}_hT", [128, 44, TH], BF16)
        wgb = [sb(f"{tag}_wg{i}", [128, 16, 128], BF16) for i in range(3)]
        wub = [sb(f"{tag}_wu{i}", [128, 16, 128], BF16) for i in range(3)]
        wdb = [sb(f"{tag}_wd{i}", [128, 44, 256], BF16) for i in range(2)]
        sil = [sb(f"{tag}_sil{i}", [128, 512], F32) for i in range(2)]
        xres = [sb(f"{tag}_xres{i}", [128, 512], F32) for i in range(2)]
        zst = [sb(f"{tag}_zst{i}", [128, 512], F32) for i in range(2)]
        wg_v = wg.rearrange("(k p) m -> p k m", p=128)
        wu_v = wu.rearrange("(k p) m -> p k m", p=128)
        wd_v = wd.rearrange("(j p) m -> p j m", p=128)
        cnt = {"gu": 0, "dn": 0, "ev": 0}
        for hi, tiles in enumerate(FHALVES):
            h0 = tiles[0][0]
            hn = sum(n for _, n in tiles)
            for kq in range(4):
                P.op("pool", lambda e, kq=kq, h0=h0, hn=hn: e.dma_start(
                    out=xT[:, kq * 4:(kq + 1) * 4, 0:hn],
                    in_=XTin[kq * 4:(kq + 1) * 4, :, h0:h0 + hn].rearrange("k p t -> p k t")),
                     writes=[("xT", kq)], chan="ldx")
            for j in range(44):
                b = cnt["gu"] % 3
                cnt["gu"] += 1
                P.op("pool", lambda e, b=b, j=j: e.dma_start(out=wgb[b][:], in_=wg_v[:, :, j * 128:(j + 1) * 128]),
                     writes=[("wg", b)], chan=f"ldw{b}")
                P.op("pool", lambda e, b=b, j=j: e.dma_start(out=wub[b][:], in_=wu_v[:, :, j * 128:(j + 1) * 128]),
                     writes=[("wu", b)], chan=f"ldu{b}")
                for ti, (t0, nt) in enumerate(tiles):
                    lt = t0 - h0
                    slot = cnt["ev"] % 4
                    cnt["ev"] += 1
                    pg, pu = psum[2 * slot], psum[2 * slot + 1]
                    for k in range(16):
                        P.op("pe", lambda e, b=b, k=k, lt=lt, nt=nt, pg=pg: e.matmul(
                            pg[:, 0:nt], wgb[b][:, k, :], xT[:, k, lt:lt + nt], start=(k == 0), stop=(k == 15)),
                             reads=[("wg", b), ("xT", k // 4)], writes=[("ps", 2 * slot)])
                    for k in range(16):
                        P.op("pe", lambda e, b=b, k=k, lt=lt, nt=nt, pu=pu: e.matmul(
                            pu[:, 0:nt], wub[b][:, k, :], xT[:, k, lt:lt + nt], start=(k == 0), stop=(k == 15)),
                             reads=[("wu", b), ("xT", k // 4)], writes=[("ps", 2 * slot + 1)])
                    sb_i = slot % 2
                    P.op("act", lambda e, sb_i=sb_i, nt=nt, pg=pg: e.activation(
                        out=sil[sb_i][:, 0:nt], in_=pg[:, 0:nt], func=AF.Silu),
                         reads=[("ps", 2 * slot)], writes=[("sil", sb_i)])
                    P.op("dve", lambda e, sb_i=sb_i, nt=nt, pu=pu, j=j, lt=lt: e.tensor_tensor(
                        out=hT[:, j, lt:lt + nt], in0=sil[sb_i][:, 0:nt], in1=pu[:, 0:nt], op=ALU.mult),
                         reads=[("sil", sb_i), ("ps", 2 * slot + 1)], writes=[("hT", j)])
            for ip in range(8):
                b = cnt["dn"] % 2
                cnt["dn"] += 1
                P.op("pool", lambda e, b=b, ip=ip: e.dma_start(out=wdb[b][:, 0:22, :], in_=wd_v[:, 0:22, ip * 256:(ip + 1) * 256]),
                     writes=[("wd", b, 0)], chan=f"ldw{b}")
                P.op("pool", lambda e, b=b, ip=ip: e.dma_start(out=wdb[b][:, 22:44, :], in_=wd_v[:, 22:44, ip * 256:(ip + 1) * 256]),
                     writes=[("wd", b, 1)], chan=f"ldw{b}")
                for ii in range(2):
                    i = ip * 2 + ii
                    for ti, (t0, nt) in enumerate(tiles):
                        lt = t0 - h0
                        slot = cnt["ev"] % 8
                        cnt["ev"] += 1
                        rb = slot % 2
                        pz = psum[slot]
                        P.op("sp", lambda e, rb=rb, i=i, t0=t0, nt=nt: e.dma_start(
                            out=xres[rb][:, 0:nt], in_=XTin[i, :, t0:t0 + nt]),
                             writes=[("xres", rb)], chan=f"ldr{rb}")
                        for j in range(44):
                            P.op("pe", lambda e, b=b, j=j, ii=ii, lt=lt, nt=nt, pz=pz: e.matmul(
                                pz[:, 0:nt], wdb[b][:, j, ii * 128:(ii + 1) * 128], hT[:, j, lt:lt + nt],
                                start=(j == 0), stop=(j == 43)),
                                 reads=[("wd", b, j // 22), ("hT", j)], writes=[("ps", slot)])
                        P.op("dve", lambda e, rb=rb, nt=nt, pz=pz: e.scalar_tensor_tensor(
                            out=zst[rb][:, 0:nt], in0=pz[:, 0:nt], scalar=0.5 / ALPHA, in1=xres[rb][:, 0:nt],
                            op0=ALU.mult, op1=ALU.add),
                             reads=[("ps", slot), ("xres", rb)], writes=[("zst", rb)])
                        P.op("sp", lambda e, rb=rb, i=i, t0=t0, nt=nt: e.dma_start(
                            out=ZTout[i, :, t0:t0 + nt], in_=zst[rb][:, 0:nt]),
                             reads=[("zst", rb)], chan=f"st{rb}")
        P.barrier()

    def phase_ln(tag, ZTin, HTout, li, final_out=None):
        arena_off[0] = CONST_END
        zt = [sb(f"{tag}_zt{i}", [128, 16, 264], F32) for i in range(2)]
        zb = [sb(f"{tag}_zb{i}", [128, 16, 264], BF16) for i in range(2)]
        zq = [sb(f"{tag}_zq{i}", [128, 16, 264], BF16) for i in range(2)]
        mean = [sb(f"{tag}_mean{i}", [128, 264], F32) for i in range(2)]
        msq = [sb(f"{tag}_msq{i}", [128, 264], F32) for i in range(2)]
        rstd = [sb(f"{tag}_rstd{i}", [128, 264], F32) for i in range(2)]
        ho = [sb(f"{tag}_ho{i}", [128, 16, 264], F32) for i in range(2)]
        eps = LN_EPS / (ALPHA * ALPHA)
        for ti, (t0, nt) in enumerate(TILES):
            b = ti % 2
            ps_s, ps_q = psum[2 * b], psum[2 * b + 1]
            P.op("sp", lambda e, b=b, t0=t0, nt=nt: e.dma_start(
                out=zt[b][:, :, 0:nt], in_=ZTin[:, :, t0:t0 + nt].rearrange("k p t -> p k t")),
                 writes=[("zt", b)], chan=f"ld{b}")
            P.op("act", lambda e, b=b, nt=nt: e.copy(out=zb[b][:, :, 0:nt], in_=zt[b][:, :, 0:nt]),
                 reads=[("zt", b)], writes=[("zb", b)])
            P.op("act", lambda e, b=b, nt=nt: e.activation(out=zq[b][:, :, 0:nt], in_=zt[b][:, :, 0:nt], func=AF.Square),
                 reads=[("zt", b)], writes=[("zq", b)])
            for k in range(16):
                P.op("pe", lambda e, b=b, k=k, nt=nt, ps_s=ps_s: e.matmul(
                    ps_s[:, 0:nt], ones_bf[:, :], zb[b][:, k, 0:nt], start=(k == 0), stop=(k == 15)),
                     reads=[("zb", b), "ones"], writes=[("ps", 2 * b)])
            for k in range(16):
                P.op("pe", lambda e, b=b, k=k, nt=nt, ps_q=ps_q: e.matmul(
                    ps_q[:, 0:nt], ones_bf[:, :], zq[b][:, k, 0:nt], start=(k == 0), stop=(k == 15)),
                     reads=[("zq", b), "ones"], writes=[("ps", 2 * b + 1)])
            P.op("dve", lambda e, b=b, nt=nt, ps_s=ps_s: e.tensor_scalar(
                out=mean[b][:, 0:nt], in0=ps_s[:, 0:nt], scalar1=1.0 / D, scalar2=None, op0=ALU.mult),
                 reads=[("ps", 2 * b)], writes=[("mean", b)])
            P.op("dve", lambda e, b=b, nt=nt: e.tensor_tensor(
                out=msq[b][:, 0:nt], in0=mean[b][:, 0:nt], in1=mean[b][:, 0:nt], op=ALU.mult),
                 reads=[("mean", b)], writes=[("msq", b)])
            P.op("dve", lambda e, b=b, nt=nt, ps_q=ps_q: e.scalar_tensor_tensor(
                out=rstd[b][:, 0:nt], in0=ps_q[:, 0:nt], scalar=1.0 / D, in1=msq[b][:, 0:nt],
                op0=ALU.mult, op1=ALU.subtract),
                 reads=[("ps", 2 * b + 1), ("msq", b)], writes=[("rstd", b)])
            P.op("act", lambda e, b=b, nt=nt: e.activation(
                out=msq[b][:, 0:nt], in_=rstd[b][:, 0:nt], func=AF.Sqrt, bias=eps, scale=1.0),
                 reads=[("rstd", b)], writes=[("msq", b)])
            P.op("dve", lambda e, b=b, nt=nt: e.reciprocal(out=rstd[b][:, 0:nt], in_=msq[b][:, 0:nt]),
                 reads=[("msq", b)], writes=[("rstd", b)])
            P.op("dve", lambda e, b=b, nt=nt: e.tensor_tensor(
                out=zt[b][:, :, 0:nt], in0=zt[b][:, :, 0:nt],
                in1=mean[b][:, 0:nt].unsqueeze(1).to_broadcast([128, 16, nt]),
                op=ALU.subtract),
                 reads=[("zt", b), ("mean", b)], writes=[("zt", b)])
            P.op("dve", lambda e, b=b, nt=nt: e.tensor_tensor(
                out=zt[b][:, :, 0:nt], in0=zt[b][:, :, 0:nt],
                in1=rstd[b][:, 0:nt].unsqueeze(1).to_broadcast([128, 16, nt]),
                op=ALU.mult),
                 reads=[("zt", b), ("rstd", b)], writes=[("zt", b)])
            for k in range(16):
                P.op("act", lambda e, b=b, k=k, nt=nt: e.activation(
                    out=ho[b][:, k, 0:nt], in_=zt[b][:, k, 0:nt], func=AF.Identity,
                    scale=lng[:, li, k:k + 1], bias=lnb[:, li, k:k + 1]),
                     reads=[("zt", b), f"lng{li}", f"lnb{li}"], writes=[("ho", b)])
            P.op("sp", lambda e, b=b, t0=t0, nt=nt: e.dma_start(
                out=HTout[:, :, t0:t0 + nt].rearrange("k p t -> p k t"), in_=ho[b][:, :, 0:nt]),
                 reads=[("ho", b)], chan=f"st{b}")
        P.barrier()

    def phase_transpose_out(HTin):
        arena_off[0] = CONST_END
        hin = [sb(f"po_in{i}", [128, 16, 128], F32) for i in range(2)]
        otok = [sb(f"po_tok{i}", [128, D], F32) for i in range(2)]
        for ci, (t0, nt) in enumerate(CHUNKS):
            b = ci % 2
            P.op("sp", lambda e, b=b, t0=t0, nt=nt: e.dma_start(
                out=hin[b][:, :, 0:nt], in_=HTin[:, :, t0:t0 + nt].rearrange("k p t -> p k t")),
                 writes=[("hin", b)], chan=f"ld{b}")
            for g in range(4):
                pb = (ci * 4 + g) % 8
                for q in range(4):
                    f = g * 4 + q
                    P.op("pe", lambda e, b=b, nt=nt, f=f, pb=pb, q=q: e.transpose(
                        psum[pb][0:nt, q * 128:(q + 1) * 128], hin[b][:, f, 0:nt], ident_sb[:, :]),
                         reads=[("hin", b), "ident"], writes=[("ps", pb)])
                en = evac_eng()
                if en == "act":
                    fn = lambda e, b=b, nt=nt, g=g, pb=pb: e.copy(out=otok[b][0:nt, g * 512:(g + 1) * 512], in_=psum[pb][0:nt, :])
                else:
                    fn = lambda e, b=b, nt=nt, g=g, pb=pb: e.tensor_copy(out=otok[b][0:nt, g * 512:(g + 1) * 512], in_=psum[pb][0:nt, :])
                P.op(en, fn, reads=[("ps", pb)], writes=[("otok", b, g)])
            P.op("sp", lambda e, b=b, t0=t0, nt=nt: e.dma_start(out=out_ap[t0:t0 + nt, :], in_=otok[b][0:nt, :]),
                 reads=[("otok", b, g) for g in range(4)], chan=f"st{b}")
        P.barrier()


    def dma(out, in_, slow=False):
        if slow:
            return lambda e: e.dma_start(out=out, in_=in_, allow_slow_non_contiguous=True)
        return lambda e: e.dma_start(out=out, in_=in_)

    def mm(out, lhsT, rhs, st, sp_):
        return lambda e: e.matmul(out, lhsT, rhs, start=st, stop=sp_)

    def fm(ap2d):
        return ap2d.rearrange("(k p) t -> k p t", p=128)

    def phase_linear(INap, nk, Wap, mchunks, OUTfn, toks=TILES, scale=1.0, bias_col=None, resid=None):
        arena_off[0] = CONST_END
        Tin = max(t0 + nt for t0, nt in toks)
        inT = sb("lin_in", [128, nk, Tin], BF16)
        wb = [sb(f"lin_w{i}", [128, nk, 128], BF16) for i in range(2)]
        st = [sb(f"lin_st{i}", [128, 512], F32) for i in range(4)]
        rs = [sb(f"lin_rs{i}", [128, 512], F32) for i in range(4)]
        if isinstance(INap, list):
            for k in range(nk):
                P.op("pool", dma(inT[:, k, 0:Tin], INap[k][:, 0:Tin]), writes=[("lin_in", k // 4)], chan="ldx")
        else:
            for kq in range(0, nk, 4):
                kn = min(4, nk - kq)
                P.op("pool", dma(inT[:, kq:kq + kn, 0:Tin], INap[kq:kq + kn, :, 0:Tin].rearrange("k p t -> p k t")),
                     writes=[("lin_in", kq // 4)], chan="ldx")
        Wv = Wap.rearrange("(k p) m -> p k m", p=128)
        cnt = 0
        for mi, (c0, mc) in enumerate(mchunks):
            b = mi % 2
            P.op("pool", dma(wb[b][:, :, 0:mc], Wv[:, :, c0:c0 + mc]), writes=[("lin_w", b)], chan=f"ldw{b}")
            for (t0, nt) in toks:
                slot = cnt % 8
                sti = cnt % 4
                cnt += 1
                if resid is not None:
                    P.op("sp", dma(rs[sti][0:mc, 0:nt], resid(mi, t0, nt)), writes=[("lin_rs", sti)], chan=f"ldr{sti}")
                for k in range(nk):
                    P.op("pe", mm(psum[slot][0:mc, 0:nt], wb[b][:, k, 0:mc], inT[:, k, t0:t0 + nt], k == 0, k == nk - 1),
                         reads=[("lin_w", b), ("lin_in", k // 4)], writes=[("ps", slot)])
                if resid is not None:
                    P.op("dve", lambda e, slot=slot, sti=sti, mc=mc, nt=nt: e.scalar_tensor_tensor(
                        out=st[sti][0:mc, 0:nt], in0=psum[slot][0:mc, 0:nt], scalar=scale, in1=rs[sti][0:mc, 0:nt],
                        op0=ALU.mult, op1=ALU.add), reads=[("ps", slot), ("lin_rs", sti)], writes=[("lin_st", sti)])
                elif bias_col is not None:
                    P.op("act", lambda e, slot=slot, sti=sti, mc=mc, nt=nt: e.activation(
                        out=st[sti][0:mc, 0:nt], in_=psum[slot][0:mc, 0:nt], func=AF.Identity, bias=bias_col[0:mc, 0:1], scale=scale),
                         reads=[("ps", slot), "consts2"], writes=[("lin_st", sti)])
                else:
                    en = evac_eng()
                    if en == "act":
                        P.op("act", lambda e, slot=slot, sti=sti, mc=mc, nt=nt: e.mul(
                            out=st[sti][0:mc, 0:nt], in_=psum[slot][0:mc, 0:nt], mul=scale),
                             reads=[("ps", slot)], writes=[("lin_st", sti)])
                    else:
                        P.op("dve", lambda e, slot=slot, sti=sti, mc=mc, nt=nt: e.tensor_scalar(
                            out=st[sti][0:mc, 0:nt], in0=psum[slot][0:mc, 0:nt], scalar1=scale, scalar2=None, op0=ALU.mult),
                             reads=[("ps", slot)], writes=[("lin_st", sti)])
                P.op("sp", dma(OUTfn(mi, t0, nt), st[sti][0:mc, 0:nt]), reads=[("lin_st", sti)], chan=f"st{sti}")
        P.barrier()

    def phase_tokproj(INap, nk, Wap, M, OUTap, chunks=CHUNKS):
        arena_off[0] = CONST_END
        Tin = max(t0 + nt for t0, nt in chunks)
        inT = sb("tp_in", [128, nk, Tin], BF16)
        wsb = sb("tp_w", [128, nk, M], BF16)
        vst = [sb(f"tp_st{i}", [128, M], BF16) for i in range(2)]
        if isinstance(INap, list):
            for k in range(nk):
                P.op("pool", dma(inT[:, k, 0:Tin], INap[k][:, 0:Tin]), writes=["tp_in"], chan="ldx")
        else:
            P.op("pool", dma(inT[:, :, 0:Tin], INap[:, :, 0:Tin].rearrange("k p t -> p k t")), writes=["tp_in"], chan="ldx")
        P.op("pool", dma(wsb[:], Wap.rearrange("(k p) m -> p k m", p=128)), writes=["tp_w"], chan="ldw0")
        cnt = 0
        for ci, (t0, nt) in enumerate(chunks):
            b = ci % 2
            for n0 in range(0, M, 512):
                w = min(512, M - n0)
                slot = cnt % 8
                cnt += 1
                for k in range(nk):
                    P.op("pe", mm(psum[slot][0:nt, 0:w], inT[:, k, t0:t0 + nt], wsb[:, k, n0:n0 + w], k == 0, k == nk - 1),
                         reads=["tp_in", "tp_w"], writes=[("ps", slot)])
                en = evac_eng()
                if en == "act":
                    P.op("act", lambda e, slot=slot, b=b, nt=nt, n0=n0, w=w: e.copy(out=vst[b][0:nt, n0:n0 + w], in_=psum[slot][0:nt, 0:w]),
                         reads=[("ps", slot)], writes=[("tp_st", b, n0)])
                else:
                    P.op("dve", lambda e, slot=slot, b=b, nt=nt, n0=n0, w=w: e.tensor_copy(out=vst[b][0:nt, n0:n0 + w], in_=psum[slot][0:nt, 0:w]),
                         reads=[("ps", slot)], writes=[("tp_st", b, n0)])
            P.op("sp", dma(OUTap[t0:t0 + nt, :], vst[b][0:nt, :]), reads=[("tp_st", b, n0) for n0 in range(0, M, 512)], chan=f"st{b}")
        P.barrier()

    def phase_norm(INfn, nch, gc, gcol, eps, OUTfn, center=False, addin=None):
        arena_off[0] = CONST_END
        xt = [sb(f"nm_x{i}", [128, nch, 264], F32) for i in range(2)]
        xb = [sb(f"nm_b{i}", [128, nch, 264], BF16) for i in range(2)]
        xq = [sb(f"nm_q{i}", [128, nch, 264], BF16) for i in range(2)]
        ng = nch // gc
        mean = [sb(f"nm_m{i}", [128, ng, 264], F32) for i in range(2)]
        msq = [sb(f"nm_s{i}", [128, ng, 264], F32) for i in range(2)]
        rstd = [sb(f"nm_r{i}", [128, ng, 264], F32) for i in range(2)]
        ho = [sb(f"nm_o{i}", [128, nch, 264], F32) for i in range(2)]
        addt = [sb(f"nm_a{i}", [128, nch, 264], F32) for i in range(2)] if addin is not None else None
        for ti, (t0, nt) in enumerate(TILES):
            b = ti % 2
            P.op("sp", dma(xt[b][:, :, 0:nt], INfn(t0, nt).rearrange("k p t -> p k t")), writes=[("nm_x", b)], chan=f"ld{b}")
            if addin is not None:
                P.op("sp", dma(addt[b][:, :, 0:nt], addin[0](t0, nt).rearrange("k p t -> p k t")), writes=[("nm_a", b)], chan=f"lda{b}")
            P.op("act", lambda e, b=b, nt=nt: e.activation(out=xq[b][:, :, 0:nt], in_=xt[b][:, :, 0:nt], func=AF.Square),
                 reads=[("nm_x", b)], writes=[("nm_q", b)])
            if center:
                P.op("act", lambda e, b=b, nt=nt: e.copy(out=xb[b][:, :, 0:nt], in_=xt[b][:, :, 0:nt]),
                     reads=[("nm_x", b)], writes=[("nm_b", b)])
            for g in range(ng):
                pq = psum[(2 * g) % 8]
                pm_ = psum[(2 * g + 1) % 8]
                for k in range(gc):
                    P.op("pe", mm(pq[:, 0:nt], ones_bf[:, :], xq[b][:, g * gc + k, 0:nt], k == 0, k == gc - 1),
                         reads=[("nm_q", b), "ones"], writes=[("ps", (2 * g) % 8)])
                if center:
                    for k in range(gc):
                        P.op("pe", mm(pm_[:, 0:nt], ones_bf[:, :], xb[b][:, g * gc + k, 0:nt], k == 0, k == gc - 1),
                             reads=[("nm_b", b), "ones"], writes=[("ps", (2 * g + 1) % 8)])
                    P.op("dve", lambda e, b=b, g=g, nt=nt, pm_=pm_: e.tensor_scalar(
                        out=mean[b][:, g, 0:nt], in0=pm_[:, 0:nt], scalar1=1.0 / (gc * 128), scalar2=None, op0=ALU.mult),
                         reads=[("ps", (2 * g + 1) % 8)], writes=[("nm_m", b, g)])
                    P.op("dve", lambda e, b=b, g=g, nt=nt: e.tensor_tensor(
                        out=msq[b][:, g, 0:nt], in0=mean[b][:, g, 0:nt], in1=mean[b][:, g, 0:nt], op=ALU.mult),
                         reads=[("nm_m", b, g)], writes=[("nm_s", b, g)])
                    P.op("dve", lambda e, b=b, g=g, nt=nt, pq=pq: e.scalar_tensor_tensor(
                        out=rstd[b][:, g, 0:nt], in0=pq[:, 0:nt], scalar=1.0 / (gc * 128), in1=msq[b][:, g, 0:nt],
                        op0=ALU.mult, op1=ALU.subtract),
                         reads=[("ps", (2 * g) % 8), ("nm_s", b, g)], writes=[("nm_r", b, g)])
                    P.op("act", lambda e, b=b, g=g, nt=nt: e.activation(
                        out=msq[b][:, g, 0:nt], in_=rstd[b][:, g, 0:nt], func=AF.Sqrt, bias=eps, scale=1.0),
                         reads=[("nm_r", b, g)], writes=[("nm_s", b, g)])
                else:
                    P.op("act", lambda e, b=b, g=g, nt=nt, pq=pq: e.activation(
                        out=msq[b][:, g, 0:nt], in_=pq[:, 0:nt], func=AF.Sqrt, bias=eps, scale=1.0 / (gc * 128)),
                         reads=[("ps", (2 * g) % 8)], writes=[("nm_s", b, g)])
                P.op("dve", lambda e, b=b, g=g, nt=nt: e.reciprocal(out=rstd[b][:, g, 0:nt], in_=msq[b][:, g, 0:nt]),
                     reads=[("nm_s", b, g)], writes=[("nm_r", b, g)])
                sl = slice(g * gc, (g + 1) * gc)
                if center:
                    P.op("dve", lambda e, b=b, g=g, nt=nt, sl=sl: e.tensor_tensor(
                        out=xt[b][:, sl, 0:nt], in0=xt[b][:, sl, 0:nt],
                        in1=mean[b][:, g, 0:nt].unsqueeze(1).to_broadcast([128, gc, nt]), op=ALU.subtract),
                         reads=[("nm_x", b), ("nm_m", b, g)], writes=[("nm_x", b)])
                P.op("dve", lambda e, b=b, g=g, nt=nt, sl=sl: e.tensor_tensor(
                    out=xt[b][:, sl, 0:nt], in0=xt[b][:, sl, 0:nt],
                    in1=rstd[b][:, g, 0:nt].unsqueeze(1).to_broadcast([128, gc, nt]), op=ALU.mult),
                     reads=[("nm_x", b), ("nm_r", b, g)], writes=[("nm_x", b)])
            for k in range(nch):
                P.op("act", lambda e, b=b, k=k, nt=nt: e.activation(
                    out=ho[b][:, k, 0:nt], in_=xt[b][:, k, 0:nt], func=AF.Copy, scale=gcol[:, k:k + 1]),
                     reads=[("nm_x", b), "consts2"], writes=[("nm_o", b)])
                if addin is not None:
                    P.op("dve", lambda e, b=b, k=k, nt=nt: e.scalar_tensor_tensor(
                        out=ho[b][:, k, 0:nt], in0=addt[b][:, k, 0:nt], scalar=addin[1][:, k:k + 1], in1=ho[b][:, k, 0:nt],
                        op0=ALU.mult, op1=ALU.add),
                         reads=[("nm_a", b), ("nm_o", b), "consts2"], writes=[("nm_o", b)])
            o_ = OUTfn(t0, nt)
            if isinstance(o_, list):
                for k in range(nch):
                    P.op("sp", dma(o_[k], ho[b][:, k, 0:nt]), reads=[("nm_o", b)], chan=f"st{b}")
            else:
                P.op("sp", dma(o_.rearrange("k p t -> p k t"), ho[b][:, :, 0:nt]), reads=[("nm_o", b)], chan=f"st{b}")
        P.barrier()


    def dump(name, ap):
        shp = list(ap.shape)
        o = nc.dram_tensor("dbg_" + name, shp, ap.dtype, kind="ExternalOutput").ap()
        dumps.append((o, ap))

    RMS_EPS = 1e-6
    WIN_CH = [(128 * i, 128) for i in range(22)] + [(2816, 64), (2880, 64)]

    def win_out(mi, t0, nt):
        if mi < 4:
            return UQ[mi, :, t0:t0 + nt]
        if mi < 6:
            return UKV[mi - 4, :, t0:t0 + nt]
        if mi < 14:
            return gx(2 + mi - 6)[:, t0:t0 + nt]
        if mi < 22:
            return ZM[mi - 14, :, t0:t0 + nt]
        return URR[mi - 22, :, t0:t0 + nt]

    def phase_krope():
        arena_off[0] = CONST_END
        a = sb("kr_a", [64, T], F32); b_ = sb("kr_b", [64, T], F32); c = sb("kr_c", [64, T], F32); d_ = sb("kr_d", [64, T], F32)
        P.op("sp", dma(a[:], URR[0, :, :]), writes=["kr_a"], chan="ld0")
        P.op("sp", dma(b_[:], URR[1, :, :]), writes=["kr_b"], chan="ld0")
        P.op("sp", dma(c[:], cos2[:, :]), writes=["kr_c"], chan="ld1")
        P.op("sp", dma(d_[:], sin2[:, :]), writes=["kr_d"], chan="ld1")
        P.op("dve", lambda e: e.tensor_tensor(out=a[:], in0=a[:], in1=c[:], op=ALU.mult), reads=["kr_a", "kr_c"], writes=["kr_a"])
        P.op("dve", lambda e: e.tensor_tensor(out=b_[:], in0=b_[:], in1=d_[:], op=ALU.mult), reads=["kr_b", "kr_d"], writes=["kr_b"])
        P.op("dve", lambda e: e.tensor_tensor(out=a[:], in0=a[:], in1=b_[:], op=ALU.add), reads=["kr_a", "kr_b"], writes=["kr_a"])
        P.op("sp", dma(gx(10)[:, :], a[:]), reads=["kr_a"], chan="st0")
        P.barrier()

    def phase_gather():
        groups = [[2 * i, 2 * i + 1] for i in range(NCORES // 2)]
        for i in range(NGX):
            P.op("pool", lambda e, i=i: e.collective_compute("AllGather", ALU.bypass, replica_groups=groups,
                                                             ins=[GX_t[i].ap().opt()], outs=[GXA_t[i].ap().opt()]),
                 chan="cc", writes=[("gather", i)], inc=1)
        P.barrier()

    ATT_SCALE = 192.0 ** -0.5
    KCH = [(b, c, b * T + t0, nt) for b in range(2) for c, (t0, nt) in enumerate(CHUNKS)]

    def phase_mla_attn():
        for hg in range(2):
            arena_off[0] = CONST_END
            QN = sb("at_qn", [128, 4, T], BF16)
            QR = sb("at_qr", [64, 4, T], BF16)
            KN = sb("at_kn", [128, 4, 2 * T], BF16)
            KR = sb("at_kr", [64, 2 * T], BF16)
            V = sb("at_v", [128, 34, 512], BF16)
            cs = sb("at_cos", [64, T], F32); sn = sb("at_sin", [64, T], F32)
            r1 = sb("at_r1", [64, T], F32); r2 = sb("at_r2", [64, T], F32)
            PT = [sb(f"at_p{i}", [128, 512], BF16) for i in range(4)]
            rsb = [sb(f"at_rs{i}", [128, 512], F32) for i in range(2)]
            ost = [sb(f"at_os{i}", [128, 512], F32) for i in range(2)]
            P.op("pool", dma(QN[:], QALL[hg * 4:(hg + 1) * 4, 0:128, :].rearrange("h p t -> p h t")), writes=["at_qn"], chan="ldx")
            P.op("pool", dma(KN[:], KNA[hg * 4:(hg + 1) * 4, :, :].rearrange("h p t -> p h t")), writes=["at_kn"], chan="ldx")
            for b in range(2):
                P.op("pool", dma(KR[:, b * T:(b + 1) * T], gxa(b, 10)), writes=["at_kr"], chan="ldx")
                P.op("sp", dma(V[:, b * 17:b * 17 + 16, :],
                               VTA[b * T:b * T + 2048, hg * 512:(hg + 1) * 512].rearrange("(c p) m -> p c m", p=128)),
                     writes=["at_v"], chan="ld0")
                P.op("sp", dma(V[0:8, b * 17 + 16, :], VTA[b * T + 2048:b * T + 2056, hg * 512:(hg + 1) * 512]),
                     writes=["at_v"], chan="ld0")
            P.op("sp", dma(cs[:], cos2[:, :]), writes=["at_cs"], chan="ld1")
            P.op("sp", dma(sn[:], sin2[:, :]), writes=["at_sn"], chan="ld1")
            for hl in range(4):
                h = hg * 4 + hl
                P.op("sp", dma(r1[:], QALL[h, 128:192, :]), writes=["at_r1"], chan="ld2")
                P.op("sp", dma(r2[:], QALL[h, 192:256, :]), writes=["at_r2"], chan="ld2")
                P.op("dve", lambda e: e.tensor_tensor(out=r1[:], in0=r1[:], in1=cs[:], op=ALU.mult), reads=["at_r1", "at_cs"], writes=["at_r1"])
                P.op("dve", lambda e: e.tensor_tensor(out=r2[:], in0=r2[:], in1=sn[:], op=ALU.mult), reads=["at_r2", "at_sn"], writes=["at_r2"])
                P.op("dve", lambda e, hl=hl: e.tensor_tensor(out=QR[:, hl, :], in0=r1[:], in1=r2[:], op=ALU.add),
                     reads=["at_r1", "at_r2"], writes=["at_qr"])
            blocks = []
            for hl in range(4):
                for qi, (t0, nt) in enumerate(ATILES):
                    for ci, (b, c, k0, nk) in enumerate(KCH):
                        blocks.append((hl, hl * 5 + qi, t0, nt, ci, k0, nk))
            DEP = 3

            def front(i, blk):
                hl, qi, t0, nt, ci, k0, nk = blk
                sbk = i % 4
                P.op("pe", mm(psum[sbk][0:nk, 0:nt], KN[:, hl, k0:k0 + nk], QN[:, hl, t0:t0 + nt], True, False),
                     reads=["at_kn", "at_qn"], writes=[("ps", sbk)])
                P.op("pe", mm(psum[sbk][0:nk, 0:nt], KR[:, k0:k0 + nk], QR[:, hl, t0:t0 + nt], False, True),
                     reads=["at_kr", "at_qr"], writes=[("ps", sbk)])
                P.op("act", lambda e, sbk=sbk, nk=nk, nt=nt: e.activation(
                    out=PT[sbk][0:nk, 0:nt], in_=psum[sbk][0:nk, 0:nt], func=AF.Exp, scale=ATT_SCALE),
                     reads=[("ps", sbk)], writes=[("at_p", sbk)])

            def back(i, blk, hg=hg):
                hl, qi, t0, nt, ci, k0, nk = blk
                h = hg * 4 + hl
                pi = i % 4
                ob = 4 + 2 * (qi % 2)
                po, pssum = psum[ob], psum[ob + 1]
                P.op("pe", mm(po[:, 0:nt], V[0:nk, ci, hl * 128:(hl + 1) * 128], PT[pi][0:nk, 0:nt], ci == 0, ci == 33),
                     reads=["at_v", ("at_p", pi)], writes=[("ps", ob)])
                P.op("pe", mm(pssum[:, 0:nt], ones_bf[0:nk, :], PT[pi][0:nk, 0:nt], ci == 0, ci == 33),
                     reads=["ones", ("at_p", pi)], writes=[("ps", ob + 1)])
                if ci == 33:
                    rb = qi % 2
                    P.op("dve", lambda e, rb=rb, nt=nt, pssum=pssum: e.reciprocal(out=rsb[rb][:, 0:nt], in_=pssum[:, 0:nt]),
                         reads=[("ps", ob + 1)], writes=[("at_rs", rb)])
                    P.op("dve", lambda e, rb=rb, nt=nt, po=po: e.tensor_tensor(
                        out=ost[rb][:, 0:nt], in0=po[:, 0:nt], in1=rsb[rb][:, 0:nt], op=ALU.mult),
                         reads=[("ps", ob), ("at_rs", rb)], writes=[("at_os", rb)])
                    P.op("sp", dma(OAT[h, :, t0:t0 + nt], ost[rb][:, 0:nt]), reads=[("at_os", rb)], chan=f"st{rb}")

            for i in range(len(blocks) + DEP):
                if i < len(blocks):
                    front(i, blocks[i])
                if i >= DEP:
                    back(i - DEP, blocks[i - DEP])
            P.barrier()


    def phase_select():
        arena_off[0] = CONST_END
        ga = [sb(f"sl_a{i}", [128, T], F32) for i in range(2)]
        gb = [sb(f"sl_b{i}", [128, T], F32) for i in range(2)]
        oo = [sb(f"sl_o{i}", [128, T], F32) for i in range(2)]
        ot = [sb(f"sl_t{i}", [128, T], F32) for i in range(2)]
        for k in range(8):
            i = k % 2
            P.op("sp", dma(ga[i][:, :], gxa(0, 2 + k)), writes=[("sl_a", i)], chan="ld")
            P.op("sp", dma(gb[i][:, :], gxa(1, 2 + k)), writes=[("sl_b", i)], chan="ld")
            P.op("dve", lambda e, i=i: e.tensor_scalar(out=oo[i][:, :], in0=ga[i][:, :], scalar1=om_sb[:, 0:1], scalar2=None, op0=ALU.mult),
                 reads=[("sl_a", i), "consts2"], writes=[("sl_o", i)])
            P.op("dve", lambda e, i=i: e.scalar_tensor_tensor(out=oo[i][:, :], in0=gb[i][:, :], scalar=om_sb[:, 1:2], in1=oo[i][:, :],
                                                             op0=ALU.mult, op1=ALU.add),
                 reads=[("sl_b", i), ("sl_o", i), "consts2"], writes=[("sl_o", i)])
            P.op("dve", lambda e, i=i: e.tensor_scalar(out=ot[i][:, :], in0=ga[i][:, :], scalar1=om_sb[:, 1:2], scalar2=None, op0=ALU.mult),
                 reads=[("sl_a", i), "consts2"], writes=[("sl_t", i)])
            P.op("dve", lambda e, i=i: e.scalar_tensor_tensor(out=ot[i][:, :], in0=gb[i][:, :], scalar=om_sb[:, 0:1], in1=ot[i][:, :],
                                                             op0=ALU.mult, op1=ALU.add),
                 reads=[("sl_b", i), ("sl_t", i), "consts2"], writes=[("sl_t", i)])
            P.op("sp", dma(XO[0, k, :, :], oo[i][:, :]), reads=[("sl_o", i)], chan="st")
            P.op("sp", dma(XO[1, k, :, :], ot[i][:, :]), reads=[("sl_t", i)], chan="st2")
        P.barrier()

    def phase_conv():
        arena_off[0] = CONST_END
        XMt = [sb(f"cv_x{i}", [128, T + 4], F32) for i in range(2)]
        acc = [sb(f"cv_a{i}", [128, T], F32) for i in range(2)]
        xco = [sb(f"cv_o{i}", [128, T], F32) for i in range(2)]
        n = 0
        for b in range(2):
            for k in range(8):
                i = n % 2
                n += 1
                src = XO[b, k]
                oth = XO[1 - b, k]
                P.op("dve", lambda e, i=i: e.memset(XMt[i][:, 0:2], 0.0), writes=[("cv_x", i)])
                P.op("sp", dma(XMt[i][:, 2:T + 2], src[:, :]), writes=[("cv_x", i)], chan="ld")
                P.op("sp", dma(XMt[i][:, T + 2:T + 3], oth[:, T - 1:T], True), writes=[("cv_x", i)], chan="ld")
                P.op("sp", dma(XMt[i][:, T + 3:T + 4], oth[:, T - 2:T - 1], True), writes=[("cv_x", i)], chan="ld")
                for j in range(5):
                    off = j if b == 0 else 4 - j
                    if j == 0:
                        P.op("dve", lambda e, i=i, k=k, off=off: e.tensor_scalar(
                            out=acc[i][:, :], in0=XMt[i][:, off:off + T], scalar1=cw_sb[:, 0, k:k + 1], scalar2=None, op0=ALU.mult),
                             reads=[("cv_x", i), "consts2"], writes=[("cv_a", i)])
                    else:
                        P.op("dve", lambda e, i=i, k=k, j=j, off=off: e.scalar_tensor_tensor(
                            out=acc[i][:, :], in0=XMt[i][:, off:off + T], scalar=cw_sb[:, j, k:k + 1], in1=acc[i][:, :],
                            op0=ALU.mult, op1=ALU.add),
                             reads=[("cv_x", i), ("cv_a", i), "consts2"], writes=[("cv_a", i)])
                P.op("act", lambda e, i=i, k=k: e.activation(out=xco[i][:, :], in_=acc[i][:, :], func=AF.Silu, bias=cb_sb[:, k:k + 1]),
                     reads=[("cv_a", i), "consts2"], writes=[("cv_o", i)])
                P.op("sp", dma(XCM[b, k, :, :], xco[i][:, :]), reads=[("cv_o", i)], chan="st")
                P.op("sp", dma(XCM[b, 8 + k, :, :], XMt[i][:, 2:T + 2]), reads=[("cv_x", i)], chan="st2")
        P.barrier()

    def phase_scans():
        arena_off[0] = CONST_END
        T2 = 2 * T
        LI = sb("sc_li", [36, T2], F32); LF = sb("sc_lf", [36, T2], F32); ONE = sb("sc_one", [36, T2], F32)
        Bt = sb("sc_b", [36, T2], F32); At = sb("sc_a", [36, T2], F32); Mt = sb("sc_m", [36, T2], F32); mt = sb("sc_mm", [36, T2], F32)
        P.op("dve", lambda e: e.memset(ONE[:, :], 1.0), writes=["sc_one"])
        for d in range(2):
            p0 = 32 * d
            ps_ = slice(p0, p0 + 4)
            for b in range(2):
                P.op("sp", dma(LI[ps_, b * T:(b + 1) * T], GRAW[b, 8 * d:8 * d + 4, :]), writes=[("sc_li", d)], chan="ld")
                P.op("sp", dma(LF[ps_, b * T:(b + 1) * T], GRAW[b, 8 * d + 4:8 * d + 8, :]), writes=[("sc_lf", d)], chan="ld")
            P.op("act", lambda e, ps_=ps_: e.activation(out=LF[ps_, :], in_=LF[ps_, :], func=AF.Exp, scale=-1.0),
                 reads=[("sc_lf", d)], writes=[("sc_lf", d)])
            P.op("act", lambda e, ps_=ps_: e.activation(out=LF[ps_, :], in_=LF[ps_, :], func=AF.Ln, bias=1.0, scale=1.0),
                 reads=[("sc_lf", d)], writes=[("sc_lf", d)])
            if d == 0:
                seg1 = lambda t, ps_=ps_: t[ps_, 0:T]
                seg2 = lambda t, ps_=ps_: t[ps_, T:T2][:, ::-1]
                last1 = lambda t, ps_=ps_: t[ps_, T - 1:T]
            else:
                seg1 = lambda t, ps_=ps_: t[ps_, T:T2]
                seg2 = lambda t, ps_=ps_: t[ps_, 0:T][:, ::-1]
                last1 = lambda t, ps_=ps_: t[ps_, T2 - 1:T2]
            P.op("dve", lambda e, seg1=seg1: e.tensor_tensor_scan(out=seg1(Bt), data0=seg1(ONE), data1=seg1(LF), initial=0.0,
                                                                  op0=ALU.mult, op1=ALU.subtract),
                 reads=[("sc_lf", d), "sc_one"], writes=[("sc_b", d)])
            P.op("dve", lambda e, seg2=seg2, last1=last1: e.tensor_tensor_scan(out=seg2(Bt), data0=seg2(ONE), data1=seg2(LF), initial=last1(Bt),
                                                                               op0=ALU.mult, op1=ALU.subtract),
                 reads=[("sc_lf", d), "sc_one", ("sc_b", d)], writes=[("sc_b", d)])
            P.op("dve", lambda e, ps_=ps_: e.tensor_tensor(out=At[ps_, :], in0=LI[ps_, :], in1=Bt[ps_, :], op=ALU.subtract),
                 reads=[("sc_li", d), ("sc_b", d)], writes=[("sc_a", d)])
            P.op("dve", lambda e, seg1=seg1: e.tensor_tensor_scan(out=seg1(Mt), data0=seg1(At), data1=seg1(At), initial=0.0,
                                                                  op0=ALU.max, op1=ALU.max),
                 reads=[("sc_a", d)], writes=[("sc_m", d)])
            P.op("dve", lambda e, seg2=seg2, last1=last1: e.tensor_tensor_scan(out=seg2(Mt), data0=seg2(At), data1=seg2(At), initial=last1(Mt),
                                                                               op0=ALU.max, op1=ALU.max),
                 reads=[("sc_a", d), ("sc_m", d)], writes=[("sc_m", d)])
            P.op("dve", lambda e, ps_=ps_: e.tensor_tensor(out=mt[ps_, :], in0=Bt[ps_, :], in1=Mt[ps_, :], op=ALU.add),
                 reads=[("sc_b", d), ("sc_m", d)], writes=[("sc_mm", d)])
            P.op("dve", lambda e, ps_=ps_: e.tensor_scalar(out=Mt[ps_, :], in0=Mt[ps_, :], scalar1=-1.0, scalar2=None, op0=ALU.mult),
                 reads=[("sc_m", d), ("sc_mm", d)], writes=[("sc_m", d)])
            P.op("dve", lambda e, ps_=ps_: e.tensor_scalar(out=mt[ps_, :], in0=mt[ps_, :], scalar1=-1.0, scalar2=None, op0=ALU.mult),
                 reads=[("sc_mm", d)], writes=[("sc_mm", d)])
            P.op("sp", dma(RW[d, 0, :, :], Mt[ps_, :]), reads=[("sc_m", d)], chan="st0")
            P.op("sp", dma(RW[d, 1, :, :], mt[ps_, :]), reads=[("sc_mm", d)], chan="st1")
            P.op("sp", dma(RW[d, 2, :, :], At[ps_, :]), reads=[("sc_a", d)], chan="st2")
        P.barrier()

    def phase_mix(h, d):
        arena_off[0] = CONST_END
        T2 = 2 * T
        Q = sb("mx_q", [128, 2, T], BF16); Kt = sb("mx_k", [128, 2, T2], BF16); V = sb("mx_v", [128, 34, 256], BF16)
        nM = sb("mx_nm", [128, T2], F32); en = sb("mx_en", [128, T2], F32); acol = sb("mx_ac", [128, 34], F32)
        mle = sb("mx_le", [128, 4, 512], F32); mge = sb("mx_ge", [128, 4, 512], F32)
        es = [sb(f"mx_es{i}", [128, 512], F32) for i in range(4)]
        at = [sb(f"mx_at{i}", [128, 512], BF16) for i in range(4)]
        xs = [sb(f"mx_xs{i}", [128, 512], F32) for i in range(4)]
        dn = [sb(f"mx_dn{i}", [128, 512], F32) for i in range(2)]
        ost = [sb(f"mx_os{i}", [128, 2, 512], F32) for i in range(2)]
        P.op("pool", dma(Q[:], QM[h, :, :, 0:T].rearrange("c p t -> p c t")), writes=["mx_q"], chan="ld")
        P.op("pool", dma(Kt[:], KM[h].rearrange("c p t -> p c t")), writes=["mx_k"], chan="ld")
        for b in range(2):
            P.op("sp", dma(V[:, b * 17:b * 17 + 16, :], VM[h, b * T:b * T + 2048, :].rearrange("(c p) m -> p c m", p=128)),
                 writes=[("mx_v", b)], chan="ld")
            P.op("sp", dma(V[0:8, b * 17 + 16, :], VM[h, b * T + 2048:b * T + 2056, :]), writes=[("mx_v", b, 1)], chan="ld")
            P.op("sp", dma(acol[:, b * 17:b * 17 + 16], RW[d, 2, h, b * T:b * T + 2048].rearrange("(c p) -> p c", p=128), True),
                 writes=[("mx_ac", b)], chan="ld")
            P.op("sp", dma(acol[0:8, b * 17 + 16:b * 17 + 17], RW[d, 2, h, b * T + 2048:b * T + 2056].rearrange("(p o) -> p o", o=1), True),
                 writes=[("mx_ac", b, 1)], chan="ld")
        P.op("sp", dma(nM[:], RW[d, 0, h:h + 1, :].partition_broadcast(128)), writes=["mx_nm"], chan="ld")
        P.op("sp", dma(en[:], RW[d, 1, h:h + 1, :].partition_broadcast(128)), writes=["mx_en"], chan="ld")
        P.op("sp", dma(mle[:], mask_le[:, :, :]), writes=["mx_le"], chan="ld")
        P.op("sp", dma(mge[:], mask_ge[:, :, :]), writes=["mx_ge"], chan="ld")
        P.op("act", lambda e: e.activation(out=en[:], in_=en[:], func=AF.Exp), reads=["mx_en"], writes=["mx_en"])
        vkeys = [("mx_v", 0), ("mx_v", 0, 1), ("mx_v", 1), ("mx_v", 1, 1)]
        akeys = [("mx_ac", 0), ("mx_ac", 0, 1), ("mx_ac", 1), ("mx_ac", 1, 1)]
        blocks = []
        gi = 0
        for qb in range(1):
            rel = "le" if d == qb else "ge"
            for (t0, nt) in ATILES:
                tq = qb * T + t0
                lst = []
                if rel == "ge":
                    ob = 1 - qb
                    for c, (s0, nk) in enumerate(CHUNKS):
                        lst.append((ob * 17 + c, ob * T + s0, nk, None))
                for c, (s0, nk) in enumerate(CHUNKS):
                    if rel == "le":
                        if s0 >= t0 + nt:
                            continue
                        mk = None if s0 < t0 else (mle, (s0 - t0) // 128)
                    else:
                        if s0 < t0:
                            continue
                        mk = None if s0 >= t0 + nt else (mge, (s0 - t0) // 128)
                    lst.append((qb * 17 + c, qb * T + s0, nk, mk))
                for li_, (ci, k0, nk, mk) in enumerate(lst):
                    blocks.append((gi, tq, nt, ci, k0, nk, mk, li_ == 0, li_ == len(lst) - 1))
                gi += 1
        DEP = 3
        pO0, pO1, pD = psum[4], psum[5], psum[6]

        def front(i, blk):
            g_, tq, nt, ci, k0, nk, mk, first, last = blk
            sbk = i % 4
            pi = i % 4
            for c2 in range(2):
                P.op("pe", mm(psum[sbk][0:nk, 0:nt], Kt[:, c2, k0:k0 + nk], Q[:, c2, tq:tq + nt], c2 == 0, c2 == 1),
                     reads=["mx_k", "mx_q"], writes=[("ps", sbk)])
            if mk is None:
                P.op("act", lambda e, pi=pi, nk=nk, nt=nt, tq=tq, ci=ci: e.activation(
                    out=es[pi][0:nk, 0:nt], in_=nM[0:nk, tq:tq + nt], func=AF.Exp, bias=acol[0:nk, ci:ci + 1]),
                     reads=["mx_nm"] + akeys, writes=[("mx_es", pi)])
            else:
                mt_, mj = mk
                P.op("dve", lambda e, pi=pi, nk=nk, nt=nt, tq=tq, ci=ci, mt_=mt_, mj=mj: e.scalar_tensor_tensor(
                    out=xs[pi][0:nk, 0:nt], in0=nM[0:nk, tq:tq + nt], scalar=acol[0:nk, ci:ci + 1], in1=mt_[0:nk, mj, 0:nt],
                    op0=ALU.add, op1=ALU.min),
                     reads=["mx_nm", "mx_le", "mx_ge"] + akeys, writes=[("mx_xs", pi)])
                P.op("act", lambda e, pi=pi, nk=nk, nt=nt: e.activation(
                    out=es[pi][0:nk, 0:nt], in_=xs[pi][0:nk, 0:nt], func=AF.Exp),
                     reads=[("mx_xs", pi)], writes=[("mx_es", pi)])
            P.op("dve", lambda e, sbk=sbk, pi=pi, nk=nk, nt=nt: e.scalar_tensor_tensor(
                out=at[pi][0:nk, 0:nt], in0=psum[sbk][0:nk, 0:nt], scalar=1.0 / 16.0, in1=es[pi][0:nk, 0:nt],
                op0=ALU.mult, op1=ALU.mult),
                 reads=[("ps", sbk), ("mx_es", pi)], writes=[("mx_at", pi)])

        def back(i, blk):
            g_, tq, nt, ci, k0, nk, mk, first, last = blk
            pi = i % 4
            P.op("pe", mm(pO0[:, 0:nt], V[0:nk, ci, 0:128], at[pi][0:nk, 0:nt], first, last),
                 reads=vkeys + [("mx_at", pi)], writes=[("ps", 4)])
            P.op("pe", mm(pO1[:, 0:nt], V[0:nk, ci, 128:256], at[pi][0:nk, 0:nt], first, last),
                 reads=vkeys + [("mx_at", pi)], writes=[("ps", 5)])
            P.op("pe", mm(pD[:, 0:nt], ones_bf[0:nk, :], at[pi][0:nk, 0:nt], first, last),
                 reads=["ones", ("mx_at", pi)], writes=[("ps", 6)])
            if last:
                oi = g_ % 2
                P.op("act", lambda e, oi=oi, nt=nt: e.activation(out=dn[oi][:, 0:nt], in_=pD[:, 0:nt], func=AF.Abs),
                     reads=[("ps", 6)], writes=[("mx_dn", oi)])
                P.op("dve", lambda e, oi=oi, nt=nt, tq=tq: e.tensor_tensor(
                    out=dn[oi][:, 0:nt], in0=dn[oi][:, 0:nt], in1=en[:, tq:tq + nt], op=ALU.max),
                     reads=[("mx_dn", oi), "mx_en"], writes=[("mx_dn", oi)])
                P.op("dve", lambda e, oi=oi, nt=nt: e.reciprocal(out=dn[oi][:, 0:nt], in_=dn[oi][:, 0:nt]),
                     reads=[("mx_dn", oi)], writes=[("mx_dn", oi)])
                P.op("dve", lambda e, oi=oi, nt=nt: e.tensor_tensor(out=ost[oi][:, 0, 0:nt], in0=pO0[:, 0:nt], in1=dn[oi][:, 0:nt], op=ALU.mult),
                     reads=[("ps", 4), ("mx_dn", oi)], writes=[("mx_os", oi)])
                P.op("dve", lambda e, oi=oi, nt=nt: e.tensor_tensor(out=ost[oi][:, 1, 0:nt], in0=pO1[:, 0:nt], in1=dn[oi][:, 0:nt], op=ALU.mult),
                     reads=[("ps", 5), ("mx_dn", oi)], writes=[("mx_os", oi)])
                P.op("sp", dma(HM[d, 2 * h:2 * h + 2, :, tq:tq + nt].rearrange("c p t -> p c t"), ost[oi][:, :, 0:nt]),
                     reads=[("mx_os", oi)], chan="st")

        for i in range(len(blocks) + DEP):
            if i < len(blocks):
                front(i, blocks[i])
            if i >= DEP:
                back(i - DEP, blocks[i - DEP])
        P.barrier()

    def phase_combine():
        arena_off[0] = CONST_END
        ha = [sb(f"cb_a{i}", [128, 8, 264], F32) for i in range(2)]
        hb = [sb(f"cb_b{i}", [128, 8, 264], F32) for i in range(2)]
        zz = [sb(f"cb_z{i}", [128, 8, 264], F32) for i in range(2)]
        for ti, (t0, nt) in enumerate(TILES):
            i = ti % 2
            P.op("sp", dma(ha[i][:, :, 0:nt], HM[0, :, :, t0:t0 + nt].rearrange("k p t -> p k t")), writes=[("cb_a", i)], chan="ld")
            P.op("sp", dma(hb[i][:, :, 0:nt], HM[1, :, :, t0:t0 + nt].rearrange("k p t -> p k t")), writes=[("cb_b", i)], chan="ld")
            P.op("sp", dma(zz[i][:, :, 0:nt], ZM[:, :, t0:t0 + nt].rearrange("k p t -> p k t")), writes=[("cb_z", i)], chan="ld")
            P.op("dve", lambda e, i=i, nt=nt: e.tensor_tensor(out=ha[i][:, :, 0:nt], in0=ha[i][:, :, 0:nt], in1=hb[i][:, :, 0:nt], op=ALU.add),
                 reads=[("cb_a", i), ("cb_b", i)], writes=[("cb_a", i)])
            P.op("act", lambda e, i=i, nt=nt: e.activation(out=zz[i][:, :, 0:nt], in_=zz[i][:, :, 0:nt], func=AF.Sigmoid),
                 reads=[("cb_z", i)], writes=[("cb_z", i)])
            P.op("dve", lambda e, i=i, nt=nt: e.tensor_tensor(out=ha[i][:, :, 0:nt], in0=ha[i][:, :, 0:nt], in1=zz[i][:, :, 0:nt], op=ALU.mult),
                 reads=[("cb_a", i), ("cb_z", i)], writes=[("cb_a", i)])
            P.op("sp", dma(HG[:, :, t0:t0 + nt].rearrange("k p t -> p k t"), ha[i][:, :, 0:nt]), reads=[("cb_a", i)], chan="st")
        P.barrier()

    def ph_mproj(h, b):
        xin = [XCM[b, 2 * h], XCM[b, 2 * h + 1]]
        if b == 0:
            phase_linear(xin, 2, w_q[h], [(0, 128), (128, 128)], lambda mi, t0, nt: QM[h, mi, :, b * T + t0:b * T + t0 + nt], toks=ATILES)
        phase_linear(xin, 2, w_k[h], [(0, 128), (128, 128)], lambda mi, t0, nt: KM[h, mi, :, b * T + t0:b * T + t0 + nt], toks=ATILES)
        phase_tokproj([XCM[b, 8 + 2 * h], XCM[b, 8 + 2 * h + 1]], 2, w_v[h], 256, VM[h, b * T:(b + 1) * T, :])

    UQ_CH = []
    for h in range(8):
        UQ_CH += [(h * 256, 128), (h * 256 + 128, 64), (h * 256 + 192, 64)]

    def uq_out(mi, t0, nt):
        h, j = mi // 3, mi % 3
        r0 = [0, 128, 192][j]
        rn = [128, 64, 64][j]
        return QALL[h, r0:r0 + rn, t0:t0 + nt]

    def ph_kv(b):
        phase_linear([gxa(b, 0), gxa(b, 1)], 2, w_ukvk, [(128 * h, 128) for h in range(8)],
                     lambda mi, t0, nt, b=b: KNA[mi, :, b * T + t0:b * T + t0 + nt])
        phase_tokproj([gxa(b, 0), gxa(b, 1)], 2, w_ukvv, 1024, VTA[b * T:(b + 1) * T, :])

    PH = [
        ("tin", phase_transpose_in),
        ("ffn1", lambda: phase_ffn("f1", XT, ZT, w1g, w1u, w1d)),
        ("ln1", lambda: phase_ln("l1", ZT, H1T, 0)),
        ("win", lambda: phase_linear(H1T, 16, w_in, WIN_CH, win_out, toks=ATILES)),
        ("nq", lambda: phase_norm(lambda t0, nt: UQ[:, :, t0:t0 + nt], 4, 4, gq_sb, RMS_EPS, lambda t0, nt: NQT[:, :, t0:t0 + nt])),
        ("nkv", lambda: phase_norm(lambda t0, nt: UKV[:, :, t0:t0 + nt], 2, 2, gkv_sb, RMS_EPS,
                                   lambda t0, nt: [gx(0)[:, t0:t0 + nt], gx(1)[:, t0:t0 + nt]])),
        ("krope", phase_krope),
        ("gather", phase_gather),
        ("uq", lambda: phase_linear(NQT, 4, w_uq, UQ_CH, uq_out)),
        ("kv0", lambda: ph_kv(0)),
        ("kv1", lambda: ph_kv(1)),
        ("attn", phase_mla_attn),
        ("nao", lambda: phase_norm(lambda t0, nt: OAT[:, :, t0:t0 + nt], 8, 8, gao_sb, RMS_EPS, lambda t0, nt: YT[0:8, :, t0:t0 + nt])),
        ("select", phase_select),
        ("conv", phase_conv),
        ("gates0", lambda: phase_linear(XCM[0], 16, w_gates, [(0, 16)], lambda mi, t0, nt: GRAW[0, :, t0:t0 + nt], bias_col=bg_sb)),
        ("gates1", lambda: phase_linear(XCM[1], 16, w_gates, [(0, 16)], lambda mi, t0, nt: GRAW[1, :, t0:t0 + nt], bias_col=bg_sb)),
        ("scans", phase_scans),
    ]
    for h_ in range(4):
        for b_ in range(2):
            PH.append((f"mproj{h_}{b_}", lambda h_=h_, b_=b_: ph_mproj(h_, b_)))
    for h_ in range(4):
        for d_ in range(2):
            PH.append((f"mix{h_}{d_}", lambda h_=h_, d_=d_: phase_mix(h_, d_)))
    PH += [
        ("combine", phase_combine),
        ("gnorm", lambda: phase_norm(lambda t0, nt: HG[:, :, t0:t0 + nt], 8, 2, gng_sb, LN_EPS, lambda t0, nt: YT[8:16, :, t0:t0 + nt],
                                     center=True, addin=(lambda t0, nt: XCM[0, 0:8, :, t0:t0 + nt], skip_sb))),
        ("wout", lambda: phase_linear(YT, 16, w_out, [(128 * i, 128) for i in range(16)], lambda mi, t0, nt: Z2T[mi, :, t0:t0 + nt], toks=ATILES,
                                      scale=1.0 / ALPHA, resid=lambda mi, t0, nt: H1T[mi, :, t0:t0 + nt])),
        ("ln2", lambda: phase_ln("l2", Z2T, H2T, 1)),
        ("ffn2", lambda: phase_ffn("f2", H2T, Z3T, w2g, w2u, w2d)),
        ("ln3", lambda: phase_ln("l3", Z3T, H3T, 2)),
    ]
    kstop = int(os.environ.get("KSTOP", "999"))
    kskip = os.environ.get("KSKIP", "").split(",")
    for i_, (nm_, f_) in enumerate(PH):
        if i_ < kstop and nm_ not in kskip:
            f_()
    if debug_out:
        for nm in debug_out:
            dump(nm, {"H1T": H1T, "UQ": UQ, "NQT": NQT, "GX0": gx(0), "GX10": gx(10), "GX2": gx(2), "GXA10": GXA_t[10].ap(), "QALL": QALL, "KNA": KNA, "VTA": VTA, "OAT": OAT, "YT": YT, "XCM": XCM, "GRAW": GRAW, "RW": RW, "HM": HM, "HG": HG, "H2T": H2T, "H3T": H3T, "Z2T": Z2T}[nm])
    for i_, (o_, a_) in enumerate(dumps):
        P.op("sp", dma(o_, a_), chan=f"st{i_ % 4}")
    phase_transpose_out(H3T)
    P.op("sp", lambda e: e.nop(), reads=[], writes=[])

    sem_names = [("e", e) for e in ENG] + [("c", c) for c in P.chan_names]
    assert len(sem_names) <= 95, len(sem_names)
    sems = {}
    for i, k in enumerate(sem_names):
        sems[k] = nc.alloc_semaphore(f"s{i}_{k[1]}")
    run = P.emit(sems)
    with nc.Block() as block:
        @block.tensor
        def _(e):
            run("pe", e)

        @block.scalar
        def _(e):
            run("act", e)

        @block.vector
        def _(e):
            run("dve", e)

        @block.gpsimd
        def _(e):
            run("pool", e)

        @block.sync
        def _(e):
            run("sp", e)
    return nc


_NC_CACHE = {}


def make_core_inputs(inputs):
    g = lambda k: np.ascontiguousarray(np.asarray(inputs[k], dtype=np.float32)[0])
    x = np.asarray(inputs["x"], dtype=np.float32)
    meta = np.asarray(inputs["meta_tokens"], dtype=np.float32)
    w_in = g("w_in")
    kr = w_in[:, 768:832]
    w_in_ext = np.ascontiguousarray(np.concatenate(
        [w_in[:, 0:768], w_in[:, 832:1856], w_in[:, 1856:2880], kr, kr[:, 32:], kr[:, :32]], axis=1))
    wuq = g("mla_w_uq")
    cols = []
    for h in range(8):
        b0 = h * 192
        cols += list(range(b0, b0 + 192)) + list(range(b0 + 160, b0 + 192)) + list(range(b0 + 128, b0 + 160))
    w_uq_ext = np.ascontiguousarray(wuq[:, cols])
    wukv = g("mla_w_ukv")
    kc = [h * 256 + j for h in range(8) for j in range(128)]
    vc = [h * 256 + 128 + j for h in range(8) for j in range(128)]
    w_ukvk = np.ascontiguousarray(wukv[:, kc])
    w_ukvv = np.ascontiguousarray(wukv[:, vc])
    inv = 10000.0 ** (-np.arange(0, 64, 2, dtype=np.float32) / 64.0)
    NEGM = -30000.0
    s_idx = np.arange(128)[:, None, None]
    j_idx = np.arange(4)[None, :, None]
    t_idx = np.arange(512)[None, None, :]
    mask_le = np.where(128 * j_idx + s_idx <= t_idx, 0.0, NEGM).astype(np.float32)
    mask_ge = np.where(128 * j_idx + s_idx >= t_idx, 0.0, NEGM).astype(np.float32)
    shared = {
        "ident": np.eye(128, dtype=np.float32),
        "w1g": g("ffn1_w_gate"), "w1u": g("ffn1_w_up"), "w1d": g("ffn1_w_down"),
        "ln1g": g("ln1_g"), "ln1b": g("ln1_b"),
        "w_in": w_in_ext, "gq": g("mla_q_norm_g"), "gkv": g("mla_kv_norm_g"), "gao": g("attn_out_g"),
        "w_uq": w_uq_ext, "w_ukvk": w_ukvk, "w_ukvv": w_ukvv,
        "w_out": g("w_out"), "ln2g": g("ln2_g"), "ln2b": g("ln2_b"), "ln3g": g("ln3_g"), "ln3b": g("ln3_b"),
        "w2g": g("ffn2_w_gate"), "w2u": g("ffn2_w_up"), "w2d": g("ffn2_w_down"),
        "conv_w": g("mlstm_conv_w"), "conv_b": g("mlstm_conv_b"),
        "w_q": g("mlstm_w_q"), "w_k": g("mlstm_w_k"), "w_v": g("mlstm_w_v"),
        "w_gates": g("mlstm_w_gates"), "b_gates": g("mlstm_b_gates"),
        "gn_g": g("mlstm_gn_g"), "skip": g("mlstm_skip"),
        "mask_le": mask_le, "mask_ge": mask_ge,
    }
    maps = []
    for c in range(NCORES):
        b, r = c // 2, c % 2
        if r == 0:
            xl = np.concatenate([meta, x[b, :T - 16]], axis=0)
            pos = np.arange(T, dtype=np.float32)
        else:
            xl = x[b, T - 16:][::-1]
            pos = (2 * T - 1 - np.arange(T)).astype(np.float32)
        ang = (pos[None, :] * inv[:, None]).astype(np.float32)
        cs, sn = np.cos(ang), np.sin(ang)
        om = np.zeros((128, 2), np.float32)
        om[:, r] = 1.0
        m = dict(shared)
        if r == 1:
            perm = list(range(8, 16)) + list(range(0, 8))
            m["w_gates"] = np.ascontiguousarray(shared["w_gates"][:, perm])
            m["b_gates"] = np.ascontiguousarray(shared["b_gates"][perm])
            m["conv_w"] = np.ascontiguousarray(shared["conv_w"][::-1])
        m["x"] = np.ascontiguousarray(xl)
        m["cos2"] = np.ascontiguousarray(np.concatenate([cs, cs], 0).astype(np.float32))
        m["sin2"] = np.ascontiguousarray(np.concatenate([-sn, sn], 0).astype(np.float32))
        m["om"] = om
        maps.append(m)
    return maps


def assemble(outs):
    B = NCORES // 2
    res = np.empty((B, 4096, D), np.float32)
    for c in range(NCORES):
        b, r = c // 2, c % 2
        o = outs[c]
        if r == 0:
            res[b, :T - 16] = o[16:]
        else:
            res[b, T - 16:] = o[::-1]
    return res


def kernel(**inputs):
    if "nc" not in _NC_CACHE:
        _NC_CACHE["nc"] = build_program()
    nc = _NC_CACHE["nc"]
    maps = make_core_inputs(inputs)
    res = run_bass_kernel_spmd(nc, maps, core_ids=list(range(NCORES)))
    outs = [np.asarray(r["out"]) for r in res.results]
    return assemble(outs)
```
